# Optimizing a Trainium2 kernel written in Bass

```python
import math
import jax, jax.numpy as jnp
from jax import lax
import numpy as np

D_MODEL = 1024
BATCH = 8
SEQ = 2048
DEPTH = 4
DEC_BATCH = 32
DEC_SEQ = 4
PAST_LEN = 8192
PAGE_SIZE = 128

F32 = jnp.float32
EPS = 1e-6
D_PLE = 256
D_REC = D_MODEL // 2
REC_HEADS = 8
REC_HEAD_DIM = D_REC // REC_HEADS
CONV_REC = 4
LRU_C = 8.0
D_POOL = D_MODEL // 2
POOL_WINDOWS = (2, 4, 8, 16)
POOL_GROUPS = len(POOL_WINDOWS)
POOL_GROUP_DIM = D_POOL // POOL_GROUPS
POOL_BUF = max(POOL_WINDOWS) - 1
N_HEADS = 8
HEAD_DIM = 128
N_KV_HEADS = 4
KV_GROUP = N_HEADS // N_KV_HEADS
IDX_HEADS = 8
IDX_DIM = 64
TOPK_MAX = 256
ROPE_THETA = 10000.0
Q_BLOCK = 128
ATTN_IN = N_HEADS * HEAD_DIM + 2 * N_KV_HEADS * HEAD_DIM + IDX_HEADS * IDX_DIM + IDX_DIM + IDX_HEADS
D_FF = 11 * D_MODEL // 4
CONV_FF = 3
N_REC_LAYERS = (DEPTH + 1) // 2
N_ATTN_LAYERS = DEPTH // 2

kernel_name = 'hawk_pool_dsa_convffn_ple_step'


def rmsnorm(x, g):
    xf = x.astype(F32)
    y = xf * lax.rsqrt(jnp.mean(xf * xf, axis=-1, keepdims=True) + EPS)
    return (y * g.astype(F32)).astype(x.dtype)


def causal_dwconv(x, buf, w, b):
    width = w.shape[0]
    T = x.shape[1]
    xe = jnp.concatenate([buf.astype(x.dtype), x], axis=1)
    y = xe[:, 0:T] * w[0]
    for k in range(1, width):
        y = y + xe[:, k:k + T] * w[k]
    return y + b, xe[:, xe.shape[1] - (width - 1):]


def rope(x, pos):
    half = x.shape[-1] // 2
    inv = jnp.power(ROPE_THETA, -jnp.arange(half, dtype=F32) / half)
    ang = pos[:, None] * inv[None, :]
    cos = jnp.cos(ang)[None, :, None, :]
    sin = jnp.sin(ang)[None, :, None, :]
    xf = x.astype(F32)
    x1, x2 = xf[..., :half], xf[..., half:]
    return jnp.concatenate([x1 * cos - x2 * sin, x1 * sin + x2 * cos], axis=-1).astype(x.dtype)


def rg_lru(x, h0, w_r, b_r, w_i, b_i, lam):
    B, T, C = x.shape
    xh = x.reshape(B, T, REC_HEADS, REC_HEAD_DIM)
    r = jax.nn.sigmoid((jnp.einsum('bthi,hij->bthj', xh, w_r).reshape(B, T, C) + b_r).astype(F32))
    g_in = jax.nn.sigmoid((jnp.einsum('bthi,hij->bthj', xh, w_i).reshape(B, T, C) + b_i).astype(F32))
    log_a = -LRU_C * r * jax.nn.softplus(-lam.astype(F32))
    a = jnp.exp(log_a)
    u = jnp.sqrt(-jnp.expm1(2.0 * log_a)) * g_in * x.astype(F32)

    def step(h, au):
        h = au[0] * h + au[1]
        return h, h

    h_last, hs = lax.scan(step, h0.astype(F32), (jnp.swapaxes(a, 0, 1), jnp.swapaxes(u, 0, 1)))
    return jnp.swapaxes(hs, 0, 1).astype(x.dtype), h_last.astype(h0.dtype)


def multiscale_pool(x, buf, pos0, w_pool, scale):
    B, T, C = x.shape
    xe = jnp.concatenate([buf.astype(x.dtype), x], axis=1)
    xf = xe.astype(F32)
    csum = jnp.concatenate([jnp.zeros((B, 1, C), F32), jnp.cumsum(xf, axis=1)], axis=1)
    end = csum[:, POOL_BUF + 1:]
    pos = pos0 + jnp.arange(T, dtype=F32)
    means = []
    for g, w in enumerate(POOL_WINDOWS):
        sl = slice(g * POOL_GROUP_DIM, (g + 1) * POOL_GROUP_DIM)
        start = csum[:, POOL_BUF + 1 - w:POOL_BUF + 1 - w + T, sl]
        cnt = jnp.minimum(jnp.float32(w), pos + 1.0)[None, :, None]
        means.append((end[..., sl] - start) / cnt)
    d = jnp.concatenate(means, axis=-1) - xf[:, POOL_BUF:]
    d = d.reshape(B, T, POOL_GROUPS, POOL_GROUP_DIM)
    y = jnp.einsum('btgc,gcd->btgd', d, w_pool.astype(F32)).reshape(B, T, C) * scale.astype(F32)
    return y.astype(x.dtype), xe[:, xe.shape[1] - POOL_BUF:]


def rec_pool_mixer(h, pos0, conv_buf, h0, pool_buf, norm_g, w_in, conv_w, conv_b,
                   w_r, b_r, w_i, b_i, lam, w_pool, pool_scale, w_out):
    xn = rmsnorm(h, norm_g)
    z = xn @ w_in
    xa, ga, xb = z[..., :D_REC], z[..., D_REC:2 * D_REC], z[..., 2 * D_REC:]
    xa_c, new_conv = causal_dwconv(xa, conv_buf, conv_w, conv_b)
    ya, h_last = rg_lru(xa_c, h0, w_r, b_r, w_i, b_i, lam)
    ya = ya * jax.nn.gelu(ga)
    yb, new_pool = multiscale_pool(xb, pool_buf, pos0, w_pool, pool_scale)
    y = jnp.concatenate([ya, yb], axis=-1) @ w_out
    return h + y, new_conv, h_last, new_pool


def attn_project(h, pos0, norm_g, w_in, q_norm, k_norm):
    B, T, _ = h.shape
    xn = rmsnorm(h, norm_g)
    z = xn @ w_in
    o1 = N_HEADS * HEAD_DIM
    o2 = o1 + N_KV_HEADS * HEAD_DIM
    o3 = o2 + N_KV_HEADS * HEAD_DIM
    o4 = o3 + IDX_HEADS * IDX_DIM
    o5 = o4 + IDX_DIM
    q = z[..., :o1].reshape(B, T, N_HEADS, HEAD_DIM)
    k = z[..., o1:o2].reshape(B, T, N_KV_HEADS, HEAD_DIM)
    v = z[..., o2:o3].reshape(B, T, N_KV_HEADS, HEAD_DIM)
    qi = z[..., o3:o4].reshape(B, T, IDX_HEADS, IDX_DIM)
    ki = z[..., o4:o5]
    wi = z[..., o5:]
    pos = pos0 + jnp.arange(T, dtype=F32)
    q = rope(rmsnorm(q, q_norm), pos)
    k = rope(rmsnorm(k, k_norm), pos)
    qi = rope(qi, pos)
    ki = rope(ki[:, :, None, :], pos)[:, :, 0]
    return q, k, v, qi, ki, wi


def index_scores(qi, wi, ki):
    s = jnp.einsum('bqhd,bsd->bqhs', qi.astype(F32), ki.astype(F32)) * (IDX_DIM ** -0.5)
    return jnp.einsum('bqhs,bqh->bqs', jax.nn.relu(s), wi.astype(F32) * (IDX_HEADS ** -0.5))


def gather_rows(a, idx):
    return jax.vmap(lambda ab, ib: ab[ib])(a, idx)


def sparse_attend(q, k_sel, v_sel, valid):
    B, Q = q.shape[0], q.shape[1]
    qg = q.reshape(B, Q, N_KV_HEADS, KV_GROUP, HEAD_DIM).astype(F32)
    s = jnp.einsum('bqhgd,bqkhd->bqhgk', qg, k_sel.astype(F32)) * (HEAD_DIM ** -0.5)
    s = jnp.where(valid[:, :, None, None, :], s, -jnp.inf)
    p = jax.nn.softmax(s, axis=-1)
    o = jnp.einsum('bqhgk,bqkhd->bqhgd', p, v_sel.astype(F32))
    return o.reshape(B, Q, N_HEADS * HEAD_DIM).astype(q.dtype)


def dsa_prompt(q, k, v, qi, ki, wi):
    B, T = q.shape[0], q.shape[1]
    topk = min(TOPK_MAX, T // 4)
    kpos = jnp.arange(T)

    def block(q0):
        qb = lax.dynamic_slice_in_dim(q, q0, Q_BLOCK, axis=1)
        qib = lax.dynamic_slice_in_dim(qi, q0, Q_BLOCK, axis=1)
        wib = lax.dynamic_slice_in_dim(wi, q0, Q_BLOCK, axis=1)
        qpos = q0 + jnp.arange(Q_BLOCK)
        sc = index_scores(qib, wib, ki)
        sc = jnp.where((kpos[None, :] <= qpos[:, None])[None], sc, -jnp.inf)
        _, idx = lax.top_k(sc, topk)
        valid = idx <= qpos[None, :, None]
        return sparse_attend(qb, gather_rows(k, idx), gather_rows(v, idx), valid)

    o = lax.map(block, jnp.arange(0, T, Q_BLOCK))
    return jnp.swapaxes(o, 0, 1).reshape(B, T, N_HEADS * HEAD_DIM)


def dsa_sample(q, k, v, qi, ki, wi, cache_k, cache_v, cache_kidx, page_table, layer):
    B, T = q.shape[0], q.shape[1]
    past = page_table.shape[1] * PAGE_SIZE
    L = past + T
    topk = min(TOPK_MAX, L // 4)
    ki_past = cache_kidx[layer, page_table].reshape(B, past, IDX_DIM)
    ki_all = jnp.concatenate([ki_past.astype(ki.dtype), ki], axis=1)
    qpos = past + jnp.arange(T)
    sc = index_scores(qi, wi, ki_all)
    sc = jnp.where((jnp.arange(L)[None, :] <= qpos[:, None])[None], sc, -jnp.inf)
    _, idx = lax.top_k(sc, topk)
    valid = idx <= qpos[None, :, None]
    in_past = idx < past
    pidx = jnp.minimum(idx, past - 1)
    phys = jax.vmap(lambda pt, pg: pt[pg])(page_table, pidx // PAGE_SIZE)
    off = pidx % PAGE_SIZE
    nidx = jnp.clip(idx - past, 0, T - 1)
    sel = in_past[..., None, None]
    k_sel = jnp.where(sel, cache_k[layer, phys, off].astype(k.dtype), gather_rows(k, nidx))
    v_sel = jnp.where(sel, cache_v[layer, phys, off].astype(v.dtype), gather_rows(v, nidx))
    return sparse_attend(q, k_sel, v_sel, valid)


def conv_ffn(h, buf, norm_g, w_up, conv_w, conv_b, w_down):
    xn = rmsnorm(h, norm_g)
    z = xn @ w_up
    g, u = z[..., :D_FF], z[..., D_FF:]
    gc, new_buf = causal_dwconv(g, buf, conv_w, conv_b)
    return h + (jax.nn.gelu(gc) * u) @ w_down, new_buf


def ple_add(h, p, norm_g, w_p, w_g):
    gate = jax.nn.sigmoid((rmsnorm(h, norm_g) @ w_g).astype(F32)).astype(h.dtype)
    return h + (p @ w_p) * gate


def setup_inputs(seed: int = 0) -> dict:
    key = jax.random.key(seed)
    ks = iter(jax.random.split(key, 48))

    def nrm(shape, scale):
        return jax.random.normal(next(ks), shape, F32) * scale

    NR, NA = N_REC_LAYERS, N_ATTN_LAYERS
    n_pages = PAST_LEN // PAGE_SIZE
    n_used = DEC_BATCH * n_pages
    n_pool = n_used + n_used // 4
    x_prompt = nrm((BATCH, SEQ, D_MODEL), 1.0)
    x_sample = nrm((DEC_BATCH, DEC_SEQ, D_MODEL), 1.0)
    cache_k = nrm((NA, n_pool, PAGE_SIZE, N_KV_HEADS, HEAD_DIM), 1.0)
    cache_v = nrm((NA, n_pool, PAGE_SIZE, N_KV_HEADS, HEAD_DIM), 1.0)
    cache_kidx = nrm((NA, n_pool, PAGE_SIZE, IDX_DIM), 1.0)
    state_rec_conv = nrm((NR, DEC_BATCH, CONV_REC - 1, D_REC), 1.0)
    state_rec_h = nrm((NR, DEC_BATCH, D_REC), 0.5)
    state_pool = nrm((NR, DEC_BATCH, POOL_BUF, D_POOL), 1.0)
    state_ffn_conv = nrm((DEPTH, DEC_BATCH, CONV_FF - 1, D_FF), 1.0)
    page_table = jax.random.permutation(next(ks), n_pool)[:n_used].reshape(DEC_BATCH, n_pages).astype(jnp.int32)
    p_prompt = nrm((DEPTH, BATCH, SEQ, D_PLE), 1.0)
    p_sample = nrm((DEPTH, DEC_BATCH, DEC_SEQ, D_PLE), 1.0)
    norm_mix = 1.0 + nrm((DEPTH, D_MODEL), 0.05)
    norm_ffn = 1.0 + nrm((DEPTH, D_MODEL), 0.05)
    norm_ple = 1.0 + nrm((DEPTH, D_MODEL), 0.05)
    w_in_rec = nrm((NR, D_MODEL, 2 * D_REC + D_POOL), D_MODEL ** -0.5)
    conv_rec_w = nrm((NR, CONV_REC, D_REC), CONV_REC ** -0.5)
    conv_rec_b = nrm((NR, D_REC), 0.02)
    w_rgate = nrm((NR, REC_HEADS, REC_HEAD_DIM, REC_HEAD_DIM), REC_HEAD_DIM ** -0.5)
    b_rgate = nrm((NR, D_REC), 0.02)
    w_igate = nrm((NR, REC_HEADS, REC_HEAD_DIM, REC_HEAD_DIM), REC_HEAD_DIM ** -0.5)
    b_igate = nrm((NR, D_REC), 0.02)
    a_c = jax.random.uniform(next(ks), (NR, D_REC), F32, 0.9, 0.999)
    s = a_c ** (1.0 / LRU_C)
    lru_lambda = jnp.log(s) - jnp.log1p(-s)
    w_pool = nrm((NR, POOL_GROUPS, POOL_GROUP_DIM, POOL_GROUP_DIM), POOL_GROUP_DIM ** -0.5)
    pool_scale = 1.0 + nrm((NR, D_POOL), 0.05)
    w_out_rec = nrm((NR, D_REC + D_POOL, D_MODEL), (D_REC + D_POOL) ** -0.5)
    w_in_attn = nrm((NA, D_MODEL, ATTN_IN), D_MODEL ** -0.5)
    q_norm = 1.0 + nrm((NA, HEAD_DIM), 0.05)
    k_norm = 1.0 + nrm((NA, HEAD_DIM), 0.05)
    w_out_attn = nrm((NA, N_HEADS * HEAD_DIM, D_MODEL), (N_HEADS * HEAD_DIM) ** -0.5)
    w_up = nrm((DEPTH, D_MODEL, 2 * D_FF), D_MODEL ** -0.5)
    conv_ff_w = nrm((DEPTH, CONV_FF, D_FF), CONV_FF ** -0.5)
    conv_ff_b = nrm((DEPTH, D_FF), 0.02)
    w_down = nrm((DEPTH, D_FF, D_MODEL), D_FF ** -0.5)
    w_ple = nrm((DEPTH, D_PLE, D_MODEL), D_PLE ** -0.5)
    w_ple_gate = nrm((DEPTH, D_MODEL, D_MODEL), D_MODEL ** -0.5)
    return {'x_prompt': x_prompt, 'x_sample': x_sample,
            'cache_k': cache_k, 'cache_v': cache_v, 'cache_kidx': cache_kidx,
            'state_rec_conv': state_rec_conv, 'state_rec_h': state_rec_h,
            'state_pool': state_pool, 'state_ffn_conv': state_ffn_conv,
            'page_table': page_table, 'p_prompt': p_prompt, 'p_sample': p_sample,
            'norm_mix': norm_mix, 'norm_ffn': norm_ffn, 'norm_ple': norm_ple,
            'w_in_rec': w_in_rec, 'conv_rec_w': conv_rec_w, 'conv_rec_b': conv_rec_b,
            'w_rgate': w_rgate, 'b_rgate': b_rgate, 'w_igate': w_igate, 'b_igate': b_igate,
            'lru_lambda': lru_lambda, 'w_pool': w_pool, 'pool_scale': pool_scale, 'w_out_rec': w_out_rec,
            'w_in_attn': w_in_attn, 'q_norm': q_norm, 'k_norm': k_norm, 'w_out_attn': w_out_attn,
            'w_up': w_up, 'conv_ff_w': conv_ff_w, 'conv_ff_b': conv_ff_b, 'w_down': w_down,
            'w_ple': w_ple, 'w_ple_gate': w_ple_gate}


def reference(x_prompt, x_sample, cache_k, cache_v, cache_kidx, state_rec_conv, state_rec_h,
              state_pool, state_ffn_conv, page_table, p_prompt, p_sample,
              norm_mix, norm_ffn, norm_ple, w_in_rec, conv_rec_w, conv_rec_b,
              w_rgate, b_rgate, w_igate, b_igate, lru_lambda, w_pool, pool_scale, w_out_rec,
              w_in_attn, q_norm, k_norm, w_out_attn, w_up, conv_ff_w, conv_ff_b, w_down,
              w_ple, w_ple_gate):
    B = x_prompt.shape[0]
    past = page_table.shape[1] * PAGE_SIZE
    dt = x_prompt.dtype
    hp, hs = x_prompt, x_sample
    rc_p, rc_s, rh_p, rh_s, pl_p, pl_s = [], [], [], [], [], []
    k_p, k_s, v_p, v_s, ki_p, ki_s = [], [], [], [], [], []
    fc_p, fc_s = [], []
    for i in range(DEPTH):
        j = i // 2
        if i % 2 == 0:
            wts = (norm_mix[i], w_in_rec[j], conv_rec_w[j], conv_rec_b[j], w_rgate[j], b_rgate[j],
                   w_igate[j], b_igate[j], lru_lambda[j], w_pool[j], pool_scale[j], w_out_rec[j])
            hp, c, hh, pb = rec_pool_mixer(hp, 0, jnp.zeros((B, CONV_REC - 1, D_REC), dt),
                                           jnp.zeros((B, D_REC), dt),
                                           jnp.zeros((B, POOL_BUF, D_POOL), dt), *wts)
            rc_p.append(c)
            rh_p.append(hh)
            pl_p.append(pb)
            hs, c, hh, pb = rec_pool_mixer(hs, past, state_rec_conv[j], state_rec_h[j], state_pool[j], *wts)
            rc_s.append(c)
            rh_s.append(hh)
            pl_s.append(pb)
        else:
            q, k, v, qi, ki, wi = attn_project(hp, 0, norm_mix[i], w_in_attn[j], q_norm[j], k_norm[j])
            hp = hp + dsa_prompt(q, k, v, qi, ki, wi) @ w_out_attn[j]
            k_p.append(k)
            v_p.append(v)
            ki_p.append(ki)
            q, k, v, qi, ki, wi = attn_project(hs, past, norm_mix[i], w_in_attn[j], q_norm[j], k_norm[j])
            hs = hs + dsa_sample(q, k, v, qi, ki, wi, cache_k, cache_v, cache_kidx, page_table, j) @ w_out_attn[j]
            k_s.append(k)
            v_s.append(v)
            ki_s.append(ki)
        fw = (norm_ffn[i], w_up[i], conv_ff_w[i], conv_ff_b[i], w_down[i])
        hp, fb = conv_ffn(hp, jnp.zeros((B, CONV_FF - 1, D_FF), dt), *fw)
        fc_p.append(fb)
        hs, fb = conv_ffn(hs, state_ffn_conv[i], *fw)
        fc_s.append(fb)
        hp = ple_add(hp, p_prompt[i], norm_ple[i], w_ple[i], w_ple_gate[i])
        hs = ple_add(hs, p_sample[i], norm_ple[i], w_ple[i], w_ple_gate[i])
    return (hp, hs,
            jnp.stack(rc_p), jnp.stack(rc_s), jnp.stack(rh_p), jnp.stack(rh_s),
            jnp.stack(pl_p), jnp.stack(pl_s),
            jnp.stack(k_p), jnp.stack(k_s), jnp.stack(v_p), jnp.stack(v_s),
            jnp.stack(ki_p), jnp.stack(ki_s),
            jnp.stack(fc_p), jnp.stack(fc_s))
```

```python
import numpy as np
import concourse.bass as bass
import concourse.mybir as mybir
from concourse.bass_utils import run_bass_kernel_spmd

F32 = mybir.dt.float32
BF16 = mybir.dt.bfloat16
I32 = mybir.dt.int32
U32 = mybir.dt.uint32
AF = mybir.ActivationFunctionType
OP = mybir.AluOpType
AX = mybir.AxisListType

D = 1024
NCH = 8
D_PLE = 256
D_REC = 512
D_POOL = 512
D_FF = 2816
NFF = 22
ATTN_IN = 2632
HD = 128
NH = 8
NKV = 4
IDX_H = 8
IDX_D = 64
EPS = 1e-6
NEG = -1.0e30


class Cfg:
    def __init__(self, T=2048, DEPTH=4, NPG=64, NPOOL=2560, NB=8, NSB=32):
        self.T = T
        self.DEPTH = DEPTH
        self.NPG = NPG
        self.NPOOL = NPOOL
        self.NB = NB
        self.NSB = NSB
        self.NS = 16
        self.TT = T + 16
        self.NR = (DEPTH + 1) // 2
        self.NA = DEPTH // 2
        self.PAST = NPG * 128
        self.NT = T // 128
        self.TOPK = min(256, T // 4)
        self.TOPK_S = min(256, (self.PAST + 4) // 4)


class Sched:
    ENG = ("pe", "act", "dve", "pool", "sp")

    def __init__(self, nc, stack):
        self.nc = nc
        self.stack = stack
        self.prog = {e: [] for e in self.ENG}
        self.sem = {e: stack.enter_context(nc.semaphore("s_" + e)) for e in self.ENG}
        self.cnt = {e: 0 for e in self.ENG}
        self.pending = {e: False for e in self.ENG}
        self.waited = {e: {} for e in self.ENG}
        self.last_w = {}
        self.readers = {}
        self.dsems = []
        self.ndma = 0

    def dma_sem(self, name):
        s = self.stack.enter_context(self.nc.semaphore("d_" + name))
        d = {"sem": s, "cnt": 0, "key": "d_" + name}
        self.dsems.append(d)
        return d

    def _need(self, eng, reads, writes, pe_accum):
        need = {}

        def add(tok, same_ok):
            key, sem, val = tok
            if key == eng and same_ok:
                return
            if need.get(key, (None, 0))[1] < val:
                need[key] = (sem, val)

        for r in reads:
            t = self.last_w.get(r)
            if t is not None:
                add(t, False)
        same_ok = (eng == "pe")
        for w in writes:
            t = self.last_w.get(w)
            if t is not None:
                add(t, same_ok)
            for t in self.readers.get(w, {}).values():
                add(t, same_ok)
        out = []
        wd = self.waited[eng]
        for key, (sem, val) in need.items():
            if wd.get(key, 0) >= val:
                continue
            wd[key] = val
            out.append((sem, val))
        return out

    def op(self, eng, fn, reads=(), writes=(), inc=True):
        waits = self._need(eng, reads, writes, False)
        for (sem, val) in waits:
            if sem is self.sem[eng] and val > self.cnt[eng]:
                raise RuntimeError("self-wait on pending increment (%s)" % eng)
        if inc:
            self.cnt[eng] += 1
            val = self.cnt[eng]
            self.pending[eng] = False
        else:
            val = self.cnt[eng] + 1
            self.pending[eng] = True
        tok = (eng, self.sem[eng], val)
        self.prog[eng].append((waits, fn, (self.sem[eng], 1) if inc else None))
        for r in reads:
            self.readers.setdefault(r, {})[eng] = tok
        for w in writes:
            self.last_w[w] = tok
            self.readers[w] = {}
        return tok

    NPOOL_SEM = 24

    def dma(self, eng, dsem, out_ap, in_ap, reads=(), writes=(), fn=None, **kw):
        if not hasattr(self, "_dpool"):
            self._dpool = {}
        if eng not in self._dpool:
            self._dpool[eng] = [self.dma_sem("p%s%d" % (eng, q)) for q in range(self.NPOOL_SEM)]
            self._drr = getattr(self, "_drr", {})
            self._drr[eng] = 0
        dsem = self._dpool[eng][self._drr[eng]]
        self._drr[eng] = (self._drr[eng] + 1) % self.NPOOL_SEM
        waits = self._need(eng, reads, writes, False)
        if dsem["cnt"] > self.waited[eng].get(dsem["key"], 0):
            waits.append((dsem["sem"], dsem["cnt"]))
            self.waited[eng][dsem["key"]] = dsem["cnt"]
        dsem["cnt"] += 16
        tok = (dsem["key"], dsem["sem"], dsem["cnt"])

        if fn is None:
            def fn(e, out_ap=out_ap, in_ap=in_ap, kw=kw):
                return e.dma_start(out=out_ap, in_=in_ap, **kw)

        self.prog[eng].append((waits, fn, (dsem["sem"], 16)))
        self.ndma += 1
        for r in reads:
            self.readers.setdefault(r, {})[dsem["key"]] = tok
        for w in writes:
            self.last_w[w] = tok
            self.readers[w] = {}
        return tok

    def raw(self, eng, fn, reads=(), writes=()):
        waits = self._need(eng, reads, writes, False)
        self.prog[eng].append((waits, fn, None))

    def finish(self):
        for e in self.ENG:
            if self.pending[e]:
                raise RuntimeError("pending ops without inc on " + e)
        waits = []
        for d in self.dsems:
            if d["cnt"]:
                waits.append((d["sem"], d["cnt"]))
        for e in self.ENG:
            if e != "sp" and self.cnt[e]:
                waits.append((self.sem[e], self.cnt[e]))
        self.prog["sp"].append((waits, None, None))

    def emit(self):
        nc = self.nc
        prog = self.prog

        def replay(name, e):
            for waits, fn, inc in prog[name]:
                for (sem, val) in waits:
                    e.wait_ge(sem, val)
                if fn is None:
                    continue
                ins = fn(e)
                if inc is not None:
                    ins.then_inc(inc[0], inc[1])

        with nc.Block() as block:
            @block.tensor
            def _(e):
                replay("pe", e)

            @block.scalar
            def _(e):
                replay("act", e)

            @block.vector
            def _(e):
                replay("dve", e)

            @block.gpsimd
            def _(e):
                replay("pool", e)

            @block.sync
            def _(e):
                replay("sp", e)


class Arena:
    def __init__(self, ap, size):
        self.ap = ap
        self.size = size
        self.top = 0

    def f32(self, n):
        a = self.top
        self.top += (n + 7) // 8 * 8
        assert self.top <= self.size, "arena overflow %d > %d" % (self.top, self.size)
        return self.ap[:, a:a + n]

    def bf16(self, n):
        w = (n + 1) // 2
        a = self.top
        self.top += (w + 7) // 8 * 8
        assert self.top <= self.size, "arena overflow %d > %d" % (self.top, self.size)
        return self.ap[:, a:a + w].bitcast(BF16)[:, 0:n]

    def i32(self, n):
        return self.f32(n).bitcast(I32)

    def mark(self):
        return self.top

    def reset(self, m):
        self.top = m


def vec_layout(cfg):
    off = {}
    n = 0

    def add(name, w):
        nonlocal n
        off[name] = n
        n += w

    for i in range(cfg.DEPTH):
        add("nmix%d" % i, 8)
        add("nffn%d" % i, 8)
        add("nple%d" % i, 8)
        add("cfw%d" % i, 3 * NFF)
        add("cfb%d" % i, NFF)
    for j in range(cfg.NR):
        add("crw%d" % j, 16)
        add("crb%d" % j, 4)
        add("brg%d" % j, 4)
        add("big%d" % j, 4)
        add("lam%d" % j, 4)
        add("psc%d" % j, 4)
    add("poolrc", 4 * 15)
    return off, n


class MK:
    def __init__(self, cfg):
        self.cfg = cfg
        self.nc = bass.Bass("TRN2", target_bir_lowering=False)
        self.voff, self.NV = vec_layout(cfg)

    def declare(self):
        nc, c = self.nc, self.cfg
        T, DP, NR, NA = c.T, c.DEPTH, c.NR, c.NA
        I = {}
        O = {}

        def inp(name, shape, dt=F32):
            I[name] = nc.dram_tensor(name, list(shape), dt, kind="ExternalInput").ap()

        def outp(name, shape):
            O[name] = nc.dram_tensor(name, list(shape), F32, kind="ExternalOutput").ap()

        inp("xp", [T, D]); inp("xs", [16, D])
        inp("pp", [DP, T, D_PLE]); inp("psm", [DP, 16, D_PLE])
        inp("st_rc", [NR, 12, D_REC]); inp("st_rh", [NR, 4, D_REC]); inp("st_pool", [NR, 60, D_POOL])
        inp("st_ffn", [DP, 8, D_FF])
        inp("vec", [128, self.NV])
        inp("w_in_rec", [NR, D, 1536]); inp("w_rg", [NR, 4, 128, 128]); inp("w_ig", [NR, 4, 128, 128])
        inp("w_pool", [NR, 4, 128, 128]); inp("w_out_rec", [NR, D, D])
        inp("w_up", [DP, D, 2 * D_FF]); inp("w_down", [DP, D_FF, D])
        inp("w_ple", [DP, D_PLE, D]); inp("w_pg", [DP, D, D])
        if NA:
            inp("ptab", [1, 4 * c.NPG], I32)
            inp("cache_k", [NA, c.NPOOL * 128, 512]); inp("cache_v", [NA, c.NPOOL * 128, 512])
            inp("cache_ki", [NA, c.NPOOL * 128, IDX_D])
            inp("w_in_attn", [NA, D, ATTN_IN]); inp("w_out_attn", [NA, D, D])
            inp("qkn", [NA, 2, HD])
            inp("rope", [c.NT + 1, 128, 2, 64]); inp("ropei", [c.NT + 1, 128, 2, 32])
        outp("y_p", [T, D]); outp("y_s", [16, D])
        outp("rc_p", [NR, 3, D_REC]); outp("rc_s", [NR, 12, D_REC])
        outp("rh_p", [NR, 1, D_REC]); outp("rh_s", [NR, 4, D_REC])
        outp("pl_p", [NR, 15, D_POOL]); outp("pl_s", [NR, 60, D_POOL])
        if NA:
            outp("k_p", [NA, T, 512]); outp("k_s", [NA, 16, 512])
            outp("v_p", [NA, T, 512]); outp("v_s", [NA, 16, 512])
            outp("ki_p", [NA, T, IDX_D]); outp("ki_s", [NA, 16, IDX_D])
            self.hscr = nc.dram_tensor("hscr", [128, NCH * c.TT], F32).ap()
        outp("fc_p", [DP, 2, D_FF]); outp("fc_s", [DP, 8, D_FF])
        self.I, self.O = I, O

    def coltiles(self):
        T = self.cfg.T
        return [(i * 512, 512) for i in range(T // 512)] + [(T, 16)]

    def ps_next(self):
        b = self.ps_rr
        self.ps_rr = (self.ps_rr + 1) % 6
        return self.PS[b], "ps%d" % b

    def pss_next(self):
        s = self.pss_rr
        self.pss_rr = (self.pss_rr + 1) % 32
        return self.PS[7][:, s * 16:(s + 1) * 16], "ps7"

    def wload(self, dst, src, region, nwords):
        k = self.k
        s = self.stage_rr
        self.stage_rr = (s + 1) % len(self.stage)
        st = self.stage[s][:, 0:nwords]
        shp = list(src.shape)
        if len(shp) == 3:
            stv = st.rearrange("p (a b) -> p a b", a=shp[1])
        else:
            stv = st
        k.dma("sp", self.dstage[s], stv, src, writes=["stage%d" % s])
        k.op("pool", lambda e, dst=dst, stv=stv: e.tensor_copy(out=dst, in_=stv), reads=["stage%d" % s], writes=[region])

    def vcol(self, name, i=0, n=1):
        o = self.voff[name] + i
        return self.vec[:, o:o + n]

    def norm(self, gname):
        k, c = self.k, self.cfg
        h, xn = self.h, self.xn
        m = self.ph.mark()
        rstd = self.ph.f32(c.TT)
        sqb = [self.ph.bf16(512), self.ph.bf16(512)]
        for ti, (c0, n) in enumerate(self.coltiles()):
            ps, psr = self.PS[6], "ps6"
            for ch in range(NCH):
                sb = sqb[ch % 2]
                src = h[:, ch, c0:c0 + n]
                k.op("pool", lambda e, sb=sb, src=src, n=n: e.tensor_tensor(out=sb[:, 0:n], in0=src, in1=src, op=OP.mult),
                     reads=["h%d_%d" % (ch, ti)], writes=["sqb%d" % (ch % 2)])
                k.op("pe", lambda e, ps=ps, sb=sb, n=n, ch=ch: e.matmul(ps[:, 0:n], lhsT=self.ones_b, rhs=sb[:, 0:n], start=(ch == 0), stop=(ch == NCH - 1)),
                     reads=["sqb%d" % (ch % 2), "ones_b"], writes=[psr], inc=True)
            rs = rstd[:, c0:c0 + n]
            k.op("act", lambda e, rs=rs, ps=ps, n=n: e.activation(out=rs, in_=ps[:, 0:n], func=AF.Sqrt, bias=self.eps_t[:, 0:1], scale=1.0 / D),
                 reads=[psr, "eps"], writes=["rstd%d" % ti])
            k.op("dve", lambda e, rs=rs: e.reciprocal(out=rs, in_=rs), reads=["rstd%d" % ti], writes=["rstd%d" % ti])
            for ch in range(NCH):
                g = self.vcol(gname, ch)
                k.op("dve", lambda e, ch=ch, c0=c0, n=n, g=g, rs=rs: e.scalar_tensor_tensor(
                    out=xn[:, ch, c0:c0 + n], in0=h[:, ch, c0:c0 + n], scalar=g, in1=rs, op0=OP.mult, op1=OP.mult),
                    reads=["h%d_%d" % (ch, ti), "rstd%d" % ti, "vec"], writes=["xn%d_%d" % (ch, ti)])
        self.barrier()
        self.ph.reset(m)

    def proj_chunk(self, lhs_of_kc, KC, rhs_of, rhs_reg_of, wregs, evac):
        k = self.k
        for ti, (c0, n) in enumerate(self.coltiles()):
            if n == 16:
                ps, psr = self.pss_next()
            else:
                ps, psr = self.ps_next()
            for kc in range(KC):
                lhs = lhs_of_kc(kc)
                rhs = rhs_of(kc, c0, n)
                k.op("pe", lambda e, ps=ps, lhs=lhs, rhs=rhs, n=n, kc=kc: e.matmul(ps[:, 0:n], lhsT=lhs, rhs=rhs, start=(kc == 0), stop=(kc == KC - 1)),
                     reads=list(wregs) + [rhs_reg_of(kc, ti)], writes=[psr], inc=(kc == KC - 1))
            evac(ti, c0, n, ps[:, 0:n], psr)

    def tr32(self, ps_ap, in_ap, reads, psr, npart):
        self.k.op("pe", lambda e: e.transpose(ps_ap, in_ap, self.ident_f[0:npart, 0:npart]),
                  reads=list(reads) + ["ident_f"], writes=[psr])

    def dbg(self, name, ap, reads):
        if not getattr(self.cfg, "debug", False):
            return
        t = self.nc.dram_tensor("dbg_" + name, list(ap.shape), ap.dtype, kind="ExternalOutput").ap()
        self.O["dbg_" + name] = t
        self.k.dma("sp", None, t, ap, reads=reads)

    def barrier(self):
        k = self.k
        for e in k.ENG:
            assert not k.pending[e]
        for e in k.ENG:
            waits = []
            for o in k.ENG:
                if k.cnt[o] > k.waited[e].get(o, 0):
                    waits.append((k.sem[o], k.cnt[o]))
                    k.waited[e][o] = k.cnt[o]
            for d in k.dsems:
                if d["cnt"] > k.waited[e].get(d["key"], 0):
                    waits.append((d["sem"], d["cnt"]))
                    k.waited[e][d["key"]] = d["cnt"]
            if waits:
                k.prog[e].append((waits, None, None))
        k.last_w.clear()
        k.readers.clear()

    def build(self, stack):
        nc, c = self.nc, self.cfg
        self.declare()
        k = self.k = Sched(nc, stack)
        AW = 52900
        A = nc.alloc_sbuf_tensor("arena", [128, AW], F32).ap()
        self.ph = Arena(A, AW)
        ph = self.ph
        self.PS = [nc.alloc_psum_tensor("psb%d" % b, [128, 512], F32).ap() for b in range(8)]
        self.ps_rr = 0
        self.pss_rr = 0
        TT = c.TT
        self.h = ph.f32(NCH * TT).rearrange("p (c t) -> p c t", c=NCH)
        self.xn = ph.bf16(NCH * TT).rearrange("p (c t) -> p c t", c=NCH)
        self.vec = ph.f32(self.NV)
        self.ident_f = ph.f32(128)
        self.ident_b = ph.bf16(128)
        self.ones_b = ph.bf16(128)
        self.eps_t = ph.f32(8)
        self.one_t = ph.f32(8)
        self.stage = [ph.f32(2048), ph.f32(2048)]
        self.stage_rr = 0
        self.dstage = [k.dma_sem("stg0"), k.dma_sem("stg1")]
        self.d_in = k.dma_sem("in")
        self.d_out = k.dma_sem("out")
        self.d_out2 = k.dma_sem("out2")
        self.dpin = [k.dma_sem("pin0"), k.dma_sem("pin1")]
        self.drp = [k.dma_sem("rp0"), k.dma_sem("rp1")]
        self.dhp = [k.dma_sem("hp0"), k.dma_sem("hp1")]
        self.hflat = self.ph.ap[:, 0:NCH * TT]
        self.base_mark = ph.mark()

        k.dma("sp", self.d_in, self.vec, self.I["vec"], writes=["vec"])
        k.op("pool", lambda e: e.memset(self.ident_f, 0.0), writes=["ident_f"])
        k.op("pool", lambda e: e.affine_select(out=self.ident_f, in_=self.ident_f, pattern=[[-1, 128]], compare_op=OP.not_equal,
                                               fill=1.0, base=0, channel_multiplier=1), reads=["ident_f"], writes=["ident_f"])
        k.op("pool", lambda e: e.tensor_copy(out=self.ident_b, in_=self.ident_f), reads=["ident_f"], writes=["ident_b"])
        k.op("pool", lambda e: e.memset(self.ones_b, 1.0), writes=["ones_b"])
        k.op("pool", lambda e: e.memset(self.eps_t, EPS), writes=["eps"])
        k.op("pool", lambda e: e.memset(self.one_t, 1.0), writes=["one_t"])

        self.load_x()
        stop = getattr(c, "stop", None)
        for i in range(c.DEPTH):
            if i % 2 == 0:
                self.rec_layer(i)
            else:
                self.attn_layer(i)
            if stop == "mix%d" % i:
                break
            self.ffn_layer(i)
            if stop == "ffn%d" % i:
                break
            self.ple_layer(i)
        self.store_y()
        k.finish()
        k.emit()

    def tok_tiles(self):
        c = self.cfg
        return [(t * 128, 128) for t in range(c.NT)] + [(c.T, 16)]

    def load_x(self):
        k, c, ph = self.k, self.cfg, self.ph
        m = ph.mark()
        xin = [ph.f32(1024), ph.f32(1024)]
        dx = [k.dma_sem("xin0"), k.dma_sem("xin1")]
        for tt, (t0, n) in enumerate(self.tok_tiles()):
            s = tt % 2
            src = self.I["xp"][t0:t0 + n, :] if n == 128 else self.I["xs"]
            k.dma("sp", dx[s], xin[s][0:n, :], src, writes=["xin%d" % s])
            ti = min(t0 // 512, len(self.coltiles()) - 1)
            for half in range(2):
                ps, psr = self.ps_next()
                for j in range(4):
                    ch = half * 4 + j
                    self.tr32(ps[:, j * 128:j * 128 + n], xin[s][0:n, ch * 128:(ch + 1) * 128], ["xin%d" % s], psr, n)
                dst = self.h[:, half * 4:half * 4 + 4, t0:t0 + n]
                srcp = ps.rearrange("p (a b) -> p a b", a=4)[:, :, 0:n]
                regs = ["h%d_%d" % (half * 4 + j, ti) for j in range(4)]
                if (tt + half) % 2 == 0:
                    k.op("act", lambda e, dst=dst, srcp=srcp: e.activation(out=dst, in_=srcp, func=AF.Copy), reads=[psr], writes=regs)
                else:
                    k.op("dve", lambda e, dst=dst, srcp=srcp: e.tensor_copy(out=dst, in_=srcp), reads=[psr], writes=regs)
        self.barrier()
        ph.reset(m)

    def store_y(self):
        k, c, ph = self.k, self.cfg, self.ph
        m = ph.mark()
        ost = [ph.f32(1024), ph.f32(1024)]
        for tt, (t0, n) in enumerate(self.tok_tiles()):
            s = tt % 2
            ti = min(t0 // 512, len(self.coltiles()) - 1)
            for half in range(2):
                ps, psr = self.ps_next()
                for j in range(4):
                    ch = half * 4 + j
                    self.tr32(ps[0:n, j * 128:(j + 1) * 128], self.h[:, ch, t0:t0 + n], ["h%d_%d" % (ch, ti)], psr, 128)
                dst = ost[s][0:n, half * 512:(half + 1) * 512]
                if (tt + half) % 2 == 0:
                    k.op("act", lambda e, dst=dst, ps=ps, n=n: e.activation(out=dst, in_=ps[0:n, :], func=AF.Copy), reads=[psr], writes=["ost%d" % s])
                else:
                    k.op("dve", lambda e, dst=dst, ps=ps, n=n: e.tensor_copy(out=dst, in_=ps[0:n, :]), reads=[psr], writes=["ost%d" % s])
            dstd = self.O["y_p"][t0:t0 + n, :] if n == 128 else self.O["y_s"]
            k.dma("sp", self.d_out if s == 0 else self.d_out2, dstd, ost[s][0:n, :], reads=["ost%d" % s])
        self.barrier()
        ph.reset(m)

    def store_fm(self, src3, nch, r, dst, reads, tag):
        k, ph = self.k, self.ph
        ot = ph.f32(nch * 128)
        for g0 in range(0, nch, 4):
            g1 = min(nch, g0 + 4)
            ps, psr = self.ps_next()
            for ch in range(g0, g1):
                self.tr32(ps[0:r, (ch - g0) * 128:(ch - g0 + 1) * 128], src3[:, ch, :], reads, psr, 128)
            w = (g1 - g0) * 128
            k.op("act", lambda e, g0=g0, w=w, ps=ps: e.activation(out=ot[0:r, g0 * 128:g0 * 128 + w], in_=ps[0:r, 0:w], func=AF.Copy),
                 reads=[psr], writes=[tag + "_ot"])
        k.dma("sp", self.d_out, dst, ot[0:r, :], reads=[tag + "_ot"])

    def load_fm(self, dst3, nch, r, src, tag):
        k, ph = self.k, self.ph
        it = ph.f32(nch * 128)
        k.dma("sp", self.d_in, it[0:r, :], src, writes=[tag + "_it"])
        for g0 in range(0, nch, 4):
            g1 = min(nch, g0 + 4)
            ps, psr = self.ps_next()
            for ch in range(g0, g1):
                self.tr32(ps[:, (ch - g0) * r:(ch - g0 + 1) * r], it[0:r, ch * 128:(ch + 1) * 128], [tag + "_it"], psr, r)
            w = (g1 - g0)
            k.op("act", lambda e, g0=g0, g1=g1, w=w, ps=ps: e.activation(out=dst3[:, g0:g1, :], in_=ps[:, 0:w * r].rearrange("p (a b) -> p a b", a=w), func=AF.Copy),
                 reads=[psr], writes=[tag])

    def ffn_layer(self, i):
        k, c, ph = self.k, self.cfg, self.ph
        T, TT = c.T, c.TT
        h, xn = self.h, self.xn
        self.norm("nffn%d" % i)
        m = ph.mark()
        fcs = ph.f32(NFF * 8).rearrange("p (a b) -> p a b", a=NFF)
        fct = ph.f32(NFF * 10).rearrange("p (a b) -> p a b", a=NFF)
        mm = ph.mark()
        self.load_fm(fcs, NFF, 8, self.I["st_ffn"][i], "fcs")
        self.barrier()
        ph.reset(mm)
        G = 4
        wup = [ph.bf16(2048).rearrange("p (a b) -> p a b", a=8) for _ in range(4)]
        wdn = ph.bf16(G * 1024).rearrange("p (a b) -> p a b", a=G)
        act = ph.bf16(G * TT).rearrange("p (a b) -> p a b", a=G)
        GE = 2 + T + 24
        gext = [ph.f32(GE), ph.f32(GE)]
        tmp = ph.f32(TT)
        gl = [ph.bf16(TT), ph.bf16(TT)]
        W = self.I["w_up"][i].rearrange("(kc p) n -> p kc n", p=128)
        Wd = self.I["w_down"][i]
        wrr = [0]

        def get_block(col0):
            sl = wrr[0]
            wrr[0] = (sl + 1) % 4
            self.wload(wup[sl], W[:, :, col0:col0 + 256], "wup%d" % sl, 2048)
            return sl

        groups = [list(range(g, min(g + G, NFF))) for g in range(0, NFF, G)]
        sg = su = 0
        for grp in groups:
            for jj, j in enumerate(grp):
                mb, r = divmod(j, 2)
                if r == 0:
                    sg = get_block(mb * 256)
                    su = get_block(D_FF + mb * 256)
                s = j % 2
                ge = gext[s]
                gs = ge[:, 2 + T:2 + T + 24].rearrange("p (a b) -> p a b", a=4)
                gr = "gext%d" % s
                k.op("pool", lambda e, ge=ge: e.memset(ge[:, 0:2], 0.0), writes=[gr])
                k.op("pool", lambda e, gs=gs, j=j: e.tensor_copy(out=gs[:, :, 0:2], in_=fcs[:, j, :].rearrange("p (a b) -> p a b", a=4)), writes=[gr])

                def evac_g(ti, c0, n, ps, psr, ge=ge, gs=gs, gr=gr):
                    if n == 16:
                        k.op("act", lambda e: e.activation(out=gs[:, :, 2:6], in_=ps.rearrange("p (a b) -> p a b", a=4), func=AF.Copy), reads=[psr], writes=[gr])
                    else:
                        k.op("act", lambda e: e.activation(out=ge[:, 2 + c0:2 + c0 + n], in_=ps, func=AF.Copy), reads=[psr], writes=[gr])

                self.proj_chunk(lambda kc, sg=sg, r=r: wup[sg][:, kc, r * 128:(r + 1) * 128], NCH,
                                lambda kc, c0, n: xn[:, kc, c0:c0 + n], lambda kc, ti: "xn%d_%d" % (kc, ti), ["wup%d" % sg], evac_g)
                w = [self.vcol("cfw%d" % i, kk * NFF + j) for kk in range(3)]
                b = self.vcol("cfb%d" % i, j)
                tp = tmp[:, 0:T]
                tsm = tmp[:, T:T + 16].rearrange("p (a b) -> p a b", a=4)
                k.op("dve", lambda e, ge=ge, w=w, b=b: e.tensor_scalar(out=tp, in0=ge[:, 0:T], scalar1=w[0], scalar2=b, op0=OP.mult, op1=OP.add),
                     reads=[gr, "vec"], writes=["tmp"])
                for kk in (1, 2):
                    k.op("dve", lambda e, ge=ge, w=w, kk=kk: e.scalar_tensor_tensor(out=tp, in0=ge[:, kk:kk + T], scalar=w[kk], in1=tp, op0=OP.mult, op1=OP.add),
                         reads=[gr, "tmp"], writes=["tmp"])
                k.op("dve", lambda e, gs=gs, w=w, b=b: e.tensor_scalar(out=tsm, in0=gs[:, :, 0:4], scalar1=w[0], scalar2=b, op0=OP.mult, op1=OP.add),
                     reads=[gr, "tmp"], writes=["tmp"])
                for kk in (1, 2):
                    k.op("dve", lambda e, gs=gs, w=w, kk=kk: e.scalar_tensor_tensor(out=tsm, in0=gs[:, :, kk:kk + 4], scalar=w[kk], in1=tsm, op0=OP.mult, op1=OP.add),
                         reads=[gr, "tmp"], writes=["tmp"])
                glr = "gl%d" % s
                k.op("act", lambda e, s=s: e.activation(out=gl[s], in_=tmp, func=AF.Gelu), reads=["tmp"], writes=[glr])
                k.op("pool", lambda e, ge=ge, j=j: e.tensor_copy(out=fct[:, j, 0:2], in_=ge[:, T:T + 2]), reads=[gr], writes=["fct"])
                k.op("pool", lambda e, gs=gs, j=j: e.tensor_copy(out=fct[:, j, 2:10].rearrange("p (a b) -> p a b", a=4), in_=gs[:, :, 4:6]), reads=[gr], writes=["fct"])

                def evac_u(ti, c0, n, ps, psr, s=s, jj=jj, glr=glr):
                    k.op("dve", lambda e: e.tensor_tensor(out=act[:, jj, c0:c0 + n], in0=gl[s][:, c0:c0 + n], in1=ps, op=OP.mult),
                         reads=[psr, glr], writes=["act%d_%d" % (jj, ti)])

                self.proj_chunk(lambda kc, su=su, r=r: wup[su][:, kc, r * 128:(r + 1) * 128], NCH,
                                lambda kc, c0, n: xn[:, kc, c0:c0 + n], lambda kc, ti: "xn%d_%d" % (kc, ti), ["wup%d" % su], evac_u)
            for jj, j in enumerate(grp):
                self.wload(wdn[:, jj, :], Wd[j * 128:(j + 1) * 128, :], "wdn%d" % jj, 1024)
            for oc in range(NCH):
                def evac_d(ti, c0, n, ps, psr, oc=oc):
                    hr = "h%d_%d" % (oc, ti)
                    k.op("dve", lambda e: e.tensor_tensor(out=h[:, oc, c0:c0 + n], in0=h[:, oc, c0:c0 + n], in1=ps, op=OP.add),
                         reads=[psr, hr], writes=[hr])

                self.proj_chunk(lambda kc, oc=oc: wdn[:, kc, oc * 128:(oc + 1) * 128], len(grp),
                                lambda kc, c0, n: act[:, kc, c0:c0 + n], lambda kc, ti: "act%d_%d" % (kc, ti),
                                ["wdn%d" % q for q in range(len(grp))], evac_d)
        self.barrier()
        ph.reset(mm)
        self.store_fm(fct[:, :, 0:2], NFF, 2, self.O["fc_p"][i], [], "fcp")
        self.store_fm(fct[:, :, 2:10], NFF, 8, self.O["fc_s"][i], [], "fcso")
        self.barrier()
        ph.reset(m)

    def ple_layer(self, i):
        k, c, ph = self.k, self.cfg, self.ph
        T, TT = c.T, c.TT
        h, xn = self.h, self.xn
        self.norm("nple%d" % i)
        m = ph.mark()
        pT = ph.bf16(2 * TT).rearrange("p (a b) -> p a b", a=2)
        pin = [ph.f32(256), ph.f32(256)]
        dpin = self.dpin
        nct = len(self.coltiles())
        for tt, (t0, n) in enumerate(self.tok_tiles()):
            s = tt % 2
            src = self.I["pp"][i, t0:t0 + n, :] if n == 128 else self.I["psm"][i]
            k.dma("sp", dpin[s], pin[s][0:n, :], src, writes=["pin%d" % s])
            ti = min(t0 // 512, nct - 1)
            ps, psr = self.ps_next()
            for kc in range(2):
                self.tr32(ps[:, kc * 128:kc * 128 + n], pin[s][0:n, kc * 128:(kc + 1) * 128], ["pin%d" % s], psr, n)
            dst = pT[:, :, t0:t0 + n]
            srcp = ps[:, 0:256].rearrange("p (a b) -> p a b", a=2)[:, :, 0:n]
            if tt % 2 == 0:
                k.op("act", lambda e, dst=dst, srcp=srcp: e.activation(out=dst, in_=srcp, func=AF.Copy), reads=[psr], writes=["pT%d" % ti])
            else:
                k.op("dve", lambda e, dst=dst, srcp=srcp: e.tensor_copy(out=dst, in_=srcp), reads=[psr], writes=["pT%d" % ti])
        wg = [ph.bf16(2048).rearrange("p (a b) -> p a b", a=8) for _ in range(2)]
        wp = [ph.bf16(512).rearrange("p (a b) -> p a b", a=2) for _ in range(2)]
        gt = [ph.f32(512), ph.f32(512)]
        t2 = [ph.f32(512), ph.f32(512)]
        Wg = self.I["w_pg"][i].rearrange("(kc p) n -> p kc n", p=128)
        Wp = self.I["w_ple"][i].rearrange("(kc p) n -> p kc n", p=128)
        q = 0
        for blk in range(4):
            s = blk % 2
            self.wload(wg[s], Wg[:, :, blk * 256:(blk + 1) * 256], "wg%d" % s, 2048)
            self.wload(wp[s], Wp[:, :, blk * 256:(blk + 1) * 256], "wp%d" % s, 512)
            for r in range(2):
                oc = blk * 2 + r
                for ti, (c0, n) in enumerate(self.coltiles()):
                    q ^= 1
                    if n == 16:
                        ps, psr = self.pss_next()
                        ps2, psr2 = self.pss_next()
                    else:
                        ps, psr = self.ps_next()
                        ps2, psr2 = self.ps_next()
                    for kc in range(NCH):
                        lhs = wg[s][:, kc, r * 128:(r + 1) * 128]
                        rhs = xn[:, kc, c0:c0 + n]
                        k.op("pe", lambda e, ps=ps, lhs=lhs, rhs=rhs, n=n, kc=kc: e.matmul(ps[:, 0:n], lhsT=lhs, rhs=rhs, start=(kc == 0), stop=(kc == NCH - 1)),
                             reads=["wg%d" % s, "xn%d_%d" % (kc, ti)], writes=[psr], inc=(kc == NCH - 1))
                    g = gt[q][:, 0:n]
                    k.op("act", lambda e, g=g, ps=ps, n=n: e.activation(out=g, in_=ps[:, 0:n], func=AF.Sigmoid), reads=[psr], writes=["gt%d" % q])
                    for kc in range(2):
                        lhs = wp[s][:, kc, r * 128:(r + 1) * 128]
                        rhs = pT[:, kc, c0:c0 + n]
                        k.op("pe", lambda e, ps2=ps2, lhs=lhs, rhs=rhs, n=n, kc=kc: e.matmul(ps2[:, 0:n], lhsT=lhs, rhs=rhs, start=(kc == 0), stop=(kc == 1)),
                             reads=["wp%d" % s, "pT%d" % ti], writes=[psr2], inc=(kc == 1))
                    tt2 = t2[q][:, 0:n]
                    k.op("dve", lambda e, tt2=tt2, g=g, ps2=ps2, n=n: e.tensor_tensor(out=tt2, in0=g, in1=ps2[:, 0:n], op=OP.mult),
                         reads=[psr2, "gt%d" % q], writes=["t2%d" % q])
                    hr = "h%d_%d" % (oc, ti)
                    hv = h[:, oc, c0:c0 + n]
                    k.op("pool", lambda e, hv=hv, tt2=tt2: e.tensor_tensor(out=hv, in0=hv, in1=tt2, op=OP.add), reads=["t2%d" % q, hr], writes=[hr])
        self.barrier()
        ph.reset(m)

    def rec_layer(self, i):
        k, c, ph = self.k, self.cfg, self.ph
        T, TT = c.T, c.TT
        h, xn = self.h, self.xn
        j = i // 2
        self.norm("nmix%d" % i)
        m = ph.mark()
        hist = ph.f32(4 * 76).rearrange("p (a b) -> p a b", a=4)
        tails = ph.f32(4 * 95).rearrange("p (a b) -> p a b", a=4)
        cneg = ph.f32(8)
        sm = [ph.f32(4) for _ in range(8)]
        mm = ph.mark()
        it = ph.f32(512)
        k.dma("sp", self.d_in, it[0:12, :], self.I["st_rc"][j], writes=["rit"])
        k.dma("sp", self.d_in, it[12:16, :], self.I["st_rh"][j], writes=["rit"])
        k.dma("sp", self.d_in, it[16:76, :], self.I["st_pool"][j], writes=["rit"])
        ps, psr = self.ps_next()
        for ch in range(4):
            self.tr32(ps[:, ch * 76:(ch + 1) * 76], it[0:76, ch * 128:(ch + 1) * 128], ["rit"], psr, 76)
        k.op("act", lambda e, ps=ps: e.activation(out=hist, in_=ps[:, 0:304].rearrange("p (a b) -> p a b", a=4), func=AF.Copy), reads=[psr], writes=["hist"])
        lam = self.vcol("lam%d" % j, 0, 4)
        al, x, z, z2, pl, rl, sp_, t_ = sm

        def dv(fn, reads, writes):
            k.op("dve", fn, reads=reads, writes=writes)

        k.op("act", lambda e: e.activation(out=al, in_=lam, func=AF.Abs), reads=["vec"], writes=["sm_al"])
        k.op("act", lambda e: e.activation(out=x, in_=al, func=AF.Exp, scale=-1.0), reads=["sm_al"], writes=["sm_x"])
        dv(lambda e: e.tensor_scalar(out=z, in0=x, scalar1=2.0, scalar2=None, op0=OP.add), ["sm_x"], ["sm_z"])
        dv(lambda e: e.reciprocal(out=z, in_=z), ["sm_z"], ["sm_z"])
        dv(lambda e: e.tensor_tensor(out=z, in0=z, in1=x, op=OP.mult), ["sm_z", "sm_x"], ["sm_z"])
        dv(lambda e: e.tensor_tensor(out=z2, in0=z, in1=z, op=OP.mult), ["sm_z"], ["sm_z2"])
        dv(lambda e: e.memset(pl, 1.0 / 17.0), [], ["sm_pl"])
        for cf in (15.0, 13.0, 11.0, 9.0, 7.0, 5.0, 3.0, 1.0):
            dv(lambda e: e.tensor_tensor(out=pl, in0=pl, in1=z2, op=OP.mult), ["sm_pl", "sm_z2"], ["sm_pl"])
            dv(lambda e, cf=cf: e.tensor_scalar(out=pl, in0=pl, scalar1=1.0 / cf, scalar2=None, op0=OP.add), ["sm_pl"], ["sm_pl"])
        dv(lambda e: e.tensor_tensor(out=pl, in0=pl, in1=z, op=OP.mult), ["sm_pl", "sm_z"], ["sm_pl"])
        dv(lambda e: e.tensor_scalar(out=rl, in0=lam, scalar1=-1.0, scalar2=0.0, op0=OP.mult, op1=OP.max), ["vec"], ["sm_rl"])
        dv(lambda e: e.scalar_tensor_tensor(out=sp_, in0=pl, scalar=2.0, in1=rl, op0=OP.mult, op1=OP.add), ["sm_pl", "sm_rl"], ["sm_sp"])
        dv(lambda e: e.tensor_scalar(out=cneg[:, 0:4], in0=sp_, scalar1=-8.0, scalar2=None, op0=OP.mult), ["sm_sp"], ["cneg"])
        dv(lambda e: e.tensor_scalar(out=cneg[:, 4:8], in0=sp_, scalar1=-16.0, scalar2=None, op0=OP.mult), ["sm_sp"], ["cneg"])
        self.barrier()
        ph.reset(mm)

        yab = ph.bf16(NCH * TT).rearrange("p (a b) -> p a b", a=NCH)
        wbf = [ph.bf16(2048).rearrange("p (a b) -> p a b", a=8) for _ in range(2)]
        wgt = ph.bf16(256)
        wpl = ph.bf16(128)
        B1 = ph.f32(T + 96)
        B2 = ph.f32(TT)
        B3 = ph.f32(TT)
        B4 = ph.f32(TT)
        Bh = ph.bf16(TT)
        ss2 = ph.f32(76).rearrange("p (a b) -> p a b", a=4)
        ss4 = ph.f32(76).rearrange("p (a b) -> p a b", a=4)
        Win = self.I["w_in_rec"][j].rearrange("(kc p) n -> p kc n", p=128)
        xnr = lambda kc, ti: "xn%d_%d" % (kc, ti)
        xnf = lambda kc, c0, n: xn[:, kc, c0:c0 + n]

        def v4(ap16):
            return ap16.rearrange("p (a b) -> p a b", a=4)

        for ch in range(4):
            blk, r = divmod(ch, 2)
            if r == 0:
                self.wload(wbf[0], Win[:, :, blk * 256:(blk + 1) * 256], "wbf0", 2048)
                self.wload(wbf[1], Win[:, :, 512 + blk * 256:512 + (blk + 1) * 256], "wbf1", 2048)
            B1s = B1[:, 3 + T:3 + T + 28].rearrange("p (a b) -> p a b", a=4)
            k.op("pool", lambda e: e.memset(B1[:, 0:3], 0.0), writes=["B1"])
            k.op("pool", lambda e, ch=ch, B1s=B1s: e.tensor_copy(out=B1s[:, :, 0:3], in_=hist[:, ch, 0:12].rearrange("p (a b) -> p a b", a=4)), reads=["hist"], writes=["B1"])

            def evac_xa(ti, c0, n, ps, psr, B1s=B1s):
                if n == 16:
                    k.op("act", lambda e: e.activation(out=B1s[:, :, 3:7], in_=v4(ps), func=AF.Copy), reads=[psr], writes=["B1"])
                else:
                    k.op("act", lambda e: e.activation(out=B1[:, 3 + c0:3 + c0 + n], in_=ps, func=AF.Copy), reads=[psr], writes=["B1"])

            self.proj_chunk(lambda kc, r=r: wbf[0][:, kc, r * 128:(r + 1) * 128], NCH, xnf, xnr, ["wbf0"], evac_xa)
            k.op("pool", lambda e, ch=ch: e.tensor_copy(out=tails[:, ch, 0:3], in_=B1[:, T:T + 3]), reads=["B1"], writes=["tails"])
            k.op("pool", lambda e, ch=ch, B1s=B1s: e.tensor_copy(out=tails[:, ch, 3:15].rearrange("p (a b) -> p a b", a=4), in_=B1s[:, :, 4:7]), reads=["B1"], writes=["tails"])
            w = [self.vcol("crw%d" % j, kk * 4 + ch) for kk in range(4)]
            b = self.vcol("crb%d" % j, ch)
            B2p = B2[:, 0:T]
            B2s = v4(B2[:, T:T + 16])
            dv(lambda e, w=w, b=b: e.tensor_scalar(out=B2p, in0=B1[:, 0:T], scalar1=w[0], scalar2=b, op0=OP.mult, op1=OP.add), ["B1", "vec"], ["B2"])
            for kk in (1, 2, 3):
                dv(lambda e, w=w, kk=kk: e.scalar_tensor_tensor(out=B2p, in0=B1[:, kk:kk + T], scalar=w[kk], in1=B2p, op0=OP.mult, op1=OP.add), ["B1", "B2"], ["B2"])
            dv(lambda e, w=w, b=b, B1s=B1s: e.tensor_scalar(out=B2s, in0=B1s[:, :, 0:4], scalar1=w[0], scalar2=b, op0=OP.mult, op1=OP.add), ["B1", "B2"], ["B2"])
            for kk in (1, 2, 3):
                dv(lambda e, w=w, kk=kk, B1s=B1s: e.scalar_tensor_tensor(out=B2s, in0=B1s[:, :, kk:kk + 4], scalar=w[kk], in1=B2s, op0=OP.mult, op1=OP.add), ["B1", "B2"], ["B2"])
            k.op("pool", lambda e: e.tensor_copy(out=Bh, in_=B2), reads=["B2"], writes=["Bh"])
            self.wload(wgt[:, 0:128], self.I["w_rg"][j, ch], "wgt", 128)
            self.wload(wgt[:, 128:256], self.I["w_ig"][j, ch], "wgt", 128)
            brg = self.vcol("brg%d" % j, ch)
            big = self.vcol("big%d" % j, ch)

            def evac_gate(dst, bias, reg):
                def f(ti, c0, n, ps, psr):
                    k.op("act", lambda e: e.activation(out=dst[:, c0:c0 + n], in_=ps, func=AF.Sigmoid, bias=bias), reads=[psr, "vec"], writes=[reg])
                return f

            bhf = lambda kc, c0, n: Bh[:, c0:c0 + n]
            self.proj_chunk(lambda kc: wgt[:, 0:128], 1, bhf, lambda kc, ti: "Bh", ["wgt"], evac_gate(B3, brg, "B3"))
            self.proj_chunk(lambda kc: wgt[:, 128:256], 1, bhf, lambda kc, ti: "Bh", ["wgt"], evac_gate(B4, big, "B4"))
            Ba = B1[:, 0:TT]
            c1 = cneg[:, ch:ch + 1]
            c2 = cneg[:, 4 + ch:5 + ch]
            k.op("act", lambda e, c1=c1: e.activation(out=Ba, in_=B3, func=AF.Exp, scale=c1), reads=["B3", "cneg", "tails"], writes=["B1"])
            k.op("act", lambda e, c2=c2: e.activation(out=B3, in_=B3, func=AF.Exp, scale=c2), reads=["B3", "cneg"], writes=["B3"])
            k.op("act", lambda e: e.activation(out=B3, in_=B3, func=AF.Sqrt, scale=-1.0, bias=self.one_t[:, 0:1]), reads=["B3"], writes=["B3"])
            dv(lambda e: e.tensor_tensor(out=B4, in0=B4, in1=B3, op=OP.mult), ["B3", "B4"], ["B4"])
            dv(lambda e: e.tensor_tensor(out=B4, in0=B4, in1=B2, op=OP.mult), ["B4", "B2", "Bh"], ["B4"])
            dv(lambda e: e.tensor_tensor_scan(out=B2[:, 0:T], data0=Ba[:, 0:T], data1=B4[:, 0:T], initial=0.0, op0=OP.mult, op1=OP.add), ["B1", "B4", "Bh"], ["B2"])
            for sq in range(4):
                a0 = T + 4 * sq
                dv(lambda e, a0=a0, sq=sq, ch=ch: e.tensor_tensor_scan(out=B2[:, a0:a0 + 4], data0=Ba[:, a0:a0 + 4], data1=B4[:, a0:a0 + 4],
                                                                     initial=hist[:, ch, 12 + sq:13 + sq], op0=OP.mult, op1=OP.add), ["B1", "B4", "hist", "B2"], ["B2"])
            k.op("pool", lambda e, ch=ch: e.tensor_copy(out=tails[:, ch, 15:16], in_=B2[:, T - 1:T]), reads=["B2"], writes=["tails"])
            k.op("pool", lambda e, ch=ch: e.tensor_copy(out=tails[:, ch, 16:20], in_=v4(B2[:, T:T + 16])[:, :, 3]), reads=["B2"], writes=["tails"])

            def evac_ga(ti, c0, n, ps, psr):
                k.op("act", lambda e: e.activation(out=B3[:, c0:c0 + n], in_=ps, func=AF.Gelu), reads=[psr], writes=["B3"])

            self.proj_chunk(lambda kc, r=r: wbf[1][:, kc, r * 128:(r + 1) * 128], NCH, xnf, xnr, ["wbf1"], evac_ga)
            for ti, (c0, n) in enumerate(self.coltiles()):
                k.op("pool", lambda e, ch=ch, c0=c0, n=n: e.tensor_tensor(out=yab[:, ch, c0:c0 + n], in0=B2[:, c0:c0 + n], in1=B3[:, c0:c0 + n], op=OP.mult),
                     reads=["B2", "B3"], writes=["yab%d_%d" % (ch, ti)])

        for g in range(4):
            wnd = 2 << g
            blk, r = divmod(g, 2)
            if r == 0:
                self.wload(wbf[0], Win[:, :, 1024 + blk * 256:1024 + (blk + 1) * 256], "wbf0", 2048)
            self.wload(wpl, self.I["w_pool"][j, g], "wpl", 128)
            B1s = B1[:, 15 + T:15 + T + 76].rearrange("p (a b) -> p a b", a=4)
            k.op("pool", lambda e: e.memset(B1[:, 0:15], 0.0), writes=["B1"])
            k.op("pool", lambda e, g=g, B1s=B1s: e.tensor_copy(out=B1s[:, :, 0:15], in_=hist[:, g, 16:76].rearrange("p (a b) -> p a b", a=4)), reads=["hist"], writes=["B1"])

            def evac_xb(ti, c0, n, ps, psr, B1s=B1s):
                if n == 16:
                    k.op("act", lambda e: e.activation(out=B1s[:, :, 15:19], in_=v4(ps), func=AF.Copy), reads=[psr], writes=["B1"])
                else:
                    k.op("act", lambda e: e.activation(out=B1[:, 15 + c0:15 + c0 + n], in_=ps, func=AF.Copy), reads=[psr], writes=["B1"])

            self.proj_chunk(lambda kc, r=r: wbf[0][:, kc, r * 128:(r + 1) * 128], NCH, xnf, xnr, ["wbf0"], evac_xb)
            k.op("pool", lambda e, g=g: e.tensor_copy(out=tails[:, g, 20:35], in_=B1[:, T:T + 15]), reads=["B1"], writes=["tails"])
            k.op("pool", lambda e, g=g, B1s=B1s: e.tensor_copy(out=tails[:, g, 35:95].rearrange("p (a b) -> p a b", a=4), in_=B1s[:, :, 4:19]), reads=["B1"], writes=["tails"])
            E = 15 + T
            src, srcs = B1, B1s
            bufs = [(B2, ss2, "B2"), (B3, ss4, "B3")]
            step = 1
            srcreg = "B1"
            for lv in range(g + 1):
                dst, dsts, dreg = bufs[lv % 2]
                v0 = 2 * step - 1
                k.op("pool", lambda e, dst=dst, src=src, step=step, v0=v0: e.tensor_tensor(out=dst[:, v0:E], in0=src[:, v0:E], in1=src[:, v0 - step:E - step], op=OP.add),
                     reads=[srcreg], writes=[dreg])
                k.op("pool", lambda e, dsts=dsts, srcs=srcs, step=step, v0=v0: e.tensor_tensor(out=dsts[:, :, v0:19], in0=srcs[:, :, v0:19], in1=srcs[:, :, v0 - step:19 - step], op=OP.add),
                     reads=[srcreg], writes=[dreg])
                src, srcs, srcreg = dst, dsts, dreg
                step *= 2
            S, Ss, Sreg = src, srcs, srcreg
            inv = 1.0 / wnd
            dv(lambda e, S=S, inv=inv: e.scalar_tensor_tensor(out=B4[:, 0:T], in0=S[:, 15:E], scalar=inv, in1=B1[:, 15:E], op0=OP.mult, op1=OP.subtract), [Sreg, "B1"], ["B4"])
            if wnd > 1:
                nf = wnd - 1
                rc = self.vcol("poolrc", g * 15, nf)
                dv(lambda e, S=S, rc=rc, nf=nf: e.tensor_tensor(out=B4[:, 0:nf], in0=S[:, 15:15 + nf], in1=rc, op=OP.mult), [Sreg, "vec", "B4"], ["B4"])
                dv(lambda e, nf=nf: e.tensor_tensor(out=B4[:, 0:nf], in0=B4[:, 0:nf], in1=B1[:, 15:15 + nf], op=OP.subtract), ["B4", "B1"], ["B4"])
            dv(lambda e, Ss=Ss, B1s=B1s, inv=inv: e.scalar_tensor_tensor(out=v4(B4[:, T:T + 16]), in0=Ss[:, :, 15:19], scalar=inv, in1=B1s[:, :, 15:19], op0=OP.mult, op1=OP.subtract),
               [Sreg, "B1", "B4"], ["B4"])
            k.op("pool", lambda e: e.tensor_copy(out=Bh, in_=B4), reads=["B4"], writes=["Bh"])
            psc = self.vcol("psc%d" % j, g)

            def evac_pl(ti, c0, n, ps, psr, g=g, psc=psc):
                k.op("act", lambda e: e.activation(out=yab[:, 4 + g, c0:c0 + n], in_=ps, func=AF.Identity, scale=psc), reads=[psr, "vec"], writes=["yab%d_%d" % (4 + g, ti)])

            self.proj_chunk(lambda kc: wpl, 1, lambda kc, c0, n: Bh[:, c0:c0 + n], lambda kc, ti: "Bh", ["wpl"], evac_pl)

        Wo = self.I["w_out_rec"][j].rearrange("(kc p) n -> p kc n", p=128)
        for blk in range(4):
            s = blk % 2
            self.wload(wbf[s], Wo[:, :, blk * 256:(blk + 1) * 256], "wbf%d" % s, 2048)
            for r in range(2):
                oc = blk * 2 + r

                def evac_o(ti, c0, n, ps, psr, oc=oc):
                    hr = "h%d_%d" % (oc, ti)
                    k.op("dve", lambda e: e.tensor_tensor(out=h[:, oc, c0:c0 + n], in0=h[:, oc, c0:c0 + n], in1=ps, op=OP.add), reads=[psr, hr], writes=[hr])

                self.proj_chunk(lambda kc, s=s, r=r: wbf[s][:, kc, r * 128:(r + 1) * 128], NCH,
                                lambda kc, c0, n: yab[:, kc, c0:c0 + n], lambda kc, ti: "yab%d_%d" % (kc, ti), ["wbf%d" % s], evac_o)
        self.barrier()
        ph.reset(mm)
        O = self.O
        self.store_fm(tails[:, :, 0:3], 4, 3, O["rc_p"][j], [], "o1")
        self.store_fm(tails[:, :, 3:15], 4, 12, O["rc_s"][j], [], "o2")
        self.store_fm(tails[:, :, 15:16], 4, 1, O["rh_p"][j], [], "o3")
        self.store_fm(tails[:, :, 16:20], 4, 4, O["rh_s"][j], [], "o4")
        self.store_fm(tails[:, :, 20:35], 4, 15, O["pl_p"][j], [], "o5")
        self.store_fm(tails[:, :, 35:95], 4, 60, O["pl_s"][j], [], "o6")
        self.barrier()
        ph.reset(m)

    def rot(self, banks):
        b = banks[self._rot % len(banks)]
        self._rot += 1
        return self.PS[b], "ps%d" % b

    def attn_layer(self, i):
        k, c, ph = self.k, self.cfg, self.ph
        T, TT, NT = c.T, c.TT, c.NT
        h, xn = self.h, self.xn
        j = i // 2
        I, O = self.I, self.O
        self._rot = 0
        RB = [0, 1, 2, 3]
        self.norm("nmix%d" % i)
        hflat = self.hflat
        k.dma("sp", self.d_out, self.hscr, hflat, reads=["h%d_%d" % (ch, ti) for ch in range(NCH) for ti in range(len(self.coltiles()))], writes=["hscr"])
        self.barrier()
        ha = Arena(self.ph.ap, NCH * TT)
        m = ph.mark()
        scale = 1.0 / float(np.sqrt(HD))
        NITER = 11

        def hbf(nel):
            w = ((nel + 1) // 2 + 7) // 8 * 8
            return ha.bf16(nel) if ha.top + w <= ha.size else ph.bf16(nel)

        def hf32(nel):
            w = (nel + 7) // 8 * 8
            return ha.f32(nel) if ha.top + w <= ha.size else ph.f32(nel)

        kT = hbf(NKV * TT).rearrange("p (a b) -> p a b", a=NKV)
        vb = hbf((NT + 1) * 512).rearrange("p (a b) -> p a b", a=NT + 1)
        kiT = hbf(TT)
        qT = hbf(NH * 512).rearrange("p (a b) -> p a b", a=NH)
        qiT = hbf(4 * 512).rearrange("p (a b) -> p a b", a=4)
        Isb = hf32(max(T, 256))
        maskq = hbf(T)
        gqk = ph.f32(256).rearrange("p (a b) -> p a b", a=2)
        negB = ph.f32(8)
        cmask = ph.bf16(4 * 512).rearrange("p (a b) -> p a b", a=4)
        pw2 = ph.f32(16)
        watt = [ph.bf16(4096).rearrange("p (a b) -> p a b", a=8) for _ in range(2)]
        zsb = [ph.f32(512), ph.f32(512)]
        outf = [ph.f32(512), ph.f32(512)]
        outb = [ph.bf16(512), ph.bf16(512)]
        rpt = [ph.f32(128).rearrange("p (a b) -> p a b", a=2) for _ in range(2)]
        rpit = [ph.f32(64).rearrange("p (a b) -> p a b", a=2) for _ in range(2)]
        tq = [ph.f32(256) for _ in range(4)]
        ssq = ph.f32(8)
        junk = ph.f32(128)
        drp = self.drp
        ha_mark2 = ha.mark()
        Wa = I["w_in_attn"][j].rearrange("(kc p) n -> p kc n", p=128)
        nct = len(self.coltiles())

        k.dma("sp", self.d_in, gqk, I["qkn"][j:j + 1].to_broadcast([128, 2, HD]), writes=["gqk"])
        k.op("dve", lambda e: e.tensor_reduce(out=ssq[:, 0:2], in_=gqk, axis=AX.X, op=OP.max, apply_absolute_value=True), reads=["gqk"], writes=["ssq"])
        k.op("dve", lambda e: e.tensor_tensor(out=negB[:, 0:1], in0=ssq[:, 0:1], in1=ssq[:, 1:2], op=OP.mult), reads=["ssq"], writes=["negB"])
        k.op("dve", lambda e: e.tensor_scalar(out=negB[:, 0:1], in0=negB[:, 0:1], scalar1=-float(np.sqrt(HD)), scalar2=None, op0=OP.mult), reads=["negB"], writes=["negB"])
        cmf = ph.f32(512)
        for pq in range(4):
            k.op("pool", lambda e: e.memset(cmf, 0.0), writes=["cmf"])
            blk = cmf[:, pq * 128:(pq + 1) * 128]
            k.op("pool", lambda e, blk=blk: e.affine_select(out=blk, in_=blk, pattern=[[-1, 128]], compare_op=OP.is_ge, fill=NEG, base=0, channel_multiplier=1), reads=["cmf"], writes=["cmf"])
            k.op("pool", lambda e, pq=pq: e.tensor_copy(out=cmask[:, pq, :], in_=cmf), reads=["cmf"], writes=["cmask"])
        for it in range(NITER):
            k.op("pool", lambda e, it=it: e.memset(pw2[:, it:it + 1], 0.5 ** (it + 1)), writes=["pw2"])

        def load_w(slot, col0, ncols):
            for hf in range(0, ncols, 256):
                w = min(256, ncols - hf)
                self.wload(watt[slot][:, :, hf:hf + w], Wa[:, :, col0 + hf:col0 + hf + w], "watt%d" % slot, 8 * w)

        def tm_proj(t0, n, slot, ncols):
            ps, psr = self.rot(RB)
            ti = min(t0 // 512, nct - 1)
            for kc in range(NCH):
                lhs = xn[:, kc, t0:t0 + n]
                rhs = watt[slot][:, kc, 0:ncols]
                k.op("pe", lambda e, ps=ps, lhs=lhs, rhs=rhs, n=n, ncols=ncols, kc=kc: e.matmul(ps[0:n, 0:ncols], lhsT=lhs, rhs=rhs, start=(kc == 0), stop=(kc == NCH - 1)),
                     reads=["watt%d" % slot, "xn%d_%d" % (kc, ti)], writes=[psr], inc=(kc == NCH - 1))
            return ps, psr

        def load_rope(tt, n):
            s = tt % 2
            k.dma("sp", drp[s], rpt[s][0:n], I["rope"][tt, 0:n], writes=["rpt%d" % s])
            k.dma("sp", drp[s], rpit[s][0:n], I["ropei"][tt, 0:n], writes=["rpit%d" % s])
            return s

        def bc(ap2, shape):
            return ap2.to_broadcast(shape)

        def rope_apply(src3, dst3, n, nh, half, cos, sin, sreg, dreg, extra_reads):
            x1, x2 = src3[:, :, 0:half], src3[:, :, half:2 * half]
            cb = cos.unsqueeze(1).to_broadcast([n, nh, half])
            sb = sin.unsqueeze(1).to_broadcast([n, nh, half])
            w = nh * half
            t1, t2, t3, t4 = [t[0:n, 0:w].rearrange("p (a b) -> p a b", a=nh) for t in tq]
            rd = [sreg] + list(extra_reads)
            k.op("dve", lambda e: e.tensor_tensor(out=t1, in0=x1, in1=cb, op=OP.mult), reads=rd, writes=["tq0"])
            k.op("pool", lambda e: e.tensor_tensor(out=t2, in0=x2, in1=sb, op=OP.mult), reads=rd, writes=["tq1"])
            k.op("pool", lambda e: e.tensor_tensor(out=t3, in0=x1, in1=sb, op=OP.mult), reads=rd, writes=["tq2"])
            k.op("dve", lambda e: e.tensor_tensor(out=t4, in0=x2, in1=cb, op=OP.mult), reads=rd, writes=["tq3"])
            k.op("dve", lambda e: e.tensor_tensor(out=dst3[:, :, 0:half], in0=t1, in1=t2, op=OP.subtract), reads=["tq0", "tq1"], writes=[dreg])
            k.op("pool", lambda e: e.tensor_tensor(out=dst3[:, :, half:2 * half], in0=t3, in1=t4, op=OP.add), reads=["tq2", "tq3"], writes=[dreg])

        def qk_post(ps, psr, n, gsel, rs_slot, q):
            z = zsb[q][0:n]
            zr = "zsb%d" % q
            k.op("act", lambda e: e.activation(out=z, in_=ps[0:n, :], func=AF.Copy), reads=[psr], writes=[zr])
            for hh in range(4):
                k.op("act", lambda e, hh=hh: e.activation(out=junk[0:n], in_=z[:, hh * 128:(hh + 1) * 128], func=AF.Square, accum_out=ssq[0:n, hh:hh + 1]),
                     reads=[zr], writes=["ssq", "junk"])
            k.op("act", lambda e: e.activation(out=ssq[0:n, 0:4], in_=ssq[0:n, 0:4], func=AF.Sqrt, bias=self.eps_t[0:n, 0:1], scale=1.0 / HD), reads=["ssq"], writes=["ssq"])
            k.op("dve", lambda e: e.reciprocal(out=ssq[0:n, 0:4], in_=ssq[0:n, 0:4]), reads=["ssq"], writes=["ssq"])
            z3 = z.rearrange("p (a b) -> p a b", a=4)
            k.op("dve", lambda e: e.tensor_tensor(out=z3, in0=z3, in1=ssq[0:n, 0:4].unsqueeze(2).to_broadcast([n, 4, HD]), op=OP.mult), reads=[zr, "ssq"], writes=[zr])
            k.op("pool", lambda e: e.tensor_tensor(out=z3, in0=z3, in1=gqk[0:n, gsel, :].unsqueeze(1).to_broadcast([n, 4, HD]), op=OP.mult), reads=[zr, "gqk"], writes=[zr])
            o3 = outf[q][0:n].rearrange("p (a b) -> p a b", a=4)
            rope_apply(z3, o3, n, 4, 64, rpt[rs_slot][0:n, 0, :], rpt[rs_slot][0:n, 1, :], zr, "outf%d" % q, ["rpt%d" % rs_slot])
            k.op("act", lambda e: e.activation(out=outb[q][0:n], in_=outf[q][0:n], func=AF.Copy), reads=["outf%d" % q], writes=["outb%d" % q])

        def transposes_to(dst3, src_b, n, nblk, reads, dreg):
            ps, psr = self.rot(RB)
            pb = ps.bitcast(BF16)
            for bq in range(nblk):
                k.op("pe", lambda e, bq=bq: e.transpose(pb[:, bq * 128:bq * 128 + n], src_b[0:n, bq * 128:(bq + 1) * 128], self.ident_b[0:n, 0:n]),
                     reads=list(reads) + ["ident_b"], writes=[psr])
            k.op("act", lambda e: e.activation(out=dst3, in_=pb[:, 0:nblk * 128].rearrange("p (a b) -> p a b", a=nblk)[:, :, 0:n], func=AF.Copy), reads=[psr], writes=[dreg])

        toks = self.tok_tiles()
        load_w(0, 1024, 512)
        for tt, (t0, n) in enumerate(toks):
            rs_slot = load_rope(tt, n)
            q = tt % 2
            ps, psr = tm_proj(t0, n, 0, 512)
            qk_post(ps, psr, n, 1, rs_slot, q)
            dst = O["k_p"][j, t0:t0 + n, :] if n == 128 else O["k_s"][j]
            k.dma("sp", self.d_out, dst, outf[q][0:n], reads=["outf%d" % q])
            transposes_to(kT[:, :, t0:t0 + n], outb[q], n, 4, ["outb%d" % q], "kT")
        load_w(1, 1536, 512)
        for tt, (t0, n) in enumerate(toks):
            q = tt % 2
            ps, psr = tm_proj(t0, n, 1, 512)
            k.op("act", lambda e, q=q, ps=ps, n=n: e.activation(out=outf[q][0:n], in_=ps[0:n, :], func=AF.Copy), reads=[psr], writes=["outf%d" % q])
            dst = O["v_p"][j, t0:t0 + n, :] if n == 128 else O["v_s"][j]
            k.dma("sp", self.d_out, dst, outf[q][0:n], reads=["outf%d" % q])
            k.op("pool", lambda e, q=q, tt=tt, n=n: e.tensor_copy(out=vb[0:n, tt, :], in_=outf[q][0:n]), reads=["outf%d" % q], writes=["vb"])
        load_w(0, 2560, 64)
        for tt, (t0, n) in enumerate(toks):
            rs_slot = load_rope(tt, n)
            q = tt % 2
            ps, psr = tm_proj(t0, n, 0, 64)
            z = zsb[q][0:n, 0:64]
            k.op("act", lambda e, z=z, ps=ps, n=n: e.activation(out=z, in_=ps[0:n, 0:64], func=AF.Copy), reads=[psr], writes=["zsb%d" % q])
            o1 = outf[q][0:n, 0:64]
            rope_apply(z.rearrange("p (a b) -> p a b", a=1), o1.rearrange("p (a b) -> p a b", a=1), n, 1, 32,
                       rpit[rs_slot][0:n, 0, :], rpit[rs_slot][0:n, 1, :], "zsb%d" % q, "outf%d" % q, ["rpit%d" % rs_slot])
            dst = O["ki_p"][j, t0:t0 + n, :] if n == 128 else O["ki_s"][j]
            k.dma("sp", self.d_out, dst, o1, reads=["outf%d" % q])
            k.op("act", lambda e, q=q, n=n, o1=o1: e.activation(out=outb[q][0:n, 0:64], in_=o1, func=AF.Copy), reads=["outf%d" % q], writes=["outb%d" % q])
            k.op("act", lambda e, q=q, n=n, o1=o1: e.activation(out=outb[q][0:n, 64:128], in_=o1, func=AF.Copy), reads=["outf%d" % q], writes=["outb%d" % q])
            transposes_to(kiT[:, t0:t0 + n].rearrange("p (a b) -> p a b", a=1), outb[q], n, 1, ["outb%d" % q], "kiT")

        wsc = ph.f32(8)
        wout = [ph.bf16(2048).rearrange("p (a b) -> p a b", a=8) for _ in range(2)]
        hp = [ph.f32(512), ph.f32(512)]
        ph_mark2 = ph.mark()
        wdiag = ph.bf16(IDX_H * 128).rearrange("p (a b) -> p a b", a=IDX_H)
        Rh = [ph.bf16(512), ph.bf16(512)]
        maskT = ph.bf16((T // 128) * 512).rearrange("p (a b) -> p a b", a=T // 128)
        Eb3 = [ph.bf16(512), ph.bf16(512), ph.bf16(512)]
        self._eb = 0
        rden = ph.f32(512)
        attnT = ph.bf16(NH * 512).rearrange("p (a b) -> p a b", a=NH)
        bs = [ph.f32(8) for _ in range(6)]
        wk = ph.f32(16)
        cjunk = maskq
        dhp = self.dhp
        Wo = I["w_out_attn"][j].rearrange("(kc p) n -> p kc n", p=128)
        hs3 = self.hscr.rearrange("p (c t) -> p c t", c=NCH)

        def q_side(tiles, nq):
            for cbi in range(2):
                load_w(cbi, cbi * 512, 512)
            for (li, tt, t0, n) in tiles:
                rs_slot = load_rope(tt, n)
                for cbi in range(2):
                    q = cbi
                    ps, psr = tm_proj(t0, n, cbi, 512)
                    qk_post(ps, psr, n, 0, rs_slot, q)
                    transposes_to(qT[:, cbi * 4:cbi * 4 + 4, li * 128:li * 128 + n], outb[q], n, 4, ["outb%d" % q], "qT")
            load_w(0, 2048, 512)
            for (li, tt, t0, n) in tiles:
                rs_slot = load_rope(tt, n)
                ps, psr = tm_proj(t0, n, 0, 512)
                z = zsb[0][0:n]
                k.op("act", lambda e, z=z, ps=ps, n=n: e.activation(out=z, in_=ps[0:n, :], func=AF.Copy), reads=[psr], writes=["zsb0"])
                o3 = outf[0][0:n].rearrange("p (a b) -> p a b", a=8)
                rope_apply(z.rearrange("p (a b) -> p a b", a=8), o3, n, 8, 32, rpit[rs_slot][0:n, 0, :], rpit[rs_slot][0:n, 1, :], "zsb0", "outf0", ["rpit%d" % rs_slot])
                k.op("act", lambda e, n=n: e.activation(out=outb[0][0:n], in_=outf[0][0:n], func=AF.Copy), reads=["outf0"], writes=["outb0"])
                transposes_to(qiT[:, :, li * 128:li * 128 + n], outb[0], n, 4, ["outb0"], "qiT")

        def w_side(t0, n):
            ps, psr = tm_proj(t0, n, 1, 8)
            k.op("act", lambda e, ps=ps, n=n: e.activation(out=wsc[0:n, 0:8], in_=ps[0:n, 0:8], func=AF.Copy, scale=float(IDX_D ** -0.5 * IDX_H ** -0.5)),
                 reads=[psr], writes=["wsc"])

        NC = T // 512
        for cq in range(NC):
            tiles = [(li, 4 * cq + li, (4 * cq + li) * 128, 128) for li in range(4)]
            q_side(tiles, 4)
            load_w(1, 2624, 8)
            for (li, tt, t0, n) in tiles:
                qt = tt
                w_side(t0, n)
                for hh in range(IDX_H):
                    k.op("pool", lambda e, hh=hh: e.tensor_scalar(out=wdiag[:, hh, :], in0=self.ident_f, scalar1=wsc[:, hh:hh + 1], scalar2=1.0, op0=OP.mult, op1=OP.mult),
                         reads=["wsc", "ident_f"], writes=["wdiag"])
                nkb = qt + 1
                nk = nkb * 128
                for kg in range((nkb + 3) // 4):
                    ncols = min(512, nk - kg * 512)
                    ia, iar = self.PS[4], "ps4"
                    diag_here = (qt // 4 == kg)
                    for hh in range(IDX_H):
                        hpair, par = divmod(hh, 2)
                        ps, psr = self.rot(RB)
                        lhs = qiT[par * 64:(par + 1) * 64, hpair, li * 128:(li + 1) * 128]
                        rhs = kiT[par * 64:(par + 1) * 64, kg * 512:kg * 512 + ncols]
                        k.op("pe", lambda e, ps=ps, lhs=lhs, rhs=rhs, ncols=ncols: e.matmul(ps[:, 0:ncols], lhsT=lhs, rhs=rhs, start=True, stop=True),
                             reads=["qiT", "kiT"], writes=[psr])
                        r = Rh[hh % 2]
                        if hh % 2 == 0:
                            k.op("act", lambda e, r=r, ps=ps, ncols=ncols: e.activation(out=r[:, 0:ncols], in_=ps[:, 0:ncols], func=AF.Relu), reads=[psr], writes=["Rh%d" % (hh % 2)])
                        else:
                            k.op("dve", lambda e, r=r, ps=ps, ncols=ncols: e.tensor_scalar(out=r[:, 0:ncols], in0=ps[:, 0:ncols], scalar1=0.0, scalar2=None, op0=OP.max), reads=[psr], writes=["Rh%d" % (hh % 2)])
                        last = (hh == IDX_H - 1) and not diag_here
                        k.op("pe", lambda e, ia=ia, hh=hh, r=r, ncols=ncols, last=last: e.matmul(ia[:, 0:ncols], lhsT=wdiag[:, hh, :], rhs=r[:, 0:ncols], start=(hh == 0), stop=last),
                             reads=["wdiag", "Rh%d" % (hh % 2)], writes=[iar], inc=True)
                    if diag_here:
                        pq = qt % 4
                        k.op("pe", lambda e, ia=ia, pq=pq, ncols=ncols: e.matmul(ia[:, 0:ncols], lhsT=self.ident_b, rhs=cmask[:, pq, 0:ncols], start=False, stop=True),
                             reads=["ident_b", "cmask"], writes=[iar])
                    k.op("act", lambda e, ia=ia, kg=kg, ncols=ncols: e.activation(out=Isb[:, kg * 512:kg * 512 + ncols], in_=ia[:, 0:ncols], func=AF.Copy), reads=[iar], writes=["Isb"])
                lo, wd, mid, cnt, tt_, thr = bs
                TK = c.TOPK
                if qt >= TK // 128:
                    k.op("dve", lambda e: e.tensor_reduce(out=lo[:, 0:1], in_=Isb[:, 0:TK], axis=AX.X, op=OP.min), reads=["Isb"], writes=["bs_lo"])
                    k.op("dve", lambda e, nk=nk: e.tensor_reduce(out=wd[:, 0:1], in_=Isb[:, 0:nk], axis=AX.X, op=OP.max), reads=["Isb"], writes=["bs_w"])
                    k.op("dve", lambda e: e.tensor_tensor(out=wd[:, 0:1], in0=wd[:, 0:1], in1=lo[:, 0:1], op=OP.subtract), reads=["bs_w", "bs_lo"], writes=["bs_w"])
                    k.op("dve", lambda e: e.tensor_scalar(out=wd[:, 0:1], in0=wd[:, 0:1], scalar1=1.0001, scalar2=1e-6, op0=OP.mult, op1=OP.add), reads=["bs_w"], writes=["bs_w"])
                    k.op("dve", lambda e: e.tensor_tensor(out=wk[:, 0:NITER], in0=pw2[:, 0:NITER], in1=wd[:, 0:1].to_broadcast([128, NITER]), op=OP.mult), reads=["bs_w", "pw2"], writes=["wk"])
                    k.op("dve", lambda e: e.tensor_tensor(out=mid[:, 0:1], in0=lo[:, 0:1], in1=wk[:, 0:1], op=OP.add), reads=["bs_lo", "wk"], writes=["bs_mid"])
                    for it in range(NITER):
                        k.op("dve", lambda e, nk=nk: e.tensor_scalar(out=cjunk[:, 0:nk], in0=Isb[:, 0:nk], scalar1=mid[:, 0:1], scalar2=None, op0=OP.is_ge, op1=OP.add, accum_out=cnt[:, 0:1]),
                             reads=["Isb", "bs_mid"], writes=["bs_cnt", "maskq"])
                        k.op("dve", lambda e: e.tensor_scalar(out=tt_[:, 0:1], in0=cnt[:, 0:1], scalar1=float(TK) - 0.5, scalar2=0.5, op0=OP.is_ge, op1=OP.subtract), reads=["bs_cnt"], writes=["bs_t"])
                        k.op("dve", lambda e, it=it: e.scalar_tensor_tensor(out=mid[:, 0:1], in0=tt_[:, 0:1], scalar=wk[:, it:it + 1], in1=mid[:, 0:1], op0=OP.mult, op1=OP.add), reads=["bs_t", "wk", "bs_mid"], writes=["bs_mid"])
                    k.op("dve", lambda e: e.scalar_tensor_tensor(out=lo[:, 0:1], in0=wk[:, NITER - 1:NITER], scalar=-0.5, in1=mid[:, 0:1], op0=OP.mult, op1=OP.add), reads=["bs_mid", "wk"], writes=["bs_lo"])
                    thr_ap, thr_reg = lo, "bs_lo"
                else:
                    k.op("dve", lambda e: e.memset(thr[:, 0:1], -1.0e29), writes=["bs_thr"])
                    thr_ap, thr_reg = thr, "bs_thr"
                k.op("dve", lambda e, nk=nk, thr_ap=thr_ap: e.tensor_scalar(out=maskq[:, 0:nk], in0=Isb[:, 0:nk], scalar1=thr_ap[:, 0:1], scalar2=-1.0e5, op0=OP.is_lt, op1=OP.mult), reads=["Isb", thr_reg], writes=["maskq"])
                for g0 in range(0, nkb, 8):
                    g1 = min(nkb, g0 + 8)
                    pm, pmr = self.PS[5], "ps5"
                    pmb = pm.bitcast(BF16)
                    for kb in range(g0, g1):
                        k.op("pe", lambda e, kb=kb, g0=g0, pmb=pmb: e.transpose(pmb[:, (kb - g0) * 128:(kb - g0 + 1) * 128], maskq[:, kb * 128:(kb + 1) * 128], self.ident_b),
                             reads=["maskq", "ident_b"], writes=[pmr])
                    nb = g1 - g0
                    k.op("act", lambda e, g0=g0, g1=g1, nb=nb, li=li, pmb=pmb: e.activation(out=maskT[:, g0:g1, li * 128:(li + 1) * 128], in_=pmb[:, 0:nb * 128].rearrange("p (a b) -> p a b", a=nb), func=AF.Copy),
                         reads=[pmr], writes=["maskT"])
            nkb_c = 4 * cq + 4
            ops, opr = self.PS[6], "ps6"
            dps, dpr = self.PS[7], "ps7"
            for hh in range(NH):
                kvh = hh // 2
                for kb in range(nkb_c):
                    c0 = max(0, kb - 4 * cq) * 128
                    ncol = 512 - c0
                    ps, psr = self.rot(RB)
                    lhs = kT[:, kvh, kb * 128:(kb + 1) * 128]
                    rhs = qT[:, hh, c0:512]
                    k.op("pe", lambda e, ps=ps, lhs=lhs, rhs=rhs, ncol=ncol: e.matmul(ps[:, 0:ncol], lhsT=lhs, rhs=rhs, start=True, stop=False), reads=["kT", "qT"], writes=[psr], inc=False)
                    k.op("pe", lambda e, ps=ps, kb=kb, c0=c0, ncol=ncol: e.matmul(ps[:, 0:ncol], lhsT=self.ident_b, rhs=maskT[:, kb, c0:512], start=False, stop=True), reads=["ident_b", "maskT"], writes=[psr])
                    es = self._eb % 3
                    self._eb += 1
                    pb_ = Eb3[es]
                    k.op("act", lambda e, pb_=pb_, ps=ps, ncol=ncol: e.activation(out=pb_[:, 0:ncol], in_=ps[:, 0:ncol], func=AF.Exp, scale=scale, bias=negB[:, 0:1]),
                         reads=[psr, "negB"], writes=["Eb%d" % es])
                    vl = vb[:, kb, kvh * 128:(kvh + 1) * 128]
                    last = (kb == nkb_c - 1)
                    k.op("pe", lambda e, vl=vl, pb_=pb_, c0=c0, ncol=ncol, kb=kb, last=last: e.matmul(ops[:, c0:512], lhsT=vl, rhs=pb_[:, 0:ncol], start=(kb == 0), stop=last),
                         reads=["vb", "Eb%d" % es], writes=[opr], inc=True)
                    k.op("pe", lambda e, pb_=pb_, c0=c0, ncol=ncol, kb=kb, last=last: e.matmul(dps[:, c0:512], lhsT=self.ones_b, rhs=pb_[:, 0:ncol], start=(kb == 0), stop=last),
                         reads=["ones_b", "Eb%d" % es], writes=[dpr], inc=True)
                k.op("dve", lambda e: e.reciprocal(out=rden, in_=dps), reads=[dpr], writes=["rden"])
                k.op("dve", lambda e, hh=hh: e.tensor_tensor(out=attnT[:, hh, :], in0=ops, in1=rden, op=OP.mult), reads=[opr, "rden"], writes=["attnT"])
            for blk in range(4):
                s = blk % 2
                self.wload(wout[s], Wo[:, :, blk * 256:(blk + 1) * 256], "wout%d" % s, 2048)
                for r in range(2):
                    oc = blk * 2 + r
                    q = oc % 2
                    k.dma("sp", dhp[q], hp[q], hs3[:, oc, cq * 512:(cq + 1) * 512], reads=["hscr%d_%d" % (oc, cq)], writes=["hp%d" % q])
                    ps, psr = self.rot(RB)
                    for hh in range(NH):
                        lhs = wout[s][:, hh, r * 128:(r + 1) * 128]
                        rhs = attnT[:, hh, :]
                        k.op("pe", lambda e, ps=ps, lhs=lhs, rhs=rhs, hh=hh: e.matmul(ps, lhsT=lhs, rhs=rhs, start=(hh == 0), stop=(hh == NH - 1)),
                             reads=["wout%d" % s, "attnT"], writes=[psr], inc=(hh == NH - 1))
                    k.op("dve", lambda e, q=q, ps=ps: e.tensor_tensor(out=hp[q], in0=hp[q], in1=ps, op=OP.add), reads=[psr, "hp%d" % q], writes=["hp%d" % q])
                    k.dma("sp", dhp[q], hs3[:, oc, cq * 512:(cq + 1) * 512], hp[q], reads=["hp%d" % q], writes=["hscr%d_%d" % (oc, cq)])

        self.attn_sample(i, locals())

        self.barrier()
        k.dma("sp", self.d_in, hflat, self.hscr, reads=["hscr"], writes=["h%d_%d" % (ch, ti) for ch in range(NCH) for ti in range(len(self.coltiles()))])
        self.barrier()
        ph.reset(m)

    def attn_sample(self, i, L):
        k, c, ph = self.k, self.cfg, self.ph
        T, TT, NT, NPG, PAST = c.T, c.TT, c.NT, c.NPG, c.PAST
        j = i // 2
        I, O = self.I, self.O
        kT, vb, kiT, qT, qiT = L["kT"], L["vb"], L["kiT"], L["qT"], L["qiT"]
        ha = L["ha"]
        negB, wsc, outb, outf, zsb = L["negB"], L["wsc"], L["outb"], L["outf"], L["zsb"]
        load_w, tm_proj, qk_post, transposes_to, rope_apply, load_rope = L["load_w"], L["tm_proj"], L["qk_post"], L["transposes_to"], L["rope_apply"], L["load_rope"]
        rpit = L["rpit"]
        scale = L["scale"]
        NITER = L["NITER"]
        pw2 = L["pw2"]
        wout, hp, dhp, Wo, hs3 = L["wout"], L["hp"], L["dhp"], L["Wo"], L["hs3"]
        RB = [0, 1, 2, 3]
        NL = PAST + 4
        self.barrier()
        ph.reset(L["ph_mark2"])
        ha.reset(L["ha_mark2"])
        qTs = ph.bf16(NH * 16).rearrange("p (b h t) -> p b h t", b=4, h=NH)
        qiTd = ph.bf16(NH * 16).rearrange("p (b h t) -> p b h t", b=4, h=NH)
        qidup = ph.bf16(NH * 128).rearrange("p (a b) -> p a b", a=NH)
        wbs = ph.f32(32).rearrange("p (a b) -> p a b", a=4)
        Xw = ph.f32(32).rearrange("p (a b) -> p a b", a=8)
        wsel = ph.bf16(16).rearrange("p (a b) -> p a b", a=4)
        ptb = ph.i32(4 * NPG)
        ptf = ph.f32(4 * NPG)
        iot = ph.f32(8)
        idxi = ph.i32(4 * NPG)
        cm4 = ph.bf16(8)
        cm4f = ph.f32(8)
        bmk = ph.f32(16)
        Mfull = ph.bf16(16)
        MfT = ph.bf16(16)
        attnTs = ph.bf16(NH * 16).rearrange("p (a b) -> p a b", a=NH)
        rs_slot = load_rope(NT, 16)
        for cbi in range(2):
            load_w(cbi, cbi * 512, 512)
        for cbi in range(2):
            ps, psr = tm_proj(T, 16, cbi, 512)
            qk_post(ps, psr, 16, 0, rs_slot, cbi)
            transposes_to(qT[:, cbi * 4:cbi * 4 + 4, 0:16], outb[cbi], 16, 4, ["outb%d" % cbi], "qT")
        k.op("act", lambda e: e.activation(out=qTs, in_=qT[:, :, 0:16].rearrange("p h (b t) -> p b h t", b=4), func=AF.Copy), reads=["qT"], writes=["qTs"])
        load_w(0, 2048, 512)
        ps, psr = tm_proj(T, 16, 0, 512)
        z = zsb[0][0:16]
        k.op("act", lambda e, ps=ps: e.activation(out=z, in_=ps[0:16, :], func=AF.Copy), reads=[psr], writes=["zsb0"])
        o3 = outf[0][0:16].rearrange("p (a b) -> p a b", a=8)
        rope_apply(z.rearrange("p (a b) -> p a b", a=8), o3, 16, 8, 32, rpit[rs_slot][0:16, 0, :], rpit[rs_slot][0:16, 1, :], "zsb0", "outf0", ["rpit%d" % rs_slot])
        k.op("act", lambda e: e.activation(out=qidup[0:16, :, 0:64], in_=o3, func=AF.Copy), reads=["outf0"], writes=["qidup"])
        k.op("act", lambda e: e.activation(out=qidup[0:16, :, 64:128], in_=o3, func=AF.Copy), reads=["outf0"], writes=["qidup"])
        self.dbg("watt0", L["watt"][0], ["watt0"])
        self.dbg("xns", self.xn[:, :, T:T + 16], [])
        self.dbg("zsb0", zsb[0][0:16], ["zsb0"])
        self.dbg("rpit", rpit[rs_slot][0:16], ["rpit%d" % rs_slot])
        self.dbg("outf0", outf[0][0:16], ["outf0"])
        self.dbg("qidup", qidup[0:16], ["qidup"])
        pq_, pqr = self.rot(RB)
        pqb = pq_.bitcast(BF16)
        for hh in range(NH):
            k.op("pe", lambda e, hh=hh: e.transpose(pqb[:, hh * 16:(hh + 1) * 16], qidup[0:16, hh, :], self.ident_b[0:16, 0:16]), reads=["qidup", "ident_b"], writes=[pqr])
        k.op("act", lambda e: e.activation(out=qiTd, in_=pqb[:, 0:NH * 16].rearrange("p (h b t) -> p b h t", h=NH, b=4), func=AF.Copy), reads=[pqr], writes=["qiTd"])
        load_w(1, 2624, 8)
        L["w_side"](T, 16)
        pw_, pwr = self.rot(RB)
        for b in range(4):
            k.op("pe", lambda e, b=b: e.matmul(pw_[0:4, b * 8:(b + 1) * 8], lhsT=self.ident_f[0:16, 4 * b:4 * b + 4], rhs=wsc[0:16, 0:8], start=True, stop=True),
                 reads=["ident_f", "wsc"], writes=[pwr])
        k.op("act", lambda e: e.activation(out=wbs[0:4], in_=pw_[0:4, 0:32].rearrange("p (a b) -> p a b", a=4), func=AF.Copy), reads=[pwr], writes=["wbs"])
        for b in range(4):
            k.op("dve", lambda e, b=b: e.tensor_tensor(out=Xw[0:4], in0=wbs[0:4, b, :].unsqueeze(2).to_broadcast([4, 8, 4]),
                                                      in1=self.ident_f[0:4, 0:4].unsqueeze(1).to_broadcast([4, 8, 4]), op=OP.mult), reads=["wbs", "ident_f"], writes=["Xw"])
            px, pxr = self.rot(RB)
            self.tr32(px[0:32, 0:4], Xw[0:4].rearrange("p a b -> p (a b)"), ["Xw"], pxr, 4)
            k.op("act", lambda e, b=b, px=px: e.activation(out=wsel[0:32, b, :], in_=px[0:32, 0:4], func=AF.Copy), reads=[pxr], writes=["wsel"])
        k.dma("sp", None, ptb, I["ptab"].to_broadcast([128, 4 * NPG]), writes=["ptb"])
        k.op("pool", lambda e: e.iota(iot[:, 0:1], [[0, 1]], base=0, channel_multiplier=1, allow_small_or_imprecise_dtypes=True), writes=["iot"])
        k.op("dve", lambda e: e.tensor_copy(out=ptf, in_=ptb), reads=["ptb"], writes=["ptf"])
        k.op("dve", lambda e: e.tensor_scalar(out=ptf, in0=ptf, scalar1=128.0, scalar2=iot[:, 0:1], op0=OP.mult, op1=OP.add), reads=["ptf", "iot"], writes=["ptf"])
        if j:
            k.op("dve", lambda e: e.tensor_scalar(out=ptf, in0=ptf, scalar1=float(j * c.NPOOL * 128), scalar2=None, op0=OP.add), reads=["ptf"], writes=["ptf"])
        k.op("dve", lambda e: e.tensor_copy(out=idxi, in_=ptf), reads=["ptf"], writes=["idxi"])
        k.op("pool", lambda e: e.memset(cm4f[:, 0:4], 0.0), writes=["cm4f"])
        k.op("pool", lambda e: e.affine_select(out=cm4f[:, 0:4], in_=cm4f[:, 0:4], pattern=[[-1, 4]], compare_op=OP.is_ge, fill=NEG, base=0, channel_multiplier=1), reads=["cm4f"], writes=["cm4f"])
        k.op("pool", lambda e: e.tensor_copy(out=cm4[:, 0:4], in_=cm4f[:, 0:4]), reads=["cm4f"], writes=["cm4"])
        bm3 = bmk.rearrange("p (a b) -> p a b", a=4)
        k.op("pool", lambda e: e.memset(bmk, 1.0), writes=["bmk"])
        k.op("pool", lambda e: e.affine_select(out=bm3, in_=bm3, pattern=[[-4, 4], [0, 4]], compare_op=OP.is_ge, fill=0.0, base=0, channel_multiplier=1), reads=["bmk"], writes=["bmk"])
        k.op("pool", lambda e: e.affine_select(out=bm3, in_=bm3, pattern=[[4, 4], [0, 4]], compare_op=OP.is_ge, fill=0.0, base=3, channel_multiplier=-1), reads=["bmk"], writes=["bmk"])
        kTn = ph.bf16(NKV * 16).rearrange("p (a b) -> p a b", a=NKV)
        vbn = ph.bf16(512)
        kiTn = ph.bf16(16)
        k.op("pool", lambda e: e.tensor_copy(out=kTn, in_=kT[:, :, T:T + 16]), reads=["kT"], writes=["kTn"])
        k.op("pool", lambda e: e.tensor_copy(out=vbn[0:16], in_=vb[0:16, NT, :]), reads=["vb"], writes=["vbn"])
        k.op("pool", lambda e: e.tensor_copy(out=kiTn, in_=kiT[:, T:T + 16]), reads=["kiT"], writes=["kiTn"])
        self.barrier()
        ha.reset(0)
        mark_sb = ph.mark()

        Iall = ha.f32(NL)
        hmark_sb = ha.mark()
        kis = ha.f32(NPG * 64).rearrange("p (a b) -> p a b", a=NPG)
        kisb = ha.bf16(NPG * 64).rearrange("p (a b) -> p a b", a=NPG)
        kiTs = ph.bf16(NPG * 64).rearrange("p (a b) -> p a b", a=NPG // 2)
        stg = [ph.f32(512), ph.f32(512)]
        Rs = [ph.bf16(512), ph.bf16(512)]
        cki = I["cache_ki"].rearrange("a r d -> (a r) d")
        sq = 0
        for b in range(4):
            for pg in range(NPG):
                col = b * NPG + pg
                k.dma("pool", None, None, None, reads=["idxi"], writes=["kis"],
                      fn=lambda e, pg=pg, col=col: e.indirect_dma_start(out=kis[:, pg, :], out_offset=None, in_=cki, in_offset=bass.IndirectOffsetOnAxis(ap=idxi[:, col:col + 1], axis=0)))
            k.op("pool", lambda e: e.tensor_copy(out=kisb, in_=kis), reads=["kis"], writes=["kisb"])
            for g0 in range(0, NPG // 2, 8):
                pt_, ptr_ = self.rot(RB)
                ptb16 = pt_.bitcast(BF16)
                for pp in range(g0, g0 + 8 if g0 + 8 <= NPG // 2 else NPG // 2):
                    k.op("pe", lambda e, pp=pp, g0=g0, ptb16=ptb16: e.transpose(ptb16[:, (pp - g0) * 128:(pp - g0 + 1) * 128], kisb[:, 2 * pp:2 * pp + 2, :].rearrange("p a b -> p (a b)"), self.ident_b),
                         reads=["kisb", "ident_b"], writes=[ptr_])
                nb = min(8, NPG // 2 - g0)
                k.op("act", lambda e, g0=g0, nb=nb, ptb16=ptb16: e.activation(out=kiTs[:, g0:g0 + nb, :], in_=ptb16[:, 0:nb * 128].rearrange("p (a b) -> p a b", a=nb), func=AF.Copy),
                     reads=[ptr_], writes=["kiTs"])
            Iv = None
            for par in range(2):
                for g4 in range(NPG // 8):
                    sq ^= 1
                    ps, psr = self.rot(RB)
                    lhs = qiTd[par * 64:(par + 1) * 64, b].rearrange("p h t -> p (h t)")
                    rhs = kiTs[par * 64:(par + 1) * 64, g4 * 4:(g4 + 1) * 4, :].rearrange("p a b -> p (a b)")
                    k.op("pe", lambda e, ps=ps, lhs=lhs, rhs=rhs: e.matmul(ps[0:32, :], lhsT=lhs, rhs=rhs, start=True, stop=True), reads=["qiTd", "kiTs"], writes=[psr])
                    r = Rs[sq]
                    if sq:
                        k.op("act", lambda e, r=r, ps=ps: e.activation(out=r[0:32], in_=ps[0:32, :], func=AF.Relu), reads=[psr], writes=["Rs%d" % sq])
                    else:
                        k.op("dve", lambda e, r=r, ps=ps: e.tensor_scalar(out=r[0:32], in0=ps[0:32, :], scalar1=0.0, scalar2=None, op0=OP.max), reads=[psr], writes=["Rs%d" % sq])
                    pi, pir = self.PS[4], "ps4"
                    k.op("pe", lambda e, pi=pi, b=b, r=r: e.matmul(pi[0:4, :], lhsT=wsel[0:32, b, :], rhs=r[0:32], start=True, stop=True), reads=["wsel", "Rs%d" % sq], writes=[pir])
                    st_ = stg[sq]
                    k.op("act", lambda e, st_=st_, pi=pi: e.activation(out=st_[0:4], in_=pi[0:4, :], func=AF.Copy), reads=[pir], writes=["stg%d" % sq])
                    dst = Iall[4 * b:4 * b + 4, 0:PAST].rearrange("p (a c d) -> p a c d", c=2, d=128)[:, g4 * 4:(g4 + 1) * 4, par, :]
                    k.dma("sp", None, dst, st_[0:4].rearrange("p (a d) -> p a d", d=128), reads=["stg%d" % sq], writes=["Iall"])
            sq ^= 1
            ps, psr = self.rot(RB)
            lhs = qiTd[0:64, b].rearrange("p h t -> p (h t)")
            rhs = kiTn[0:64, 4 * b:4 * b + 4]
            k.op("pe", lambda e, ps=ps, lhs=lhs, rhs=rhs: e.matmul(ps[0:32, 0:4], lhsT=lhs, rhs=rhs, start=True, stop=True), reads=["qiTd", "kiTn"], writes=[psr])
            r = Rs[sq]
            k.op("act", lambda e, r=r, ps=ps: e.activation(out=r[0:32, 0:4], in_=ps[0:32, 0:4], func=AF.Relu), reads=[psr], writes=["Rs%d" % sq])
            pi, pir = self.PS[4], "ps4"
            k.op("pe", lambda e, pi=pi, b=b, r=r: e.matmul(pi[0:4, 0:4], lhsT=wsel[0:32, b, :], rhs=r[0:32, 0:4], start=True, stop=False), reads=["wsel", "Rs%d" % sq], writes=[pir], inc=False)
            k.op("pe", lambda e, pi=pi: e.matmul(pi[0:4, 0:4], lhsT=self.ident_b[0:4, 0:4], rhs=cm4[0:4, 0:4], start=False, stop=True), reads=["ident_b", "cm4"], writes=[pir])
            st_ = stg[sq]
            k.op("act", lambda e, st_=st_, pi=pi: e.activation(out=st_[0:4, 0:4], in_=pi[0:4, 0:4], func=AF.Copy), reads=[pir], writes=["stg%d" % sq])
            k.dma("sp", None, Iall[4 * b:4 * b + 4, PAST:PAST + 4], st_[0:4, 0:4], reads=["stg%d" % sq], writes=["Iall"])
        self.dbg("Iall", Iall[0:16, :], ["Iall"])
        self.dbg("qTs", qTs, ["qTs"])
        self.dbg("qiTd", qiTd, ["qiTd"])
        self.dbg("wsel", wsel[0:32], ["wsel"])
        self.dbg("idxi", idxi, ["idxi"])
        self.barrier()
        ha.reset(hmark_sb)
        ph.reset(mark_sb)

        TK = c.TOPK_S
        bs = [ph.f32(8) for _ in range(5)]
        lo, wd, mid, cnt, tt_ = bs
        wk = ph.f32(16)
        mask_s = ha.bf16(NL)
        cj = mask_s
        maskTs = ph.bf16(NPG * 16).rearrange("p (a b) -> p a b", a=NPG)
        R16 = slice(0, 16)
        k.op("dve", lambda e: e.tensor_reduce(out=lo[R16, 0:1], in_=Iall[R16, 0:TK], axis=AX.X, op=OP.min), reads=["Iall"], writes=["bs_lo"])
        k.op("dve", lambda e: e.tensor_reduce(out=wd[R16, 0:1], in_=Iall[R16, :], axis=AX.X, op=OP.max), reads=["Iall"], writes=["bs_w"])
        k.op("dve", lambda e: e.tensor_tensor(out=wd[R16, 0:1], in0=wd[R16, 0:1], in1=lo[R16, 0:1], op=OP.subtract), reads=["bs_w", "bs_lo"], writes=["bs_w"])
        k.op("dve", lambda e: e.tensor_scalar(out=wd[R16, 0:1], in0=wd[R16, 0:1], scalar1=1.0001, scalar2=1e-6, op0=OP.mult, op1=OP.add), reads=["bs_w"], writes=["bs_w"])
        k.op("dve", lambda e: e.tensor_tensor(out=wk[R16, 0:NITER], in0=pw2[R16, 0:NITER], in1=wd[R16, 0:1].to_broadcast([16, NITER]), op=OP.mult), reads=["bs_w", "pw2"], writes=["wk"])
        for it in range(NITER):
            k.op("dve", lambda e, it=it: e.tensor_tensor(out=mid[R16, 0:1], in0=lo[R16, 0:1], in1=wk[R16, it:it + 1], op=OP.add), reads=["bs_lo", "wk"], writes=["bs_mid"])
            k.op("dve", lambda e: e.tensor_scalar(out=cj[R16, :], in0=Iall[R16, :], scalar1=mid[R16, 0:1], scalar2=None, op0=OP.is_ge, op1=OP.add, accum_out=cnt[R16, 0:1]),
                 reads=["Iall", "bs_mid"], writes=["bs_cnt", "mask_s"])
            k.op("dve", lambda e, it=it: e.scalar_tensor_tensor(out=tt_[R16, 0:1], in0=cnt[R16, 0:1], scalar=float(TK) - 0.5, in1=wk[R16, it:it + 1], op0=OP.is_ge, op1=OP.mult), reads=["bs_cnt", "wk"], writes=["bs_t"])
            k.op("dve", lambda e: e.tensor_tensor(out=lo[R16, 0:1], in0=lo[R16, 0:1], in1=tt_[R16, 0:1], op=OP.add), reads=["bs_lo", "bs_t"], writes=["bs_lo"])
        k.op("dve", lambda e: e.tensor_scalar(out=mask_s[R16, :], in0=Iall[R16, :], scalar1=lo[R16, 0:1], scalar2=None, op0=OP.is_ge), reads=["Iall", "bs_lo"], writes=["mask_s"])
        for g0 in range(0, NPG, 32):
            pm, pmr = self.rot(RB)
            nb = min(32, NPG - g0)
            for pg in range(g0, g0 + nb):
                k.op("pe", lambda e, pg=pg, g0=g0, pm=pm: e.matmul(pm[:, (pg - g0) * 16:(pg - g0 + 1) * 16], lhsT=mask_s[R16, pg * 128:(pg + 1) * 128], rhs=self.ident_b[0:16, 0:16], start=True, stop=True),
                     reads=["mask_s", "ident_b"], writes=[pmr])
            k.op("act", lambda e, g0=g0, nb=nb, pm=pm: e.activation(out=maskTs[:, g0:g0 + nb, :], in_=pm[:, 0:nb * 16].rearrange("p (a b) -> p a b", a=nb), func=AF.Copy), reads=[pmr], writes=["maskTs"])
        k.op("dve", lambda e: e.tensor_tensor(out=Mfull[R16, :].rearrange("p (a b) -> p a b", a=4), in0=bm3[R16], in1=mask_s[R16, PAST:PAST + 4].unsqueeze(1).to_broadcast([16, 4, 4]), op=OP.mult),
             reads=["bmk", "mask_s"], writes=["Mfull"])
        pm, pmr = self.rot(RB)
        k.op("pe", lambda e, pm=pm: e.matmul(pm[0:16, 0:16], lhsT=Mfull[R16, :], rhs=self.ident_b[0:16, 0:16], start=True, stop=True), reads=["Mfull", "ident_b"], writes=[pmr])
        k.op("act", lambda e, pm=pm: e.activation(out=MfT[R16, :], in_=pm[0:16, 0:16], func=AF.Copy), reads=[pmr], writes=["MfT"])

        self.dbg("mask_s", mask_s[0:16, :], ["mask_s"])
        self.dbg("thr", lo[0:16, 0:1], ["bs_lo"])
        self.dbg("MfT", MfT[0:16, :], ["MfT"])
        self.dbg("maskTs", maskTs, ["maskTs"])
        GP = 4
        NSL = 4
        hf = lambda nw: ha.f32(nw) if ha.top + (nw + 7) // 8 * 8 <= ha.size else ph.f32(nw)
        kpg = [hf(512) for _ in range(NSL)]
        vpg = [hf(512) for _ in range(NSL)]
        kpb = [ph.bf16(512) for _ in range(NSL)]
        kTp = [ph.bf16(GP * 512).rearrange("p (u a b) -> p u a b", u=GP, a=NKV) for _ in range(2)]
        vpb = [ph.bf16(GP * 512).rearrange("p (u b) -> p u b", u=GP) for _ in range(2)]
        Es = ph.bf16(GP * 32)
        Ps_ = [ph.bf16(GP * 32), ph.bf16(GP * 32)]
        En = ph.bf16(32)
        Pn = ph.bf16(32)
        rdn = ph.f32(32)
        ck, cv = I["cache_k"].rearrange("a r d -> (a r) d"), I["cache_v"].rearrange("a r d -> (a r) d")
        slot = 0
        for b in range(4):
            for pgrp in range(NPG // GP):
                gs = pgrp % 2
                for u in range(GP):
                    pg = pgrp * GP + u
                    col = b * NPG + pg
                    slot = (slot + 1) % NSL
                    k.dma("pool", None, None, None, reads=["idxi"], writes=["kpg%d" % slot],
                          fn=lambda e, slot=slot, col=col: e.indirect_dma_start(out=kpg[slot], out_offset=None, in_=ck, in_offset=bass.IndirectOffsetOnAxis(ap=idxi[:, col:col + 1], axis=0)))
                    k.dma("pool", None, None, None, reads=["idxi"], writes=["vpg%d" % slot],
                          fn=lambda e, slot=slot, col=col: e.indirect_dma_start(out=vpg[slot], out_offset=None, in_=cv, in_offset=bass.IndirectOffsetOnAxis(ap=idxi[:, col:col + 1], axis=0)))
                    k.op("dve", lambda e, slot=slot: e.tensor_copy(out=kpb[slot], in_=kpg[slot]), reads=["kpg%d" % slot], writes=["kpb%d" % slot])
                    k.op("act", lambda e, slot=slot, gs=gs, u=u: e.activation(out=vpb[gs][:, u, :], in_=vpg[slot], func=AF.Copy), reads=["vpg%d" % slot], writes=["vpb%d" % gs])
                    pt_, ptr_ = self.PS[6], "ps6"
                    ptb16 = pt_.bitcast(BF16)
                    for kvh in range(NKV):
                        k.op("pe", lambda e, kvh=kvh, slot=slot, ptb16=ptb16: e.transpose(ptb16[:, kvh * 128:(kvh + 1) * 128], kpb[slot][:, kvh * 128:(kvh + 1) * 128], self.ident_b),
                             reads=["kpb%d" % slot, "ident_b"], writes=[ptr_])
                    k.op("act", lambda e, gs=gs, u=u, ptb16=ptb16: e.activation(out=kTp[gs][:, u], in_=ptb16[:, 0:512].rearrange("p (a b) -> p a b", a=NKV), func=AF.Copy), reads=[ptr_], writes=["kTp%d" % gs])
                sc, scr = self.PS[5], "ps5"
                for u in range(GP):
                    for kvh in range(NKV):
                        o_ = u * 32 + kvh * 8
                        rhs = qTs[:, b, 2 * kvh:2 * kvh + 2, :].rearrange("p h t -> p (h t)")
                        k.op("pe", lambda e, gs=gs, u=u, kvh=kvh, o_=o_, rhs=rhs: e.matmul(sc[:, o_:o_ + 8], lhsT=kTp[gs][:, u, kvh, :], rhs=rhs, start=True, stop=True),
                             reads=["kTp%d" % gs, "qTs"], writes=[scr])
                k.op("act", lambda e: e.activation(out=Es, in_=sc[:, 0:GP * 32], func=AF.Exp, scale=scale, bias=negB[:, 0:1]), reads=[scr, "negB"], writes=["Es"])
                pp_ = Ps_[gs]
                k.op("dve", lambda e, pp_=pp_, pgrp=pgrp, b=b: e.tensor_tensor(out=pp_.rearrange("p (u h t) -> p u h t", u=GP, h=NH), in0=Es.rearrange("p (u h t) -> p u h t", u=GP, h=NH),
                                                                            in1=maskTs[:, pgrp * GP:(pgrp + 1) * GP, 4 * b:4 * b + 4].unsqueeze(2).to_broadcast([128, GP, NH, 4]), op=OP.mult),
                     reads=["Es", "maskTs"], writes=["Ps%d" % gs])
                for u in range(GP):
                    first = (pgrp == 0 and u == 0)
                    for kvh in range(NKV):
                        rhs = pp_[:, u * 32 + kvh * 8:u * 32 + kvh * 8 + 8]
                        k.op("pe", lambda e, gs=gs, u=u, kvh=kvh, rhs=rhs, first=first: e.matmul(self.PS[kvh][:, 0:8], lhsT=vpb[gs][:, u, kvh * 128:(kvh + 1) * 128], rhs=rhs, start=first, stop=False),
                             reads=["vpb%d" % gs, "Ps%d" % gs], writes=["ps%d" % kvh], inc=True)
                    k.op("pe", lambda e, u=u, pp_=pp_, first=first: e.matmul(self.PS[4][:, 0:32], lhsT=self.ones_b, rhs=pp_[:, u * 32:(u + 1) * 32], start=first, stop=False),
                         reads=["ones_b", "Ps%d" % gs], writes=["ps4"], inc=True)
            scn, scnr = self.PS[7], "ps7"
            for kvh in range(NKV):
                rhs = qTs[:, b, 2 * kvh:2 * kvh + 2, :].rearrange("p h t -> p (h t)")
                k.op("pe", lambda e, kvh=kvh, rhs=rhs: e.matmul(scn[0:16, kvh * 8:(kvh + 1) * 8], lhsT=kTn[:, kvh, :], rhs=rhs, start=True, stop=True), reads=["kTn", "qTs"], writes=[scnr])
            k.op("act", lambda e: e.activation(out=En[R16, :], in_=scn[0:16, 0:32], func=AF.Exp, scale=scale, bias=negB[R16, 0:1]), reads=[scnr, "negB"], writes=["En"])
            k.op("dve", lambda e, b=b: e.tensor_tensor(out=Pn[R16, :].rearrange("p (h t) -> p h t", h=NH), in0=En[R16, :].rearrange("p (h t) -> p h t", h=NH),
                                                      in1=MfT[R16, 4 * b:4 * b + 4].unsqueeze(1).to_broadcast([16, NH, 4]), op=OP.mult), reads=["En", "MfT"], writes=["Pn"])
            for kvh in range(NKV):
                k.op("pe", lambda e, kvh=kvh: e.matmul(self.PS[kvh][:, 0:8], lhsT=vbn[0:16, kvh * 128:(kvh + 1) * 128], rhs=Pn[R16, kvh * 8:(kvh + 1) * 8], start=False, stop=True),
                     reads=["vbn", "Pn"], writes=["ps%d" % kvh], inc=True)
            k.op("pe", lambda e: e.matmul(self.PS[4][:, 0:32], lhsT=self.ones_b[0:16, :], rhs=Pn[R16, :], start=False, stop=True), reads=["ones_b", "Pn"], writes=["ps4"], inc=True)
            k.op("dve", lambda e: e.reciprocal(out=rdn, in_=self.PS[4][:, 0:32]), reads=["ps4"], writes=["rdn"])
            for kvh in range(NKV):
                k.op("dve", lambda e, kvh=kvh, b=b: e.tensor_tensor(out=attnTs[:, 2 * kvh:2 * kvh + 2, 4 * b:4 * b + 4], in0=self.PS[kvh][:, 0:8].rearrange("p (h t) -> p h t", h=2),
                                                                  in1=rdn[:, kvh * 8:(kvh + 1) * 8].rearrange("p (h t) -> p h t", h=2), op=OP.mult), reads=["ps%d" % kvh, "rdn"], writes=["attnTs"])
        self.dbg("attnTs", attnTs, ["attnTs"])
        for blk in range(4):
            s = blk % 2
            self.wload(wout[s], Wo[:, :, blk * 256:(blk + 1) * 256], "wout%d" % s, 2048)
            for r in range(2):
                oc = blk * 2 + r
                q = oc % 2
                k.dma("sp", None, hp[q][:, 0:16], hs3[:, oc, T:T + 16], reads=["hscr_s%d" % oc], writes=["hp%d" % q])
                ps, psr = self.PS[5], "ps5"
                for hh in range(NH):
                    lhs = wout[s][:, hh, r * 128:(r + 1) * 128]
                    rhs = attnTs[:, hh, :]
                    k.op("pe", lambda e, ps=ps, lhs=lhs, rhs=rhs, hh=hh: e.matmul(ps[:, 0:16], lhsT=lhs, rhs=rhs, start=(hh == 0), stop=(hh == NH - 1)),
                         reads=["wout%d" % s, "attnTs"], writes=[psr], inc=(hh == NH - 1))
                k.op("dve", lambda e, q=q, ps=ps: e.tensor_tensor(out=hp[q][:, 0:16], in0=hp[q][:, 0:16], in1=ps[:, 0:16], op=OP.add), reads=[psr, "hp%d" % q], writes=["hp%d" % q])
                k.dma("sp", None, hs3[:, oc, T:T + 16], hp[q][:, 0:16], reads=["hp%d" % q], writes=["hscr_s%d" % oc])


def _fm(v, nch):
    return np.ascontiguousarray(np.asarray(v, np.float32).reshape(nch, 128).T)


def build_vec(cfg, inp):
    off, NV = vec_layout(cfg)
    vec = np.zeros((128, NV), np.float32)
    for i in range(cfg.DEPTH):
        vec[:, off["nmix%d" % i]:off["nmix%d" % i] + 8] = _fm(inp["norm_mix"][i], 8)
        vec[:, off["nffn%d" % i]:off["nffn%d" % i] + 8] = _fm(inp["norm_ffn"][i], 8)
        vec[:, off["nple%d" % i]:off["nple%d" % i] + 8] = _fm(inp["norm_ple"][i], 8)
        for kk in range(3):
            o = off["cfw%d" % i] + kk * NFF
            vec[:, o:o + NFF] = _fm(inp["conv_ff_w"][i][kk], NFF)
        vec[:, off["cfb%d" % i]:off["cfb%d" % i] + NFF] = _fm(inp["conv_ff_b"][i], NFF)
    for j in range(cfg.NR):
        for kk in range(4):
            o = off["crw%d" % j] + kk * 4
            vec[:, o:o + 4] = _fm(inp["conv_rec_w"][j][kk], 4)
        for nm, key in (("crb", "conv_rec_b"), ("brg", "b_rgate"), ("big", "b_igate"), ("lam", "lru_lambda"), ("psc", "pool_scale")):
            o = off["%s%d" % (nm, j)]
            vec[:, o:o + 4] = _fm(inp[key][j], 4)
    o = off["poolrc"]
    for g in range(4):
        w = 2 << g
        for t in range(15):
            vec[:, o + g * 15 + t] = 1.0 / min(w, t + 1)
    return vec


def block_diag(w):
    w = np.asarray(w, np.float32)
    NR = w.shape[0]
    out = np.zeros((NR, 4, 128, 128), np.float32)
    for c in range(4):
        for a in range(2):
            out[:, c, 64 * a:64 * a + 64, 64 * a:64 * a + 64] = w[:, 2 * c + a]
    return out


def rope_tables(cfg):
    NT = cfg.NT
    pos = np.zeros((NT + 1, 128), np.float32)
    for t in range(NT):
        pos[t] = np.arange(t * 128, (t + 1) * 128, dtype=np.float32)
    for r in range(16):
        pos[NT, r] = cfg.PAST + (r % 4)
    outs = []
    for half in (64, 32):
        inv = np.power(np.float32(10000.0), -np.arange(half, dtype=np.float32) / np.float32(half)).astype(np.float32)
        ang = (pos[:, :, None] * inv[None, None, :]).astype(np.float32)
        tab = np.stack([np.cos(ang), np.sin(ang)], axis=2).astype(np.float32)
        outs.append(np.ascontiguousarray(tab))
    return outs


def make_in_maps(cfg, inp):
    A = lambda x: np.ascontiguousarray(np.asarray(x))
    vec = build_vec(cfg, inp)
    shared = {
        "vec": vec,
        "w_in_rec": A(inp["w_in_rec"]), "w_rg": block_diag(inp["w_rgate"]), "w_ig": block_diag(inp["w_igate"]),
        "w_pool": A(inp["w_pool"]), "w_out_rec": A(inp["w_out_rec"]),
        "w_up": A(inp["w_up"]), "w_down": A(inp["w_down"]), "w_ple": A(inp["w_ple"]), "w_pg": A(inp["w_ple_gate"]),
    }
    NA, NR, DP = cfg.NA, cfg.NR, cfg.DEPTH
    if NA:
        rp, rpi = rope_tables(cfg)
        shared.update({
            "cache_k": A(inp["cache_k"]).reshape(NA, cfg.NPOOL * 128, 512),
            "cache_v": A(inp["cache_v"]).reshape(NA, cfg.NPOOL * 128, 512),
            "cache_ki": A(inp["cache_kidx"]).reshape(NA, cfg.NPOOL * 128, IDX_D),
            "w_in_attn": A(inp["w_in_attn"]), "w_out_attn": A(inp["w_out_attn"]),
            "qkn": np.ascontiguousarray(np.stack([np.asarray(inp["q_norm"]), np.asarray(inp["k_norm"])], axis=1).astype(np.float32)),
            "rope": rp, "ropei": rpi,
        })
    maps = []
    for b in range(cfg.NB):
        sb = slice(4 * b, 4 * b + 4)
        m = dict(shared)
        m["xp"] = A(inp["x_prompt"][b])
        m["xs"] = A(inp["x_sample"][sb]).reshape(16, D)
        m["pp"] = A(inp["p_prompt"][:, b])
        m["psm"] = A(inp["p_sample"][:, sb]).reshape(DP, 16, D_PLE)
        m["st_rc"] = A(inp["state_rec_conv"][:, sb]).reshape(NR, 12, D_REC)
        m["st_rh"] = A(inp["state_rec_h"][:, sb]).reshape(NR, 4, D_REC)
        m["st_pool"] = A(inp["state_pool"][:, sb]).reshape(NR, 60, D_POOL)
        m["st_ffn"] = A(inp["state_ffn_conv"][:, sb]).reshape(DP, 8, D_FF)
        if NA:
            m["ptab"] = A(inp["page_table"][sb]).astype(np.int32).reshape(1, -1)
        maps.append(m)
    return maps


def gather_outputs(cfg, res):
    NB, T, NR, NA, DP = cfg.NB, cfg.T, cfg.NR, cfg.NA, cfg.DEPTH
    R = lambda name: [np.asarray(r[name]) for r in res]
    y_p = np.stack(R("y_p"))
    y_s = np.concatenate([a.reshape(4, 4, D) for a in R("y_s")])
    rc_p = np.stack(R("rc_p"), axis=1)
    rc_s = np.concatenate([a.reshape(NR, 4, 3, D_REC) for a in R("rc_s")], axis=1)
    rh_p = np.stack([a[:, 0] for a in R("rh_p")], axis=1)
    rh_s = np.concatenate(R("rh_s"), axis=1)
    pl_p = np.stack(R("pl_p"), axis=1)
    pl_s = np.concatenate([a.reshape(NR, 4, 15, D_POOL) for a in R("pl_s")], axis=1)
    if NA:
        k_p = np.stack([a.reshape(NA, T, NKV, HD) for a in R("k_p")], axis=1)
        k_s = np.concatenate([a.reshape(NA, 4, 4, NKV, HD) for a in R("k_s")], axis=1)
        v_p = np.stack([a.reshape(NA, T, NKV, HD) for a in R("v_p")], axis=1)
        v_s = np.concatenate([a.reshape(NA, 4, 4, NKV, HD) for a in R("v_s")], axis=1)
        ki_p = np.stack(R("ki_p"), axis=1)
        ki_s = np.concatenate([a.reshape(NA, 4, 4, IDX_D) for a in R("ki_s")], axis=1)
    else:
        z = np.zeros((0,), np.float32)
        k_p = k_s = v_p = v_s = ki_p = ki_s = z
    fc_p = np.stack(R("fc_p"), axis=1)
    fc_s = np.concatenate([a.reshape(DP, 4, 2, D_FF) for a in R("fc_s")], axis=1)
    return (y_p, y_s, rc_p, rc_s, rh_p, rh_s, pl_p, pl_s, k_p, k_s, v_p, v_s, ki_p, ki_s, fc_p, fc_s)


def run_cfg(cfg, inputs, trace=False):
    from contextlib import ExitStack
    mk = MK(cfg)
    with ExitStack() as st:
        mk.build(st)
    maps = make_in_maps(cfg, inputs)
    res = run_bass_kernel_spmd(mk.nc, maps, core_ids=list(range(cfg.NB)), **({"trace": True} if trace else {}))
    return gather_outputs(cfg, res.results), res


def kernel(**inputs):
    cfg = Cfg()
    outs, _ = run_cfg(cfg, inputs)
    return tuple(np.ascontiguousarray(o, dtype=np.float32) for o in outs)
```

```python
import numpy as np
import concourse.bass as bass
import concourse.mybir as mybir
from concourse.bass_utils import run_bass_kernel_spmd

F32 = mybir.dt.float32
BF16 = mybir.dt.bfloat16
I32 = mybir.dt.int32
U32 = mybir.dt.uint32
AF = mybir.ActivationFunctionType
OP = mybir.AluOpType
AX = mybir.AxisListType

D = 1024
NCH = 8
D_PLE = 256
D_REC = 512
D_POOL = 512
D_FF = 2816
NFF = 22
ATTN_IN = 2632
HD = 128
NH = 8
NKV = 4
IDX_H = 8
IDX_D = 64
EPS = 1e-6
NEG = -1.0e30


class Cfg:
    def __init__(self, T=2048, DEPTH=4, NPG=64, NPOOL=2560, NB=8, NSB=32):
        self.T = T
        self.DEPTH = DEPTH
        self.NPG = NPG
        self.NPOOL = NPOOL
        self.NB = NB
        self.NSB = NSB
        self.NS = 16
        self.TT = T + 16
        self.NR = (DEPTH + 1) // 2
        self.NA = DEPTH // 2
        self.PAST = NPG * 128
        self.NT = T // 128
        self.TOPK = min(256, T // 4)
        self.TOPK_S = min(256, (self.PAST + 4) // 4)


class Sched:
    ENG = ("pe", "act", "dve", "pool", "sp")

    def __init__(self, nc, stack):
        self.nc = nc
        self.stack = stack
        self.prog = {e: [] for e in self.ENG}
        self.sem = {e: stack.enter_context(nc.semaphore("s_" + e)) for e in self.ENG}
        self.cnt = {e: 0 for e in self.ENG}
        self.pending = {e: False for e in self.ENG}
        self.waited = {e: {} for e in self.ENG}
        self.last_w = {}
        self.readers = {}
        self.dsems = []
        self.ndma = 0

    def dma_sem(self, name):
        s = self.stack.enter_context(self.nc.semaphore("d_" + name))
        d = {"sem": s, "cnt": 0, "key": "d_" + name}
        self.dsems.append(d)
        return d

    def _need(self, eng, reads, writes, pe_accum):
        need = {}

        def add(tok, same_ok):
            key, sem, val = tok
            if key == eng and same_ok:
                return
            if need.get(key, (None, 0))[1] < val:
                need[key] = (sem, val)

        for r in reads:
            t = self.last_w.get(r)
            if t is not None:
                add(t, False)
        same_ok = (eng == "pe")
        for w in writes:
            t = self.last_w.get(w)
            if t is not None:
                add(t, same_ok)
            for t in self.readers.get(w, {}).values():
                add(t, same_ok)
        out = []
        wd = self.waited[eng]
        for key, (sem, val) in need.items():
            if wd.get(key, 0) >= val:
                continue
            wd[key] = val
            out.append((sem, val))
        return out

    def op(self, eng, fn, reads=(), writes=(), inc=True):
        waits = self._need(eng, reads, writes, False)
        for (sem, val) in waits:
            if sem is self.sem[eng] and val > self.cnt[eng]:
                raise RuntimeError("self-wait on pending increment (%s)" % eng)
        if inc:
            self.cnt[eng] += 1
            val = self.cnt[eng]
            self.pending[eng] = False
        else:
            val = self.cnt[eng] + 1
            self.pending[eng] = True
        tok = (eng, self.sem[eng], val)
        self.prog[eng].append((waits, fn, (self.sem[eng], 1) if inc else None))
        for r in reads:
            self.readers.setdefault(r, {})[eng] = tok
        for w in writes:
            self.last_w[w] = tok
            self.readers[w] = {}
        return tok

    NPOOL_SEM = 24

    def dma(self, eng, dsem, out_ap, in_ap, reads=(), writes=(), fn=None, **kw):
        if not hasattr(self, "_dpool"):
            self._dpool = {}
        if eng not in self._dpool:
            self._dpool[eng] = [self.dma_sem("p%s%d" % (eng, q)) for q in range(self.NPOOL_SEM)]
            self._drr = getattr(self, "_drr", {})
            self._drr[eng] = 0
        dsem = self._dpool[eng][self._drr[eng]]
        self._drr[eng] = (self._drr[eng] + 1) % self.NPOOL_SEM
        waits = self._need(eng, reads, writes, False)
        if dsem["cnt"] > self.waited[eng].get(dsem["key"], 0):
            waits.append((dsem["sem"], dsem["cnt"]))
            self.waited[eng][dsem["key"]] = dsem["cnt"]
        dsem["cnt"] += 16
        tok = (dsem["key"], dsem["sem"], dsem["cnt"])

        if fn is None:
            def fn(e, out_ap=out_ap, in_ap=in_ap, kw=kw):
                return e.dma_start(out=out_ap, in_=in_ap, **kw)

        self.prog[eng].append((waits, fn, (dsem["sem"], 16)))
        self.ndma += 1
        for r in reads:
            self.readers.setdefault(r, {})[dsem["key"]] = tok
        for w in writes:
            self.last_w[w] = tok
            self.readers[w] = {}
        return tok

    def raw(self, eng, fn, reads=(), writes=()):
        waits = self._need(eng, reads, writes, False)
        self.prog[eng].append((waits, fn, None))

    def finish(self):
        for e in self.ENG:
            if self.pending[e]:
                raise RuntimeError("pending ops without inc on " + e)
        waits = []
        for d in self.dsems:
            if d["cnt"]:
                waits.append((d["sem"], d["cnt"]))
        for e in self.ENG:
            if e != "sp" and self.cnt[e]:
                waits.append((self.sem[e], self.cnt[e]))
        self.prog["sp"].append((waits, None, None))

    def emit(self):
        nc = self.nc
        prog = self.prog

        def replay(name, e):
            for waits, fn, inc in prog[name]:
                for (sem, val) in waits:
                    e.wait_ge(sem, val)
                if fn is None:
                    continue
                ins = fn(e)
                if inc is not None:
                    ins.then_inc(inc[0], inc[1])

        with nc.Block() as block:
            @block.tensor
            def _(e):
                replay("pe", e)

            @block.scalar
            def _(e):
                replay("act", e)

            @block.vector
            def _(e):
                replay("dve", e)

            @block.gpsimd
            def _(e):
                replay("pool", e)

            @block.sync
            def _(e):
                replay("sp", e)


class Arena:
    def __init__(self, ap, size):
        self.ap = ap
        self.size = size
        self.top = 0

    def f32(self, n):
        a = self.top
        self.top += (n + 7) // 8 * 8
        assert self.top <= self.size, "arena overflow %d > %d" % (self.top, self.size)
        return self.ap[:, a:a + n]

    def bf16(self, n):
        w = (n + 1) // 2
        a = self.top
        self.top += (w + 7) // 8 * 8
        assert self.top <= self.size, "arena overflow %d > %d" % (self.top, self.size)
        return self.ap[:, a:a + w].bitcast(BF16)[:, 0:n]

    def i32(self, n):
        return self.f32(n).bitcast(I32)

    def mark(self):
        return self.top

    def reset(self, m):
        self.top = m


def vec_layout(cfg):
    off = {}
    n = 0

    def add(name, w):
        nonlocal n
        off[name] = n
        n += w

    for i in range(cfg.DEPTH):
        add("nmix%d" % i, 8)
        add("nffn%d" % i, 8)
        add("nple%d" % i, 8)
        add("cfw%d" % i, 3 * NFF)
        add("cfb%d" % i, NFF)
    for j in range(cfg.NR):
        add("crw%d" % j, 16)
        add("crb%d" % j, 4)
        add("brg%d" % j, 4)
        add("big%d" % j, 4)
        add("lam%d" % j, 4)
        add("psc%d" % j, 4)
    add("poolrc", 4 * 15)
    return off, n


class MK:
    def __init__(self, cfg):
        self.cfg = cfg
        self.nc = bass.Bass("TRN2", target_bir_lowering=False)
        self.voff, self.NV = vec_layout(cfg)

    def declare(self):
        nc, c = self.nc, self.cfg
        T, DP, NR, NA = c.T, c.DEPTH, c.NR, c.NA
        I = {}
        O = {}

        def inp(name, shape, dt=F32):
            I[name] = nc.dram_tensor(name, list(shape), dt, kind="ExternalInput").ap()

        def outp(name, shape):
            O[name] = nc.dram_tensor(name, list(shape), F32, kind="ExternalOutput").ap()

        inp("xp", [T, D]); inp("xs", [16, D])
        inp("pp", [DP, T, D_PLE]); inp("psm", [DP, 16, D_PLE])
        inp("st_rc", [NR, 12, D_REC]); inp("st_rh", [NR, 4, D_REC]); inp("st_pool", [NR, 60, D_POOL])
        inp("st_ffn", [DP, 8, D_FF])
        inp("vec", [128, self.NV])
        inp("w_in_rec", [NR, D, 1536]); inp("w_rg", [NR, 4, 128, 128]); inp("w_ig", [NR, 4, 128, 128])
        inp("w_pool", [NR, 4, 128, 128]); inp("w_out_rec", [NR, D, D])
        inp("w_up", [DP, D, 2 * D_FF]); inp("w_down", [DP, D_FF, D])
        inp("w_ple", [DP, D_PLE, D]); inp("w_pg", [DP, D, D])
        if NA:
            inp("ptab", [1, 4 * c.NPG], I32)
            inp("cache_kv", [NA, c.NPOOL * 128, 1024])
            inp("cache_ki", [NA, c.NPOOL * 128, IDX_D])
            inp("w_in_attn", [NA, D, ATTN_IN]); inp("w_out_attn", [NA, D, D])
            inp("qkn", [NA, 2, HD])
            inp("rope", [c.NT + 1, 128, 2, 64]); inp("ropei", [c.NT + 1, 128, 2, 32])
        outp("y_p", [T, D]); outp("y_s", [16, D])
        outp("rc_p", [NR, 3, D_REC]); outp("rc_s", [NR, 12, D_REC])
        outp("rh_p", [NR, 1, D_REC]); outp("rh_s", [NR, 4, D_REC])
        outp("pl_p", [NR, 15, D_POOL]); outp("pl_s", [NR, 60, D_POOL])
        if NA:
            outp("k_p", [NA, T, 512]); outp("k_s", [NA, 16, 512])
            outp("v_p", [NA, T, 512]); outp("v_s", [NA, 16, 512])
            outp("ki_p", [NA, T, IDX_D]); outp("ki_s", [NA, 16, IDX_D])
            self.hscr = nc.dram_tensor("hscr", [128, NCH * c.TT], F32).ap()
        outp("fc_p", [DP, 2, D_FF]); outp("fc_s", [DP, 8, D_FF])
        self.I, self.O = I, O

    def coltiles(self):
        T = self.cfg.T
        return [(i * 512, 512) for i in range(T // 512)] + [(T, 16)]

    def ps_next(self):
        b = self.ps_rr
        self.ps_rr = (self.ps_rr + 1) % 6
        return self.PS[b], "ps%d" % b

    def pss_next(self):
        s = self.pss_rr
        self.pss_rr = (self.pss_rr + 1) % 32
        return self.PS[7][:, s * 16:(s + 1) * 16], "ps7"

    def wload(self, dst, src, region, nwords):
        k = self.k
        s = self.stage_rr
        self.stage_rr = (s + 1) % len(self.stage)
        st = self.stage[s][:, 0:nwords]
        shp = list(src.shape)
        if len(shp) == 3:
            stv = st.rearrange("p (a b) -> p a b", a=shp[1])
        else:
            stv = st
        k.dma("sp", self.dstage[s], stv, src, writes=["stage%d" % s])
        k.op("pool", lambda e, dst=dst, stv=stv: e.tensor_copy(out=dst, in_=stv), reads=["stage%d" % s], writes=[region])

    def vcol(self, name, i=0, n=1):
        o = self.voff[name] + i
        return self.vec[:, o:o + n]

    def norm(self, gname):
        k, c = self.k, self.cfg
        h, xn = self.h, self.xn
        m = self.ph.mark()
        rstd = self.ph.f32(c.TT)
        sqb = [self.ph.bf16(512), self.ph.bf16(512)]
        for ti, (c0, n) in enumerate(self.coltiles()):
            ps, psr = self.PS[6], "ps6"
            for ch in range(NCH):
                sb = sqb[ch % 2]
                src = h[:, ch, c0:c0 + n]
                k.op("act", lambda e, sb=sb, src=src, n=n: e.activation(out=sb[:, 0:n], in_=src, func=AF.Square),
                     reads=["h%d_%d" % (ch, ti)], writes=["sqb%d" % (ch % 2)])
                k.op("pe", lambda e, ps=ps, sb=sb, n=n, ch=ch: e.matmul(ps[:, 0:n], lhsT=self.ones_b, rhs=sb[:, 0:n], start=(ch == 0), stop=(ch == NCH - 1)),
                     reads=["sqb%d" % (ch % 2), "ones_b"], writes=[psr], inc=True)
            rs = rstd[:, c0:c0 + n]
            k.op("act", lambda e, rs=rs, ps=ps, n=n: e.activation(out=rs, in_=ps[:, 0:n], func=AF.Sqrt, bias=self.eps_t[:, 0:1], scale=1.0 / D),
                 reads=[psr, "eps"], writes=["rstd%d" % ti])
            k.op("dve", lambda e, rs=rs: e.reciprocal(out=rs, in_=rs), reads=["rstd%d" % ti], writes=["rstd%d" % ti])
            for ch in range(NCH):
                g = self.vcol(gname, ch)
                k.op("dve", lambda e, ch=ch, c0=c0, n=n, g=g, rs=rs: e.scalar_tensor_tensor(
                    out=xn[:, ch, c0:c0 + n], in0=h[:, ch, c0:c0 + n], scalar=g, in1=rs, op0=OP.mult, op1=OP.mult),
                    reads=["h%d_%d" % (ch, ti), "rstd%d" % ti, "vec"], writes=["xn%d_%d" % (ch, ti)])
        self.barrier()
        self.ph.reset(m)

    def proj_chunk(self, lhs_of_kc, KC, rhs_of, rhs_reg_of, wregs, evac):
        k = self.k
        for ti, (c0, n) in enumerate(self.coltiles()):
            if n == 16:
                ps, psr = self.pss_next()
            else:
                ps, psr = self.ps_next()
            for kc in range(KC):
                lhs = lhs_of_kc(kc)
                rhs = rhs_of(kc, c0, n)
                k.op("pe", lambda e, ps=ps, lhs=lhs, rhs=rhs, n=n, kc=kc: e.matmul(ps[:, 0:n], lhsT=lhs, rhs=rhs, start=(kc == 0), stop=(kc == KC - 1)),
                     reads=list(wregs) + [rhs_reg_of(kc, ti)], writes=[psr], inc=(kc == KC - 1))
            evac(ti, c0, n, ps[:, 0:n], psr)

    def tr32(self, ps_ap, in_ap, reads, psr, npart):
        self.k.op("pe", lambda e: e.transpose(ps_ap, in_ap, self.ident_f[0:npart, 0:npart]),
                  reads=list(reads) + ["ident_f"], writes=[psr])

    def dbg(self, name, ap, reads):
        if not getattr(self.cfg, "debug", False):
            return
        t = self.nc.dram_tensor("dbg_" + name, list(ap.shape), ap.dtype, kind="ExternalOutput").ap()
        self.O["dbg_" + name] = t
        self.k.dma("sp", None, t, ap, reads=reads)

    def barrier(self):
        k = self.k
        for e in k.ENG:
            assert not k.pending[e]
        for e in k.ENG:
            waits = []
            for o in k.ENG:
                if k.cnt[o] > k.waited[e].get(o, 0):
                    waits.append((k.sem[o], k.cnt[o]))
                    k.waited[e][o] = k.cnt[o]
            for d in k.dsems:
                if d["cnt"] > k.waited[e].get(d["key"], 0):
                    waits.append((d["sem"], d["cnt"]))
                    k.waited[e][d["key"]] = d["cnt"]
            if waits:
                k.prog[e].append((waits, None, None))
        k.last_w.clear()
        k.readers.clear()

    def build(self, stack):
        nc, c = self.nc, self.cfg
        self.declare()
        k = self.k = Sched(nc, stack)
        AW = 52900
        A = nc.alloc_sbuf_tensor("arena", [128, AW], F32).ap()
        self.ph = Arena(A, AW)
        ph = self.ph
        self.PS = [nc.alloc_psum_tensor("psb%d" % b, [128, 512], F32).ap() for b in range(8)]
        self.ps_rr = 0
        self.pss_rr = 0
        TT = c.TT
        self.h = ph.f32(NCH * TT).rearrange("p (c t) -> p c t", c=NCH)
        self.xn = ph.bf16(NCH * TT).rearrange("p (c t) -> p c t", c=NCH)
        self.vec = ph.f32(self.NV)
        self.ident_f = ph.f32(128)
        self.ident_b = ph.bf16(128)
        self.ones_b = ph.bf16(128)
        self.eps_t = ph.f32(8)
        self.one_t = ph.f32(8)
        self.stage = [ph.f32(2048), ph.f32(2048)]
        self.stage_rr = 0
        self.dstage = [k.dma_sem("stg0"), k.dma_sem("stg1")]
        self.d_in = k.dma_sem("in")
        self.d_out = k.dma_sem("out")
        self.d_out2 = k.dma_sem("out2")
        self.dpin = [k.dma_sem("pin0"), k.dma_sem("pin1")]
        self.drp = [k.dma_sem("rp0"), k.dma_sem("rp1")]
        self.dhp = [k.dma_sem("hp0"), k.dma_sem("hp1")]
        self.hflat = self.ph.ap[:, 0:NCH * TT]
        self.base_mark = ph.mark()

        k.dma("sp", self.d_in, self.vec, self.I["vec"], writes=["vec"])
        k.op("pool", lambda e: e.memset(self.ident_f, 0.0), writes=["ident_f"])
        k.op("pool", lambda e: e.affine_select(out=self.ident_f, in_=self.ident_f, pattern=[[-1, 128]], compare_op=OP.not_equal,
                                               fill=1.0, base=0, channel_multiplier=1), reads=["ident_f"], writes=["ident_f"])
        k.op("pool", lambda e: e.tensor_copy(out=self.ident_b, in_=self.ident_f), reads=["ident_f"], writes=["ident_b"])
        k.op("pool", lambda e: e.memset(self.ones_b, 1.0), writes=["ones_b"])
        k.op("pool", lambda e: e.memset(self.eps_t, EPS), writes=["eps"])
        k.op("pool", lambda e: e.memset(self.one_t, 1.0), writes=["one_t"])

        self.load_x()
        stop = getattr(c, "stop", None)
        for i in range(c.DEPTH):
            if i % 2 == 0:
                self.rec_layer(i)
            else:
                self.attn_layer(i)
            if stop == "mix%d" % i:
                break
            self.ffn_layer(i)
            if stop == "ffn%d" % i:
                break
            self.ple_layer(i)
        self.store_y()
        k.finish()
        k.emit()

    def tok_tiles(self):
        c = self.cfg
        return [(t * 128, 128) for t in range(c.NT)] + [(c.T, 16)]

    def load_x(self):
        k, c, ph = self.k, self.cfg, self.ph
        m = ph.mark()
        xin = [ph.f32(1024), ph.f32(1024)]
        dx = [k.dma_sem("xin0"), k.dma_sem("xin1")]
        for tt, (t0, n) in enumerate(self.tok_tiles()):
            s = tt % 2
            src = self.I["xp"][t0:t0 + n, :] if n == 128 else self.I["xs"]
            k.dma("sp", dx[s], xin[s][0:n, :], src, writes=["xin%d" % s])
            ti = min(t0 // 512, len(self.coltiles()) - 1)
            for half in range(2):
                ps, psr = self.ps_next()
                for j in range(4):
                    ch = half * 4 + j
                    self.tr32(ps[:, j * 128:j * 128 + n], xin[s][0:n, ch * 128:(ch + 1) * 128], ["xin%d" % s], psr, n)
                dst = self.h[:, half * 4:half * 4 + 4, t0:t0 + n]
                srcp = ps.rearrange("p (a b) -> p a b", a=4)[:, :, 0:n]
                regs = ["h%d_%d" % (half * 4 + j, ti) for j in range(4)]
                if (tt + half) % 2 == 0:
                    k.op("act", lambda e, dst=dst, srcp=srcp: e.activation(out=dst, in_=srcp, func=AF.Copy), reads=[psr], writes=regs)
                else:
                    k.op("dve", lambda e, dst=dst, srcp=srcp: e.tensor_copy(out=dst, in_=srcp), reads=[psr], writes=regs)
        self.barrier()
        ph.reset(m)

    def store_y(self):
        k, c, ph = self.k, self.cfg, self.ph
        m = ph.mark()
        ost = [ph.f32(1024), ph.f32(1024)]
        for tt, (t0, n) in enumerate(self.tok_tiles()):
            s = tt % 2
            ti = min(t0 // 512, len(self.coltiles()) - 1)
            for half in range(2):
                ps, psr = self.ps_next()
                for j in range(4):
                    ch = half * 4 + j
                    self.tr32(ps[0:n, j * 128:(j + 1) * 128], self.h[:, ch, t0:t0 + n], ["h%d_%d" % (ch, ti)], psr, 128)
                dst = ost[s][0:n, half * 512:(half + 1) * 512]
                if (tt + half) % 2 == 0:
                    k.op("act", lambda e, dst=dst, ps=ps, n=n: e.activation(out=dst, in_=ps[0:n, :], func=AF.Copy), reads=[psr], writes=["ost%d" % s])
                else:
                    k.op("dve", lambda e, dst=dst, ps=ps, n=n: e.tensor_copy(out=dst, in_=ps[0:n, :]), reads=[psr], writes=["ost%d" % s])
            dstd = self.O["y_p"][t0:t0 + n, :] if n == 128 else self.O["y_s"]
            k.dma("sp", self.d_out if s == 0 else self.d_out2, dstd, ost[s][0:n, :], reads=["ost%d" % s])
        self.barrier()
        ph.reset(m)

    def store_fm(self, src3, nch, r, dst, reads, tag):
        k, ph = self.k, self.ph
        ot = ph.f32(nch * 128)
        for g0 in range(0, nch, 4):
            g1 = min(nch, g0 + 4)
            ps, psr = self.ps_next()
            for ch in range(g0, g1):
                self.tr32(ps[0:r, (ch - g0) * 128:(ch - g0 + 1) * 128], src3[:, ch, :], reads, psr, 128)
            w = (g1 - g0) * 128
            k.op("act", lambda e, g0=g0, w=w, ps=ps: e.activation(out=ot[0:r, g0 * 128:g0 * 128 + w], in_=ps[0:r, 0:w], func=AF.Copy),
                 reads=[psr], writes=[tag + "_ot"])
        k.dma("sp", self.d_out, dst, ot[0:r, :], reads=[tag + "_ot"])

    def load_fm(self, dst3, nch, r, src, tag):
        k, ph = self.k, self.ph
        it = ph.f32(nch * 128)
        k.dma("sp", self.d_in, it[0:r, :], src, writes=[tag + "_it"])
        for g0 in range(0, nch, 4):
            g1 = min(nch, g0 + 4)
            ps, psr = self.ps_next()
            for ch in range(g0, g1):
                self.tr32(ps[:, (ch - g0) * r:(ch - g0 + 1) * r], it[0:r, ch * 128:(ch + 1) * 128], [tag + "_it"], psr, r)
            w = (g1 - g0)
            k.op("act", lambda e, g0=g0, g1=g1, w=w, ps=ps: e.activation(out=dst3[:, g0:g1, :], in_=ps[:, 0:w * r].rearrange("p (a b) -> p a b", a=w), func=AF.Copy),
                 reads=[psr], writes=[tag])

    def ffn_layer(self, i):
        k, c, ph = self.k, self.cfg, self.ph
        T, TT = c.T, c.TT
        h, xn = self.h, self.xn
        self.norm("nffn%d" % i)
        m = ph.mark()
        fcs = ph.f32(NFF * 8).rearrange("p (a b) -> p a b", a=NFF)
        fct = ph.f32(NFF * 10).rearrange("p (a b) -> p a b", a=NFF)
        mm = ph.mark()
        self.load_fm(fcs, NFF, 8, self.I["st_ffn"][i], "fcs")
        self.barrier()
        ph.reset(mm)
        G = 4
        wup = [ph.bf16(2048).rearrange("p (a b) -> p a b", a=8) for _ in range(4)]
        wdn = ph.bf16(G * 1024).rearrange("p (a b) -> p a b", a=G)
        act = ph.bf16(G * TT).rearrange("p (a b) -> p a b", a=G)
        GE = 2 + T + 24
        gext = [ph.f32(GE), ph.f32(GE)]
        tmp = ph.f32(TT)
        gl = [ph.bf16(TT), ph.bf16(TT)]
        W = self.I["w_up"][i].rearrange("(kc p) n -> p kc n", p=128)
        Wd = self.I["w_down"][i]
        wrr = [0]

        def get_block(col0):
            sl = wrr[0]
            wrr[0] = (sl + 1) % 4
            self.wload(wup[sl], W[:, :, col0:col0 + 256], "wup%d" % sl, 2048)
            return sl

        groups = [list(range(g, min(g + G, NFF))) for g in range(0, NFF, G)]
        sg = su = 0
        for grp in groups:
            for jj, j in enumerate(grp):
                mb, r = divmod(j, 2)
                if r == 0:
                    sg = get_block(mb * 256)
                    su = get_block(D_FF + mb * 256)
                s = j % 2
                ge = gext[s]
                gs = ge[:, 2 + T:2 + T + 24].rearrange("p (a b) -> p a b", a=4)
                gr = "gext%d" % s
                k.op("pool", lambda e, ge=ge: e.memset(ge[:, 0:2], 0.0), writes=[gr])
                k.op("pool", lambda e, gs=gs, j=j: e.tensor_copy(out=gs[:, :, 0:2], in_=fcs[:, j, :].rearrange("p (a b) -> p a b", a=4)), writes=[gr])

                def evac_g(ti, c0, n, ps, psr, ge=ge, gs=gs, gr=gr):
                    if n == 16:
                        k.op("act", lambda e: e.activation(out=gs[:, :, 2:6], in_=ps.rearrange("p (a b) -> p a b", a=4), func=AF.Copy), reads=[psr], writes=[gr])
                    else:
                        k.op("act", lambda e: e.activation(out=ge[:, 2 + c0:2 + c0 + n], in_=ps, func=AF.Copy), reads=[psr], writes=[gr])

                self.proj_chunk(lambda kc, sg=sg, r=r: wup[sg][:, kc, r * 128:(r + 1) * 128], NCH,
                                lambda kc, c0, n: xn[:, kc, c0:c0 + n], lambda kc, ti: "xn%d_%d" % (kc, ti), ["wup%d" % sg], evac_g)
                w = [self.vcol("cfw%d" % i, kk * NFF + j) for kk in range(3)]
                b = self.vcol("cfb%d" % i, j)
                tp = tmp[:, 0:T]
                tsm = tmp[:, T:T + 16].rearrange("p (a b) -> p a b", a=4)
                k.op("dve", lambda e, ge=ge, w=w, b=b: e.tensor_scalar(out=tp, in0=ge[:, 0:T], scalar1=w[0], scalar2=b, op0=OP.mult, op1=OP.add),
                     reads=[gr, "vec"], writes=["tmp"])
                for kk in (1, 2):
                    k.op("dve", lambda e, ge=ge, w=w, kk=kk: e.scalar_tensor_tensor(out=tp, in0=ge[:, kk:kk + T], scalar=w[kk], in1=tp, op0=OP.mult, op1=OP.add),
                         reads=[gr, "tmp"], writes=["tmp"])
                k.op("dve", lambda e, gs=gs, w=w, b=b: e.tensor_scalar(out=tsm, in0=gs[:, :, 0:4], scalar1=w[0], scalar2=b, op0=OP.mult, op1=OP.add),
                     reads=[gr, "tmp"], writes=["tmp"])
                for kk in (1, 2):
                    k.op("dve", lambda e, gs=gs, w=w, kk=kk: e.scalar_tensor_tensor(out=tsm, in0=gs[:, :, kk:kk + 4], scalar=w[kk], in1=tsm, op0=OP.mult, op1=OP.add),
                         reads=[gr, "tmp"], writes=["tmp"])
                glr = "gl%d" % s
                k.op("act", lambda e, s=s: e.activation(out=gl[s], in_=tmp, func=AF.Gelu), reads=["tmp"], writes=[glr])
                k.op("pool", lambda e, ge=ge, j=j: e.tensor_copy(out=fct[:, j, 0:2], in_=ge[:, T:T + 2]), reads=[gr], writes=["fct"])
                k.op("pool", lambda e, gs=gs, j=j: e.tensor_copy(out=fct[:, j, 2:10].rearrange("p (a b) -> p a b", a=4), in_=gs[:, :, 4:6]), reads=[gr], writes=["fct"])

                def evac_u(ti, c0, n, ps, psr, s=s, jj=jj, glr=glr):
                    k.op("dve", lambda e: e.tensor_tensor(out=act[:, jj, c0:c0 + n], in0=gl[s][:, c0:c0 + n], in1=ps, op=OP.mult),
                         reads=[psr, glr], writes=["act%d_%d" % (jj, ti)])

                self.proj_chunk(lambda kc, su=su, r=r: wup[su][:, kc, r * 128:(r + 1) * 128], NCH,
                                lambda kc, c0, n: xn[:, kc, c0:c0 + n], lambda kc, ti: "xn%d_%d" % (kc, ti), ["wup%d" % su], evac_u)
            for jj, j in enumerate(grp):
                self.wload(wdn[:, jj, :], Wd[j * 128:(j + 1) * 128, :], "wdn%d" % jj, 1024)
            for oc in range(NCH):
                def evac_d(ti, c0, n, ps, psr, oc=oc):
                    hr = "h%d_%d" % (oc, ti)
                    k.op("dve", lambda e: e.tensor_tensor(out=h[:, oc, c0:c0 + n], in0=h[:, oc, c0:c0 + n], in1=ps, op=OP.add),
                         reads=[psr, hr], writes=[hr])

                self.proj_chunk(lambda kc, oc=oc: wdn[:, kc, oc * 128:(oc + 1) * 128], len(grp),
                                lambda kc, c0, n: act[:, kc, c0:c0 + n], lambda kc, ti: "act%d_%d" % (kc, ti),
                                ["wdn%d" % q for q in range(len(grp))], evac_d)
        self.barrier()
        ph.reset(mm)
        self.store_fm(fct[:, :, 0:2], NFF, 2, self.O["fc_p"][i], [], "fcp")
        self.store_fm(fct[:, :, 2:10], NFF, 8, self.O["fc_s"][i], [], "fcso")
        self.barrier()
        ph.reset(m)

    def ple_layer(self, i):
        k, c, ph = self.k, self.cfg, self.ph
        T, TT = c.T, c.TT
        h, xn = self.h, self.xn
        self.norm("nple%d" % i)
        m = ph.mark()
        pT = ph.bf16(2 * TT).rearrange("p (a b) -> p a b", a=2)
        pin = [ph.f32(256), ph.f32(256)]
        dpin = self.dpin
        nct = len(self.coltiles())
        for tt, (t0, n) in enumerate(self.tok_tiles()):
            s = tt % 2
            src = self.I["pp"][i, t0:t0 + n, :] if n == 128 else self.I["psm"][i]
            k.dma("sp", dpin[s], pin[s][0:n, :], src, writes=["pin%d" % s])
            ti = min(t0 // 512, nct - 1)
            ps, psr = self.ps_next()
            for kc in range(2):
                self.tr32(ps[:, kc * 128:kc * 128 + n], pin[s][0:n, kc * 128:(kc + 1) * 128], ["pin%d" % s], psr, n)
            dst = pT[:, :, t0:t0 + n]
            srcp = ps[:, 0:256].rearrange("p (a b) -> p a b", a=2)[:, :, 0:n]
            if tt % 2 == 0:
                k.op("act", lambda e, dst=dst, srcp=srcp: e.activation(out=dst, in_=srcp, func=AF.Copy), reads=[psr], writes=["pT%d" % ti])
            else:
                k.op("dve", lambda e, dst=dst, srcp=srcp: e.tensor_copy(out=dst, in_=srcp), reads=[psr], writes=["pT%d" % ti])
        wg = [ph.bf16(2048).rearrange("p (a b) -> p a b", a=8) for _ in range(2)]
        wp = [ph.bf16(512).rearrange("p (a b) -> p a b", a=2) for _ in range(2)]
        gt = [ph.f32(512), ph.f32(512)]
        t2 = [ph.f32(512), ph.f32(512)]
        Wg = self.I["w_pg"][i].rearrange("(kc p) n -> p kc n", p=128)
        Wp = self.I["w_ple"][i].rearrange("(kc p) n -> p kc n", p=128)
        q = 0
        for blk in range(4):
            s = blk % 2
            self.wload(wg[s], Wg[:, :, blk * 256:(blk + 1) * 256], "wg%d" % s, 2048)
            self.wload(wp[s], Wp[:, :, blk * 256:(blk + 1) * 256], "wp%d" % s, 512)
            for r in range(2):
                oc = blk * 2 + r
                for ti, (c0, n) in enumerate(self.coltiles()):
                    q ^= 1
                    if n == 16:
                        ps, psr = self.pss_next()
                        ps2, psr2 = self.pss_next()
                    else:
                        ps, psr = self.ps_next()
                        ps2, psr2 = self.ps_next()
                    for kc in range(NCH):
                        lhs = wg[s][:, kc, r * 128:(r + 1) * 128]
                        rhs = xn[:, kc, c0:c0 + n]
                        k.op("pe", lambda e, ps=ps, lhs=lhs, rhs=rhs, n=n, kc=kc: e.matmul(ps[:, 0:n], lhsT=lhs, rhs=rhs, start=(kc == 0), stop=(kc == NCH - 1)),
                             reads=["wg%d" % s, "xn%d_%d" % (kc, ti)], writes=[psr], inc=(kc == NCH - 1))
                    g = gt[q][:, 0:n]
                    k.op("act", lambda e, g=g, ps=ps, n=n: e.activation(out=g, in_=ps[:, 0:n], func=AF.Sigmoid), reads=[psr], writes=["gt%d" % q])
                    for kc in range(2):
                        lhs = wp[s][:, kc, r * 128:(r + 1) * 128]
                        rhs = pT[:, kc, c0:c0 + n]
                        k.op("pe", lambda e, ps2=ps2, lhs=lhs, rhs=rhs, n=n, kc=kc: e.matmul(ps2[:, 0:n], lhsT=lhs, rhs=rhs, start=(kc == 0), stop=(kc == 1)),
                             reads=["wp%d" % s, "pT%d" % ti], writes=[psr2], inc=(kc == 1))
                    tt2 = t2[q][:, 0:n]
                    k.op("dve", lambda e, tt2=tt2, g=g, ps2=ps2, n=n: e.tensor_tensor(out=tt2, in0=g, in1=ps2[:, 0:n], op=OP.mult),
                         reads=[psr2, "gt%d" % q], writes=["t2%d" % q])
                    hr = "h%d_%d" % (oc, ti)
                    hv = h[:, oc, c0:c0 + n]
                    k.op("pool", lambda e, hv=hv, tt2=tt2: e.tensor_tensor(out=hv, in0=hv, in1=tt2, op=OP.add), reads=["t2%d" % q, hr], writes=[hr])
        self.barrier()
        ph.reset(m)

    def rec_layer(self, i):
        k, c, ph = self.k, self.cfg, self.ph
        T, TT = c.T, c.TT
        h, xn = self.h, self.xn
        j = i // 2
        self.norm("nmix%d" % i)
        m = ph.mark()
        hist = ph.f32(4 * 76).rearrange("p (a b) -> p a b", a=4)
        tails = ph.f32(4 * 95).rearrange("p (a b) -> p a b", a=4)
        cneg = ph.f32(8)
        sm = [ph.f32(4) for _ in range(8)]
        mm = ph.mark()
        it = ph.f32(512)
        k.dma("sp", self.d_in, it[0:12, :], self.I["st_rc"][j], writes=["rit"])
        k.dma("sp", self.d_in, it[12:16, :], self.I["st_rh"][j], writes=["rit"])
        k.dma("sp", self.d_in, it[16:76, :], self.I["st_pool"][j], writes=["rit"])
        ps, psr = self.ps_next()
        for ch in range(4):
            self.tr32(ps[:, ch * 76:(ch + 1) * 76], it[0:76, ch * 128:(ch + 1) * 128], ["rit"], psr, 76)
        k.op("act", lambda e, ps=ps: e.activation(out=hist, in_=ps[:, 0:304].rearrange("p (a b) -> p a b", a=4), func=AF.Copy), reads=[psr], writes=["hist"])
        lam = self.vcol("lam%d" % j, 0, 4)
        al, x, z, z2, pl, rl, sp_, t_ = sm

        def dv(fn, reads, writes):
            k.op("dve", fn, reads=reads, writes=writes)

        k.op("act", lambda e: e.activation(out=al, in_=lam, func=AF.Abs), reads=["vec"], writes=["sm_al"])
        k.op("act", lambda e: e.activation(out=x, in_=al, func=AF.Exp, scale=-1.0), reads=["sm_al"], writes=["sm_x"])
        dv(lambda e: e.tensor_scalar(out=z, in0=x, scalar1=2.0, scalar2=None, op0=OP.add), ["sm_x"], ["sm_z"])
        dv(lambda e: e.reciprocal(out=z, in_=z), ["sm_z"], ["sm_z"])
        dv(lambda e: e.tensor_tensor(out=z, in0=z, in1=x, op=OP.mult), ["sm_z", "sm_x"], ["sm_z"])
        dv(lambda e: e.tensor_tensor(out=z2, in0=z, in1=z, op=OP.mult), ["sm_z"], ["sm_z2"])
        dv(lambda e: e.memset(pl, 1.0 / 17.0), [], ["sm_pl"])
        for cf in (15.0, 13.0, 11.0, 9.0, 7.0, 5.0, 3.0, 1.0):
            dv(lambda e: e.tensor_tensor(out=pl, in0=pl, in1=z2, op=OP.mult), ["sm_pl", "sm_z2"], ["sm_pl"])
            dv(lambda e, cf=cf: e.tensor_scalar(out=pl, in0=pl, scalar1=1.0 / cf, scalar2=None, op0=OP.add), ["sm_pl"], ["sm_pl"])
        dv(lambda e: e.tensor_tensor(out=pl, in0=pl, in1=z, op=OP.mult), ["sm_pl", "sm_z"], ["sm_pl"])
        dv(lambda e: e.tensor_scalar(out=rl, in0=lam, scalar1=-1.0, scalar2=0.0, op0=OP.mult, op1=OP.max), ["vec"], ["sm_rl"])
        dv(lambda e: e.scalar_tensor_tensor(out=sp_, in0=pl, scalar=2.0, in1=rl, op0=OP.mult, op1=OP.add), ["sm_pl", "sm_rl"], ["sm_sp"])
        dv(lambda e: e.tensor_scalar(out=cneg[:, 0:4], in0=sp_, scalar1=-8.0, scalar2=None, op0=OP.mult), ["sm_sp"], ["cneg"])
        dv(lambda e: e.tensor_scalar(out=cneg[:, 4:8], in0=sp_, scalar1=-16.0, scalar2=None, op0=OP.mult), ["sm_sp"], ["cneg"])
        self.barrier()
        ph.reset(mm)

        yab = ph.bf16(NCH * TT).rearrange("p (a b) -> p a b", a=NCH)
        wbf = [ph.bf16(2048).rearrange("p (a b) -> p a b", a=8) for _ in range(2)]
        wgt = ph.bf16(256)
        wpl = ph.bf16(128)
        B1 = ph.f32(T + 96)
        B2 = ph.f32(TT)
        B3 = ph.f32(TT)
        B4 = ph.f32(TT)
        Bh = ph.bf16(TT)
        ss2 = ph.f32(76).rearrange("p (a b) -> p a b", a=4)
        ss4 = ph.f32(76).rearrange("p (a b) -> p a b", a=4)
        Win = self.I["w_in_rec"][j].rearrange("(kc p) n -> p kc n", p=128)
        xnr = lambda kc, ti: "xn%d_%d" % (kc, ti)
        xnf = lambda kc, c0, n: xn[:, kc, c0:c0 + n]

        def v4(ap16):
            return ap16.rearrange("p (a b) -> p a b", a=4)

        for ch in range(4):
            blk, r = divmod(ch, 2)
            if r == 0:
                self.wload(wbf[0], Win[:, :, blk * 256:(blk + 1) * 256], "wbf0", 2048)
                self.wload(wbf[1], Win[:, :, 512 + blk * 256:512 + (blk + 1) * 256], "wbf1", 2048)
            B1s = B1[:, 3 + T:3 + T + 28].rearrange("p (a b) -> p a b", a=4)
            k.op("pool", lambda e: e.memset(B1[:, 0:3], 0.0), writes=["B1"])
            k.op("pool", lambda e, ch=ch, B1s=B1s: e.tensor_copy(out=B1s[:, :, 0:3], in_=hist[:, ch, 0:12].rearrange("p (a b) -> p a b", a=4)), reads=["hist"], writes=["B1"])

            def evac_xa(ti, c0, n, ps, psr, B1s=B1s):
                if n == 16:
                    k.op("act", lambda e: e.activation(out=B1s[:, :, 3:7], in_=v4(ps), func=AF.Copy), reads=[psr], writes=["B1"])
                else:
                    k.op("act", lambda e: e.activation(out=B1[:, 3 + c0:3 + c0 + n], in_=ps, func=AF.Copy), reads=[psr], writes=["B1"])

            self.proj_chunk(lambda kc, r=r: wbf[0][:, kc, r * 128:(r + 1) * 128], NCH, xnf, xnr, ["wbf0"], evac_xa)
            k.op("pool", lambda e, ch=ch: e.tensor_copy(out=tails[:, ch, 0:3], in_=B1[:, T:T + 3]), reads=["B1"], writes=["tails"])
            k.op("pool", lambda e, ch=ch, B1s=B1s: e.tensor_copy(out=tails[:, ch, 3:15].rearrange("p (a b) -> p a b", a=4), in_=B1s[:, :, 4:7]), reads=["B1"], writes=["tails"])
            w = [self.vcol("crw%d" % j, kk * 4 + ch) for kk in range(4)]
            b = self.vcol("crb%d" % j, ch)
            B2p = B2[:, 0:T]
            B2s = v4(B2[:, T:T + 16])
            dv(lambda e, w=w, b=b: e.tensor_scalar(out=B2p, in0=B1[:, 0:T], scalar1=w[0], scalar2=b, op0=OP.mult, op1=OP.add), ["B1", "vec"], ["B2"])
            for kk in (1, 2, 3):
                dv(lambda e, w=w, kk=kk: e.scalar_tensor_tensor(out=B2p, in0=B1[:, kk:kk + T], scalar=w[kk], in1=B2p, op0=OP.mult, op1=OP.add), ["B1", "B2"], ["B2"])
            dv(lambda e, w=w, b=b, B1s=B1s: e.tensor_scalar(out=B2s, in0=B1s[:, :, 0:4], scalar1=w[0], scalar2=b, op0=OP.mult, op1=OP.add), ["B1", "B2"], ["B2"])
            for kk in (1, 2, 3):
                dv(lambda e, w=w, kk=kk, B1s=B1s: e.scalar_tensor_tensor(out=B2s, in0=B1s[:, :, kk:kk + 4], scalar=w[kk], in1=B2s, op0=OP.mult, op1=OP.add), ["B1", "B2"], ["B2"])
            k.op("act", lambda e: e.activation(out=Bh, in_=B2, func=AF.Copy), reads=["B2"], writes=["Bh"])
            self.wload(wgt[:, 0:128], self.I["w_rg"][j, ch], "wgt", 128)
            self.wload(wgt[:, 128:256], self.I["w_ig"][j, ch], "wgt", 128)
            brg = self.vcol("brg%d" % j, ch)
            big = self.vcol("big%d" % j, ch)

            def evac_gate(dst, bias, reg):
                def f(ti, c0, n, ps, psr):
                    k.op("act", lambda e: e.activation(out=dst[:, c0:c0 + n], in_=ps, func=AF.Sigmoid, bias=bias), reads=[psr, "vec"], writes=[reg])
                return f

            bhf = lambda kc, c0, n: Bh[:, c0:c0 + n]
            self.proj_chunk(lambda kc: wgt[:, 0:128], 1, bhf, lambda kc, ti: "Bh", ["wgt"], evac_gate(B3, brg, "B3"))
            self.proj_chunk(lambda kc: wgt[:, 128:256], 1, bhf, lambda kc, ti: "Bh", ["wgt"], evac_gate(B4, big, "B4"))
            Ba = B1[:, 0:TT]
            c1 = cneg[:, ch:ch + 1]
            c2 = cneg[:, 4 + ch:5 + ch]
            k.op("act", lambda e, c1=c1: e.activation(out=Ba, in_=B3, func=AF.Exp, scale=c1), reads=["B3", "cneg", "tails"], writes=["B1"])
            k.op("act", lambda e, c2=c2: e.activation(out=B3, in_=B3, func=AF.Exp, scale=c2), reads=["B3", "cneg"], writes=["B3"])
            k.op("act", lambda e: e.activation(out=B3, in_=B3, func=AF.Sqrt, scale=-1.0, bias=self.one_t[:, 0:1]), reads=["B3"], writes=["B3"])
            dv(lambda e: e.tensor_tensor(out=B4, in0=B4, in1=B3, op=OP.mult), ["B3", "B4"], ["B4"])
            dv(lambda e: e.tensor_tensor(out=B4, in0=B4, in1=B2, op=OP.mult), ["B4", "B2", "Bh"], ["B4"])
            dv(lambda e: e.tensor_tensor_scan(out=B2[:, 0:T], data0=Ba[:, 0:T], data1=B4[:, 0:T], initial=0.0, op0=OP.mult, op1=OP.add), ["B1", "B4", "Bh"], ["B2"])
            for sq in range(4):
                a0 = T + 4 * sq
                dv(lambda e, a0=a0, sq=sq, ch=ch: e.tensor_tensor_scan(out=B2[:, a0:a0 + 4], data0=Ba[:, a0:a0 + 4], data1=B4[:, a0:a0 + 4],
                                                                     initial=hist[:, ch, 12 + sq:13 + sq], op0=OP.mult, op1=OP.add), ["B1", "B4", "hist", "B2"], ["B2"])
            k.op("pool", lambda e, ch=ch: e.tensor_copy(out=tails[:, ch, 15:16], in_=B2[:, T - 1:T]), reads=["B2"], writes=["tails"])
            k.op("pool", lambda e, ch=ch: e.tensor_copy(out=tails[:, ch, 16:20], in_=v4(B2[:, T:T + 16])[:, :, 3]), reads=["B2"], writes=["tails"])

            def evac_ga(ti, c0, n, ps, psr):
                k.op("act", lambda e: e.activation(out=B3[:, c0:c0 + n], in_=ps, func=AF.Gelu), reads=[psr], writes=["B3"])

            self.proj_chunk(lambda kc, r=r: wbf[1][:, kc, r * 128:(r + 1) * 128], NCH, xnf, xnr, ["wbf1"], evac_ga)
            for ti, (c0, n) in enumerate(self.coltiles()):
                k.op("dve", lambda e, ch=ch, c0=c0, n=n: e.tensor_tensor(out=yab[:, ch, c0:c0 + n], in0=B2[:, c0:c0 + n], in1=B3[:, c0:c0 + n], op=OP.mult),
                     reads=["B2", "B3"], writes=["yab%d_%d" % (ch, ti)])

        for g in range(4):
            wnd = 2 << g
            blk, r = divmod(g, 2)
            if r == 0:
                self.wload(wbf[0], Win[:, :, 1024 + blk * 256:1024 + (blk + 1) * 256], "wbf0", 2048)
            self.wload(wpl, self.I["w_pool"][j, g], "wpl", 128)
            B1s = B1[:, 15 + T:15 + T + 76].rearrange("p (a b) -> p a b", a=4)
            k.op("pool", lambda e: e.memset(B1[:, 0:15], 0.0), writes=["B1"])
            k.op("pool", lambda e, g=g, B1s=B1s: e.tensor_copy(out=B1s[:, :, 0:15], in_=hist[:, g, 16:76].rearrange("p (a b) -> p a b", a=4)), reads=["hist"], writes=["B1"])

            def evac_xb(ti, c0, n, ps, psr, B1s=B1s):
                if n == 16:
                    k.op("act", lambda e: e.activation(out=B1s[:, :, 15:19], in_=v4(ps), func=AF.Copy), reads=[psr], writes=["B1"])
                else:
                    k.op("act", lambda e: e.activation(out=B1[:, 15 + c0:15 + c0 + n], in_=ps, func=AF.Copy), reads=[psr], writes=["B1"])

            self.proj_chunk(lambda kc, r=r: wbf[0][:, kc, r * 128:(r + 1) * 128], NCH, xnf, xnr, ["wbf0"], evac_xb)
            k.op("pool", lambda e, g=g: e.tensor_copy(out=tails[:, g, 20:35], in_=B1[:, T:T + 15]), reads=["B1"], writes=["tails"])
            k.op("pool", lambda e, g=g, B1s=B1s: e.tensor_copy(out=tails[:, g, 35:95].rearrange("p (a b) -> p a b", a=4), in_=B1s[:, :, 4:19]), reads=["B1"], writes=["tails"])
            E = 15 + T
            src, srcs = B1, B1s
            bufs = [(B2, ss2, "B2"), (B3, ss4, "B3")]
            step = 1
            srcreg = "B1"
            for lv in range(g + 1):
                dst, dsts, dreg = bufs[lv % 2]
                v0 = 2 * step - 1
                k.op("dve", lambda e, dst=dst, src=src, step=step, v0=v0: e.tensor_tensor(out=dst[:, v0:E], in0=src[:, v0:E], in1=src[:, v0 - step:E - step], op=OP.add),
                     reads=[srcreg], writes=[dreg])
                k.op("pool", lambda e, dsts=dsts, srcs=srcs, step=step, v0=v0: e.tensor_tensor(out=dsts[:, :, v0:19], in0=srcs[:, :, v0:19], in1=srcs[:, :, v0 - step:19 - step], op=OP.add),
                     reads=[srcreg], writes=[dreg])
                src, srcs, srcreg = dst, dsts, dreg
                step *= 2
            S, Ss, Sreg = src, srcs, srcreg
            inv = 1.0 / wnd
            dv(lambda e, S=S, inv=inv: e.scalar_tensor_tensor(out=B4[:, 0:T], in0=S[:, 15:E], scalar=inv, in1=B1[:, 15:E], op0=OP.mult, op1=OP.subtract), [Sreg, "B1"], ["B4"])
            if wnd > 1:
                nf = wnd - 1
                rc = self.vcol("poolrc", g * 15, nf)
                dv(lambda e, S=S, rc=rc, nf=nf: e.tensor_tensor(out=B4[:, 0:nf], in0=S[:, 15:15 + nf], in1=rc, op=OP.mult), [Sreg, "vec", "B4"], ["B4"])
                dv(lambda e, nf=nf: e.tensor_tensor(out=B4[:, 0:nf], in0=B4[:, 0:nf], in1=B1[:, 15:15 + nf], op=OP.subtract), ["B4", "B1"], ["B4"])
            dv(lambda e, Ss=Ss, B1s=B1s, inv=inv: e.scalar_tensor_tensor(out=v4(B4[:, T:T + 16]), in0=Ss[:, :, 15:19], scalar=inv, in1=B1s[:, :, 15:19], op0=OP.mult, op1=OP.subtract),
               [Sreg, "B1", "B4"], ["B4"])
            k.op("act", lambda e: e.activation(out=Bh, in_=B4, func=AF.Copy), reads=["B4"], writes=["Bh"])
            psc = self.vcol("psc%d" % j, g)

            def evac_pl(ti, c0, n, ps, psr, g=g, psc=psc):
                k.op("act", lambda e: e.activation(out=yab[:, 4 + g, c0:c0 + n], in_=ps, func=AF.Identity, scale=psc), reads=[psr, "vec"], writes=["yab%d_%d" % (4 + g, ti)])

            self.proj_chunk(lambda kc: wpl, 1, lambda kc, c0, n: Bh[:, c0:c0 + n], lambda kc, ti: "Bh", ["wpl"], evac_pl)

        Wo = self.I["w_out_rec"][j].rearrange("(kc p) n -> p kc n", p=128)
        for blk in range(4):
            s = blk % 2
            self.wload(wbf[s], Wo[:, :, blk * 256:(blk + 1) * 256], "wbf%d" % s, 2048)
            for r in range(2):
                oc = blk * 2 + r

                def evac_o(ti, c0, n, ps, psr, oc=oc):
                    hr = "h%d_%d" % (oc, ti)
                    k.op("dve", lambda e: e.tensor_tensor(out=h[:, oc, c0:c0 + n], in0=h[:, oc, c0:c0 + n], in1=ps, op=OP.add), reads=[psr, hr], writes=[hr])

                self.proj_chunk(lambda kc, s=s, r=r: wbf[s][:, kc, r * 128:(r + 1) * 128], NCH,
                                lambda kc, c0, n: yab[:, kc, c0:c0 + n], lambda kc, ti: "yab%d_%d" % (kc, ti), ["wbf%d" % s], evac_o)
        self.barrier()
        ph.reset(mm)
        O = self.O
        self.store_fm(tails[:, :, 0:3], 4, 3, O["rc_p"][j], [], "o1")
        self.store_fm(tails[:, :, 3:15], 4, 12, O["rc_s"][j], [], "o2")
        self.store_fm(tails[:, :, 15:16], 4, 1, O["rh_p"][j], [], "o3")
        self.store_fm(tails[:, :, 16:20], 4, 4, O["rh_s"][j], [], "o4")
        self.store_fm(tails[:, :, 20:35], 4, 15, O["pl_p"][j], [], "o5")
        self.store_fm(tails[:, :, 35:95], 4, 60, O["pl_s"][j], [], "o6")
        self.barrier()
        ph.reset(m)

    def rot(self, banks):
        b = banks[self._rot % len(banks)]
        self._rot += 1
        return self.PS[b], "ps%d" % b

    def attn_layer(self, i):
        k, c, ph = self.k, self.cfg, self.ph
        T, TT, NT = c.T, c.TT, c.NT
        h, xn = self.h, self.xn
        j = i // 2
        I, O = self.I, self.O
        self._rot = 0
        RB = [0, 1, 2, 3]
        self.norm("nmix%d" % i)
        hflat = self.hflat
        k.dma("sp", self.d_out, self.hscr, hflat, reads=["h%d_%d" % (ch, ti) for ch in range(NCH) for ti in range(len(self.coltiles()))], writes=["hscr"])
        self.barrier()
        ha = Arena(self.ph.ap, NCH * TT)
        m = ph.mark()
        scale = 1.0 / float(np.sqrt(HD))
        NITER = 11

        def hbf(nel):
            w = ((nel + 1) // 2 + 7) // 8 * 8
            return ha.bf16(nel) if ha.top + w <= ha.size else ph.bf16(nel)

        def hf32(nel):
            w = (nel + 7) // 8 * 8
            return ha.f32(nel) if ha.top + w <= ha.size else ph.f32(nel)

        kT = hbf(NKV * TT).rearrange("p (a b) -> p a b", a=NKV)
        vb = hbf((NT + 1) * 512).rearrange("p (a b) -> p a b", a=NT + 1)
        kiT = hbf(TT)
        qT = hbf(NH * 512).rearrange("p (a b) -> p a b", a=NH)
        qiT = hbf(4 * 512).rearrange("p (a b) -> p a b", a=4)
        Isb = hf32(max(T, 256))
        maskq = hbf(T)
        gqk = ph.f32(256).rearrange("p (a b) -> p a b", a=2)
        negB = ph.f32(8)
        cmask = ph.bf16(4 * 512).rearrange("p (a b) -> p a b", a=4)
        pw2 = ph.f32(16)
        watt = [ph.bf16(4096).rearrange("p (a b) -> p a b", a=8) for _ in range(2)]
        zsb = [ph.f32(512), ph.f32(512)]
        outf = [ph.f32(512), ph.f32(512)]
        outb = [ph.bf16(512), ph.bf16(512)]
        rpt = [ph.f32(128).rearrange("p (a b) -> p a b", a=2) for _ in range(2)]
        rpit = [ph.f32(64).rearrange("p (a b) -> p a b", a=2) for _ in range(2)]
        tq = [ph.f32(256) for _ in range(4)]
        ssq = ph.f32(8)
        junk = ph.f32(128)
        drp = self.drp
        ha_mark2 = ha.mark()
        Wa = I["w_in_attn"][j].rearrange("(kc p) n -> p kc n", p=128)
        nct = len(self.coltiles())

        k.dma("sp", self.d_in, gqk, I["qkn"][j:j + 1].to_broadcast([128, 2, HD]), writes=["gqk"])
        k.op("dve", lambda e: e.tensor_reduce(out=ssq[:, 0:2], in_=gqk, axis=AX.X, op=OP.max, apply_absolute_value=True), reads=["gqk"], writes=["ssq"])
        k.op("dve", lambda e: e.tensor_tensor(out=negB[:, 0:1], in0=ssq[:, 0:1], in1=ssq[:, 1:2], op=OP.mult), reads=["ssq"], writes=["negB"])
        k.op("dve", lambda e: e.tensor_scalar(out=negB[:, 0:1], in0=negB[:, 0:1], scalar1=-float(np.sqrt(HD)), scalar2=None, op0=OP.mult), reads=["negB"], writes=["negB"])
        cmf = ph.f32(512)
        for pq in range(4):
            k.op("pool", lambda e: e.memset(cmf, 0.0), writes=["cmf"])
            blk = cmf[:, pq * 128:(pq + 1) * 128]
            k.op("pool", lambda e, blk=blk: e.affine_select(out=blk, in_=blk, pattern=[[-1, 128]], compare_op=OP.is_ge, fill=NEG, base=0, channel_multiplier=1), reads=["cmf"], writes=["cmf"])
            k.op("pool", lambda e, pq=pq: e.tensor_copy(out=cmask[:, pq, :], in_=cmf), reads=["cmf"], writes=["cmask"])
        for it in range(NITER):
            k.op("pool", lambda e, it=it: e.memset(pw2[:, it:it + 1], 0.5 ** (it + 1)), writes=["pw2"])

        def load_w(slot, col0, ncols):
            for hf in range(0, ncols, 256):
                w = min(256, ncols - hf)
                self.wload(watt[slot][:, :, hf:hf + w], Wa[:, :, col0 + hf:col0 + hf + w], "watt%d" % slot, 8 * w)

        def tm_proj(t0, n, slot, ncols):
            ps, psr = self.rot(RB)
            ti = min(t0 // 512, nct - 1)
            for kc in range(NCH):
                lhs = xn[:, kc, t0:t0 + n]
                rhs = watt[slot][:, kc, 0:ncols]
                k.op("pe", lambda e, ps=ps, lhs=lhs, rhs=rhs, n=n, ncols=ncols, kc=kc: e.matmul(ps[0:n, 0:ncols], lhsT=lhs, rhs=rhs, start=(kc == 0), stop=(kc == NCH - 1)),
                     reads=["watt%d" % slot, "xn%d_%d" % (kc, ti)], writes=[psr], inc=(kc == NCH - 1))
            return ps, psr

        def load_rope(tt, n):
            s = tt % 2
            k.dma("sp", drp[s], rpt[s][0:n], I["rope"][tt, 0:n], writes=["rpt%d" % s])
            k.dma("sp", drp[s], rpit[s][0:n], I["ropei"][tt, 0:n], writes=["rpit%d" % s])
            return s

        def bc(ap2, shape):
            return ap2.to_broadcast(shape)

        def rope_apply(src3, dst3, n, nh, half, cos, sin, sreg, dreg, extra_reads):
            x1, x2 = src3[:, :, 0:half], src3[:, :, half:2 * half]
            cb = cos.unsqueeze(1).to_broadcast([n, nh, half])
            sb = sin.unsqueeze(1).to_broadcast([n, nh, half])
            w = nh * half
            t1, t2, t3, t4 = [t[0:n, 0:w].rearrange("p (a b) -> p a b", a=nh) for t in tq]
            rd = [sreg] + list(extra_reads)
            k.op("dve", lambda e: e.tensor_tensor(out=t1, in0=x1, in1=cb, op=OP.mult), reads=rd, writes=["tq0"])
            k.op("pool", lambda e: e.tensor_tensor(out=t2, in0=x2, in1=sb, op=OP.mult), reads=rd, writes=["tq1"])
            k.op("pool", lambda e: e.tensor_tensor(out=t3, in0=x1, in1=sb, op=OP.mult), reads=rd, writes=["tq2"])
            k.op("dve", lambda e: e.tensor_tensor(out=t4, in0=x2, in1=cb, op=OP.mult), reads=rd, writes=["tq3"])
            k.op("dve", lambda e: e.tensor_tensor(out=dst3[:, :, 0:half], in0=t1, in1=t2, op=OP.subtract), reads=["tq0", "tq1"], writes=[dreg])
            k.op("pool", lambda e: e.tensor_tensor(out=dst3[:, :, half:2 * half], in0=t3, in1=t4, op=OP.add), reads=["tq2", "tq3"], writes=[dreg])

        def qk_post(ps, psr, n, gsel, rs_slot, q):
            z = zsb[q][0:n]
            zr = "zsb%d" % q
            k.op("act", lambda e: e.activation(out=z, in_=ps[0:n, :], func=AF.Copy), reads=[psr], writes=[zr])
            for hh in range(4):
                k.op("act", lambda e, hh=hh: e.activation(out=junk[0:n], in_=z[:, hh * 128:(hh + 1) * 128], func=AF.Square, accum_out=ssq[0:n, hh:hh + 1]),
                     reads=[zr], writes=["ssq", "junk"])
            k.op("act", lambda e: e.activation(out=ssq[0:n, 0:4], in_=ssq[0:n, 0:4], func=AF.Sqrt, bias=self.eps_t[0:n, 0:1], scale=1.0 / HD), reads=["ssq"], writes=["ssq"])
            k.op("dve", lambda e: e.reciprocal(out=ssq[0:n, 0:4], in_=ssq[0:n, 0:4]), reads=["ssq"], writes=["ssq"])
            z3 = z.rearrange("p (a b) -> p a b", a=4)
            k.op("dve", lambda e: e.tensor_tensor(out=z3, in0=z3, in1=ssq[0:n, 0:4].unsqueeze(2).to_broadcast([n, 4, HD]), op=OP.mult), reads=[zr, "ssq"], writes=[zr])
            k.op("pool", lambda e: e.tensor_tensor(out=z3, in0=z3, in1=gqk[0:n, gsel, :].unsqueeze(1).to_broadcast([n, 4, HD]), op=OP.mult), reads=[zr, "gqk"], writes=[zr])
            o3 = outf[q][0:n].rearrange("p (a b) -> p a b", a=4)
            rope_apply(z3, o3, n, 4, 64, rpt[rs_slot][0:n, 0, :], rpt[rs_slot][0:n, 1, :], zr, "outf%d" % q, ["rpt%d" % rs_slot])
            k.op("act", lambda e: e.activation(out=outb[q][0:n], in_=outf[q][0:n], func=AF.Copy), reads=["outf%d" % q], writes=["outb%d" % q])

        def transposes_to(dst3, src_b, n, nblk, reads, dreg):
            ps, psr = self.rot(RB)
            pb = ps.bitcast(BF16)
            for bq in range(nblk):
                k.op("pe", lambda e, bq=bq: e.transpose(pb[:, bq * 128:bq * 128 + n], src_b[0:n, bq * 128:(bq + 1) * 128], self.ident_b[0:n, 0:n]),
                     reads=list(reads) + ["ident_b"], writes=[psr])
            k.op("act", lambda e: e.activation(out=dst3, in_=pb[:, 0:nblk * 128].rearrange("p (a b) -> p a b", a=nblk)[:, :, 0:n], func=AF.Copy), reads=[psr], writes=[dreg])

        toks = self.tok_tiles()
        load_w(0, 1024, 512)
        for tt, (t0, n) in enumerate(toks):
            rs_slot = load_rope(tt, n)
            q = tt % 2
            ps, psr = tm_proj(t0, n, 0, 512)
            qk_post(ps, psr, n, 1, rs_slot, q)
            dst = O["k_p"][j, t0:t0 + n, :] if n == 128 else O["k_s"][j]
            k.dma("sp", self.d_out, dst, outf[q][0:n], reads=["outf%d" % q])
            transposes_to(kT[:, :, t0:t0 + n], outb[q], n, 4, ["outb%d" % q], "kT")
        load_w(1, 1536, 512)
        for tt, (t0, n) in enumerate(toks):
            q = tt % 2
            ps, psr = tm_proj(t0, n, 1, 512)
            k.op("act", lambda e, q=q, ps=ps, n=n: e.activation(out=outf[q][0:n], in_=ps[0:n, :], func=AF.Copy), reads=[psr], writes=["outf%d" % q])
            dst = O["v_p"][j, t0:t0 + n, :] if n == 128 else O["v_s"][j]
            k.dma("sp", self.d_out, dst, outf[q][0:n], reads=["outf%d" % q])
            k.op("pool", lambda e, q=q, tt=tt, n=n: e.tensor_copy(out=vb[0:n, tt, :], in_=outf[q][0:n]), reads=["outf%d" % q], writes=["vb"])
        load_w(0, 2560, 64)
        for tt, (t0, n) in enumerate(toks):
            rs_slot = load_rope(tt, n)
            q = tt % 2
            ps, psr = tm_proj(t0, n, 0, 64)
            z = zsb[q][0:n, 0:64]
            k.op("act", lambda e, z=z, ps=ps, n=n: e.activation(out=z, in_=ps[0:n, 0:64], func=AF.Copy), reads=[psr], writes=["zsb%d" % q])
            o1 = outf[q][0:n, 0:64]
            rope_apply(z.rearrange("p (a b) -> p a b", a=1), o1.rearrange("p (a b) -> p a b", a=1), n, 1, 32,
                       rpit[rs_slot][0:n, 0, :], rpit[rs_slot][0:n, 1, :], "zsb%d" % q, "outf%d" % q, ["rpit%d" % rs_slot])
            dst = O["ki_p"][j, t0:t0 + n, :] if n == 128 else O["ki_s"][j]
            k.dma("sp", self.d_out, dst, o1, reads=["outf%d" % q])
            k.op("act", lambda e, q=q, n=n, o1=o1: e.activation(out=outb[q][0:n, 0:64], in_=o1, func=AF.Copy), reads=["outf%d" % q], writes=["outb%d" % q])
            k.op("act", lambda e, q=q, n=n, o1=o1: e.activation(out=outb[q][0:n, 64:128], in_=o1, func=AF.Copy), reads=["outf%d" % q], writes=["outb%d" % q])
            transposes_to(kiT[:, t0:t0 + n].rearrange("p (a b) -> p a b", a=1), outb[q], n, 1, ["outb%d" % q], "kiT")

        wsc = ph.f32(8)
        wout = [ph.bf16(2048).rearrange("p (a b) -> p a b", a=8) for _ in range(2)]
        hp = [ph.f32(512), ph.f32(512)]
        ph_mark2 = ph.mark()
        wdiag = ph.bf16(IDX_H * 128).rearrange("p (a b) -> p a b", a=IDX_H)
        Rh = [ph.bf16(512), ph.bf16(512)]
        maskT = ph.bf16((T // 128) * 512).rearrange("p (a b) -> p a b", a=T // 128)
        Eb3 = [ph.bf16(512), ph.bf16(512), ph.bf16(512)]
        self._eb = 0
        rden = ph.f32(512)
        attnT = ph.bf16(NH * 512).rearrange("p (a b) -> p a b", a=NH)
        bs = [ph.f32(8) for _ in range(6)]
        wk = ph.f32(16)
        cjunk = maskq
        dhp = self.dhp
        Wo = I["w_out_attn"][j].rearrange("(kc p) n -> p kc n", p=128)
        hs3 = self.hscr.rearrange("p (c t) -> p c t", c=NCH)

        def q_side(tiles, nq):
            for cbi in range(2):
                load_w(cbi, cbi * 512, 512)
            for (li, tt, t0, n) in tiles:
                rs_slot = load_rope(tt, n)
                for cbi in range(2):
                    q = cbi
                    ps, psr = tm_proj(t0, n, cbi, 512)
                    qk_post(ps, psr, n, 0, rs_slot, q)
                    transposes_to(qT[:, cbi * 4:cbi * 4 + 4, li * 128:li * 128 + n], outb[q], n, 4, ["outb%d" % q], "qT")
            load_w(0, 2048, 512)
            for (li, tt, t0, n) in tiles:
                rs_slot = load_rope(tt, n)
                ps, psr = tm_proj(t0, n, 0, 512)
                z = zsb[0][0:n]
                k.op("act", lambda e, z=z, ps=ps, n=n: e.activation(out=z, in_=ps[0:n, :], func=AF.Copy), reads=[psr], writes=["zsb0"])
                o3 = outf[0][0:n].rearrange("p (a b) -> p a b", a=8)
                rope_apply(z.rearrange("p (a b) -> p a b", a=8), o3, n, 8, 32, rpit[rs_slot][0:n, 0, :], rpit[rs_slot][0:n, 1, :], "zsb0", "outf0", ["rpit%d" % rs_slot])
                k.op("act", lambda e, n=n: e.activation(out=outb[0][0:n], in_=outf[0][0:n], func=AF.Copy), reads=["outf0"], writes=["outb0"])
                transposes_to(qiT[:, :, li * 128:li * 128 + n], outb[0], n, 4, ["outb0"], "qiT")

        def w_side(t0, n):
            ps, psr = tm_proj(t0, n, 1, 8)
            k.op("act", lambda e, ps=ps, n=n: e.activation(out=wsc[0:n, 0:8], in_=ps[0:n, 0:8], func=AF.Copy, scale=float(IDX_D ** -0.5 * IDX_H ** -0.5)),
                 reads=[psr], writes=["wsc"])

        NC = T // 512
        for cq in range(NC):
            tiles = [(li, 4 * cq + li, (4 * cq + li) * 128, 128) for li in range(4)]
            q_side(tiles, 4)
            load_w(1, 2624, 8)
            for (li, tt, t0, n) in tiles:
                qt = tt
                w_side(t0, n)
                for hh in range(IDX_H):
                    k.op("pool", lambda e, hh=hh: e.tensor_scalar(out=wdiag[:, hh, :], in0=self.ident_f, scalar1=wsc[:, hh:hh + 1], scalar2=1.0, op0=OP.mult, op1=OP.mult),
                         reads=["wsc", "ident_f"], writes=["wdiag"])
                nkb = qt + 1
                nk = nkb * 128
                for kg in range((nkb + 3) // 4):
                    ncols = min(512, nk - kg * 512)
                    ia, iar = self.PS[4], "ps4"
                    diag_here = (qt // 4 == kg)
                    for hh in range(IDX_H):
                        hpair, par = divmod(hh, 2)
                        ps, psr = self.rot(RB)
                        lhs = qiT[par * 64:(par + 1) * 64, hpair, li * 128:(li + 1) * 128]
                        rhs = kiT[par * 64:(par + 1) * 64, kg * 512:kg * 512 + ncols]
                        k.op("pe", lambda e, ps=ps, lhs=lhs, rhs=rhs, ncols=ncols: e.matmul(ps[:, 0:ncols], lhsT=lhs, rhs=rhs, start=True, stop=True),
                             reads=["qiT", "kiT"], writes=[psr])
                        r = Rh[hh % 2]
                        if hh % 2 == 0:
                            k.op("act", lambda e, r=r, ps=ps, ncols=ncols: e.activation(out=r[:, 0:ncols], in_=ps[:, 0:ncols], func=AF.Relu), reads=[psr], writes=["Rh%d" % (hh % 2)])
                        else:
                            k.op("dve", lambda e, r=r, ps=ps, ncols=ncols: e.tensor_scalar(out=r[:, 0:ncols], in0=ps[:, 0:ncols], scalar1=0.0, scalar2=None, op0=OP.max), reads=[psr], writes=["Rh%d" % (hh % 2)])
                        last = (hh == IDX_H - 1) and not diag_here
                        k.op("pe", lambda e, ia=ia, hh=hh, r=r, ncols=ncols, last=last: e.matmul(ia[:, 0:ncols], lhsT=wdiag[:, hh, :], rhs=r[:, 0:ncols], start=(hh == 0), stop=last),
                             reads=["wdiag", "Rh%d" % (hh % 2)], writes=[iar], inc=True)
                    if diag_here:
                        pq = qt % 4
                        k.op("pe", lambda e, ia=ia, pq=pq, ncols=ncols: e.matmul(ia[:, 0:ncols], lhsT=self.ident_b, rhs=cmask[:, pq, 0:ncols], start=False, stop=True),
                             reads=["ident_b", "cmask"], writes=[iar])
                    k.op("act", lambda e, ia=ia, kg=kg, ncols=ncols: e.activation(out=Isb[:, kg * 512:kg * 512 + ncols], in_=ia[:, 0:ncols], func=AF.Copy), reads=[iar], writes=["Isb"])
                lo, wd, mid, cnt, tt_, thr = bs
                TK = c.TOPK
                if qt >= TK // 128:
                    k.op("dve", lambda e: e.tensor_reduce(out=lo[:, 0:1], in_=Isb[:, 0:TK], axis=AX.X, op=OP.min), reads=["Isb"], writes=["bs_lo"])
                    k.op("dve", lambda e, nk=nk: e.tensor_reduce(out=wd[:, 0:1], in_=Isb[:, 0:nk], axis=AX.X, op=OP.max), reads=["Isb"], writes=["bs_w"])
                    k.op("dve", lambda e: e.tensor_tensor(out=wd[:, 0:1], in0=wd[:, 0:1], in1=lo[:, 0:1], op=OP.subtract), reads=["bs_w", "bs_lo"], writes=["bs_w"])
                    k.op("dve", lambda e: e.tensor_scalar(out=wd[:, 0:1], in0=wd[:, 0:1], scalar1=1.0001, scalar2=1e-6, op0=OP.mult, op1=OP.add), reads=["bs_w"], writes=["bs_w"])
                    k.op("dve", lambda e: e.tensor_tensor(out=wk[:, 0:NITER], in0=pw2[:, 0:NITER], in1=wd[:, 0:1].to_broadcast([128, NITER]), op=OP.mult), reads=["bs_w", "pw2"], writes=["wk"])
                    k.op("dve", lambda e: e.tensor_tensor(out=mid[:, 0:1], in0=lo[:, 0:1], in1=wk[:, 0:1], op=OP.add), reads=["bs_lo", "wk"], writes=["bs_mid"])
                    for it in range(NITER):
                        k.op("dve", lambda e, nk=nk: e.tensor_scalar(out=cjunk[:, 0:nk], in0=Isb[:, 0:nk], scalar1=mid[:, 0:1], scalar2=None, op0=OP.is_ge, op1=OP.add, accum_out=cnt[:, 0:1]),
                             reads=["Isb", "bs_mid"], writes=["bs_cnt", "maskq"])
                        k.op("dve", lambda e: e.tensor_scalar(out=tt_[:, 0:1], in0=cnt[:, 0:1], scalar1=float(TK) - 0.5, scalar2=0.5, op0=OP.is_ge, op1=OP.subtract), reads=["bs_cnt"], writes=["bs_t"])
                        k.op("dve", lambda e, it=it: e.scalar_tensor_tensor(out=mid[:, 0:1], in0=tt_[:, 0:1], scalar=wk[:, it:it + 1], in1=mid[:, 0:1], op0=OP.mult, op1=OP.add), reads=["bs_t", "wk", "bs_mid"], writes=["bs_mid"])
                    k.op("dve", lambda e: e.scalar_tensor_tensor(out=lo[:, 0:1], in0=wk[:, NITER - 1:NITER], scalar=-0.5, in1=mid[:, 0:1], op0=OP.mult, op1=OP.add), reads=["bs_mid", "wk"], writes=["bs_lo"])
                    thr_ap, thr_reg = lo, "bs_lo"
                else:
                    k.op("dve", lambda e: e.memset(thr[:, 0:1], -1.0e29), writes=["bs_thr"])
                    thr_ap, thr_reg = thr, "bs_thr"
                k.op("dve", lambda e, nk=nk, thr_ap=thr_ap: e.tensor_scalar(out=maskq[:, 0:nk], in0=Isb[:, 0:nk], scalar1=thr_ap[:, 0:1], scalar2=-1.0e5, op0=OP.is_lt, op1=OP.mult), reads=["Isb", thr_reg], writes=["maskq"])
                for g0 in range(0, nkb, 8):
                    g1 = min(nkb, g0 + 8)
                    pm, pmr = self.PS[5], "ps5"
                    pmb = pm.bitcast(BF16)
                    for kb in range(g0, g1):
                        k.op("pe", lambda e, kb=kb, g0=g0, pmb=pmb: e.transpose(pmb[:, (kb - g0) * 128:(kb - g0 + 1) * 128], maskq[:, kb * 128:(kb + 1) * 128], self.ident_b),
                             reads=["maskq", "ident_b"], writes=[pmr])
                    nb = g1 - g0
                    k.op("act", lambda e, g0=g0, g1=g1, nb=nb, li=li, pmb=pmb: e.activation(out=maskT[:, g0:g1, li * 128:(li + 1) * 128], in_=pmb[:, 0:nb * 128].rearrange("p (a b) -> p a b", a=nb), func=AF.Copy),
                         reads=[pmr], writes=["maskT"])
            nkb_c = 4 * cq + 4
            ops, opr = self.PS[6], "ps6"
            dps, dpr = self.PS[7], "ps7"
            for hh in range(NH):
                kvh = hh // 2
                for kb in range(nkb_c):
                    c0 = max(0, kb - 4 * cq) * 128
                    ncol = 512 - c0
                    ps, psr = self.rot(RB)
                    lhs = kT[:, kvh, kb * 128:(kb + 1) * 128]
                    rhs = qT[:, hh, c0:512]
                    k.op("pe", lambda e, ps=ps, lhs=lhs, rhs=rhs, ncol=ncol: e.matmul(ps[:, 0:ncol], lhsT=lhs, rhs=rhs, start=True, stop=False), reads=["kT", "qT"], writes=[psr], inc=False)
                    k.op("pe", lambda e, ps=ps, kb=kb, c0=c0, ncol=ncol: e.matmul(ps[:, 0:ncol], lhsT=self.ident_b, rhs=maskT[:, kb, c0:512], start=False, stop=True), reads=["ident_b", "maskT"], writes=[psr])
                    es = self._eb % 3
                    self._eb += 1
                    pb_ = Eb3[es]
                    k.op("act", lambda e, pb_=pb_, ps=ps, ncol=ncol: e.activation(out=pb_[:, 0:ncol], in_=ps[:, 0:ncol], func=AF.Exp, scale=scale, bias=negB[:, 0:1]),
                         reads=[psr, "negB"], writes=["Eb%d" % es])
                    vl = vb[:, kb, kvh * 128:(kvh + 1) * 128]
                    last = (kb == nkb_c - 1)
                    k.op("pe", lambda e, vl=vl, pb_=pb_, c0=c0, ncol=ncol, kb=kb, last=last: e.matmul(ops[:, c0:512], lhsT=vl, rhs=pb_[:, 0:ncol], start=(kb == 0), stop=last),
                         reads=["vb", "Eb%d" % es], writes=[opr], inc=True)
                    k.op("pe", lambda e, pb_=pb_, c0=c0, ncol=ncol, kb=kb, last=last: e.matmul(dps[:, c0:512], lhsT=self.ones_b, rhs=pb_[:, 0:ncol], start=(kb == 0), stop=last),
                         reads=["ones_b", "Eb%d" % es], writes=[dpr], inc=True)
                k.op("dve", lambda e: e.reciprocal(out=rden, in_=dps), reads=[dpr], writes=["rden"])
                k.op("dve", lambda e, hh=hh: e.tensor_tensor(out=attnT[:, hh, :], in0=ops, in1=rden, op=OP.mult), reads=[opr, "rden"], writes=["attnT"])
            for blk in range(4):
                s = blk % 2
                self.wload(wout[s], Wo[:, :, blk * 256:(blk + 1) * 256], "wout%d" % s, 2048)
                for r in range(2):
                    oc = blk * 2 + r
                    q = oc % 2
                    k.dma("sp", dhp[q], hp[q], hs3[:, oc, cq * 512:(cq + 1) * 512], reads=["hscr%d_%d" % (oc, cq)], writes=["hp%d" % q])
                    ps, psr = self.rot(RB)
                    for hh in range(NH):
                        lhs = wout[s][:, hh, r * 128:(r + 1) * 128]
                        rhs = attnT[:, hh, :]
                        k.op("pe", lambda e, ps=ps, lhs=lhs, rhs=rhs, hh=hh: e.matmul(ps, lhsT=lhs, rhs=rhs, start=(hh == 0), stop=(hh == NH - 1)),
                             reads=["wout%d" % s, "attnT"], writes=[psr], inc=(hh == NH - 1))
                    k.op("dve", lambda e, q=q, ps=ps: e.tensor_tensor(out=hp[q], in0=hp[q], in1=ps, op=OP.add), reads=[psr, "hp%d" % q], writes=["hp%d" % q])
                    k.dma("sp", dhp[q], hs3[:, oc, cq * 512:(cq + 1) * 512], hp[q], reads=["hp%d" % q], writes=["hscr%d_%d" % (oc, cq)])

        self.attn_sample(i, locals())

        self.barrier()
        k.dma("sp", self.d_in, hflat, self.hscr, reads=["hscr"], writes=["h%d_%d" % (ch, ti) for ch in range(NCH) for ti in range(len(self.coltiles()))])
        self.barrier()
        ph.reset(m)

    def attn_sample(self, i, L):
        k, c, ph = self.k, self.cfg, self.ph
        T, TT, NT, NPG, PAST = c.T, c.TT, c.NT, c.NPG, c.PAST
        j = i // 2
        I, O = self.I, self.O
        kT, vb, kiT, qT, qiT = L["kT"], L["vb"], L["kiT"], L["qT"], L["qiT"]
        ha = L["ha"]
        negB, wsc, outb, outf, zsb = L["negB"], L["wsc"], L["outb"], L["outf"], L["zsb"]
        load_w, tm_proj, qk_post, transposes_to, rope_apply, load_rope = L["load_w"], L["tm_proj"], L["qk_post"], L["transposes_to"], L["rope_apply"], L["load_rope"]
        rpit = L["rpit"]
        scale = L["scale"]
        NITER = L["NITER"]
        pw2 = L["pw2"]
        wout, hp, dhp, Wo, hs3 = L["wout"], L["hp"], L["dhp"], L["Wo"], L["hs3"]
        RB = [0, 1, 2, 3]
        NL = PAST + 4
        self.barrier()
        ph.reset(L["ph_mark2"])
        ha.reset(L["ha_mark2"])
        qTs = ph.bf16(NH * 16).rearrange("p (b h t) -> p b h t", b=4, h=NH)
        qiTd = ph.bf16(NH * 16).rearrange("p (b h t) -> p b h t", b=4, h=NH)
        qidup = ph.bf16(NH * 128).rearrange("p (a b) -> p a b", a=NH)
        wbs = ph.f32(32).rearrange("p (a b) -> p a b", a=4)
        Xw = ph.f32(32).rearrange("p (a b) -> p a b", a=8)
        wsel = ph.bf16(16).rearrange("p (a b) -> p a b", a=4)
        ptb = ph.i32(4 * NPG)
        ptf = ph.f32(4 * NPG)
        iot = ph.f32(8)
        idxi = ph.i32(4 * NPG)
        cm4 = ph.bf16(8)
        cm4f = ph.f32(8)
        bmk = ph.f32(16)
        Mfull = ph.bf16(16)
        MfT = ph.bf16(16)
        attnTs = ph.bf16(NH * 16).rearrange("p (a b) -> p a b", a=NH)
        rs_slot = load_rope(NT, 16)
        for cbi in range(2):
            load_w(cbi, cbi * 512, 512)
        for cbi in range(2):
            ps, psr = tm_proj(T, 16, cbi, 512)
            qk_post(ps, psr, 16, 0, rs_slot, cbi)
            transposes_to(qT[:, cbi * 4:cbi * 4 + 4, 0:16], outb[cbi], 16, 4, ["outb%d" % cbi], "qT")
        k.op("act", lambda e: e.activation(out=qTs, in_=qT[:, :, 0:16].rearrange("p h (b t) -> p b h t", b=4), func=AF.Copy), reads=["qT"], writes=["qTs"])
        load_w(0, 2048, 512)
        ps, psr = tm_proj(T, 16, 0, 512)
        z = zsb[0][0:16]
        k.op("act", lambda e, ps=ps: e.activation(out=z, in_=ps[0:16, :], func=AF.Copy), reads=[psr], writes=["zsb0"])
        o3 = outf[0][0:16].rearrange("p (a b) -> p a b", a=8)
        rope_apply(z.rearrange("p (a b) -> p a b", a=8), o3, 16, 8, 32, rpit[rs_slot][0:16, 0, :], rpit[rs_slot][0:16, 1, :], "zsb0", "outf0", ["rpit%d" % rs_slot])
        k.op("act", lambda e: e.activation(out=qidup[0:16, :, 0:64], in_=o3, func=AF.Copy), reads=["outf0"], writes=["qidup"])
        k.op("act", lambda e: e.activation(out=qidup[0:16, :, 64:128], in_=o3, func=AF.Copy), reads=["outf0"], writes=["qidup"])
        self.dbg("watt0", L["watt"][0], ["watt0"])
        self.dbg("xns", self.xn[:, :, T:T + 16], [])
        self.dbg("zsb0", zsb[0][0:16], ["zsb0"])
        self.dbg("rpit", rpit[rs_slot][0:16], ["rpit%d" % rs_slot])
        self.dbg("outf0", outf[0][0:16], ["outf0"])
        self.dbg("qidup", qidup[0:16], ["qidup"])
        pq_, pqr = self.rot(RB)
        pqb = pq_.bitcast(BF16)
        for hh in range(NH):
            k.op("pe", lambda e, hh=hh: e.transpose(pqb[:, hh * 16:(hh + 1) * 16], qidup[0:16, hh, :], self.ident_b[0:16, 0:16]), reads=["qidup", "ident_b"], writes=[pqr])
        k.op("act", lambda e: e.activation(out=qiTd, in_=pqb[:, 0:NH * 16].rearrange("p (h b t) -> p b h t", h=NH, b=4), func=AF.Copy), reads=[pqr], writes=["qiTd"])
        load_w(1, 2624, 8)
        L["w_side"](T, 16)
        pw_, pwr = self.rot(RB)
        for b in range(4):
            k.op("pe", lambda e, b=b: e.matmul(pw_[0:4, b * 8:(b + 1) * 8], lhsT=self.ident_f[0:16, 4 * b:4 * b + 4], rhs=wsc[0:16, 0:8], start=True, stop=True),
                 reads=["ident_f", "wsc"], writes=[pwr])
        k.op("act", lambda e: e.activation(out=wbs[0:4], in_=pw_[0:4, 0:32].rearrange("p (a b) -> p a b", a=4), func=AF.Copy), reads=[pwr], writes=["wbs"])
        for b in range(4):
            k.op("dve", lambda e, b=b: e.tensor_tensor(out=Xw[0:4], in0=wbs[0:4, b, :].unsqueeze(2).to_broadcast([4, 8, 4]),
                                                      in1=self.ident_f[0:4, 0:4].unsqueeze(1).to_broadcast([4, 8, 4]), op=OP.mult), reads=["wbs", "ident_f"], writes=["Xw"])
            px, pxr = self.rot(RB)
            self.tr32(px[0:32, 0:4], Xw[0:4].rearrange("p a b -> p (a b)"), ["Xw"], pxr, 4)
            k.op("act", lambda e, b=b, px=px: e.activation(out=wsel[0:32, b, :], in_=px[0:32, 0:4], func=AF.Copy), reads=[pxr], writes=["wsel"])
        k.dma("sp", None, ptb, I["ptab"].to_broadcast([128, 4 * NPG]), writes=["ptb"])
        k.op("pool", lambda e: e.iota(iot[:, 0:1], [[0, 1]], base=0, channel_multiplier=1, allow_small_or_imprecise_dtypes=True), writes=["iot"])
        k.op("dve", lambda e: e.tensor_copy(out=ptf, in_=ptb), reads=["ptb"], writes=["ptf"])
        k.op("dve", lambda e: e.tensor_scalar(out=ptf, in0=ptf, scalar1=128.0, scalar2=iot[:, 0:1], op0=OP.mult, op1=OP.add), reads=["ptf", "iot"], writes=["ptf"])
        if j:
            k.op("dve", lambda e: e.tensor_scalar(out=ptf, in0=ptf, scalar1=float(j * c.NPOOL * 128), scalar2=None, op0=OP.add), reads=["ptf"], writes=["ptf"])
        k.op("dve", lambda e: e.tensor_copy(out=idxi, in_=ptf), reads=["ptf"], writes=["idxi"])
        k.op("pool", lambda e: e.memset(cm4f[:, 0:4], 0.0), writes=["cm4f"])
        k.op("pool", lambda e: e.affine_select(out=cm4f[:, 0:4], in_=cm4f[:, 0:4], pattern=[[-1, 4]], compare_op=OP.is_ge, fill=NEG, base=0, channel_multiplier=1), reads=["cm4f"], writes=["cm4f"])
        k.op("pool", lambda e: e.tensor_copy(out=cm4[:, 0:4], in_=cm4f[:, 0:4]), reads=["cm4f"], writes=["cm4"])
        bm3 = bmk.rearrange("p (a b) -> p a b", a=4)
        k.op("pool", lambda e: e.memset(bmk, 1.0), writes=["bmk"])
        k.op("pool", lambda e: e.affine_select(out=bm3, in_=bm3, pattern=[[-4, 4], [0, 4]], compare_op=OP.is_ge, fill=0.0, base=0, channel_multiplier=1), reads=["bmk"], writes=["bmk"])
        k.op("pool", lambda e: e.affine_select(out=bm3, in_=bm3, pattern=[[4, 4], [0, 4]], compare_op=OP.is_ge, fill=0.0, base=3, channel_multiplier=-1), reads=["bmk"], writes=["bmk"])
        kTn = ph.bf16(NKV * 16).rearrange("p (a b) -> p a b", a=NKV)
        vbn = ph.bf16(512)
        kiTn = ph.bf16(16)
        k.op("pool", lambda e: e.tensor_copy(out=kTn, in_=kT[:, :, T:T + 16]), reads=["kT"], writes=["kTn"])
        k.op("pool", lambda e: e.tensor_copy(out=vbn[0:16], in_=vb[0:16, NT, :]), reads=["vb"], writes=["vbn"])
        k.op("pool", lambda e: e.tensor_copy(out=kiTn, in_=kiT[:, T:T + 16]), reads=["kiT"], writes=["kiTn"])
        self.barrier()
        ha.reset(0)
        mark_sb = ph.mark()

        Iall = ha.f32(NL)
        hmark_sb = ha.mark()
        kis = ha.f32(NPG * 64).rearrange("p (a b) -> p a b", a=NPG)
        kisb = ha.bf16(NPG * 64).rearrange("p (a b) -> p a b", a=NPG)
        kiTs = ph.bf16(NPG * 64).rearrange("p (a b) -> p a b", a=NPG // 2)
        stg = [ph.f32(512), ph.f32(512)]
        Rs = [ph.bf16(512), ph.bf16(512)]
        cki = I["cache_ki"].rearrange("a r d -> (a r) d")
        sq = 0
        for b in range(4):
            for pg in range(NPG):
                col = b * NPG + pg
                k.dma("pool", None, None, None, reads=["idxi"], writes=["kis"],
                      fn=lambda e, pg=pg, col=col: e.indirect_dma_start(out=kis[:, pg, :], out_offset=None, in_=cki, in_offset=bass.IndirectOffsetOnAxis(ap=idxi[:, col:col + 1], axis=0)))
            k.op("pool", lambda e: e.tensor_copy(out=kisb, in_=kis), reads=["kis"], writes=["kisb"])
            for g0 in range(0, NPG // 2, 8):
                pt_, ptr_ = self.rot(RB)
                ptb16 = pt_.bitcast(BF16)
                for pp in range(g0, g0 + 8 if g0 + 8 <= NPG // 2 else NPG // 2):
                    k.op("pe", lambda e, pp=pp, g0=g0, ptb16=ptb16: e.transpose(ptb16[:, (pp - g0) * 128:(pp - g0 + 1) * 128], kisb[:, 2 * pp:2 * pp + 2, :].rearrange("p a b -> p (a b)"), self.ident_b),
                         reads=["kisb", "ident_b"], writes=[ptr_])
                nb = min(8, NPG // 2 - g0)
                k.op("act", lambda e, g0=g0, nb=nb, ptb16=ptb16: e.activation(out=kiTs[:, g0:g0 + nb, :], in_=ptb16[:, 0:nb * 128].rearrange("p (a b) -> p a b", a=nb), func=AF.Copy),
                     reads=[ptr_], writes=["kiTs"])
            Iv = None
            for par in range(2):
                for g4 in range(NPG // 8):
                    sq ^= 1
                    ps, psr = self.rot(RB)
                    lhs = qiTd[par * 64:(par + 1) * 64, b].rearrange("p h t -> p (h t)")
                    rhs = kiTs[par * 64:(par + 1) * 64, g4 * 4:(g4 + 1) * 4, :].rearrange("p a b -> p (a b)")
                    k.op("pe", lambda e, ps=ps, lhs=lhs, rhs=rhs: e.matmul(ps[0:32, :], lhsT=lhs, rhs=rhs, start=True, stop=True), reads=["qiTd", "kiTs"], writes=[psr])
                    r = Rs[sq]
                    if sq:
                        k.op("act", lambda e, r=r, ps=ps: e.activation(out=r[0:32], in_=ps[0:32, :], func=AF.Relu), reads=[psr], writes=["Rs%d" % sq])
                    else:
                        k.op("dve", lambda e, r=r, ps=ps: e.tensor_scalar(out=r[0:32], in0=ps[0:32, :], scalar1=0.0, scalar2=None, op0=OP.max), reads=[psr], writes=["Rs%d" % sq])
                    pi, pir = self.PS[4], "ps4"
                    k.op("pe", lambda e, pi=pi, b=b, r=r: e.matmul(pi[0:4, :], lhsT=wsel[0:32, b, :], rhs=r[0:32], start=True, stop=True), reads=["wsel", "Rs%d" % sq], writes=[pir])
                    st_ = stg[sq]
                    k.op("act", lambda e, st_=st_, pi=pi: e.activation(out=st_[0:4], in_=pi[0:4, :], func=AF.Copy), reads=[pir], writes=["stg%d" % sq])
                    dst = Iall[4 * b:4 * b + 4, 0:PAST].rearrange("p (a c d) -> p a c d", c=2, d=128)[:, g4 * 4:(g4 + 1) * 4, par, :]
                    k.dma("sp", None, dst, st_[0:4].rearrange("p (a d) -> p a d", d=128), reads=["stg%d" % sq], writes=["Iall"])
            sq ^= 1
            ps, psr = self.rot(RB)
            lhs = qiTd[0:64, b].rearrange("p h t -> p (h t)")
            rhs = kiTn[0:64, 4 * b:4 * b + 4]
            k.op("pe", lambda e, ps=ps, lhs=lhs, rhs=rhs: e.matmul(ps[0:32, 0:4], lhsT=lhs, rhs=rhs, start=True, stop=True), reads=["qiTd", "kiTn"], writes=[psr])
            r = Rs[sq]
            k.op("act", lambda e, r=r, ps=ps: e.activation(out=r[0:32, 0:4], in_=ps[0:32, 0:4], func=AF.Relu), reads=[psr], writes=["Rs%d" % sq])
            pi, pir = self.PS[4], "ps4"
            k.op("pe", lambda e, pi=pi, b=b, r=r: e.matmul(pi[0:4, 0:4], lhsT=wsel[0:32, b, :], rhs=r[0:32, 0:4], start=True, stop=False), reads=["wsel", "Rs%d" % sq], writes=[pir], inc=False)
            k.op("pe", lambda e, pi=pi: e.matmul(pi[0:4, 0:4], lhsT=self.ident_b[0:4, 0:4], rhs=cm4[0:4, 0:4], start=False, stop=True), reads=["ident_b", "cm4"], writes=[pir])
            st_ = stg[sq]
            k.op("act", lambda e, st_=st_, pi=pi: e.activation(out=st_[0:4, 0:4], in_=pi[0:4, 0:4], func=AF.Copy), reads=[pir], writes=["stg%d" % sq])
            k.dma("sp", None, Iall[4 * b:4 * b + 4, PAST:PAST + 4], st_[0:4, 0:4], reads=["stg%d" % sq], writes=["Iall"])
        self.dbg("Iall", Iall[0:16, :], ["Iall"])
        self.dbg("qTs", qTs, ["qTs"])
        self.dbg("qiTd", qiTd, ["qiTd"])
        self.dbg("wsel", wsel[0:32], ["wsel"])
        self.dbg("idxi", idxi, ["idxi"])
        self.barrier()
        ha.reset(hmark_sb)
        ph.reset(mark_sb)

        TK = c.TOPK_S
        bs = [ph.f32(8) for _ in range(5)]
        lo, wd, mid, cnt, tt_ = bs
        wk = ph.f32(16)
        mask_s = ha.bf16(NL)
        cj = mask_s
        maskTs = ph.bf16(NPG * 16).rearrange("p (a b) -> p a b", a=NPG)
        R16 = slice(0, 16)
        k.op("dve", lambda e: e.tensor_reduce(out=lo[R16, 0:1], in_=Iall[R16, 0:TK], axis=AX.X, op=OP.min), reads=["Iall"], writes=["bs_lo"])
        k.op("dve", lambda e: e.tensor_reduce(out=wd[R16, 0:1], in_=Iall[R16, :], axis=AX.X, op=OP.max), reads=["Iall"], writes=["bs_w"])
        k.op("dve", lambda e: e.tensor_tensor(out=wd[R16, 0:1], in0=wd[R16, 0:1], in1=lo[R16, 0:1], op=OP.subtract), reads=["bs_w", "bs_lo"], writes=["bs_w"])
        k.op("dve", lambda e: e.tensor_scalar(out=wd[R16, 0:1], in0=wd[R16, 0:1], scalar1=1.0001, scalar2=1e-6, op0=OP.mult, op1=OP.add), reads=["bs_w"], writes=["bs_w"])
        k.op("dve", lambda e: e.tensor_tensor(out=wk[R16, 0:NITER], in0=pw2[R16, 0:NITER], in1=wd[R16, 0:1].to_broadcast([16, NITER]), op=OP.mult), reads=["bs_w", "pw2"], writes=["wk"])
        for it in range(NITER):
            k.op("dve", lambda e, it=it: e.tensor_tensor(out=mid[R16, 0:1], in0=lo[R16, 0:1], in1=wk[R16, it:it + 1], op=OP.add), reads=["bs_lo", "wk"], writes=["bs_mid"])
            k.op("dve", lambda e: e.tensor_scalar(out=cj[R16, :], in0=Iall[R16, :], scalar1=mid[R16, 0:1], scalar2=None, op0=OP.is_ge, op1=OP.add, accum_out=cnt[R16, 0:1]),
                 reads=["Iall", "bs_mid"], writes=["bs_cnt", "mask_s"])
            k.op("dve", lambda e, it=it: e.scalar_tensor_tensor(out=tt_[R16, 0:1], in0=cnt[R16, 0:1], scalar=float(TK) - 0.5, in1=wk[R16, it:it + 1], op0=OP.is_ge, op1=OP.mult), reads=["bs_cnt", "wk"], writes=["bs_t"])
            k.op("dve", lambda e: e.tensor_tensor(out=lo[R16, 0:1], in0=lo[R16, 0:1], in1=tt_[R16, 0:1], op=OP.add), reads=["bs_lo", "bs_t"], writes=["bs_lo"])
        k.op("dve", lambda e: e.tensor_scalar(out=mask_s[R16, :], in0=Iall[R16, :], scalar1=lo[R16, 0:1], scalar2=None, op0=OP.is_ge), reads=["Iall", "bs_lo"], writes=["mask_s"])
        for g0 in range(0, NPG, 32):
            pm, pmr = self.rot(RB)
            nb = min(32, NPG - g0)
            for pg in range(g0, g0 + nb):
                k.op("pe", lambda e, pg=pg, g0=g0, pm=pm: e.matmul(pm[:, (pg - g0) * 16:(pg - g0 + 1) * 16], lhsT=mask_s[R16, pg * 128:(pg + 1) * 128], rhs=self.ident_b[0:16, 0:16], start=True, stop=True),
                     reads=["mask_s", "ident_b"], writes=[pmr])
            k.op("act", lambda e, g0=g0, nb=nb, pm=pm: e.activation(out=maskTs[:, g0:g0 + nb, :], in_=pm[:, 0:nb * 16].rearrange("p (a b) -> p a b", a=nb), func=AF.Copy), reads=[pmr], writes=["maskTs"])
        k.op("dve", lambda e: e.tensor_tensor(out=Mfull[R16, :].rearrange("p (a b) -> p a b", a=4), in0=bm3[R16], in1=mask_s[R16, PAST:PAST + 4].unsqueeze(1).to_broadcast([16, 4, 4]), op=OP.mult),
             reads=["bmk", "mask_s"], writes=["Mfull"])
        pm, pmr = self.rot(RB)
        k.op("pe", lambda e, pm=pm: e.matmul(pm[0:16, 0:16], lhsT=Mfull[R16, :], rhs=self.ident_b[0:16, 0:16], start=True, stop=True), reads=["Mfull", "ident_b"], writes=[pmr])
        k.op("act", lambda e, pm=pm: e.activation(out=MfT[R16, :], in_=pm[0:16, 0:16], func=AF.Copy), reads=[pmr], writes=["MfT"])

        self.dbg("mask_s", mask_s[0:16, :], ["mask_s"])
        self.dbg("thr", lo[0:16, 0:1], ["bs_lo"])
        self.dbg("MfT", MfT[0:16, :], ["MfT"])
        self.dbg("maskTs", maskTs, ["maskTs"])
        GP = 4
        NSL = 4
        hf = lambda nw: ha.f32(nw) if ha.top + (nw + 7) // 8 * 8 <= ha.size else ph.f32(nw)
        kvpg = [hf(1024) for _ in range(NSL)]
        kpb = [ph.bf16(512) for _ in range(NSL)]
        kTp = [ph.bf16(GP * 512).rearrange("p (u a b) -> p u a b", u=GP, a=NKV) for _ in range(2)]
        vpb = [ph.bf16(GP * 512).rearrange("p (u b) -> p u b", u=GP) for _ in range(2)]
        Es = ph.bf16(GP * 32)
        Ps_ = [ph.bf16(GP * 32), ph.bf16(GP * 32)]
        En = ph.bf16(32)
        Pn = ph.bf16(32)
        rdn = ph.f32(32)
        ckv = I["cache_kv"].rearrange("a r d -> (a r) d")
        slot = 0
        for b in range(4):
            for pgrp in range(NPG // GP):
                gs = pgrp % 2
                for u in range(GP):
                    pg = pgrp * GP + u
                    col = b * NPG + pg
                    slot = (slot + 1) % NSL
                    k.dma("pool", None, None, None, reads=["idxi"], writes=["kvpg%d" % slot],
                          fn=lambda e, slot=slot, col=col: e.indirect_dma_start(out=kvpg[slot], out_offset=None, in_=ckv, in_offset=bass.IndirectOffsetOnAxis(ap=idxi[:, col:col + 1], axis=0)))
                    k.op("dve", lambda e, slot=slot: e.tensor_copy(out=kpb[slot], in_=kvpg[slot][:, 0:512]), reads=["kvpg%d" % slot], writes=["kpb%d" % slot])
                    k.op("act", lambda e, slot=slot, gs=gs, u=u: e.activation(out=vpb[gs][:, u, :], in_=kvpg[slot][:, 512:1024], func=AF.Copy), reads=["kvpg%d" % slot], writes=["vpb%d" % gs])
                    pt_, ptr_ = self.PS[6], "ps6"
                    ptb16 = pt_.bitcast(BF16)
                    for kvh in range(NKV):
                        k.op("pe", lambda e, kvh=kvh, slot=slot, ptb16=ptb16: e.transpose(ptb16[:, kvh * 128:(kvh + 1) * 128], kpb[slot][:, kvh * 128:(kvh + 1) * 128], self.ident_b),
                             reads=["kpb%d" % slot, "ident_b"], writes=[ptr_])
                    k.op("act", lambda e, gs=gs, u=u, ptb16=ptb16: e.activation(out=kTp[gs][:, u], in_=ptb16[:, 0:512].rearrange("p (a b) -> p a b", a=NKV), func=AF.Copy), reads=[ptr_], writes=["kTp%d" % gs])
                sc, scr = self.PS[5], "ps5"
                for u in range(GP):
                    for kvh in range(NKV):
                        o_ = u * 32 + kvh * 8
                        rhs = qTs[:, b, 2 * kvh:2 * kvh + 2, :].rearrange("p h t -> p (h t)")
                        k.op("pe", lambda e, gs=gs, u=u, kvh=kvh, o_=o_, rhs=rhs: e.matmul(sc[:, o_:o_ + 8], lhsT=kTp[gs][:, u, kvh, :], rhs=rhs, start=True, stop=True),
                             reads=["kTp%d" % gs, "qTs"], writes=[scr])
                k.op("act", lambda e: e.activation(out=Es, in_=sc[:, 0:GP * 32], func=AF.Exp, scale=scale, bias=negB[:, 0:1]), reads=[scr, "negB"], writes=["Es"])
                pp_ = Ps_[gs]
                k.op("dve", lambda e, pp_=pp_, pgrp=pgrp, b=b: e.tensor_tensor(out=pp_.rearrange("p (u h t) -> p u h t", u=GP, h=NH), in0=Es.rearrange("p (u h t) -> p u h t", u=GP, h=NH),
                                                                            in1=maskTs[:, pgrp * GP:(pgrp + 1) * GP, 4 * b:4 * b + 4].unsqueeze(2).to_broadcast([128, GP, NH, 4]), op=OP.mult),
                     reads=["Es", "maskTs"], writes=["Ps%d" % gs])
                for u in range(GP):
                    first = (pgrp == 0 and u == 0)
                    for kvh in range(NKV):
                        rhs = pp_[:, u * 32 + kvh * 8:u * 32 + kvh * 8 + 8]
                        k.op("pe", lambda e, gs=gs, u=u, kvh=kvh, rhs=rhs, first=first: e.matmul(self.PS[kvh][:, 0:8], lhsT=vpb[gs][:, u, kvh * 128:(kvh + 1) * 128], rhs=rhs, start=first, stop=False),
                             reads=["vpb%d" % gs, "Ps%d" % gs], writes=["ps%d" % kvh], inc=True)
                    k.op("pe", lambda e, u=u, pp_=pp_, first=first: e.matmul(self.PS[4][:, 0:32], lhsT=self.ones_b, rhs=pp_[:, u * 32:(u + 1) * 32], start=first, stop=False),
                         reads=["ones_b", "Ps%d" % gs], writes=["ps4"], inc=True)
            scn, scnr = self.PS[7], "ps7"
            for kvh in range(NKV):
                rhs = qTs[:, b, 2 * kvh:2 * kvh + 2, :].rearrange("p h t -> p (h t)")
                k.op("pe", lambda e, kvh=kvh, rhs=rhs: e.matmul(scn[0:16, kvh * 8:(kvh + 1) * 8], lhsT=kTn[:, kvh, :], rhs=rhs, start=True, stop=True), reads=["kTn", "qTs"], writes=[scnr])
            k.op("act", lambda e: e.activation(out=En[R16, :], in_=scn[0:16, 0:32], func=AF.Exp, scale=scale, bias=negB[R16, 0:1]), reads=[scnr, "negB"], writes=["En"])
            k.op("dve", lambda e, b=b: e.tensor_tensor(out=Pn[R16, :].rearrange("p (h t) -> p h t", h=NH), in0=En[R16, :].rearrange("p (h t) -> p h t", h=NH),
                                                      in1=MfT[R16, 4 * b:4 * b + 4].unsqueeze(1).to_broadcast([16, NH, 4]), op=OP.mult), reads=["En", "MfT"], writes=["Pn"])
            for kvh in range(NKV):
                k.op("pe", lambda e, kvh=kvh: e.matmul(self.PS[kvh][:, 0:8], lhsT=vbn[0:16, kvh * 128:(kvh + 1) * 128], rhs=Pn[R16, kvh * 8:(kvh + 1) * 8], start=False, stop=True),
                     reads=["vbn", "Pn"], writes=["ps%d" % kvh], inc=True)
            k.op("pe", lambda e: e.matmul(self.PS[4][:, 0:32], lhsT=self.ones_b[0:16, :], rhs=Pn[R16, :], start=False, stop=True), reads=["ones_b", "Pn"], writes=["ps4"], inc=True)
            k.op("dve", lambda e: e.reciprocal(out=rdn, in_=self.PS[4][:, 0:32]), reads=["ps4"], writes=["rdn"])
            for kvh in range(NKV):
                k.op("dve", lambda e, kvh=kvh, b=b: e.tensor_tensor(out=attnTs[:, 2 * kvh:2 * kvh + 2, 4 * b:4 * b + 4], in0=self.PS[kvh][:, 0:8].rearrange("p (h t) -> p h t", h=2),
                                                                  in1=rdn[:, kvh * 8:(kvh + 1) * 8].rearrange("p (h t) -> p h t", h=2), op=OP.mult), reads=["ps%d" % kvh, "rdn"], writes=["attnTs"])
        self.dbg("attnTs", attnTs, ["attnTs"])
        for blk in range(4):
            s = blk % 2
            self.wload(wout[s], Wo[:, :, blk * 256:(blk + 1) * 256], "wout%d" % s, 2048)
            for r in range(2):
                oc = blk * 2 + r
                q = oc % 2
                k.dma("sp", None, hp[q][:, 0:16], hs3[:, oc, T:T + 16], reads=["hscr_s%d" % oc], writes=["hp%d" % q])
                ps, psr = self.PS[5], "ps5"
                for hh in range(NH):
                    lhs = wout[s][:, hh, r * 128:(r + 1) * 128]
                    rhs = attnTs[:, hh, :]
                    k.op("pe", lambda e, ps=ps, lhs=lhs, rhs=rhs, hh=hh: e.matmul(ps[:, 0:16], lhsT=lhs, rhs=rhs, start=(hh == 0), stop=(hh == NH - 1)),
                         reads=["wout%d" % s, "attnTs"], writes=[psr], inc=(hh == NH - 1))
                k.op("dve", lambda e, q=q, ps=ps: e.tensor_tensor(out=hp[q][:, 0:16], in0=hp[q][:, 0:16], in1=ps[:, 0:16], op=OP.add), reads=[psr, "hp%d" % q], writes=["hp%d" % q])
                k.dma("sp", None, hs3[:, oc, T:T + 16], hp[q][:, 0:16], reads=["hp%d" % q], writes=["hscr_s%d" % oc])


def _fm(v, nch):
    return np.ascontiguousarray(np.asarray(v, np.float32).reshape(nch, 128).T)


def build_vec(cfg, inp):
    off, NV = vec_layout(cfg)
    vec = np.zeros((128, NV), np.float32)
    for i in range(cfg.DEPTH):
        vec[:, off["nmix%d" % i]:off["nmix%d" % i] + 8] = _fm(inp["norm_mix"][i], 8)
        vec[:, off["nffn%d" % i]:off["nffn%d" % i] + 8] = _fm(inp["norm_ffn"][i], 8)
        vec[:, off["nple%d" % i]:off["nple%d" % i] + 8] = _fm(inp["norm_ple"][i], 8)
        for kk in range(3):
            o = off["cfw%d" % i] + kk * NFF
            vec[:, o:o + NFF] = _fm(inp["conv_ff_w"][i][kk], NFF)
        vec[:, off["cfb%d" % i]:off["cfb%d" % i] + NFF] = _fm(inp["conv_ff_b"][i], NFF)
    for j in range(cfg.NR):
        for kk in range(4):
            o = off["crw%d" % j] + kk * 4
            vec[:, o:o + 4] = _fm(inp["conv_rec_w"][j][kk], 4)
        for nm, key in (("crb", "conv_rec_b"), ("brg", "b_rgate"), ("big", "b_igate"), ("lam", "lru_lambda"), ("psc", "pool_scale")):
            o = off["%s%d" % (nm, j)]
            vec[:, o:o + 4] = _fm(inp[key][j], 4)
    o = off["poolrc"]
    for g in range(4):
        w = 2 << g
        for t in range(15):
            vec[:, o + g * 15 + t] = 1.0 / min(w, t + 1)
    return vec


def block_diag(w):
    w = np.asarray(w, np.float32)
    NR = w.shape[0]
    out = np.zeros((NR, 4, 128, 128), np.float32)
    for c in range(4):
        for a in range(2):
            out[:, c, 64 * a:64 * a + 64, 64 * a:64 * a + 64] = w[:, 2 * c + a]
    return out


def rope_tables(cfg):
    NT = cfg.NT
    pos = np.zeros((NT + 1, 128), np.float32)
    for t in range(NT):
        pos[t] = np.arange(t * 128, (t + 1) * 128, dtype=np.float32)
    for r in range(16):
        pos[NT, r] = cfg.PAST + (r % 4)
    outs = []
    for half in (64, 32):
        inv = np.power(np.float32(10000.0), -np.arange(half, dtype=np.float32) / np.float32(half)).astype(np.float32)
        ang = (pos[:, :, None] * inv[None, None, :]).astype(np.float32)
        tab = np.stack([np.cos(ang), np.sin(ang)], axis=2).astype(np.float32)
        outs.append(np.ascontiguousarray(tab))
    return outs


def make_in_maps(cfg, inp):
    A = lambda x: np.ascontiguousarray(np.asarray(x))
    vec = build_vec(cfg, inp)
    shared = {
        "vec": vec,
        "w_in_rec": A(inp["w_in_rec"]), "w_rg": block_diag(inp["w_rgate"]), "w_ig": block_diag(inp["w_igate"]),
        "w_pool": A(inp["w_pool"]), "w_out_rec": A(inp["w_out_rec"]),
        "w_up": A(inp["w_up"]), "w_down": A(inp["w_down"]), "w_ple": A(inp["w_ple"]), "w_pg": A(inp["w_ple_gate"]),
    }
    NA, NR, DP = cfg.NA, cfg.NR, cfg.DEPTH
    if NA:
        rp, rpi = rope_tables(cfg)
        shared.update({
            "cache_kv": np.concatenate([np.asarray(inp["cache_k"]).reshape(NA, cfg.NPOOL * 128, 512),
                                        np.asarray(inp["cache_v"]).reshape(NA, cfg.NPOOL * 128, 512)], axis=-1),
            "cache_ki": A(inp["cache_kidx"]).reshape(NA, cfg.NPOOL * 128, IDX_D),
            "w_in_attn": A(inp["w_in_attn"]), "w_out_attn": A(inp["w_out_attn"]),
            "qkn": np.ascontiguousarray(np.stack([np.asarray(inp["q_norm"]), np.asarray(inp["k_norm"])], axis=1).astype(np.float32)),
            "rope": rp, "ropei": rpi,
        })
    maps = []
    for b in range(cfg.NB):
        sb = slice(4 * b, 4 * b + 4)
        m = dict(shared)
        m["xp"] = A(inp["x_prompt"][b])
        m["xs"] = A(inp["x_sample"][sb]).reshape(16, D)
        m["pp"] = A(inp["p_prompt"][:, b])
        m["psm"] = A(inp["p_sample"][:, sb]).reshape(DP, 16, D_PLE)
        m["st_rc"] = A(inp["state_rec_conv"][:, sb]).reshape(NR, 12, D_REC)
        m["st_rh"] = A(inp["state_rec_h"][:, sb]).reshape(NR, 4, D_REC)
        m["st_pool"] = A(inp["state_pool"][:, sb]).reshape(NR, 60, D_POOL)
        m["st_ffn"] = A(inp["state_ffn_conv"][:, sb]).reshape(DP, 8, D_FF)
        if NA:
            m["ptab"] = A(inp["page_table"][sb]).astype(np.int32).reshape(1, -1)
        maps.append(m)
    return maps


def gather_outputs(cfg, res):
    NB, T, NR, NA, DP = cfg.NB, cfg.T, cfg.NR, cfg.NA, cfg.DEPTH
    R = lambda name: [np.asarray(r[name]) for r in res]
    y_p = np.stack(R("y_p"))
    y_s = np.concatenate([a.reshape(4, 4, D) for a in R("y_s")])
    rc_p = np.stack(R("rc_p"), axis=1)
    rc_s = np.concatenate([a.reshape(NR, 4, 3, D_REC) for a in R("rc_s")], axis=1)
    rh_p = np.stack([a[:, 0] for a in R("rh_p")], axis=1)
    rh_s = np.concatenate(R("rh_s"), axis=1)
    pl_p = np.stack(R("pl_p"), axis=1)
    pl_s = np.concatenate([a.reshape(NR, 4, 15, D_POOL) for a in R("pl_s")], axis=1)
    if NA:
        k_p = np.stack([a.reshape(NA, T, NKV, HD) for a in R("k_p")], axis=1)
        k_s = np.concatenate([a.reshape(NA, 4, 4, NKV, HD) for a in R("k_s")], axis=1)
        v_p = np.stack([a.reshape(NA, T, NKV, HD) for a in R("v_p")], axis=1)
        v_s = np.concatenate([a.reshape(NA, 4, 4, NKV, HD) for a in R("v_s")], axis=1)
        ki_p = np.stack(R("ki_p"), axis=1)
        ki_s = np.concatenate([a.reshape(NA, 4, 4, IDX_D) for a in R("ki_s")], axis=1)
    else:
        z = np.zeros((0,), np.float32)
        k_p = k_s = v_p = v_s = ki_p = ki_s = z
    fc_p = np.stack(R("fc_p"), axis=1)
    fc_s = np.concatenate([a.reshape(DP, 4, 2, D_FF) for a in R("fc_s")], axis=1)
    return (y_p, y_s, rc_p, rc_s, rh_p, rh_s, pl_p, pl_s, k_p, k_s, v_p, v_s, ki_p, ki_s, fc_p, fc_s)


def run_cfg(cfg, inputs, trace=False):
    from contextlib import ExitStack
    mk = MK(cfg)
    with ExitStack() as st:
        mk.build(st)
    maps = make_in_maps(cfg, inputs)
    res = run_bass_kernel_spmd(mk.nc, maps, core_ids=list(range(cfg.NB)), **({"trace": True} if trace else {}))
    return gather_outputs(cfg, res.results), res


def kernel(**inputs):
    cfg = Cfg()
    outs, _ = run_cfg(cfg, inputs)
    return tuple(np.ascontiguousarray(o, dtype=np.float32) for o in outs)
```

```python
import numpy as np
import concourse.bass as bass
import concourse.mybir as mybir
from concourse.bass_utils import run_bass_kernel_spmd

F32 = mybir.dt.float32
BF16 = mybir.dt.bfloat16
I32 = mybir.dt.int32
U32 = mybir.dt.uint32
AF = mybir.ActivationFunctionType
OP = mybir.AluOpType
AX = mybir.AxisListType

D = 1024
NCH = 8
D_PLE = 256
D_REC = 512
D_POOL = 512
D_FF = 2816
NFF = 22
ATTN_IN = 2632
HD = 128
NH = 8
NKV = 4
IDX_H = 8
IDX_D = 64
EPS = 1e-6
NEG = -1.0e30


class Cfg:
    def __init__(self, T=2048, DEPTH=4, NPG=64, NPOOL=2560, NB=8, NSB=32):
        self.T = T
        self.DEPTH = DEPTH
        self.NPG = NPG
        self.NPOOL = NPOOL
        self.NB = NB
        self.NSB = NSB
        self.NS = 16
        self.TT = T + 16
        self.NR = (DEPTH + 1) // 2
        self.NA = DEPTH // 2
        self.PAST = NPG * 128
        self.NT = T // 128
        self.TOPK = min(256, T // 4)
        self.TOPK_S = min(256, (self.PAST + 4) // 4)


class Sched:
    ENG = ("pe", "act", "dve", "pool", "sp")

    def __init__(self, nc, stack):
        self.nc = nc
        self.stack = stack
        self.prog = {e: [] for e in self.ENG}
        self.sem = {e: stack.enter_context(nc.semaphore("s_" + e)) for e in self.ENG}
        self.cnt = {e: 0 for e in self.ENG}
        self.pending = {e: False for e in self.ENG}
        self.waited = {e: {} for e in self.ENG}
        self.last_w = {}
        self.readers = {}
        self.dsems = []
        self.ndma = 0

    def dma_sem(self, name):
        s = self.stack.enter_context(self.nc.semaphore("d_" + name))
        d = {"sem": s, "cnt": 0, "key": "d_" + name}
        self.dsems.append(d)
        return d

    def _need(self, eng, reads, writes, pe_accum):
        need = {}

        def add(tok, same_ok):
            key, sem, val = tok
            if key == eng and same_ok:
                return
            if need.get(key, (None, 0))[1] < val:
                need[key] = (sem, val)

        for r in reads:
            t = self.last_w.get(r)
            if t is not None:
                add(t, False)
        same_ok = (eng == "pe")
        for w in writes:
            t = self.last_w.get(w)
            if t is not None:
                add(t, same_ok)
            for t in self.readers.get(w, {}).values():
                add(t, same_ok)
        out = []
        wd = self.waited[eng]
        for key, (sem, val) in need.items():
            if wd.get(key, 0) >= val:
                continue
            wd[key] = val
            out.append((sem, val))
        return out

    def op(self, eng, fn, reads=(), writes=(), inc=True):
        waits = self._need(eng, reads, writes, False)
        for (sem, val) in waits:
            if sem is self.sem[eng] and val > self.cnt[eng]:
                raise RuntimeError("self-wait on pending increment (%s)" % eng)
        if inc:
            self.cnt[eng] += 1
            val = self.cnt[eng]
            self.pending[eng] = False
        else:
            val = self.cnt[eng] + 1
            self.pending[eng] = True
        tok = (eng, self.sem[eng], val)
        self.prog[eng].append((waits, fn, (self.sem[eng], 1) if inc else None))
        for r in reads:
            self.readers.setdefault(r, {})[eng] = tok
        for w in writes:
            self.last_w[w] = tok
            self.readers[w] = {}
        return tok

    NPOOL_SEM = 24

    def dma(self, eng, dsem, out_ap, in_ap, reads=(), writes=(), fn=None, **kw):
        if not hasattr(self, "_dpool"):
            self._dpool = {}
        if eng not in self._dpool:
            self._dpool[eng] = [self.dma_sem("p%s%d" % (eng, q)) for q in range(self.NPOOL_SEM)]
            self._drr = getattr(self, "_drr", {})
            self._drr[eng] = 0
        dsem = self._dpool[eng][self._drr[eng]]
        self._drr[eng] = (self._drr[eng] + 1) % self.NPOOL_SEM
        waits = self._need(eng, reads, writes, False)
        if dsem["cnt"] > self.waited[eng].get(dsem["key"], 0):
            waits.append((dsem["sem"], dsem["cnt"]))
            self.waited[eng][dsem["key"]] = dsem["cnt"]
        dsem["cnt"] += 16
        tok = (dsem["key"], dsem["sem"], dsem["cnt"])

        if fn is None:
            def fn(e, out_ap=out_ap, in_ap=in_ap, kw=kw):
                return e.dma_start(out=out_ap, in_=in_ap, **kw)

        self.prog[eng].append((waits, fn, (dsem["sem"], 16)))
        self.ndma += 1
        for r in reads:
            self.readers.setdefault(r, {})[dsem["key"]] = tok
        for w in writes:
            self.last_w[w] = tok
            self.readers[w] = {}
        return tok

    def raw(self, eng, fn, reads=(), writes=()):
        waits = self._need(eng, reads, writes, False)
        self.prog[eng].append((waits, fn, None))

    def finish(self):
        for e in self.ENG:
            if self.pending[e]:
                raise RuntimeError("pending ops without inc on " + e)
        waits = []
        for d in self.dsems:
            if d["cnt"]:
                waits.append((d["sem"], d["cnt"]))
        for e in self.ENG:
            if e != "sp" and self.cnt[e]:
                waits.append((self.sem[e], self.cnt[e]))
        self.prog["sp"].append((waits, None, None))

    def emit(self):
        nc = self.nc
        prog = self.prog

        def replay(name, e):
            for waits, fn, inc in prog[name]:
                for (sem, val) in waits:
                    e.wait_ge(sem, val)
                if fn is None:
                    continue
                ins = fn(e)
                if inc is not None:
                    ins.then_inc(inc[0], inc[1])

        with nc.Block() as block:
            @block.tensor
            def _(e):
                replay("pe", e)

            @block.scalar
            def _(e):
                replay("act", e)

            @block.vector
            def _(e):
                replay("dve", e)

            @block.gpsimd
            def _(e):
                replay("pool", e)

            @block.sync
            def _(e):
                replay("sp", e)


class Arena:
    def __init__(self, ap, size):
        self.ap = ap
        self.size = size
        self.top = 0

    def f32(self, n):
        a = self.top
        self.top += (n + 7) // 8 * 8
        assert self.top <= self.size, "arena overflow %d > %d" % (self.top, self.size)
        return self.ap[:, a:a + n]

    def bf16(self, n):
        w = (n + 1) // 2
        a = self.top
        self.top += (w + 7) // 8 * 8
        assert self.top <= self.size, "arena overflow %d > %d" % (self.top, self.size)
        return self.ap[:, a:a + w].bitcast(BF16)[:, 0:n]

    def i32(self, n):
        return self.f32(n).bitcast(I32)

    def mark(self):
        return self.top

    def reset(self, m):
        self.top = m


def vec_layout(cfg):
    off = {}
    n = 0

    def add(name, w):
        nonlocal n
        off[name] = n
        n += w

    for i in range(cfg.DEPTH):
        add("nmix%d" % i, 8)
        add("nffn%d" % i, 8)
        add("nple%d" % i, 8)
        add("cfw%d" % i, 3 * NFF)
        add("cfb%d" % i, NFF)
    for j in range(cfg.NR):
        add("crw%d" % j, 16)
        add("crb%d" % j, 4)
        add("brg%d" % j, 4)
        add("big%d" % j, 4)
        add("lam%d" % j, 4)
        add("psc%d" % j, 4)
    add("poolrc", 4 * 15)
    return off, n


class MK:
    def __init__(self, cfg):
        self.cfg = cfg
        self.nc = bass.Bass("TRN2", target_bir_lowering=False)
        self.voff, self.NV = vec_layout(cfg)

    def declare(self):
        nc, c = self.nc, self.cfg
        T, DP, NR, NA = c.T, c.DEPTH, c.NR, c.NA
        I = {}
        O = {}

        def inp(name, shape, dt=F32):
            I[name] = nc.dram_tensor(name, list(shape), dt, kind="ExternalInput").ap()

        def outp(name, shape):
            O[name] = nc.dram_tensor(name, list(shape), F32, kind="ExternalOutput").ap()

        inp("xp", [T, D]); inp("xs", [16, D])
        inp("pp", [DP, T, D_PLE]); inp("psm", [DP, 16, D_PLE])
        inp("st_rc", [NR, 12, D_REC]); inp("st_rh", [NR, 4, D_REC]); inp("st_pool", [NR, 60, D_POOL])
        inp("st_ffn", [DP, 8, D_FF])
        inp("vec", [128, self.NV])
        inp("w_in_rec", [NR, D, 1536]); inp("w_rg", [NR, 4, 128, 128]); inp("w_ig", [NR, 4, 128, 128])
        inp("w_pool", [NR, 4, 128, 128]); inp("w_out_rec", [NR, D, D])
        inp("w_up", [DP, D, 2 * D_FF]); inp("w_down", [DP, D_FF, D])
        inp("w_ple", [DP, D_PLE, D]); inp("w_pg", [DP, D, D])
        if NA:
            inp("ptab", [1, 4 * c.NPG], I32)
            inp("cache_kv", [NA, c.NPOOL * 128, 1024])
            inp("cache_ki", [NA, c.NPOOL * 128, IDX_D])
            inp("w_in_attn", [NA, D, ATTN_IN]); inp("w_out_attn", [NA, D, D])
            inp("qkn", [NA, 2, HD])
            inp("rope", [c.NT + 1, 128, 2, 64]); inp("ropei", [c.NT + 1, 128, 2, 32])
        outp("y_p", [T, D]); outp("y_s", [16, D])
        outp("rc_p", [NR, 3, D_REC]); outp("rc_s", [NR, 12, D_REC])
        outp("rh_p", [NR, 1, D_REC]); outp("rh_s", [NR, 4, D_REC])
        outp("pl_p", [NR, 15, D_POOL]); outp("pl_s", [NR, 60, D_POOL])
        if NA:
            outp("k_p", [NA, T, 512]); outp("k_s", [NA, 16, 512])
            outp("v_p", [NA, T, 512]); outp("v_s", [NA, 16, 512])
            outp("ki_p", [NA, T, IDX_D]); outp("ki_s", [NA, 16, IDX_D])
            self.hscr = nc.dram_tensor("hscr", [128, NCH * c.TT], F32).ap()
        outp("fc_p", [DP, 2, D_FF]); outp("fc_s", [DP, 8, D_FF])
        self.I, self.O = I, O

    def coltiles(self):
        T = self.cfg.T
        return [(i * 512, 512) for i in range(T // 512)] + [(T, 16)]

    def ps_next(self):
        b = self.ps_rr
        self.ps_rr = (self.ps_rr + 1) % 6
        return self.PS[b], "ps%d" % b

    def pss_next(self):
        s = self.pss_rr
        self.pss_rr = (self.pss_rr + 1) % 32
        return self.PS[7][:, s * 16:(s + 1) * 16], "ps7"

    def wload(self, dst, src, region, nwords):
        k = self.k
        s = self.stage_rr
        self.stage_rr = (s + 1) % len(self.stage)
        st = self.stage[s][:, 0:nwords]
        shp = list(src.shape)
        if len(shp) == 3:
            stv = st.rearrange("p (a b) -> p a b", a=shp[1])
        else:
            stv = st
        k.dma("sp", self.dstage[s], stv, src, writes=["stage%d" % s])
        k.op("pool", lambda e, dst=dst, stv=stv: e.tensor_copy(out=dst, in_=stv), reads=["stage%d" % s], writes=[region])

    def vcol(self, name, i=0, n=1):
        o = self.voff[name] + i
        return self.vec[:, o:o + n]

    def norm(self, gname):
        k, c = self.k, self.cfg
        h, xn = self.h, self.xn
        m = self.ph.mark()
        rstd = self.ph.f32(c.TT)
        sqb = [self.ph.bf16(512), self.ph.bf16(512)]
        for ti, (c0, n) in enumerate(self.coltiles()):
            ps, psr = self.PS[6], "ps6"
            for ch in range(NCH):
                sb = sqb[ch % 2]
                src = h[:, ch, c0:c0 + n]
                k.op("act", lambda e, sb=sb, src=src, n=n: e.activation(out=sb[:, 0:n], in_=src, func=AF.Square),
                     reads=["h%d_%d" % (ch, ti)], writes=["sqb%d" % (ch % 2)])
                k.op("pe", lambda e, ps=ps, sb=sb, n=n, ch=ch: e.matmul(ps[:, 0:n], lhsT=self.ones_b, rhs=sb[:, 0:n], start=(ch == 0), stop=(ch == NCH - 1)),
                     reads=["sqb%d" % (ch % 2), "ones_b"], writes=[psr], inc=True)
            rs = rstd[:, c0:c0 + n]
            k.op("act", lambda e, rs=rs, ps=ps, n=n: e.activation(out=rs, in_=ps[:, 0:n], func=AF.Sqrt, bias=self.eps_t[:, 0:1], scale=1.0 / D),
                 reads=[psr, "eps"], writes=["rstd%d" % ti])
            k.op("dve", lambda e, rs=rs: e.reciprocal(out=rs, in_=rs), reads=["rstd%d" % ti], writes=["rstd%d" % ti])
            for ch in range(NCH):
                g = self.vcol(gname, ch)
                k.op("dve", lambda e, ch=ch, c0=c0, n=n, g=g, rs=rs: e.scalar_tensor_tensor(
                    out=xn[:, ch, c0:c0 + n], in0=h[:, ch, c0:c0 + n], scalar=g, in1=rs, op0=OP.mult, op1=OP.mult),
                    reads=["h%d_%d" % (ch, ti), "rstd%d" % ti, "vec"], writes=["xn%d_%d" % (ch, ti)])
        self.barrier()
        self.ph.reset(m)

    def proj_chunk(self, lhs_of_kc, KC, rhs_of, rhs_reg_of, wregs, evac):
        k = self.k
        for ti, (c0, n) in enumerate(self.coltiles()):
            if n == 16:
                ps, psr = self.pss_next()
            else:
                ps, psr = self.ps_next()
            for kc in range(KC):
                lhs = lhs_of_kc(kc)
                rhs = rhs_of(kc, c0, n)
                k.op("pe", lambda e, ps=ps, lhs=lhs, rhs=rhs, n=n, kc=kc: e.matmul(ps[:, 0:n], lhsT=lhs, rhs=rhs, start=(kc == 0), stop=(kc == KC - 1)),
                     reads=list(wregs) + [rhs_reg_of(kc, ti)], writes=[psr], inc=(kc == KC - 1))
            evac(ti, c0, n, ps[:, 0:n], psr)

    def tr32(self, ps_ap, in_ap, reads, psr, npart):
        self.k.op("pe", lambda e: e.transpose(ps_ap, in_ap, self.ident_f[0:npart, 0:npart]),
                  reads=list(reads) + ["ident_f"], writes=[psr])

    def dbg(self, name, ap, reads):
        if not getattr(self.cfg, "debug", False):
            return
        t = self.nc.dram_tensor("dbg_" + name, list(ap.shape), ap.dtype, kind="ExternalOutput").ap()
        self.O["dbg_" + name] = t
        self.k.dma("sp", None, t, ap, reads=reads)

    def barrier(self):
        k = self.k
        for e in k.ENG:
            assert not k.pending[e]
        for e in k.ENG:
            waits = []
            for o in k.ENG:
                if k.cnt[o] > k.waited[e].get(o, 0):
                    waits.append((k.sem[o], k.cnt[o]))
                    k.waited[e][o] = k.cnt[o]
            for d in k.dsems:
                if d["cnt"] > k.waited[e].get(d["key"], 0):
                    waits.append((d["sem"], d["cnt"]))
                    k.waited[e][d["key"]] = d["cnt"]
            if waits:
                k.prog[e].append((waits, None, None))
        k.last_w.clear()
        k.readers.clear()

    def build(self, stack):
        nc, c = self.nc, self.cfg
        self.declare()
        k = self.k = Sched(nc, stack)
        AW = 52900
        A = nc.alloc_sbuf_tensor("arena", [128, AW], F32).ap()
        self.ph = Arena(A, AW)
        ph = self.ph
        self.PS = [nc.alloc_psum_tensor("psb%d" % b, [128, 512], F32).ap() for b in range(8)]
        self.ps_rr = 0
        self.pss_rr = 0
        TT = c.TT
        self.h = ph.f32(NCH * TT).rearrange("p (c t) -> p c t", c=NCH)
        self.xn = ph.bf16(NCH * TT).rearrange("p (c t) -> p c t", c=NCH)
        self.vec = ph.f32(self.NV)
        self.ident_f = ph.f32(128)
        self.ident_b = ph.bf16(128)
        self.ones_b = ph.bf16(128)
        self.eps_t = ph.f32(8)
        self.one_t = ph.f32(8)
        self.stage = [ph.f32(2048), ph.f32(2048)]
        self.stage_rr = 0
        self.dstage = [k.dma_sem("stg0"), k.dma_sem("stg1")]
        self.d_in = k.dma_sem("in")
        self.d_out = k.dma_sem("out")
        self.d_out2 = k.dma_sem("out2")
        self.dpin = [k.dma_sem("pin0"), k.dma_sem("pin1")]
        self.drp = [k.dma_sem("rp0"), k.dma_sem("rp1")]
        self.dhp = [k.dma_sem("hp0"), k.dma_sem("hp1")]
        self.hflat = self.ph.ap[:, 0:NCH * TT]
        self.base_mark = ph.mark()

        k.dma("sp", self.d_in, self.vec, self.I["vec"], writes=["vec"])
        k.op("pool", lambda e: e.memset(self.ident_f, 0.0), writes=["ident_f"])
        k.op("pool", lambda e: e.affine_select(out=self.ident_f, in_=self.ident_f, pattern=[[-1, 128]], compare_op=OP.not_equal,
                                               fill=1.0, base=0, channel_multiplier=1), reads=["ident_f"], writes=["ident_f"])
        k.op("pool", lambda e: e.tensor_copy(out=self.ident_b, in_=self.ident_f), reads=["ident_f"], writes=["ident_b"])
        k.op("pool", lambda e: e.memset(self.ones_b, 1.0), writes=["ones_b"])
        k.op("pool", lambda e: e.memset(self.eps_t, EPS), writes=["eps"])
        k.op("pool", lambda e: e.memset(self.one_t, 1.0), writes=["one_t"])

        self.load_x()
        stop = getattr(c, "stop", None)
        for i in range(c.DEPTH):
            if i % 2 == 0:
                self.rec_layer(i)
            else:
                self.attn_layer(i)
            if stop == "mix%d" % i:
                break
            self.ffn_layer(i)
            if stop == "ffn%d" % i:
                break
            self.ple_layer(i)
        self.store_y()
        k.finish()
        k.emit()

    def tok_tiles(self):
        c = self.cfg
        return [(t * 128, 128) for t in range(c.NT)] + [(c.T, 16)]

    def load_x(self):
        k, c, ph = self.k, self.cfg, self.ph
        m = ph.mark()
        xin = [ph.f32(1024), ph.f32(1024)]
        dx = [k.dma_sem("xin0"), k.dma_sem("xin1")]
        for tt, (t0, n) in enumerate(self.tok_tiles()):
            s = tt % 2
            src = self.I["xp"][t0:t0 + n, :] if n == 128 else self.I["xs"]
            k.dma("sp", dx[s], xin[s][0:n, :], src, writes=["xin%d" % s])
            ti = min(t0 // 512, len(self.coltiles()) - 1)
            for half in range(2):
                ps, psr = self.ps_next()
                for j in range(4):
                    ch = half * 4 + j
                    self.tr32(ps[:, j * 128:j * 128 + n], xin[s][0:n, ch * 128:(ch + 1) * 128], ["xin%d" % s], psr, n)
                dst = self.h[:, half * 4:half * 4 + 4, t0:t0 + n]
                srcp = ps.rearrange("p (a b) -> p a b", a=4)[:, :, 0:n]
                regs = ["h%d_%d" % (half * 4 + j, ti) for j in range(4)]
                if (tt + half) % 2 == 0:
                    k.op("act", lambda e, dst=dst, srcp=srcp: e.activation(out=dst, in_=srcp, func=AF.Copy), reads=[psr], writes=regs)
                else:
                    k.op("dve", lambda e, dst=dst, srcp=srcp: e.tensor_copy(out=dst, in_=srcp), reads=[psr], writes=regs)
        self.barrier()
        ph.reset(m)

    def store_y(self):
        k, c, ph = self.k, self.cfg, self.ph
        m = ph.mark()
        ost = [ph.f32(1024), ph.f32(1024)]
        for tt, (t0, n) in enumerate(self.tok_tiles()):
            s = tt % 2
            ti = min(t0 // 512, len(self.coltiles()) - 1)
            for half in range(2):
                ps, psr = self.ps_next()
                for j in range(4):
                    ch = half * 4 + j
                    self.tr32(ps[0:n, j * 128:(j + 1) * 128], self.h[:, ch, t0:t0 + n], ["h%d_%d" % (ch, ti)], psr, 128)
                dst = ost[s][0:n, half * 512:(half + 1) * 512]
                if (tt + half) % 2 == 0:
                    k.op("act", lambda e, dst=dst, ps=ps, n=n: e.activation(out=dst, in_=ps[0:n, :], func=AF.Copy), reads=[psr], writes=["ost%d" % s])
                else:
                    k.op("dve", lambda e, dst=dst, ps=ps, n=n: e.tensor_copy(out=dst, in_=ps[0:n, :]), reads=[psr], writes=["ost%d" % s])
            dstd = self.O["y_p"][t0:t0 + n, :] if n == 128 else self.O["y_s"]
            k.dma("sp", self.d_out if s == 0 else self.d_out2, dstd, ost[s][0:n, :], reads=["ost%d" % s])
        self.barrier()
        ph.reset(m)

    def store_fm(self, src3, nch, r, dst, reads, tag):
        k, ph = self.k, self.ph
        ot = ph.f32(nch * 128)
        for g0 in range(0, nch, 4):
            g1 = min(nch, g0 + 4)
            ps, psr = self.ps_next()
            for ch in range(g0, g1):
                self.tr32(ps[0:r, (ch - g0) * 128:(ch - g0 + 1) * 128], src3[:, ch, :], reads, psr, 128)
            w = (g1 - g0) * 128
            k.op("act", lambda e, g0=g0, w=w, ps=ps: e.activation(out=ot[0:r, g0 * 128:g0 * 128 + w], in_=ps[0:r, 0:w], func=AF.Copy),
                 reads=[psr], writes=[tag + "_ot"])
        k.dma("sp", self.d_out, dst, ot[0:r, :], reads=[tag + "_ot"])

    def load_fm(self, dst3, nch, r, src, tag):
        k, ph = self.k, self.ph
        it = ph.f32(nch * 128)
        k.dma("sp", self.d_in, it[0:r, :], src, writes=[tag + "_it"])
        for g0 in range(0, nch, 4):
            g1 = min(nch, g0 + 4)
            ps, psr = self.ps_next()
            for ch in range(g0, g1):
                self.tr32(ps[:, (ch - g0) * r:(ch - g0 + 1) * r], it[0:r, ch * 128:(ch + 1) * 128], [tag + "_it"], psr, r)
            w = (g1 - g0)
            k.op("act", lambda e, g0=g0, g1=g1, w=w, ps=ps: e.activation(out=dst3[:, g0:g1, :], in_=ps[:, 0:w * r].rearrange("p (a b) -> p a b", a=w), func=AF.Copy),
                 reads=[psr], writes=[tag])

    def ffn_layer(self, i):
        k, c, ph = self.k, self.cfg, self.ph
        T, TT = c.T, c.TT
        h, xn = self.h, self.xn
        self.norm("nffn%d" % i)
        m = ph.mark()
        fcs = ph.f32(NFF * 8).rearrange("p (a b) -> p a b", a=NFF)
        fct = ph.f32(NFF * 10).rearrange("p (a b) -> p a b", a=NFF)
        mm = ph.mark()
        self.load_fm(fcs, NFF, 8, self.I["st_ffn"][i], "fcs")
        self.barrier()
        ph.reset(mm)
        G = 4
        wup = [ph.bf16(2048).rearrange("p (a b) -> p a b", a=8) for _ in range(4)]
        wdn = ph.bf16(G * 1024).rearrange("p (a b) -> p a b", a=G)
        act = ph.bf16(G * TT).rearrange("p (a b) -> p a b", a=G)
        GE = 2 + T + 24
        gext = [ph.f32(GE), ph.f32(GE)]
        tmp = ph.f32(TT)
        gl = [ph.bf16(TT), ph.bf16(TT)]
        W = self.I["w_up"][i].rearrange("(kc p) n -> p kc n", p=128)
        Wd = self.I["w_down"][i]
        wrr = [0]

        def get_block(col0):
            sl = wrr[0]
            wrr[0] = (sl + 1) % 4
            self.wload(wup[sl], W[:, :, col0:col0 + 256], "wup%d" % sl, 2048)
            return sl

        groups = [list(range(g, min(g + G, NFF))) for g in range(0, NFF, G)]
        sg = su = 0
        for grp in groups:
            for jj, j in enumerate(grp):
                mb, r = divmod(j, 2)
                if r == 0:
                    sg = get_block(mb * 256)
                    su = get_block(D_FF + mb * 256)
                s = j % 2
                ge = gext[s]
                gs = ge[:, 2 + T:2 + T + 24].rearrange("p (a b) -> p a b", a=4)
                gr = "gext%d" % s
                k.op("pool", lambda e, ge=ge: e.memset(ge[:, 0:2], 0.0), writes=[gr])
                k.op("pool", lambda e, gs=gs, j=j: e.tensor_copy(out=gs[:, :, 0:2], in_=fcs[:, j, :].rearrange("p (a b) -> p a b", a=4)), writes=[gr])

                def evac_g(ti, c0, n, ps, psr, ge=ge, gs=gs, gr=gr):
                    if n == 16:
                        k.op("act", lambda e: e.activation(out=gs[:, :, 2:6], in_=ps.rearrange("p (a b) -> p a b", a=4), func=AF.Copy), reads=[psr], writes=[gr])
                    else:
                        k.op("act", lambda e: e.activation(out=ge[:, 2 + c0:2 + c0 + n], in_=ps, func=AF.Copy), reads=[psr], writes=[gr])

                self.proj_chunk(lambda kc, sg=sg, r=r: wup[sg][:, kc, r * 128:(r + 1) * 128], NCH,
                                lambda kc, c0, n: xn[:, kc, c0:c0 + n], lambda kc, ti: "xn%d_%d" % (kc, ti), ["wup%d" % sg], evac_g)
                w = [self.vcol("cfw%d" % i, kk * NFF + j) for kk in range(3)]
                b = self.vcol("cfb%d" % i, j)
                tp = tmp[:, 0:T]
                tsm = tmp[:, T:T + 16].rearrange("p (a b) -> p a b", a=4)
                k.op("dve", lambda e, ge=ge, w=w, b=b: e.tensor_scalar(out=tp, in0=ge[:, 0:T], scalar1=w[0], scalar2=b, op0=OP.mult, op1=OP.add),
                     reads=[gr, "vec"], writes=["tmp"])
                for kk in (1, 2):
                    k.op("dve", lambda e, ge=ge, w=w, kk=kk: e.scalar_tensor_tensor(out=tp, in0=ge[:, kk:kk + T], scalar=w[kk], in1=tp, op0=OP.mult, op1=OP.add),
                         reads=[gr, "tmp"], writes=["tmp"])
                k.op("dve", lambda e, gs=gs, w=w, b=b: e.tensor_scalar(out=tsm, in0=gs[:, :, 0:4], scalar1=w[0], scalar2=b, op0=OP.mult, op1=OP.add),
                     reads=[gr, "tmp"], writes=["tmp"])
                for kk in (1, 2):
                    k.op("dve", lambda e, gs=gs, w=w, kk=kk: e.scalar_tensor_tensor(out=tsm, in0=gs[:, :, kk:kk + 4], scalar=w[kk], in1=tsm, op0=OP.mult, op1=OP.add),
                         reads=[gr, "tmp"], writes=["tmp"])
                glr = "gl%d" % s
                k.op("act", lambda e, s=s: e.activation(out=gl[s], in_=tmp, func=AF.Gelu), reads=["tmp"], writes=[glr])
                k.op("pool", lambda e, ge=ge, j=j: e.tensor_copy(out=fct[:, j, 0:2], in_=ge[:, T:T + 2]), reads=[gr], writes=["fct"])
                k.op("pool", lambda e, gs=gs, j=j: e.tensor_copy(out=fct[:, j, 2:10].rearrange("p (a b) -> p a b", a=4), in_=gs[:, :, 4:6]), reads=[gr], writes=["fct"])

                def evac_u(ti, c0, n, ps, psr, s=s, jj=jj, glr=glr):
                    k.op("dve", lambda e: e.tensor_tensor(out=act[:, jj, c0:c0 + n], in0=gl[s][:, c0:c0 + n], in1=ps, op=OP.mult),
                         reads=[psr, glr], writes=["act%d_%d" % (jj, ti)])

                self.proj_chunk(lambda kc, su=su, r=r: wup[su][:, kc, r * 128:(r + 1) * 128], NCH,
                                lambda kc, c0, n: xn[:, kc, c0:c0 + n], lambda kc, ti: "xn%d_%d" % (kc, ti), ["wup%d" % su], evac_u)
            for jj, j in enumerate(grp):
                self.wload(wdn[:, jj, :], Wd[j * 128:(j + 1) * 128, :], "wdn%d" % jj, 1024)
            for oc in range(NCH):
                def evac_d(ti, c0, n, ps, psr, oc=oc):
                    hr = "h%d_%d" % (oc, ti)
                    k.op("dve", lambda e: e.tensor_tensor(out=h[:, oc, c0:c0 + n], in0=h[:, oc, c0:c0 + n], in1=ps, op=OP.add),
                         reads=[psr, hr], writes=[hr])

                self.proj_chunk(lambda kc, oc=oc: wdn[:, kc, oc * 128:(oc + 1) * 128], len(grp),
                                lambda kc, c0, n: act[:, kc, c0:c0 + n], lambda kc, ti: "act%d_%d" % (kc, ti),
                                ["wdn%d" % q for q in range(len(grp))], evac_d)
        self.barrier()
        ph.reset(mm)
        self.store_fm(fct[:, :, 0:2], NFF, 2, self.O["fc_p"][i], [], "fcp")
        self.store_fm(fct[:, :, 2:10], NFF, 8, self.O["fc_s"][i], [], "fcso")
        self.barrier()
        ph.reset(m)

    def ple_layer(self, i):
        k, c, ph = self.k, self.cfg, self.ph
        T, TT = c.T, c.TT
        h, xn = self.h, self.xn
        self.norm("nple%d" % i)
        m = ph.mark()
        pT = ph.bf16(2 * TT).rearrange("p (a b) -> p a b", a=2)
        pin = [ph.f32(256), ph.f32(256)]
        dpin = self.dpin
        nct = len(self.coltiles())
        for tt, (t0, n) in enumerate(self.tok_tiles()):
            s = tt % 2
            src = self.I["pp"][i, t0:t0 + n, :] if n == 128 else self.I["psm"][i]
            k.dma("sp", dpin[s], pin[s][0:n, :], src, writes=["pin%d" % s])
            ti = min(t0 // 512, nct - 1)
            ps, psr = self.ps_next()
            for kc in range(2):
                self.tr32(ps[:, kc * 128:kc * 128 + n], pin[s][0:n, kc * 128:(kc + 1) * 128], ["pin%d" % s], psr, n)
            dst = pT[:, :, t0:t0 + n]
            srcp = ps[:, 0:256].rearrange("p (a b) -> p a b", a=2)[:, :, 0:n]
            if tt % 2 == 0:
                k.op("act", lambda e, dst=dst, srcp=srcp: e.activation(out=dst, in_=srcp, func=AF.Copy), reads=[psr], writes=["pT%d" % ti])
            else:
                k.op("dve", lambda e, dst=dst, srcp=srcp: e.tensor_copy(out=dst, in_=srcp), reads=[psr], writes=["pT%d" % ti])
        wg = [ph.bf16(2048).rearrange("p (a b) -> p a b", a=8) for _ in range(2)]
        wp = [ph.bf16(512).rearrange("p (a b) -> p a b", a=2) for _ in range(2)]
        gt = [ph.f32(512), ph.f32(512)]
        t2 = [ph.f32(512), ph.f32(512)]
        Wg = self.I["w_pg"][i].rearrange("(kc p) n -> p kc n", p=128)
        Wp = self.I["w_ple"][i].rearrange("(kc p) n -> p kc n", p=128)
        q = 0
        for blk in range(4):
            s = blk % 2
            self.wload(wg[s], Wg[:, :, blk * 256:(blk + 1) * 256], "wg%d" % s, 2048)
            self.wload(wp[s], Wp[:, :, blk * 256:(blk + 1) * 256], "wp%d" % s, 512)
            for r in range(2):
                oc = blk * 2 + r
                for ti, (c0, n) in enumerate(self.coltiles()):
                    q ^= 1
                    if n == 16:
                        ps, psr = self.pss_next()
                        ps2, psr2 = self.pss_next()
                    else:
                        ps, psr = self.ps_next()
                        ps2, psr2 = self.ps_next()
                    for kc in range(NCH):
                        lhs = wg[s][:, kc, r * 128:(r + 1) * 128]
                        rhs = xn[:, kc, c0:c0 + n]
                        k.op("pe", lambda e, ps=ps, lhs=lhs, rhs=rhs, n=n, kc=kc: e.matmul(ps[:, 0:n], lhsT=lhs, rhs=rhs, start=(kc == 0), stop=(kc == NCH - 1)),
                             reads=["wg%d" % s, "xn%d_%d" % (kc, ti)], writes=[psr], inc=(kc == NCH - 1))
                    g = gt[q][:, 0:n]
                    k.op("act", lambda e, g=g, ps=ps, n=n: e.activation(out=g, in_=ps[:, 0:n], func=AF.Sigmoid), reads=[psr], writes=["gt%d" % q])
                    for kc in range(2):
                        lhs = wp[s][:, kc, r * 128:(r + 1) * 128]
                        rhs = pT[:, kc, c0:c0 + n]
                        k.op("pe", lambda e, ps2=ps2, lhs=lhs, rhs=rhs, n=n, kc=kc: e.matmul(ps2[:, 0:n], lhsT=lhs, rhs=rhs, start=(kc == 0), stop=(kc == 1)),
                             reads=["wp%d" % s, "pT%d" % ti], writes=[psr2], inc=(kc == 1))
                    tt2 = t2[q][:, 0:n]
                    k.op("dve", lambda e, tt2=tt2, g=g, ps2=ps2, n=n: e.tensor_tensor(out=tt2, in0=g, in1=ps2[:, 0:n], op=OP.mult),
                         reads=[psr2, "gt%d" % q], writes=["t2%d" % q])
                    hr = "h%d_%d" % (oc, ti)
                    hv = h[:, oc, c0:c0 + n]
                    k.op("pool", lambda e, hv=hv, tt2=tt2: e.tensor_tensor(out=hv, in0=hv, in1=tt2, op=OP.add), reads=["t2%d" % q, hr], writes=[hr])
        self.barrier()
        ph.reset(m)

    def rec_layer(self, i):
        k, c, ph = self.k, self.cfg, self.ph
        T, TT = c.T, c.TT
        h, xn = self.h, self.xn
        j = i // 2
        self.norm("nmix%d" % i)
        m = ph.mark()
        hist = ph.f32(4 * 76).rearrange("p (a b) -> p a b", a=4)
        tails = ph.f32(4 * 95).rearrange("p (a b) -> p a b", a=4)
        cneg = ph.f32(8)
        sm = [ph.f32(4) for _ in range(8)]
        mm = ph.mark()
        it = ph.f32(512)
        k.dma("sp", self.d_in, it[0:12, :], self.I["st_rc"][j], writes=["rit"])
        k.dma("sp", self.d_in, it[12:16, :], self.I["st_rh"][j], writes=["rit"])
        k.dma("sp", self.d_in, it[16:76, :], self.I["st_pool"][j], writes=["rit"])
        ps, psr = self.ps_next()
        for ch in range(4):
            self.tr32(ps[:, ch * 76:(ch + 1) * 76], it[0:76, ch * 128:(ch + 1) * 128], ["rit"], psr, 76)
        k.op("act", lambda e, ps=ps: e.activation(out=hist, in_=ps[:, 0:304].rearrange("p (a b) -> p a b", a=4), func=AF.Copy), reads=[psr], writes=["hist"])
        lam = self.vcol("lam%d" % j, 0, 4)
        al, x, z, z2, pl, rl, sp_, t_ = sm

        def dv(fn, reads, writes):
            k.op("dve", fn, reads=reads, writes=writes)

        k.op("act", lambda e: e.activation(out=al, in_=lam, func=AF.Abs), reads=["vec"], writes=["sm_al"])
        k.op("act", lambda e: e.activation(out=x, in_=al, func=AF.Exp, scale=-1.0), reads=["sm_al"], writes=["sm_x"])
        dv(lambda e: e.tensor_scalar(out=z, in0=x, scalar1=2.0, scalar2=None, op0=OP.add), ["sm_x"], ["sm_z"])
        dv(lambda e: e.reciprocal(out=z, in_=z), ["sm_z"], ["sm_z"])
        dv(lambda e: e.tensor_tensor(out=z, in0=z, in1=x, op=OP.mult), ["sm_z", "sm_x"], ["sm_z"])
        dv(lambda e: e.tensor_tensor(out=z2, in0=z, in1=z, op=OP.mult), ["sm_z"], ["sm_z2"])
        dv(lambda e: e.memset(pl, 1.0 / 17.0), [], ["sm_pl"])
        for cf in (15.0, 13.0, 11.0, 9.0, 7.0, 5.0, 3.0, 1.0):
            dv(lambda e: e.tensor_tensor(out=pl, in0=pl, in1=z2, op=OP.mult), ["sm_pl", "sm_z2"], ["sm_pl"])
            dv(lambda e, cf=cf: e.tensor_scalar(out=pl, in0=pl, scalar1=1.0 / cf, scalar2=None, op0=OP.add), ["sm_pl"], ["sm_pl"])
        dv(lambda e: e.tensor_tensor(out=pl, in0=pl, in1=z, op=OP.mult), ["sm_pl", "sm_z"], ["sm_pl"])
        dv(lambda e: e.tensor_scalar(out=rl, in0=lam, scalar1=-1.0, scalar2=0.0, op0=OP.mult, op1=OP.max), ["vec"], ["sm_rl"])
        dv(lambda e: e.scalar_tensor_tensor(out=sp_, in0=pl, scalar=2.0, in1=rl, op0=OP.mult, op1=OP.add), ["sm_pl", "sm_rl"], ["sm_sp"])
        dv(lambda e: e.tensor_scalar(out=cneg[:, 0:4], in0=sp_, scalar1=-8.0, scalar2=None, op0=OP.mult), ["sm_sp"], ["cneg"])
        dv(lambda e: e.tensor_scalar(out=cneg[:, 4:8], in0=sp_, scalar1=-16.0, scalar2=None, op0=OP.mult), ["sm_sp"], ["cneg"])
        self.barrier()
        ph.reset(mm)

        yab = ph.bf16(NCH * TT).rearrange("p (a b) -> p a b", a=NCH)
        wbf = [ph.bf16(2048).rearrange("p (a b) -> p a b", a=8) for _ in range(2)]
        wgt = ph.bf16(256)
        wpl = ph.bf16(128)
        B1 = ph.f32(T + 96)
        B2 = ph.f32(TT)
        B3 = ph.f32(TT)
        B4 = ph.f32(TT)
        Bh = ph.bf16(TT)
        ss2 = ph.f32(76).rearrange("p (a b) -> p a b", a=4)
        ss4 = ph.f32(76).rearrange("p (a b) -> p a b", a=4)
        Win = self.I["w_in_rec"][j].rearrange("(kc p) n -> p kc n", p=128)
        xnr = lambda kc, ti: "xn%d_%d" % (kc, ti)
        xnf = lambda kc, c0, n: xn[:, kc, c0:c0 + n]

        def v4(ap16):
            return ap16.rearrange("p (a b) -> p a b", a=4)

        for ch in range(4):
            blk, r = divmod(ch, 2)
            if r == 0:
                self.wload(wbf[0], Win[:, :, blk * 256:(blk + 1) * 256], "wbf0", 2048)
                self.wload(wbf[1], Win[:, :, 512 + blk * 256:512 + (blk + 1) * 256], "wbf1", 2048)
            B1s = B1[:, 3 + T:3 + T + 28].rearrange("p (a b) -> p a b", a=4)
            k.op("pool", lambda e: e.memset(B1[:, 0:3], 0.0), writes=["B1"])
            k.op("pool", lambda e, ch=ch, B1s=B1s: e.tensor_copy(out=B1s[:, :, 0:3], in_=hist[:, ch, 0:12].rearrange("p (a b) -> p a b", a=4)), reads=["hist"], writes=["B1"])

            def evac_xa(ti, c0, n, ps, psr, B1s=B1s):
                if n == 16:
                    k.op("act", lambda e: e.activation(out=B1s[:, :, 3:7], in_=v4(ps), func=AF.Copy), reads=[psr], writes=["B1"])
                else:
                    k.op("act", lambda e: e.activation(out=B1[:, 3 + c0:3 + c0 + n], in_=ps, func=AF.Copy), reads=[psr], writes=["B1"])

            self.proj_chunk(lambda kc, r=r: wbf[0][:, kc, r * 128:(r + 1) * 128], NCH, xnf, xnr, ["wbf0"], evac_xa)
            k.op("pool", lambda e, ch=ch: e.tensor_copy(out=tails[:, ch, 0:3], in_=B1[:, T:T + 3]), reads=["B1"], writes=["tails"])
            k.op("pool", lambda e, ch=ch, B1s=B1s: e.tensor_copy(out=tails[:, ch, 3:15].rearrange("p (a b) -> p a b", a=4), in_=B1s[:, :, 4:7]), reads=["B1"], writes=["tails"])
            w = [self.vcol("crw%d" % j, kk * 4 + ch) for kk in range(4)]
            b = self.vcol("crb%d" % j, ch)
            B2p = B2[:, 0:T]
            B2s = v4(B2[:, T:T + 16])
            dv(lambda e, w=w, b=b: e.tensor_scalar(out=B2p, in0=B1[:, 0:T], scalar1=w[0], scalar2=b, op0=OP.mult, op1=OP.add), ["B1", "vec"], ["B2"])
            for kk in (1, 2, 3):
                dv(lambda e, w=w, kk=kk: e.scalar_tensor_tensor(out=B2p, in0=B1[:, kk:kk + T], scalar=w[kk], in1=B2p, op0=OP.mult, op1=OP.add), ["B1", "B2"], ["B2"])
            dv(lambda e, w=w, b=b, B1s=B1s: e.tensor_scalar(out=B2s, in0=B1s[:, :, 0:4], scalar1=w[0], scalar2=b, op0=OP.mult, op1=OP.add), ["B1", "B2"], ["B2"])
            for kk in (1, 2, 3):
                dv(lambda e, w=w, kk=kk, B1s=B1s: e.scalar_tensor_tensor(out=B2s, in0=B1s[:, :, kk:kk + 4], scalar=w[kk], in1=B2s, op0=OP.mult, op1=OP.add), ["B1", "B2"], ["B2"])
            k.op("act", lambda e: e.activation(out=Bh, in_=B2, func=AF.Copy), reads=["B2"], writes=["Bh"])
            self.wload(wgt[:, 0:128], self.I["w_rg"][j, ch], "wgt", 128)
            self.wload(wgt[:, 128:256], self.I["w_ig"][j, ch], "wgt", 128)
            brg = self.vcol("brg%d" % j, ch)
            big = self.vcol("big%d" % j, ch)

            def evac_gate(dst, bias, reg):
                def f(ti, c0, n, ps, psr):
                    k.op("act", lambda e: e.activation(out=dst[:, c0:c0 + n], in_=ps, func=AF.Sigmoid, bias=bias), reads=[psr, "vec"], writes=[reg])
                return f

            bhf = lambda kc, c0, n: Bh[:, c0:c0 + n]
            self.proj_chunk(lambda kc: wgt[:, 0:128], 1, bhf, lambda kc, ti: "Bh", ["wgt"], evac_gate(B3, brg, "B3"))
            self.proj_chunk(lambda kc: wgt[:, 128:256], 1, bhf, lambda kc, ti: "Bh", ["wgt"], evac_gate(B4, big, "B4"))
            Ba = B1[:, 0:TT]
            c1 = cneg[:, ch:ch + 1]
            c2 = cneg[:, 4 + ch:5 + ch]
            k.op("act", lambda e, c1=c1: e.activation(out=Ba, in_=B3, func=AF.Exp, scale=c1), reads=["B3", "cneg", "tails"], writes=["B1"])
            k.op("act", lambda e, c2=c2: e.activation(out=B3, in_=B3, func=AF.Exp, scale=c2), reads=["B3", "cneg"], writes=["B3"])
            k.op("act", lambda e: e.activation(out=B3, in_=B3, func=AF.Sqrt, scale=-1.0, bias=self.one_t[:, 0:1]), reads=["B3"], writes=["B3"])
            dv(lambda e: e.tensor_tensor(out=B4, in0=B4, in1=B3, op=OP.mult), ["B3", "B4"], ["B4"])
            dv(lambda e: e.tensor_tensor(out=B4, in0=B4, in1=B2, op=OP.mult), ["B4", "B2", "Bh"], ["B4"])
            dv(lambda e: e.tensor_tensor_scan(out=B2[:, 0:T], data0=Ba[:, 0:T], data1=B4[:, 0:T], initial=0.0, op0=OP.mult, op1=OP.add), ["B1", "B4", "Bh"], ["B2"])
            for sq in range(4):
                a0 = T + 4 * sq
                dv(lambda e, a0=a0, sq=sq, ch=ch: e.tensor_tensor_scan(out=B2[:, a0:a0 + 4], data0=Ba[:, a0:a0 + 4], data1=B4[:, a0:a0 + 4],
                                                                     initial=hist[:, ch, 12 + sq:13 + sq], op0=OP.mult, op1=OP.add), ["B1", "B4", "hist", "B2"], ["B2"])
            k.op("pool", lambda e, ch=ch: e.tensor_copy(out=tails[:, ch, 15:16], in_=B2[:, T - 1:T]), reads=["B2"], writes=["tails"])
            k.op("pool", lambda e, ch=ch: e.tensor_copy(out=tails[:, ch, 16:20], in_=v4(B2[:, T:T + 16])[:, :, 3]), reads=["B2"], writes=["tails"])

            def evac_ga(ti, c0, n, ps, psr):
                k.op("act", lambda e: e.activation(out=B3[:, c0:c0 + n], in_=ps, func=AF.Gelu), reads=[psr], writes=["B3"])

            self.proj_chunk(lambda kc, r=r: wbf[1][:, kc, r * 128:(r + 1) * 128], NCH, xnf, xnr, ["wbf1"], evac_ga)
            for ti, (c0, n) in enumerate(self.coltiles()):
                k.op("dve", lambda e, ch=ch, c0=c0, n=n: e.tensor_tensor(out=yab[:, ch, c0:c0 + n], in0=B2[:, c0:c0 + n], in1=B3[:, c0:c0 + n], op=OP.mult),
                     reads=["B2", "B3"], writes=["yab%d_%d" % (ch, ti)])

        for g in range(4):
            wnd = 2 << g
            blk, r = divmod(g, 2)
            if r == 0:
                self.wload(wbf[0], Win[:, :, 1024 + blk * 256:1024 + (blk + 1) * 256], "wbf0", 2048)
            self.wload(wpl, self.I["w_pool"][j, g], "wpl", 128)
            B1s = B1[:, 15 + T:15 + T + 76].rearrange("p (a b) -> p a b", a=4)
            k.op("pool", lambda e: e.memset(B1[:, 0:15], 0.0), writes=["B1"])
            k.op("pool", lambda e, g=g, B1s=B1s: e.tensor_copy(out=B1s[:, :, 0:15], in_=hist[:, g, 16:76].rearrange("p (a b) -> p a b", a=4)), reads=["hist"], writes=["B1"])

            def evac_xb(ti, c0, n, ps, psr, B1s=B1s):
                if n == 16:
                    k.op("act", lambda e: e.activation(out=B1s[:, :, 15:19], in_=v4(ps), func=AF.Copy), reads=[psr], writes=["B1"])
                else:
                    k.op("act", lambda e: e.activation(out=B1[:, 15 + c0:15 + c0 + n], in_=ps, func=AF.Copy), reads=[psr], writes=["B1"])

            self.proj_chunk(lambda kc, r=r: wbf[0][:, kc, r * 128:(r + 1) * 128], NCH, xnf, xnr, ["wbf0"], evac_xb)
            k.op("pool", lambda e, g=g: e.tensor_copy(out=tails[:, g, 20:35], in_=B1[:, T:T + 15]), reads=["B1"], writes=["tails"])
            k.op("pool", lambda e, g=g, B1s=B1s: e.tensor_copy(out=tails[:, g, 35:95].rearrange("p (a b) -> p a b", a=4), in_=B1s[:, :, 4:19]), reads=["B1"], writes=["tails"])
            E = 15 + T
            src, srcs = B1, B1s
            bufs = [(B2, ss2, "B2"), (B3, ss4, "B3")]
            step = 1
            srcreg = "B1"
            for lv in range(g + 1):
                dst, dsts, dreg = bufs[lv % 2]
                v0 = 2 * step - 1
                k.op("dve", lambda e, dst=dst, src=src, step=step, v0=v0: e.tensor_tensor(out=dst[:, v0:E], in0=src[:, v0:E], in1=src[:, v0 - step:E - step], op=OP.add),
                     reads=[srcreg], writes=[dreg])
                k.op("pool", lambda e, dsts=dsts, srcs=srcs, step=step, v0=v0: e.tensor_tensor(out=dsts[:, :, v0:19], in0=srcs[:, :, v0:19], in1=srcs[:, :, v0 - step:19 - step], op=OP.add),
                     reads=[srcreg], writes=[dreg])
                src, srcs, srcreg = dst, dsts, dreg
                step *= 2
            S, Ss, Sreg = src, srcs, srcreg
            inv = 1.0 / wnd
            dv(lambda e, S=S, inv=inv: e.scalar_tensor_tensor(out=B4[:, 0:T], in0=S[:, 15:E], scalar=inv, in1=B1[:, 15:E], op0=OP.mult, op1=OP.subtract), [Sreg, "B1"], ["B4"])
            if wnd > 1:
                nf = wnd - 1
                rc = self.vcol("poolrc", g * 15, nf)
                dv(lambda e, S=S, rc=rc, nf=nf: e.tensor_tensor(out=B4[:, 0:nf], in0=S[:, 15:15 + nf], in1=rc, op=OP.mult), [Sreg, "vec", "B4"], ["B4"])
                dv(lambda e, nf=nf: e.tensor_tensor(out=B4[:, 0:nf], in0=B4[:, 0:nf], in1=B1[:, 15:15 + nf], op=OP.subtract), ["B4", "B1"], ["B4"])
            dv(lambda e, Ss=Ss, B1s=B1s, inv=inv: e.scalar_tensor_tensor(out=v4(B4[:, T:T + 16]), in0=Ss[:, :, 15:19], scalar=inv, in1=B1s[:, :, 15:19], op0=OP.mult, op1=OP.subtract),
               [Sreg, "B1", "B4"], ["B4"])
            k.op("act", lambda e: e.activation(out=Bh, in_=B4, func=AF.Copy), reads=["B4"], writes=["Bh"])
            psc = self.vcol("psc%d" % j, g)

            def evac_pl(ti, c0, n, ps, psr, g=g, psc=psc):
                k.op("act", lambda e: e.activation(out=yab[:, 4 + g, c0:c0 + n], in_=ps, func=AF.Identity, scale=psc), reads=[psr, "vec"], writes=["yab%d_%d" % (4 + g, ti)])

            self.proj_chunk(lambda kc: wpl, 1, lambda kc, c0, n: Bh[:, c0:c0 + n], lambda kc, ti: "Bh", ["wpl"], evac_pl)

        Wo = self.I["w_out_rec"][j].rearrange("(kc p) n -> p kc n", p=128)
        for blk in range(4):
            s = blk % 2
            self.wload(wbf[s], Wo[:, :, blk * 256:(blk + 1) * 256], "wbf%d" % s, 2048)
            for r in range(2):
                oc = blk * 2 + r

                def evac_o(ti, c0, n, ps, psr, oc=oc):
                    hr = "h%d_%d" % (oc, ti)
                    k.op("dve", lambda e: e.tensor_tensor(out=h[:, oc, c0:c0 + n], in0=h[:, oc, c0:c0 + n], in1=ps, op=OP.add), reads=[psr, hr], writes=[hr])

                self.proj_chunk(lambda kc, s=s, r=r: wbf[s][:, kc, r * 128:(r + 1) * 128], NCH,
                                lambda kc, c0, n: yab[:, kc, c0:c0 + n], lambda kc, ti: "yab%d_%d" % (kc, ti), ["wbf%d" % s], evac_o)
        self.barrier()
        ph.reset(mm)
        O = self.O
        self.store_fm(tails[:, :, 0:3], 4, 3, O["rc_p"][j], [], "o1")
        self.store_fm(tails[:, :, 3:15], 4, 12, O["rc_s"][j], [], "o2")
        self.store_fm(tails[:, :, 15:16], 4, 1, O["rh_p"][j], [], "o3")
        self.store_fm(tails[:, :, 16:20], 4, 4, O["rh_s"][j], [], "o4")
        self.store_fm(tails[:, :, 20:35], 4, 15, O["pl_p"][j], [], "o5")
        self.store_fm(tails[:, :, 35:95], 4, 60, O["pl_s"][j], [], "o6")
        self.barrier()
        ph.reset(m)

    def rot(self, banks):
        b = banks[self._rot % len(banks)]
        self._rot += 1
        return self.PS[b], "ps%d" % b

    def attn_layer(self, i):
        k, c, ph = self.k, self.cfg, self.ph
        T, TT, NT = c.T, c.TT, c.NT
        h, xn = self.h, self.xn
        j = i // 2
        I, O = self.I, self.O
        self._rot = 0
        RB = [0, 1, 2, 3]
        self.norm("nmix%d" % i)
        hflat = self.hflat
        k.dma("sp", self.d_out, self.hscr, hflat, reads=["h%d_%d" % (ch, ti) for ch in range(NCH) for ti in range(len(self.coltiles()))], writes=["hscr"])
        self.barrier()
        ha = Arena(self.ph.ap, NCH * TT)
        m = ph.mark()
        scale = 1.0 / float(np.sqrt(HD))
        NITER = 10

        def hbf(nel):
            w = ((nel + 1) // 2 + 7) // 8 * 8
            return ha.bf16(nel) if ha.top + w <= ha.size else ph.bf16(nel)

        def hf32(nel):
            w = (nel + 7) // 8 * 8
            return ha.f32(nel) if ha.top + w <= ha.size else ph.f32(nel)

        kT = hbf(NKV * TT).rearrange("p (a b) -> p a b", a=NKV)
        vb = hbf((NT + 1) * 512).rearrange("p (a b) -> p a b", a=NT + 1)
        kiT = hbf(TT)
        qT = hbf(NH * 512).rearrange("p (a b) -> p a b", a=NH)
        qiT = hbf(4 * 512).rearrange("p (a b) -> p a b", a=4)
        Isb = hf32(max(T, 256))
        maskq = hbf(T)
        gqk = ph.f32(256).rearrange("p (a b) -> p a b", a=2)
        negB = ph.f32(8)
        cmask = ph.bf16(4 * 512).rearrange("p (a b) -> p a b", a=4)
        pw2 = ph.f32(16)
        watt = [ph.bf16(4096).rearrange("p (a b) -> p a b", a=8) for _ in range(2)]
        zsb = [ph.f32(512), ph.f32(512)]
        outf = [ph.f32(512), ph.f32(512)]
        outb = [ph.bf16(512), ph.bf16(512)]
        rpt = [ph.f32(128).rearrange("p (a b) -> p a b", a=2) for _ in range(2)]
        rpit = [ph.f32(64).rearrange("p (a b) -> p a b", a=2) for _ in range(2)]
        tq = [ph.f32(256) for _ in range(4)]
        ssq = ph.f32(8)
        junk = ph.f32(128)
        drp = self.drp
        ha_mark2 = ha.mark()
        Wa = I["w_in_attn"][j].rearrange("(kc p) n -> p kc n", p=128)
        nct = len(self.coltiles())

        k.dma("sp", self.d_in, gqk, I["qkn"][j:j + 1].to_broadcast([128, 2, HD]), writes=["gqk"])
        k.op("dve", lambda e: e.tensor_reduce(out=ssq[:, 0:2], in_=gqk, axis=AX.X, op=OP.max, apply_absolute_value=True), reads=["gqk"], writes=["ssq"])
        k.op("dve", lambda e: e.tensor_tensor(out=negB[:, 0:1], in0=ssq[:, 0:1], in1=ssq[:, 1:2], op=OP.mult), reads=["ssq"], writes=["negB"])
        k.op("dve", lambda e: e.tensor_scalar(out=negB[:, 0:1], in0=negB[:, 0:1], scalar1=-float(np.sqrt(HD)), scalar2=None, op0=OP.mult), reads=["negB"], writes=["negB"])
        cmf = ph.f32(512)
        for pq in range(4):
            k.op("pool", lambda e: e.memset(cmf, 0.0), writes=["cmf"])
            blk = cmf[:, pq * 128:(pq + 1) * 128]
            k.op("pool", lambda e, blk=blk: e.affine_select(out=blk, in_=blk, pattern=[[-1, 128]], compare_op=OP.is_ge, fill=NEG, base=0, channel_multiplier=1), reads=["cmf"], writes=["cmf"])
            k.op("pool", lambda e, pq=pq: e.tensor_copy(out=cmask[:, pq, :], in_=cmf), reads=["cmf"], writes=["cmask"])
        for it in range(NITER):
            k.op("pool", lambda e, it=it: e.memset(pw2[:, it:it + 1], 0.5 ** (it + 1)), writes=["pw2"])

        def load_w(slot, col0, ncols):
            for hf in range(0, ncols, 256):
                w = min(256, ncols - hf)
                self.wload(watt[slot][:, :, hf:hf + w], Wa[:, :, col0 + hf:col0 + hf + w], "watt%d" % slot, 8 * w)

        def tm_proj(t0, n, slot, ncols):
            ps, psr = self.rot(RB)
            ti = min(t0 // 512, nct - 1)
            for kc in range(NCH):
                lhs = xn[:, kc, t0:t0 + n]
                rhs = watt[slot][:, kc, 0:ncols]
                k.op("pe", lambda e, ps=ps, lhs=lhs, rhs=rhs, n=n, ncols=ncols, kc=kc: e.matmul(ps[0:n, 0:ncols], lhsT=lhs, rhs=rhs, start=(kc == 0), stop=(kc == NCH - 1)),
                     reads=["watt%d" % slot, "xn%d_%d" % (kc, ti)], writes=[psr], inc=(kc == NCH - 1))
            return ps, psr

        def load_rope(tt, n):
            s = tt % 2
            k.dma("sp", drp[s], rpt[s][0:n], I["rope"][tt, 0:n], writes=["rpt%d" % s])
            k.dma("sp", drp[s], rpit[s][0:n], I["ropei"][tt, 0:n], writes=["rpit%d" % s])
            return s

        def bc(ap2, shape):
            return ap2.to_broadcast(shape)

        def rope_apply(src3, dst3, n, nh, half, cos, sin, sreg, dreg, extra_reads):
            x1, x2 = src3[:, :, 0:half], src3[:, :, half:2 * half]
            cb = cos.unsqueeze(1).to_broadcast([n, nh, half])
            sb = sin.unsqueeze(1).to_broadcast([n, nh, half])
            w = nh * half
            t1, t2, t3, t4 = [t[0:n, 0:w].rearrange("p (a b) -> p a b", a=nh) for t in tq]
            rd = [sreg] + list(extra_reads)
            k.op("dve", lambda e: e.tensor_tensor(out=t1, in0=x1, in1=cb, op=OP.mult), reads=rd, writes=["tq0"])
            k.op("pool", lambda e: e.tensor_tensor(out=t2, in0=x2, in1=sb, op=OP.mult), reads=rd, writes=["tq1"])
            k.op("pool", lambda e: e.tensor_tensor(out=t3, in0=x1, in1=sb, op=OP.mult), reads=rd, writes=["tq2"])
            k.op("dve", lambda e: e.tensor_tensor(out=t4, in0=x2, in1=cb, op=OP.mult), reads=rd, writes=["tq3"])
            k.op("dve", lambda e: e.tensor_tensor(out=dst3[:, :, 0:half], in0=t1, in1=t2, op=OP.subtract), reads=["tq0", "tq1"], writes=[dreg])
            k.op("pool", lambda e: e.tensor_tensor(out=dst3[:, :, half:2 * half], in0=t3, in1=t4, op=OP.add), reads=["tq2", "tq3"], writes=[dreg])

        def qk_post(ps, psr, n, gsel, rs_slot, q):
            z = zsb[q][0:n]
            zr = "zsb%d" % q
            k.op("act", lambda e: e.activation(out=z, in_=ps[0:n, :], func=AF.Copy), reads=[psr], writes=[zr])
            for hh in range(4):
                k.op("act", lambda e, hh=hh: e.activation(out=junk[0:n], in_=z[:, hh * 128:(hh + 1) * 128], func=AF.Square, accum_out=ssq[0:n, hh:hh + 1]),
                     reads=[zr], writes=["ssq", "junk"])
            k.op("act", lambda e: e.activation(out=ssq[0:n, 0:4], in_=ssq[0:n, 0:4], func=AF.Sqrt, bias=self.eps_t[0:n, 0:1], scale=1.0 / HD), reads=["ssq"], writes=["ssq"])
            k.op("dve", lambda e: e.reciprocal(out=ssq[0:n, 0:4], in_=ssq[0:n, 0:4]), reads=["ssq"], writes=["ssq"])
            z3 = z.rearrange("p (a b) -> p a b", a=4)
            k.op("dve", lambda e: e.tensor_tensor(out=z3, in0=z3, in1=ssq[0:n, 0:4].unsqueeze(2).to_broadcast([n, 4, HD]), op=OP.mult), reads=[zr, "ssq"], writes=[zr])
            k.op("pool", lambda e: e.tensor_tensor(out=z3, in0=z3, in1=gqk[0:n, gsel, :].unsqueeze(1).to_broadcast([n, 4, HD]), op=OP.mult), reads=[zr, "gqk"], writes=[zr])
            o3 = outf[q][0:n].rearrange("p (a b) -> p a b", a=4)
            rope_apply(z3, o3, n, 4, 64, rpt[rs_slot][0:n, 0, :], rpt[rs_slot][0:n, 1, :], zr, "outf%d" % q, ["rpt%d" % rs_slot])
            k.op("act", lambda e: e.activation(out=outb[q][0:n], in_=outf[q][0:n], func=AF.Copy), reads=["outf%d" % q], writes=["outb%d" % q])

        def transposes_to(dst3, src_b, n, nblk, reads, dreg):
            ps, psr = self.rot(RB)
            pb = ps.bitcast(BF16)
            for bq in range(nblk):
                k.op("pe", lambda e, bq=bq: e.transpose(pb[:, bq * 128:bq * 128 + n], src_b[0:n, bq * 128:(bq + 1) * 128], self.ident_b[0:n, 0:n]),
                     reads=list(reads) + ["ident_b"], writes=[psr])
            k.op("act", lambda e: e.activation(out=dst3, in_=pb[:, 0:nblk * 128].rearrange("p (a b) -> p a b", a=nblk)[:, :, 0:n], func=AF.Copy), reads=[psr], writes=[dreg])

        toks = self.tok_tiles()
        load_w(0, 1024, 512)
        load_w(1, 1536, 512)
        vst = [ph.f32(512), ph.f32(512)]
        for tt, (t0, n) in enumerate(toks):
            rs_slot = load_rope(tt, n)
            q = tt % 2
            ps, psr = tm_proj(t0, n, 0, 512)
            ps2, psr2 = tm_proj(t0, n, 1, 512)
            qk_post(ps, psr, n, 1, rs_slot, q)
            dst = O["k_p"][j, t0:t0 + n, :] if n == 128 else O["k_s"][j]
            k.dma("sp", self.d_out, dst, outf[q][0:n], reads=["outf%d" % q])
            transposes_to(kT[:, :, t0:t0 + n], outb[q], n, 4, ["outb%d" % q], "kT")
            k.op("dve", lambda e, q=q, ps2=ps2, n=n: e.tensor_copy(out=vst[q][0:n], in_=ps2[0:n, :]), reads=[psr2], writes=["vst%d" % q])
            dst = O["v_p"][j, t0:t0 + n, :] if n == 128 else O["v_s"][j]
            k.dma("sp", self.d_out, dst, vst[q][0:n], reads=["vst%d" % q])
            k.op("pool", lambda e, q=q, tt=tt, n=n: e.tensor_copy(out=vb[0:n, tt, :], in_=vst[q][0:n]), reads=["vst%d" % q], writes=["vb"])
        load_w(0, 2560, 64)
        for tt, (t0, n) in enumerate(toks):
            rs_slot = load_rope(tt, n)
            q = tt % 2
            ps, psr = tm_proj(t0, n, 0, 64)
            z = zsb[q][0:n, 0:64]
            k.op("act", lambda e, z=z, ps=ps, n=n: e.activation(out=z, in_=ps[0:n, 0:64], func=AF.Copy), reads=[psr], writes=["zsb%d" % q])
            o1 = outf[q][0:n, 0:64]
            rope_apply(z.rearrange("p (a b) -> p a b", a=1), o1.rearrange("p (a b) -> p a b", a=1), n, 1, 32,
                       rpit[rs_slot][0:n, 0, :], rpit[rs_slot][0:n, 1, :], "zsb%d" % q, "outf%d" % q, ["rpit%d" % rs_slot])
            dst = O["ki_p"][j, t0:t0 + n, :] if n == 128 else O["ki_s"][j]
            k.dma("sp", self.d_out, dst, o1, reads=["outf%d" % q])
            k.op("act", lambda e, q=q, n=n, o1=o1: e.activation(out=outb[q][0:n, 0:64], in_=o1, func=AF.Copy), reads=["outf%d" % q], writes=["outb%d" % q])
            k.op("act", lambda e, q=q, n=n, o1=o1: e.activation(out=outb[q][0:n, 64:128], in_=o1, func=AF.Copy), reads=["outf%d" % q], writes=["outb%d" % q])
            transposes_to(kiT[:, t0:t0 + n].rearrange("p (a b) -> p a b", a=1), outb[q], n, 1, ["outb%d" % q], "kiT")

        wsc = ph.f32(8)
        wout = [ph.bf16(2048).rearrange("p (a b) -> p a b", a=8) for _ in range(2)]
        hp = [ph.f32(512), ph.f32(512)]
        ph_mark2 = ph.mark()
        wdiag = ph.bf16(IDX_H * 128).rearrange("p (a b) -> p a b", a=IDX_H)
        Rh = [ph.bf16(512), ph.bf16(512)]
        maskT = ph.bf16((T // 128) * 512).rearrange("p (a b) -> p a b", a=T // 128)
        Eb3 = [ph.bf16(512), ph.bf16(512), ph.bf16(512)]
        self._eb = 0
        rden = ph.f32(512)
        attnT = ph.bf16(NH * 512).rearrange("p (a b) -> p a b", a=NH)
        bs = [ph.f32(8) for _ in range(6)]
        wk = ph.f32(16)
        cjunk = maskq
        dhp = self.dhp
        Wo = I["w_out_attn"][j].rearrange("(kc p) n -> p kc n", p=128)
        hs3 = self.hscr.rearrange("p (c t) -> p c t", c=NCH)

        def q_side(tiles, nq):
            for cbi in range(2):
                load_w(cbi, cbi * 512, 512)
            for (li, tt, t0, n) in tiles:
                rs_slot = load_rope(tt, n)
                for cbi in range(2):
                    q = cbi
                    ps, psr = tm_proj(t0, n, cbi, 512)
                    qk_post(ps, psr, n, 0, rs_slot, q)
                    transposes_to(qT[:, cbi * 4:cbi * 4 + 4, li * 128:li * 128 + n], outb[q], n, 4, ["outb%d" % q], "qT")
            load_w(0, 2048, 512)
            for (li, tt, t0, n) in tiles:
                rs_slot = load_rope(tt, n)
                ps, psr = tm_proj(t0, n, 0, 512)
                z = zsb[0][0:n]
                k.op("act", lambda e, z=z, ps=ps, n=n: e.activation(out=z, in_=ps[0:n, :], func=AF.Copy), reads=[psr], writes=["zsb0"])
                o3 = outf[0][0:n].rearrange("p (a b) -> p a b", a=8)
                rope_apply(z.rearrange("p (a b) -> p a b", a=8), o3, n, 8, 32, rpit[rs_slot][0:n, 0, :], rpit[rs_slot][0:n, 1, :], "zsb0", "outf0", ["rpit%d" % rs_slot])
                k.op("act", lambda e, n=n: e.activation(out=outb[0][0:n], in_=outf[0][0:n], func=AF.Copy), reads=["outf0"], writes=["outb0"])
                transposes_to(qiT[:, :, li * 128:li * 128 + n], outb[0], n, 4, ["outb0"], "qiT")

        def w_side(t0, n):
            ps, psr = tm_proj(t0, n, 1, 8)
            k.op("act", lambda e, ps=ps, n=n: e.activation(out=wsc[0:n, 0:8], in_=ps[0:n, 0:8], func=AF.Copy, scale=float(IDX_D ** -0.5 * IDX_H ** -0.5)),
                 reads=[psr], writes=["wsc"])

        NC = T // 512
        for cq in range(NC):
            tiles = [(li, 4 * cq + li, (4 * cq + li) * 128, 128) for li in range(4)]
            q_side(tiles, 4)
            load_w(1, 2624, 8)
            for (li, tt, t0, n) in tiles:
                qt = tt
                w_side(t0, n)
                for hh in range(IDX_H):
                    k.op("pool", lambda e, hh=hh: e.tensor_scalar(out=wdiag[:, hh, :], in0=self.ident_f, scalar1=wsc[:, hh:hh + 1], scalar2=1.0, op0=OP.mult, op1=OP.mult),
                         reads=["wsc", "ident_f"], writes=["wdiag"])
                nkb = qt + 1
                nk = nkb * 128
                for kg in range((nkb + 3) // 4):
                    ncols = min(512, nk - kg * 512)
                    ia, iar = self.PS[4], "ps4"
                    diag_here = (qt // 4 == kg)
                    for hh in range(IDX_H):
                        hpair, par = divmod(hh, 2)
                        ps, psr = self.rot(RB)
                        lhs = qiT[par * 64:(par + 1) * 64, hpair, li * 128:(li + 1) * 128]
                        rhs = kiT[par * 64:(par + 1) * 64, kg * 512:kg * 512 + ncols]
                        k.op("pe", lambda e, ps=ps, lhs=lhs, rhs=rhs, ncols=ncols: e.matmul(ps[:, 0:ncols], lhsT=lhs, rhs=rhs, start=True, stop=True),
                             reads=["qiT", "kiT"], writes=[psr])
                        r = Rh[hh % 2]
                        if hh % 2 == 0:
                            k.op("act", lambda e, r=r, ps=ps, ncols=ncols: e.activation(out=r[:, 0:ncols], in_=ps[:, 0:ncols], func=AF.Relu), reads=[psr], writes=["Rh%d" % (hh % 2)])
                        else:
                            k.op("dve", lambda e, r=r, ps=ps, ncols=ncols: e.tensor_scalar(out=r[:, 0:ncols], in0=ps[:, 0:ncols], scalar1=0.0, scalar2=None, op0=OP.max), reads=[psr], writes=["Rh%d" % (hh % 2)])
                        last = (hh == IDX_H - 1) and not diag_here
                        k.op("pe", lambda e, ia=ia, hh=hh, r=r, ncols=ncols, last=last: e.matmul(ia[:, 0:ncols], lhsT=wdiag[:, hh, :], rhs=r[:, 0:ncols], start=(hh == 0), stop=last),
                             reads=["wdiag", "Rh%d" % (hh % 2)], writes=[iar], inc=True)
                    if diag_here:
                        pq = qt % 4
                        k.op("pe", lambda e, ia=ia, pq=pq, ncols=ncols: e.matmul(ia[:, 0:ncols], lhsT=self.ident_b, rhs=cmask[:, pq, 0:ncols], start=False, stop=True),
                             reads=["ident_b", "cmask"], writes=[iar])
                    k.op("act", lambda e, ia=ia, kg=kg, ncols=ncols: e.activation(out=Isb[:, kg * 512:kg * 512 + ncols], in_=ia[:, 0:ncols], func=AF.Copy), reads=[iar], writes=["Isb"])
                lo, wd, mid, cnt, tt_, thr = bs
                TK = c.TOPK
                if qt >= TK // 128:
                    k.op("dve", lambda e: e.tensor_reduce(out=lo[:, 0:1], in_=Isb[:, 0:TK], axis=AX.X, op=OP.min), reads=["Isb"], writes=["bs_lo"])
                    k.op("dve", lambda e, nk=nk: e.tensor_reduce(out=wd[:, 0:1], in_=Isb[:, 0:nk], axis=AX.X, op=OP.max), reads=["Isb"], writes=["bs_w"])
                    k.op("dve", lambda e: e.tensor_tensor(out=wd[:, 0:1], in0=wd[:, 0:1], in1=lo[:, 0:1], op=OP.subtract), reads=["bs_w", "bs_lo"], writes=["bs_w"])
                    k.op("dve", lambda e: e.tensor_scalar(out=wd[:, 0:1], in0=wd[:, 0:1], scalar1=1.0001, scalar2=1e-6, op0=OP.mult, op1=OP.add), reads=["bs_w"], writes=["bs_w"])
                    k.op("dve", lambda e: e.tensor_tensor(out=wk[:, 0:NITER], in0=pw2[:, 0:NITER], in1=wd[:, 0:1].to_broadcast([128, NITER]), op=OP.mult), reads=["bs_w", "pw2"], writes=["wk"])
                    k.op("dve", lambda e: e.tensor_tensor(out=mid[:, 0:1], in0=lo[:, 0:1], in1=wk[:, 0:1], op=OP.add), reads=["bs_lo", "wk"], writes=["bs_mid"])
                    for it in range(NITER):
                        k.op("dve", lambda e, nk=nk: e.tensor_scalar(out=cjunk[:, 0:nk], in0=Isb[:, 0:nk], scalar1=mid[:, 0:1], scalar2=None, op0=OP.is_ge, op1=OP.add, accum_out=cnt[:, 0:1]),
                             reads=["Isb", "bs_mid"], writes=["bs_cnt", "maskq"])
                        k.op("dve", lambda e: e.tensor_scalar(out=tt_[:, 0:1], in0=cnt[:, 0:1], scalar1=float(TK) - 0.5, scalar2=0.5, op0=OP.is_ge, op1=OP.subtract), reads=["bs_cnt"], writes=["bs_t"])
                        k.op("dve", lambda e, it=it: e.scalar_tensor_tensor(out=mid[:, 0:1], in0=tt_[:, 0:1], scalar=wk[:, it:it + 1], in1=mid[:, 0:1], op0=OP.mult, op1=OP.add), reads=["bs_t", "wk", "bs_mid"], writes=["bs_mid"])
                    k.op("dve", lambda e: e.scalar_tensor_tensor(out=lo[:, 0:1], in0=wk[:, NITER - 1:NITER], scalar=-0.5, in1=mid[:, 0:1], op0=OP.mult, op1=OP.add), reads=["bs_mid", "wk"], writes=["bs_lo"])
                    thr_ap, thr_reg = lo, "bs_lo"
                else:
                    k.op("dve", lambda e: e.memset(thr[:, 0:1], -1.0e29), writes=["bs_thr"])
                    thr_ap, thr_reg = thr, "bs_thr"
                k.op("dve", lambda e, nk=nk, thr_ap=thr_ap: e.tensor_scalar(out=maskq[:, 0:nk], in0=Isb[:, 0:nk], scalar1=thr_ap[:, 0:1], scalar2=-1.0e5, op0=OP.is_lt, op1=OP.mult), reads=["Isb", thr_reg], writes=["maskq"])
                for g0 in range(0, nkb, 8):
                    g1 = min(nkb, g0 + 8)
                    pm, pmr = self.PS[5], "ps5"
                    pmb = pm.bitcast(BF16)
                    for kb in range(g0, g1):
                        k.op("pe", lambda e, kb=kb, g0=g0, pmb=pmb: e.transpose(pmb[:, (kb - g0) * 128:(kb - g0 + 1) * 128], maskq[:, kb * 128:(kb + 1) * 128], self.ident_b),
                             reads=["maskq", "ident_b"], writes=[pmr])
                    nb = g1 - g0
                    k.op("act", lambda e, g0=g0, g1=g1, nb=nb, li=li, pmb=pmb: e.activation(out=maskT[:, g0:g1, li * 128:(li + 1) * 128], in_=pmb[:, 0:nb * 128].rearrange("p (a b) -> p a b", a=nb), func=AF.Copy),
                         reads=[pmr], writes=["maskT"])
            nkb_c = 4 * cq + 4
            ops, opr = self.PS[6], "ps6"
            dps, dpr = self.PS[7], "ps7"
            for hh in range(NH):
                kvh = hh // 2
                for kb in range(nkb_c):
                    c0 = max(0, kb - 4 * cq) * 128
                    ncol = 512 - c0
                    ps, psr = self.rot(RB)
                    lhs = kT[:, kvh, kb * 128:(kb + 1) * 128]
                    rhs = qT[:, hh, c0:512]
                    k.op("pe", lambda e, ps=ps, lhs=lhs, rhs=rhs, ncol=ncol: e.matmul(ps[:, 0:ncol], lhsT=lhs, rhs=rhs, start=True, stop=False), reads=["kT", "qT"], writes=[psr], inc=False)
                    k.op("pe", lambda e, ps=ps, kb=kb, c0=c0, ncol=ncol: e.matmul(ps[:, 0:ncol], lhsT=self.ident_b, rhs=maskT[:, kb, c0:512], start=False, stop=True), reads=["ident_b", "maskT"], writes=[psr])
                    es = self._eb % 3
                    self._eb += 1
                    pb_ = Eb3[es]
                    k.op("act", lambda e, pb_=pb_, ps=ps, ncol=ncol: e.activation(out=pb_[:, 0:ncol], in_=ps[:, 0:ncol], func=AF.Exp, scale=scale, bias=negB[:, 0:1]),
                         reads=[psr, "negB"], writes=["Eb%d" % es])
                    vl = vb[:, kb, kvh * 128:(kvh + 1) * 128]
                    last = (kb == nkb_c - 1)
                    k.op("pe", lambda e, vl=vl, pb_=pb_, c0=c0, ncol=ncol, kb=kb, last=last: e.matmul(ops[:, c0:512], lhsT=vl, rhs=pb_[:, 0:ncol], start=(kb == 0), stop=last),
                         reads=["vb", "Eb%d" % es], writes=[opr], inc=True)
                    k.op("pe", lambda e, pb_=pb_, c0=c0, ncol=ncol, kb=kb, last=last: e.matmul(dps[:, c0:512], lhsT=self.ones_b, rhs=pb_[:, 0:ncol], start=(kb == 0), stop=last),
                         reads=["ones_b", "Eb%d" % es], writes=[dpr], inc=True)
                k.op("dve", lambda e: e.reciprocal(out=rden, in_=dps), reads=[dpr], writes=["rden"])
                k.op("dve", lambda e, hh=hh: e.tensor_tensor(out=attnT[:, hh, :], in0=ops, in1=rden, op=OP.mult), reads=[opr, "rden"], writes=["attnT"])
            for blk in range(4):
                s = blk % 2
                self.wload(wout[s], Wo[:, :, blk * 256:(blk + 1) * 256], "wout%d" % s, 2048)
                for r in range(2):
                    oc = blk * 2 + r
                    q = oc % 2
                    k.dma("sp", dhp[q], hp[q], hs3[:, oc, cq * 512:(cq + 1) * 512], reads=["hscr%d_%d" % (oc, cq)], writes=["hp%d" % q])
                    ps, psr = self.rot(RB)
                    for hh in range(NH):
                        lhs = wout[s][:, hh, r * 128:(r + 1) * 128]
                        rhs = attnT[:, hh, :]
                        k.op("pe", lambda e, ps=ps, lhs=lhs, rhs=rhs, hh=hh: e.matmul(ps, lhsT=lhs, rhs=rhs, start=(hh == 0), stop=(hh == NH - 1)),
                             reads=["wout%d" % s, "attnT"], writes=[psr], inc=(hh == NH - 1))
                    k.op("dve", lambda e, q=q, ps=ps: e.tensor_tensor(out=hp[q], in0=hp[q], in1=ps, op=OP.add), reads=[psr, "hp%d" % q], writes=["hp%d" % q])
                    k.dma("sp", dhp[q], hs3[:, oc, cq * 512:(cq + 1) * 512], hp[q], reads=["hp%d" % q], writes=["hscr%d_%d" % (oc, cq)])

        self.attn_sample(i, locals())

        self.barrier()
        k.dma("sp", self.d_in, hflat, self.hscr, reads=["hscr"], writes=["h%d_%d" % (ch, ti) for ch in range(NCH) for ti in range(len(self.coltiles()))])
        self.barrier()
        ph.reset(m)

    def attn_sample(self, i, L):
        k, c, ph = self.k, self.cfg, self.ph
        T, TT, NT, NPG, PAST = c.T, c.TT, c.NT, c.NPG, c.PAST
        j = i // 2
        I, O = self.I, self.O
        kT, vb, kiT, qT, qiT = L["kT"], L["vb"], L["kiT"], L["qT"], L["qiT"]
        ha = L["ha"]
        negB, wsc, outb, outf, zsb = L["negB"], L["wsc"], L["outb"], L["outf"], L["zsb"]
        load_w, tm_proj, qk_post, transposes_to, rope_apply, load_rope = L["load_w"], L["tm_proj"], L["qk_post"], L["transposes_to"], L["rope_apply"], L["load_rope"]
        rpit = L["rpit"]
        scale = L["scale"]
        NITER = L["NITER"]
        pw2 = L["pw2"]
        wout, hp, dhp, Wo, hs3 = L["wout"], L["hp"], L["dhp"], L["Wo"], L["hs3"]
        RB = [0, 1, 2, 3]
        NL = PAST + 4
        self.barrier()
        ph.reset(L["ph_mark2"])
        ha.reset(L["ha_mark2"])
        qTs = ph.bf16(NH * 16).rearrange("p (b h t) -> p b h t", b=4, h=NH)
        qiTd = ph.bf16(NH * 16).rearrange("p (b h t) -> p b h t", b=4, h=NH)
        qidup = ph.bf16(NH * 128).rearrange("p (a b) -> p a b", a=NH)
        wbs = ph.f32(32).rearrange("p (a b) -> p a b", a=4)
        Xw = ph.f32(32).rearrange("p (a b) -> p a b", a=8)
        wsel = ph.bf16(16).rearrange("p (a b) -> p a b", a=4)
        ptb = ph.i32(4 * NPG)
        ptf = ph.f32(4 * NPG)
        iot = ph.f32(8)
        idxi = ph.i32(4 * NPG)
        cm4 = ph.bf16(8)
        cm4f = ph.f32(8)
        bmk = ph.f32(16)
        Mfull = ph.bf16(16)
        MfT = ph.bf16(16)
        attnTs = ph.bf16(NH * 16).rearrange("p (a b) -> p a b", a=NH)
        rs_slot = load_rope(NT, 16)
        for cbi in range(2):
            load_w(cbi, cbi * 512, 512)
        for cbi in range(2):
            ps, psr = tm_proj(T, 16, cbi, 512)
            qk_post(ps, psr, 16, 0, rs_slot, cbi)
            transposes_to(qT[:, cbi * 4:cbi * 4 + 4, 0:16], outb[cbi], 16, 4, ["outb%d" % cbi], "qT")
        k.op("act", lambda e: e.activation(out=qTs, in_=qT[:, :, 0:16].rearrange("p h (b t) -> p b h t", b=4), func=AF.Copy), reads=["qT"], writes=["qTs"])
        load_w(0, 2048, 512)
        ps, psr = tm_proj(T, 16, 0, 512)
        z = zsb[0][0:16]
        k.op("act", lambda e, ps=ps: e.activation(out=z, in_=ps[0:16, :], func=AF.Copy), reads=[psr], writes=["zsb0"])
        o3 = outf[0][0:16].rearrange("p (a b) -> p a b", a=8)
        rope_apply(z.rearrange("p (a b) -> p a b", a=8), o3, 16, 8, 32, rpit[rs_slot][0:16, 0, :], rpit[rs_slot][0:16, 1, :], "zsb0", "outf0", ["rpit%d" % rs_slot])
        k.op("act", lambda e: e.activation(out=qidup[0:16, :, 0:64], in_=o3, func=AF.Copy), reads=["outf0"], writes=["qidup"])
        k.op("act", lambda e: e.activation(out=qidup[0:16, :, 64:128], in_=o3, func=AF.Copy), reads=["outf0"], writes=["qidup"])
        self.dbg("watt0", L["watt"][0], ["watt0"])
        self.dbg("xns", self.xn[:, :, T:T + 16], [])
        self.dbg("zsb0", zsb[0][0:16], ["zsb0"])
        self.dbg("rpit", rpit[rs_slot][0:16], ["rpit%d" % rs_slot])
        self.dbg("outf0", outf[0][0:16], ["outf0"])
        self.dbg("qidup", qidup[0:16], ["qidup"])
        pq_, pqr = self.rot(RB)
        pqb = pq_.bitcast(BF16)
        for hh in range(NH):
            k.op("pe", lambda e, hh=hh: e.transpose(pqb[:, hh * 16:(hh + 1) * 16], qidup[0:16, hh, :], self.ident_b[0:16, 0:16]), reads=["qidup", "ident_b"], writes=[pqr])
        k.op("act", lambda e: e.activation(out=qiTd, in_=pqb[:, 0:NH * 16].rearrange("p (h b t) -> p b h t", h=NH, b=4), func=AF.Copy), reads=[pqr], writes=["qiTd"])
        load_w(1, 2624, 8)
        L["w_side"](T, 16)
        pw_, pwr = self.rot(RB)
        for b in range(4):
            k.op("pe", lambda e, b=b: e.matmul(pw_[0:4, b * 8:(b + 1) * 8], lhsT=self.ident_f[0:16, 4 * b:4 * b + 4], rhs=wsc[0:16, 0:8], start=True, stop=True),
                 reads=["ident_f", "wsc"], writes=[pwr])
        k.op("act", lambda e: e.activation(out=wbs[0:4], in_=pw_[0:4, 0:32].rearrange("p (a b) -> p a b", a=4), func=AF.Copy), reads=[pwr], writes=["wbs"])
        for b in range(4):
            k.op("dve", lambda e, b=b: e.tensor_tensor(out=Xw[0:4], in0=wbs[0:4, b, :].unsqueeze(2).to_broadcast([4, 8, 4]),
                                                      in1=self.ident_f[0:4, 0:4].unsqueeze(1).to_broadcast([4, 8, 4]), op=OP.mult), reads=["wbs", "ident_f"], writes=["Xw"])
            px, pxr = self.rot(RB)
            self.tr32(px[0:32, 0:4], Xw[0:4].rearrange("p a b -> p (a b)"), ["Xw"], pxr, 4)
            k.op("act", lambda e, b=b, px=px: e.activation(out=wsel[0:32, b, :], in_=px[0:32, 0:4], func=AF.Copy), reads=[pxr], writes=["wsel"])
        k.dma("sp", None, ptb, I["ptab"].to_broadcast([128, 4 * NPG]), writes=["ptb"])
        k.op("pool", lambda e: e.iota(iot[:, 0:1], [[0, 1]], base=0, channel_multiplier=1, allow_small_or_imprecise_dtypes=True), writes=["iot"])
        k.op("dve", lambda e: e.tensor_copy(out=ptf, in_=ptb), reads=["ptb"], writes=["ptf"])
        k.op("dve", lambda e: e.tensor_scalar(out=ptf, in0=ptf, scalar1=128.0, scalar2=iot[:, 0:1], op0=OP.mult, op1=OP.add), reads=["ptf", "iot"], writes=["ptf"])
        if j:
            k.op("dve", lambda e: e.tensor_scalar(out=ptf, in0=ptf, scalar1=float(j * c.NPOOL * 128), scalar2=None, op0=OP.add), reads=["ptf"], writes=["ptf"])
        k.op("dve", lambda e: e.tensor_copy(out=idxi, in_=ptf), reads=["ptf"], writes=["idxi"])
        k.op("pool", lambda e: e.memset(cm4f[:, 0:4], 0.0), writes=["cm4f"])
        k.op("pool", lambda e: e.affine_select(out=cm4f[:, 0:4], in_=cm4f[:, 0:4], pattern=[[-1, 4]], compare_op=OP.is_ge, fill=NEG, base=0, channel_multiplier=1), reads=["cm4f"], writes=["cm4f"])
        k.op("pool", lambda e: e.tensor_copy(out=cm4[:, 0:4], in_=cm4f[:, 0:4]), reads=["cm4f"], writes=["cm4"])
        bm3 = bmk.rearrange("p (a b) -> p a b", a=4)
        k.op("pool", lambda e: e.memset(bmk, 1.0), writes=["bmk"])
        k.op("pool", lambda e: e.affine_select(out=bm3, in_=bm3, pattern=[[-4, 4], [0, 4]], compare_op=OP.is_ge, fill=0.0, base=0, channel_multiplier=1), reads=["bmk"], writes=["bmk"])
        k.op("pool", lambda e: e.affine_select(out=bm3, in_=bm3, pattern=[[4, 4], [0, 4]], compare_op=OP.is_ge, fill=0.0, base=3, channel_multiplier=-1), reads=["bmk"], writes=["bmk"])
        kTn = ph.bf16(NKV * 16).rearrange("p (a b) -> p a b", a=NKV)
        vbn = ph.bf16(512)
        kiTn = ph.bf16(16)
        k.op("pool", lambda e: e.tensor_copy(out=kTn, in_=kT[:, :, T:T + 16]), reads=["kT"], writes=["kTn"])
        k.op("pool", lambda e: e.tensor_copy(out=vbn[0:16], in_=vb[0:16, NT, :]), reads=["vb"], writes=["vbn"])
        k.op("pool", lambda e: e.tensor_copy(out=kiTn, in_=kiT[:, T:T + 16]), reads=["kiT"], writes=["kiTn"])
        self.barrier()
        ha.reset(0)
        mark_sb = ph.mark()

        Iall = ha.f32(NL)
        hmark_sb = ha.mark()
        kis = ha.f32(NPG * 64).rearrange("p (a b) -> p a b", a=NPG)
        kisb = ha.bf16(NPG * 64).rearrange("p (a b) -> p a b", a=NPG)
        kiTs = ph.bf16(NPG * 64).rearrange("p (a b) -> p a b", a=NPG // 2)
        stg = [ph.f32(512), ph.f32(512)]
        Rs = [ph.bf16(512), ph.bf16(512)]
        cki = I["cache_ki"].rearrange("a r d -> (a r) d")
        sq = 0
        for b in range(4):
            for pg in range(NPG):
                col = b * NPG + pg
                k.dma("pool", None, None, None, reads=["idxi"], writes=["kis"],
                      fn=lambda e, pg=pg, col=col: e.indirect_dma_start(out=kis[:, pg, :], out_offset=None, in_=cki, in_offset=bass.IndirectOffsetOnAxis(ap=idxi[:, col:col + 1], axis=0)))
            k.op("pool", lambda e: e.tensor_copy(out=kisb, in_=kis), reads=["kis"], writes=["kisb"])
            for g0 in range(0, NPG // 2, 8):
                pt_, ptr_ = self.rot(RB)
                ptb16 = pt_.bitcast(BF16)
                for pp in range(g0, g0 + 8 if g0 + 8 <= NPG // 2 else NPG // 2):
                    k.op("pe", lambda e, pp=pp, g0=g0, ptb16=ptb16: e.transpose(ptb16[:, (pp - g0) * 128:(pp - g0 + 1) * 128], kisb[:, 2 * pp:2 * pp + 2, :].rearrange("p a b -> p (a b)"), self.ident_b),
                         reads=["kisb", "ident_b"], writes=[ptr_])
                nb = min(8, NPG // 2 - g0)
                k.op("act", lambda e, g0=g0, nb=nb, ptb16=ptb16: e.activation(out=kiTs[:, g0:g0 + nb, :], in_=ptb16[:, 0:nb * 128].rearrange("p (a b) -> p a b", a=nb), func=AF.Copy),
                     reads=[ptr_], writes=["kiTs"])
            Iv = None
            for par in range(2):
                for g4 in range(NPG // 8):
                    sq ^= 1
                    ps, psr = self.rot(RB)
                    lhs = qiTd[par * 64:(par + 1) * 64, b].rearrange("p h t -> p (h t)")
                    rhs = kiTs[par * 64:(par + 1) * 64, g4 * 4:(g4 + 1) * 4, :].rearrange("p a b -> p (a b)")
                    k.op("pe", lambda e, ps=ps, lhs=lhs, rhs=rhs: e.matmul(ps[0:32, :], lhsT=lhs, rhs=rhs, start=True, stop=True), reads=["qiTd", "kiTs"], writes=[psr])
                    r = Rs[sq]
                    if sq:
                        k.op("act", lambda e, r=r, ps=ps: e.activation(out=r[0:32], in_=ps[0:32, :], func=AF.Relu), reads=[psr], writes=["Rs%d" % sq])
                    else:
                        k.op("dve", lambda e, r=r, ps=ps: e.tensor_scalar(out=r[0:32], in0=ps[0:32, :], scalar1=0.0, scalar2=None, op0=OP.max), reads=[psr], writes=["Rs%d" % sq])
                    pi, pir = self.PS[4], "ps4"
                    k.op("pe", lambda e, pi=pi, b=b, r=r: e.matmul(pi[0:4, :], lhsT=wsel[0:32, b, :], rhs=r[0:32], start=True, stop=True), reads=["wsel", "Rs%d" % sq], writes=[pir])
                    st_ = stg[sq]
                    k.op("act", lambda e, st_=st_, pi=pi: e.activation(out=st_[0:4], in_=pi[0:4, :], func=AF.Copy), reads=[pir], writes=["stg%d" % sq])
                    dst = Iall[4 * b:4 * b + 4, 0:PAST].rearrange("p (a c d) -> p a c d", c=2, d=128)[:, g4 * 4:(g4 + 1) * 4, par, :]
                    k.dma("sp", None, dst, st_[0:4].rearrange("p (a d) -> p a d", d=128), reads=["stg%d" % sq], writes=["Iall"])
            sq ^= 1
            ps, psr = self.rot(RB)
            lhs = qiTd[0:64, b].rearrange("p h t -> p (h t)")
            rhs = kiTn[0:64, 4 * b:4 * b + 4]
            k.op("pe", lambda e, ps=ps, lhs=lhs, rhs=rhs: e.matmul(ps[0:32, 0:4], lhsT=lhs, rhs=rhs, start=True, stop=True), reads=["qiTd", "kiTn"], writes=[psr])
            r = Rs[sq]
            k.op("act", lambda e, r=r, ps=ps: e.activation(out=r[0:32, 0:4], in_=ps[0:32, 0:4], func=AF.Relu), reads=[psr], writes=["Rs%d" % sq])
            pi, pir = self.PS[4], "ps4"
            k.op("pe", lambda e, pi=pi, b=b, r=r: e.matmul(pi[0:4, 0:4], lhsT=wsel[0:32, b, :], rhs=r[0:32, 0:4], start=True, stop=False), reads=["wsel", "Rs%d" % sq], writes=[pir], inc=False)
            k.op("pe", lambda e, pi=pi: e.matmul(pi[0:4, 0:4], lhsT=self.ident_b[0:4, 0:4], rhs=cm4[0:4, 0:4], start=False, stop=True), reads=["ident_b", "cm4"], writes=[pir])
            st_ = stg[sq]
            k.op("act", lambda e, st_=st_, pi=pi: e.activation(out=st_[0:4, 0:4], in_=pi[0:4, 0:4], func=AF.Copy), reads=[pir], writes=["stg%d" % sq])
            k.dma("sp", None, Iall[4 * b:4 * b + 4, PAST:PAST + 4], st_[0:4, 0:4], reads=["stg%d" % sq], writes=["Iall"])
        self.dbg("Iall", Iall[0:16, :], ["Iall"])
        self.dbg("qTs", qTs, ["qTs"])
        self.dbg("qiTd", qiTd, ["qiTd"])
        self.dbg("wsel", wsel[0:32], ["wsel"])
        self.dbg("idxi", idxi, ["idxi"])
        self.barrier()
        ha.reset(hmark_sb)
        ph.reset(mark_sb)

        TK = c.TOPK_S
        bs = [ph.f32(8) for _ in range(5)]
        lo, wd, mid, cnt, tt_ = bs
        wk = ph.f32(16)
        mask_s = ha.bf16(NL)
        cj = mask_s
        maskTs = ph.bf16(NPG * 16).rearrange("p (a b) -> p a b", a=NPG)
        R16 = slice(0, 16)
        k.op("dve", lambda e: e.tensor_reduce(out=lo[R16, 0:1], in_=Iall[R16, 0:TK], axis=AX.X, op=OP.min), reads=["Iall"], writes=["bs_lo"])
        k.op("dve", lambda e: e.tensor_reduce(out=wd[R16, 0:1], in_=Iall[R16, :], axis=AX.X, op=OP.max), reads=["Iall"], writes=["bs_w"])
        k.op("dve", lambda e: e.tensor_tensor(out=wd[R16, 0:1], in0=wd[R16, 0:1], in1=lo[R16, 0:1], op=OP.subtract), reads=["bs_w", "bs_lo"], writes=["bs_w"])
        k.op("dve", lambda e: e.tensor_scalar(out=wd[R16, 0:1], in0=wd[R16, 0:1], scalar1=1.0001, scalar2=1e-6, op0=OP.mult, op1=OP.add), reads=["bs_w"], writes=["bs_w"])
        k.op("dve", lambda e: e.tensor_tensor(out=wk[R16, 0:NITER], in0=pw2[R16, 0:NITER], in1=wd[R16, 0:1].to_broadcast([16, NITER]), op=OP.mult), reads=["bs_w", "pw2"], writes=["wk"])
        for it in range(NITER):
            k.op("dve", lambda e, it=it: e.tensor_tensor(out=mid[R16, 0:1], in0=lo[R16, 0:1], in1=wk[R16, it:it + 1], op=OP.add), reads=["bs_lo", "wk"], writes=["bs_mid"])
            k.op("dve", lambda e: e.tensor_scalar(out=cj[R16, :], in0=Iall[R16, :], scalar1=mid[R16, 0:1], scalar2=None, op0=OP.is_ge, op1=OP.add, accum_out=cnt[R16, 0:1]),
                 reads=["Iall", "bs_mid"], writes=["bs_cnt", "mask_s"])
            k.op("dve", lambda e, it=it: e.scalar_tensor_tensor(out=tt_[R16, 0:1], in0=cnt[R16, 0:1], scalar=float(TK) - 0.5, in1=wk[R16, it:it + 1], op0=OP.is_ge, op1=OP.mult), reads=["bs_cnt", "wk"], writes=["bs_t"])
            k.op("dve", lambda e: e.tensor_tensor(out=lo[R16, 0:1], in0=lo[R16, 0:1], in1=tt_[R16, 0:1], op=OP.add), reads=["bs_lo", "bs_t"], writes=["bs_lo"])
        k.op("dve", lambda e: e.tensor_scalar(out=mask_s[R16, :], in0=Iall[R16, :], scalar1=lo[R16, 0:1], scalar2=None, op0=OP.is_ge), reads=["Iall", "bs_lo"], writes=["mask_s"])
        for g0 in range(0, NPG, 32):
            pm, pmr = self.rot(RB)
            nb = min(32, NPG - g0)
            for pg in range(g0, g0 + nb):
                k.op("pe", lambda e, pg=pg, g0=g0, pm=pm: e.matmul(pm[:, (pg - g0) * 16:(pg - g0 + 1) * 16], lhsT=mask_s[R16, pg * 128:(pg + 1) * 128], rhs=self.ident_b[0:16, 0:16], start=True, stop=True),
                     reads=["mask_s", "ident_b"], writes=[pmr])
            k.op("act", lambda e, g0=g0, nb=nb, pm=pm: e.activation(out=maskTs[:, g0:g0 + nb, :], in_=pm[:, 0:nb * 16].rearrange("p (a b) -> p a b", a=nb), func=AF.Copy), reads=[pmr], writes=["maskTs"])
        k.op("dve", lambda e: e.tensor_tensor(out=Mfull[R16, :].rearrange("p (a b) -> p a b", a=4), in0=bm3[R16], in1=mask_s[R16, PAST:PAST + 4].unsqueeze(1).to_broadcast([16, 4, 4]), op=OP.mult),
             reads=["bmk", "mask_s"], writes=["Mfull"])
        pm, pmr = self.rot(RB)
        k.op("pe", lambda e, pm=pm: e.matmul(pm[0:16, 0:16], lhsT=Mfull[R16, :], rhs=self.ident_b[0:16, 0:16], start=True, stop=True), reads=["Mfull", "ident_b"], writes=[pmr])
        k.op("act", lambda e, pm=pm: e.activation(out=MfT[R16, :], in_=pm[0:16, 0:16], func=AF.Copy), reads=[pmr], writes=["MfT"])

        self.dbg("mask_s", mask_s[0:16, :], ["mask_s"])
        self.dbg("thr", lo[0:16, 0:1], ["bs_lo"])
        self.dbg("MfT", MfT[0:16, :], ["MfT"])
        self.dbg("maskTs", maskTs, ["maskTs"])
        GP = 4
        NSL = 4
        hf = lambda nw: ha.f32(nw) if ha.top + (nw + 7) // 8 * 8 <= ha.size else ph.f32(nw)
        kvpg = [hf(1024) for _ in range(NSL)]
        kpb = [ph.bf16(512) for _ in range(NSL)]
        kTp = [ph.bf16(GP * 512).rearrange("p (u a b) -> p u a b", u=GP, a=NKV) for _ in range(2)]
        vpb = [ph.bf16(GP * 512).rearrange("p (u b) -> p u b", u=GP) for _ in range(2)]
        Es = ph.bf16(GP * 32)
        Ps_ = [ph.bf16(GP * 32), ph.bf16(GP * 32)]
        En = ph.bf16(32)
        Pn = ph.bf16(32)
        rdn = ph.f32(32)
        ckv = I["cache_kv"].rearrange("a r d -> (a r) d")
        groups = [(b, pgrp) for b in range(4) for pgrp in range(NPG // GP)]
        slotc = [0]

        def sd_load(gi):
            b, pgrp = groups[gi]
            gs = gi % 2
            for u in range(GP):
                pg = pgrp * GP + u
                col = b * NPG + pg
                slotc[0] = (slotc[0] + 1) % NSL
                slot = slotc[0]
                k.dma("pool", None, None, None, reads=["idxi"], writes=["kvpg%d" % slot],
                      fn=lambda e, slot=slot, col=col: e.indirect_dma_start(out=kvpg[slot], out_offset=None, in_=ckv, in_offset=bass.IndirectOffsetOnAxis(ap=idxi[:, col:col + 1], axis=0)))
                k.op("dve", lambda e, slot=slot: e.tensor_copy(out=kpb[slot], in_=kvpg[slot][:, 0:512]), reads=["kvpg%d" % slot], writes=["kpb%d" % slot])
                k.op("dve", lambda e, slot=slot, gs=gs, u=u: e.tensor_copy(out=vpb[gs][:, u, :], in_=kvpg[slot][:, 512:1024]), reads=["kvpg%d" % slot], writes=["vpb%d" % gs])
                pt_, ptr_ = self.PS[6], "ps6"
                ptb16 = pt_.bitcast(BF16)
                for kvh in range(NKV):
                    k.op("pe", lambda e, kvh=kvh, slot=slot, ptb16=ptb16: e.transpose(ptb16[:, kvh * 128:(kvh + 1) * 128], kpb[slot][:, kvh * 128:(kvh + 1) * 128], self.ident_b),
                         reads=["kpb%d" % slot, "ident_b"], writes=[ptr_])
                k.op("act", lambda e, gs=gs, u=u, ptb16=ptb16: e.activation(out=kTp[gs][:, u], in_=ptb16[:, 0:512].rearrange("p (a b) -> p a b", a=NKV), func=AF.Copy), reads=[ptr_], writes=["kTp%d" % gs])

        def sd_score(gi):
            b, pgrp = groups[gi]
            gs = gi % 2
            sc, scr = self.PS[5], "ps5"
            for u in range(GP):
                for kvh in range(NKV):
                    o_ = u * 32 + kvh * 8
                    rhs = qTs[:, b, 2 * kvh:2 * kvh + 2, :].rearrange("p h t -> p (h t)")
                    k.op("pe", lambda e, gs=gs, u=u, kvh=kvh, o_=o_, rhs=rhs, sc=sc: e.matmul(sc[:, o_:o_ + 8], lhsT=kTp[gs][:, u, kvh, :], rhs=rhs, start=True, stop=True),
                         reads=["kTp%d" % gs, "qTs"], writes=[scr])
            k.op("act", lambda e, sc=sc: e.activation(out=Es, in_=sc[:, 0:GP * 32], func=AF.Exp, scale=scale, bias=negB[:, 0:1]), reads=[scr, "negB"], writes=["Es"])
            pp_ = Ps_[gs]
            k.op("dve", lambda e, pp_=pp_, pgrp=pgrp, b=b: e.tensor_tensor(out=pp_.rearrange("p (u h t) -> p u h t", u=GP, h=NH), in0=Es.rearrange("p (u h t) -> p u h t", u=GP, h=NH),
                                                                        in1=maskTs[:, pgrp * GP:(pgrp + 1) * GP, 4 * b:4 * b + 4].unsqueeze(2).to_broadcast([128, GP, NH, 4]), op=OP.mult),
                 reads=["Es", "maskTs"], writes=["Ps%d" % gs])

        def sd_pv(gi):
            b, pgrp = groups[gi]
            gs = gi % 2
            pp_ = Ps_[gs]
            for u in range(GP):
                first = (pgrp == 0 and u == 0)
                for kvh in range(NKV):
                    rhs = pp_[:, u * 32 + kvh * 8:u * 32 + kvh * 8 + 8]
                    k.op("pe", lambda e, gs=gs, u=u, kvh=kvh, rhs=rhs, first=first: e.matmul(self.PS[kvh][:, 0:8], lhsT=vpb[gs][:, u, kvh * 128:(kvh + 1) * 128], rhs=rhs, start=first, stop=False),
                         reads=["vpb%d" % gs, "Ps%d" % gs], writes=["ps%d" % kvh], inc=True)
                k.op("pe", lambda e, u=u, pp_=pp_, first=first: e.matmul(self.PS[4][:, 0:32], lhsT=self.ones_b, rhs=pp_[:, u * 32:(u + 1) * 32], start=first, stop=False),
                     reads=["ones_b", "Ps%d" % gs], writes=["ps4"], inc=True)

        def sd_finish(b):
            scn, scnr = self.PS[7], "ps7"
            for kvh in range(NKV):
                rhs = qTs[:, b, 2 * kvh:2 * kvh + 2, :].rearrange("p h t -> p (h t)")
                k.op("pe", lambda e, kvh=kvh, rhs=rhs: e.matmul(scn[0:16, kvh * 8:(kvh + 1) * 8], lhsT=kTn[:, kvh, :], rhs=rhs, start=True, stop=True), reads=["kTn", "qTs"], writes=[scnr])
            k.op("act", lambda e: e.activation(out=En[R16, :], in_=scn[0:16, 0:32], func=AF.Exp, scale=scale, bias=negB[R16, 0:1]), reads=[scnr, "negB"], writes=["En"])
            k.op("dve", lambda e, b=b: e.tensor_tensor(out=Pn[R16, :].rearrange("p (h t) -> p h t", h=NH), in0=En[R16, :].rearrange("p (h t) -> p h t", h=NH),
                                                      in1=MfT[R16, 4 * b:4 * b + 4].unsqueeze(1).to_broadcast([16, NH, 4]), op=OP.mult), reads=["En", "MfT"], writes=["Pn"])
            for kvh in range(NKV):
                k.op("pe", lambda e, kvh=kvh: e.matmul(self.PS[kvh][:, 0:8], lhsT=vbn[0:16, kvh * 128:(kvh + 1) * 128], rhs=Pn[R16, kvh * 8:(kvh + 1) * 8], start=False, stop=True),
                     reads=["vbn", "Pn"], writes=["ps%d" % kvh], inc=True)
            k.op("pe", lambda e: e.matmul(self.PS[4][:, 0:32], lhsT=self.ones_b[0:16, :], rhs=Pn[R16, :], start=False, stop=True), reads=["ones_b", "Pn"], writes=["ps4"], inc=True)
            k.op("dve", lambda e: e.reciprocal(out=rdn, in_=self.PS[4][:, 0:32]), reads=["ps4"], writes=["rdn"])
            for kvh in range(NKV):
                k.op("dve", lambda e, kvh=kvh, b=b: e.tensor_tensor(out=attnTs[:, 2 * kvh:2 * kvh + 2, 4 * b:4 * b + 4], in0=self.PS[kvh][:, 0:8].rearrange("p (h t) -> p h t", h=2),
                                                                  in1=rdn[:, kvh * 8:(kvh + 1) * 8].rearrange("p (h t) -> p h t", h=2), op=OP.mult), reads=["ps%d" % kvh, "rdn"], writes=["attnTs"])

        sd_load(0)
        for gi in range(len(groups)):
            sd_score(gi)
            if gi + 1 < len(groups):
                sd_load(gi + 1)
            sd_pv(gi)
            if groups[gi][1] == NPG // GP - 1:
                sd_finish(groups[gi][0])
        self.dbg("attnTs", attnTs, ["attnTs"])
        for blk in range(4):
            s = blk % 2
            self.wload(wout[s], Wo[:, :, blk * 256:(blk + 1) * 256], "wout%d" % s, 2048)
            for r in range(2):
                oc = blk * 2 + r
                q = oc % 2
                k.dma("sp", None, hp[q][:, 0:16], hs3[:, oc, T:T + 16], reads=["hscr_s%d" % oc], writes=["hp%d" % q])
                ps, psr = self.PS[5], "ps5"
                for hh in range(NH):
                    lhs = wout[s][:, hh, r * 128:(r + 1) * 128]
                    rhs = attnTs[:, hh, :]
                    k.op("pe", lambda e, ps=ps, lhs=lhs, rhs=rhs, hh=hh: e.matmul(ps[:, 0:16], lhsT=lhs, rhs=rhs, start=(hh == 0), stop=(hh == NH - 1)),
                         reads=["wout%d" % s, "attnTs"], writes=[psr], inc=(hh == NH - 1))
                k.op("dve", lambda e, q=q, ps=ps: e.tensor_tensor(out=hp[q][:, 0:16], in0=hp[q][:, 0:16], in1=ps[:, 0:16], op=OP.add), reads=[psr, "hp%d" % q], writes=["hp%d" % q])
                k.dma("sp", None, hs3[:, oc, T:T + 16], hp[q][:, 0:16], reads=["hp%d" % q], writes=["hscr_s%d" % oc])


def _fm(v, nch):
    return np.ascontiguousarray(np.asarray(v, np.float32).reshape(nch, 128).T)


def build_vec(cfg, inp):
    off, NV = vec_layout(cfg)
    vec = np.zeros((128, NV), np.float32)
    for i in range(cfg.DEPTH):
        vec[:, off["nmix%d" % i]:off["nmix%d" % i] + 8] = _fm(inp["norm_mix"][i], 8)
        vec[:, off["nffn%d" % i]:off["nffn%d" % i] + 8] = _fm(inp["norm_ffn"][i], 8)
        vec[:, off["nple%d" % i]:off["nple%d" % i] + 8] = _fm(inp["norm_ple"][i], 8)
        for kk in range(3):
            o = off["cfw%d" % i] + kk * NFF
            vec[:, o:o + NFF] = _fm(inp["conv_ff_w"][i][kk], NFF)
        vec[:, off["cfb%d" % i]:off["cfb%d" % i] + NFF] = _fm(inp["conv_ff_b"][i], NFF)
    for j in range(cfg.NR):
        for kk in range(4):
            o = off["crw%d" % j] + kk * 4
            vec[:, o:o + 4] = _fm(inp["conv_rec_w"][j][kk], 4)
        for nm, key in (("crb", "conv_rec_b"), ("brg", "b_rgate"), ("big", "b_igate"), ("lam", "lru_lambda"), ("psc", "pool_scale")):
            o = off["%s%d" % (nm, j)]
            vec[:, o:o + 4] = _fm(inp[key][j], 4)
    o = off["poolrc"]
    for g in range(4):
        w = 2 << g
        for t in range(15):
            vec[:, o + g * 15 + t] = 1.0 / min(w, t + 1)
    return vec


def block_diag(w):
    w = np.asarray(w, np.float32)
    NR = w.shape[0]
    out = np.zeros((NR, 4, 128, 128), np.float32)
    for c in range(4):
        for a in range(2):
            out[:, c, 64 * a:64 * a + 64, 64 * a:64 * a + 64] = w[:, 2 * c + a]
    return out


def rope_tables(cfg):
    NT = cfg.NT
    pos = np.zeros((NT + 1, 128), np.float32)
    for t in range(NT):
        pos[t] = np.arange(t * 128, (t + 1) * 128, dtype=np.float32)
    for r in range(16):
        pos[NT, r] = cfg.PAST + (r % 4)
    outs = []
    for half in (64, 32):
        inv = np.power(np.float32(10000.0), -np.arange(half, dtype=np.float32) / np.float32(half)).astype(np.float32)
        ang = (pos[:, :, None] * inv[None, None, :]).astype(np.float32)
        tab = np.stack([np.cos(ang), np.sin(ang)], axis=2).astype(np.float32)
        outs.append(np.ascontiguousarray(tab))
    return outs


def make_in_maps(cfg, inp):
    A = lambda x: np.ascontiguousarray(np.asarray(x))
    vec = build_vec(cfg, inp)
    shared = {
        "vec": vec,
        "w_in_rec": A(inp["w_in_rec"]), "w_rg": block_diag(inp["w_rgate"]), "w_ig": block_diag(inp["w_igate"]),
        "w_pool": A(inp["w_pool"]), "w_out_rec": A(inp["w_out_rec"]),
        "w_up": A(inp["w_up"]), "w_down": A(inp["w_down"]), "w_ple": A(inp["w_ple"]), "w_pg": A(inp["w_ple_gate"]),
    }
    NA, NR, DP = cfg.NA, cfg.NR, cfg.DEPTH
    if NA:
        rp, rpi = rope_tables(cfg)
        shared.update({
            "cache_kv": np.concatenate([np.asarray(inp["cache_k"]).reshape(NA, cfg.NPOOL * 128, 512),
                                        np.asarray(inp["cache_v"]).reshape(NA, cfg.NPOOL * 128, 512)], axis=-1),
            "cache_ki": A(inp["cache_kidx"]).reshape(NA, cfg.NPOOL * 128, IDX_D),
            "w_in_attn": A(inp["w_in_attn"]), "w_out_attn": A(inp["w_out_attn"]),
            "qkn": np.ascontiguousarray(np.stack([np.asarray(inp["q_norm"]), np.asarray(inp["k_norm"])], axis=1).astype(np.float32)),
            "rope": rp, "ropei": rpi,
        })
    maps = []
    for b in range(cfg.NB):
        sb = slice(4 * b, 4 * b + 4)
        m = dict(shared)
        m["xp"] = A(inp["x_prompt"][b])
        m["xs"] = A(inp["x_sample"][sb]).reshape(16, D)
        m["pp"] = A(inp["p_prompt"][:, b])
        m["psm"] = A(inp["p_sample"][:, sb]).reshape(DP, 16, D_PLE)
        m["st_rc"] = A(inp["state_rec_conv"][:, sb]).reshape(NR, 12, D_REC)
        m["st_rh"] = A(inp["state_rec_h"][:, sb]).reshape(NR, 4, D_REC)
        m["st_pool"] = A(inp["state_pool"][:, sb]).reshape(NR, 60, D_POOL)
        m["st_ffn"] = A(inp["state_ffn_conv"][:, sb]).reshape(DP, 8, D_FF)
        if NA:
            m["ptab"] = A(inp["page_table"][sb]).astype(np.int32).reshape(1, -1)
        maps.append(m)
    return maps


def gather_outputs(cfg, res):
    NB, T, NR, NA, DP = cfg.NB, cfg.T, cfg.NR, cfg.NA, cfg.DEPTH
    R = lambda name: [np.asarray(r[name]) for r in res]
    y_p = np.stack(R("y_p"))
    y_s = np.concatenate([a.reshape(4, 4, D) for a in R("y_s")])
    rc_p = np.stack(R("rc_p"), axis=1)
    rc_s = np.concatenate([a.reshape(NR, 4, 3, D_REC) for a in R("rc_s")], axis=1)
    rh_p = np.stack([a[:, 0] for a in R("rh_p")], axis=1)
    rh_s = np.concatenate(R("rh_s"), axis=1)
    pl_p = np.stack(R("pl_p"), axis=1)
    pl_s = np.concatenate([a.reshape(NR, 4, 15, D_POOL) for a in R("pl_s")], axis=1)
    if NA:
        k_p = np.stack([a.reshape(NA, T, NKV, HD) for a in R("k_p")], axis=1)
        k_s = np.concatenate([a.reshape(NA, 4, 4, NKV, HD) for a in R("k_s")], axis=1)
        v_p = np.stack([a.reshape(NA, T, NKV, HD) for a in R("v_p")], axis=1)
        v_s = np.concatenate([a.reshape(NA, 4, 4, NKV, HD) for a in R("v_s")], axis=1)
        ki_p = np.stack(R("ki_p"), axis=1)
        ki_s = np.concatenate([a.reshape(NA, 4, 4, IDX_D) for a in R("ki_s")], axis=1)
    else:
        z = np.zeros((0,), np.float32)
        k_p = k_s = v_p = v_s = ki_p = ki_s = z
    fc_p = np.stack(R("fc_p"), axis=1)
    fc_s = np.concatenate([a.reshape(DP, 4, 2, D_FF) for a in R("fc_s")], axis=1)
    return (y_p, y_s, rc_p, rc_s, rh_p, rh_s, pl_p, pl_s, k_p, k_s, v_p, v_s, ki_p, ki_s, fc_p, fc_s)


def run_cfg(cfg, inputs, trace=False):
    from contextlib import ExitStack
    mk = MK(cfg)
    with ExitStack() as st:
        mk.build(st)
    maps = make_in_maps(cfg, inputs)
    res = run_bass_kernel_spmd(mk.nc, maps, core_ids=list(range(cfg.NB)), **({"trace": True} if trace else {}))
    return gather_outputs(cfg, res.results), res


def kernel(**inputs):
    cfg = Cfg()
    outs, _ = run_cfg(cfg, inputs)
    return tuple(np.ascontiguousarray(o, dtype=np.float32) for o in outs)
```

```python
import numpy as np
import concourse.bass as bass
import concourse.mybir as mybir
from concourse.bass_utils import run_bass_kernel_spmd

F32 = mybir.dt.float32
BF16 = mybir.dt.bfloat16
I32 = mybir.dt.int32
U32 = mybir.dt.uint32
AF = mybir.ActivationFunctionType
OP = mybir.AluOpType
AX = mybir.AxisListType

D = 1024
NCH = 8
D_PLE = 256
D_REC = 512
D_POOL = 512
D_FF = 2816
NFF = 22
ATTN_IN = 2632
HD = 128
NH = 8
NKV = 4
IDX_H = 8
IDX_D = 64
EPS = 1e-6
NEG = -1.0e30


class Cfg:
    def __init__(self, T=2048, DEPTH=4, NPG=64, NPOOL=2560, NB=8, NSB=32):
        self.T = T
        self.DEPTH = DEPTH
        self.NPG = NPG
        self.NPOOL = NPOOL
        self.NB = NB
        self.NSB = NSB
        self.NS = 16
        self.TT = T + 16
        self.NR = (DEPTH + 1) // 2
        self.NA = DEPTH // 2
        self.PAST = NPG * 128
        self.NT = T // 128
        self.TOPK = min(256, T // 4)
        self.TOPK_S = min(256, (self.PAST + 4) // 4)


class Sched:
    ENG = ("pe", "act", "dve", "pool", "sp")

    def __init__(self, nc, stack):
        self.nc = nc
        self.stack = stack
        self.prog = {e: [] for e in self.ENG}
        self.sem = {e: stack.enter_context(nc.semaphore("s_" + e)) for e in self.ENG}
        self.cnt = {e: 0 for e in self.ENG}
        self.pending = {e: False for e in self.ENG}
        self.waited = {e: {} for e in self.ENG}
        self.last_w = {}
        self.readers = {}
        self.dsems = []
        self.ndma = 0

    def dma_sem(self, name):
        s = self.stack.enter_context(self.nc.semaphore("d_" + name))
        d = {"sem": s, "cnt": 0, "key": "d_" + name}
        self.dsems.append(d)
        return d

    def _need(self, eng, reads, writes, pe_accum):
        need = {}

        def add(tok, same_ok):
            key, sem, val = tok
            if key == eng and same_ok:
                return
            if need.get(key, (None, 0))[1] < val:
                need[key] = (sem, val)

        for r in reads:
            t = self.last_w.get(r)
            if t is not None:
                add(t, False)
        same_ok = (eng == "pe")
        for w in writes:
            t = self.last_w.get(w)
            if t is not None:
                add(t, same_ok)
            for t in self.readers.get(w, {}).values():
                add(t, same_ok)
        out = []
        wd = self.waited[eng]
        for key, (sem, val) in need.items():
            if wd.get(key, 0) >= val:
                continue
            wd[key] = val
            out.append((sem, val))
        return out

    def op(self, eng, fn, reads=(), writes=(), inc=True):
        waits = self._need(eng, reads, writes, False)
        for (sem, val) in waits:
            if sem is self.sem[eng] and val > self.cnt[eng]:
                raise RuntimeError("self-wait on pending increment (%s)" % eng)
        if inc:
            self.cnt[eng] += 1
            val = self.cnt[eng]
            self.pending[eng] = False
        else:
            val = self.cnt[eng] + 1
            self.pending[eng] = True
        tok = (eng, self.sem[eng], val)
        self.prog[eng].append((waits, fn, (self.sem[eng], 1) if inc else None))
        for r in reads:
            self.readers.setdefault(r, {})[eng] = tok
        for w in writes:
            self.last_w[w] = tok
            self.readers[w] = {}
        return tok

    NPOOL_SEM = 24

    def dma(self, eng, dsem, out_ap, in_ap, reads=(), writes=(), fn=None, **kw):
        if not hasattr(self, "_dpool"):
            self._dpool = {}
        if eng not in self._dpool:
            self._dpool[eng] = [self.dma_sem("p%s%d" % (eng, q)) for q in range(self.NPOOL_SEM)]
            self._drr = getattr(self, "_drr", {})
            self._drr[eng] = 0
        dsem = self._dpool[eng][self._drr[eng]]
        self._drr[eng] = (self._drr[eng] + 1) % self.NPOOL_SEM
        waits = self._need(eng, reads, writes, False)
        if dsem["cnt"] > self.waited[eng].get(dsem["key"], 0):
            waits.append((dsem["sem"], dsem["cnt"]))
            self.waited[eng][dsem["key"]] = dsem["cnt"]
        dsem["cnt"] += 16
        tok = (dsem["key"], dsem["sem"], dsem["cnt"])

        if fn is None:
            def fn(e, out_ap=out_ap, in_ap=in_ap, kw=kw):
                return e.dma_start(out=out_ap, in_=in_ap, **kw)

        self.prog[eng].append((waits, fn, (dsem["sem"], 16)))
        self.ndma += 1
        for r in reads:
            self.readers.setdefault(r, {})[dsem["key"]] = tok
        for w in writes:
            self.last_w[w] = tok
            self.readers[w] = {}
        return tok

    def raw(self, eng, fn, reads=(), writes=()):
        waits = self._need(eng, reads, writes, False)
        self.prog[eng].append((waits, fn, None))

    def finish(self):
        for e in self.ENG:
            if self.pending[e]:
                raise RuntimeError("pending ops without inc on " + e)
        waits = []
        for d in self.dsems:
            if d["cnt"]:
                waits.append((d["sem"], d["cnt"]))
        for e in self.ENG:
            if e != "sp" and self.cnt[e]:
                waits.append((self.sem[e], self.cnt[e]))
        self.prog["sp"].append((waits, None, None))

    def emit(self):
        nc = self.nc
        prog = self.prog

        def replay(name, e):
            for waits, fn, inc in prog[name]:
                for (sem, val) in waits:
                    e.wait_ge(sem, val)
                if fn is None:
                    continue
                ins = fn(e)
                if inc is not None:
                    ins.then_inc(inc[0], inc[1])

        with nc.Block() as block:
            @block.tensor
            def _(e):
                replay("pe", e)

            @block.scalar
            def _(e):
                replay("act", e)

            @block.vector
            def _(e):
                replay("dve", e)

            @block.gpsimd
            def _(e):
                replay("pool", e)

            @block.sync
            def _(e):
                replay("sp", e)


class Arena:
    def __init__(self, ap, size):
        self.ap = ap
        self.size = size
        self.top = 0

    def f32(self, n):
        a = self.top
        self.top += (n + 7) // 8 * 8
        assert self.top <= self.size, "arena overflow %d > %d" % (self.top, self.size)
        return self.ap[:, a:a + n]

    def bf16(self, n):
        w = (n + 1) // 2
        a = self.top
        self.top += (w + 7) // 8 * 8
        assert self.top <= self.size, "arena overflow %d > %d" % (self.top, self.size)
        return self.ap[:, a:a + w].bitcast(BF16)[:, 0:n]

    def i32(self, n):
        return self.f32(n).bitcast(I32)

    def mark(self):
        return self.top

    def reset(self, m):
        self.top = m


def vec_layout(cfg):
    off = {}
    n = 0

    def add(name, w):
        nonlocal n
        off[name] = n
        n += w

    for i in range(cfg.DEPTH):
        add("nmix%d" % i, 8)
        add("nffn%d" % i, 8)
        add("nple%d" % i, 8)
        add("cfw%d" % i, 3 * NFF)
        add("cfb%d" % i, NFF)
    for j in range(cfg.NR):
        add("crw%d" % j, 16)
        add("crb%d" % j, 4)
        add("brg%d" % j, 4)
        add("big%d" % j, 4)
        add("lam%d" % j, 4)
        add("psc%d" % j, 4)
    add("poolrc", 4 * 15)
    return off, n


class MK:
    def __init__(self, cfg):
        self.cfg = cfg
        self.nc = bass.Bass("TRN2", target_bir_lowering=False)
        self.voff, self.NV = vec_layout(cfg)

    def declare(self):
        nc, c = self.nc, self.cfg
        T, DP, NR, NA = c.T, c.DEPTH, c.NR, c.NA
        I = {}
        O = {}

        def inp(name, shape, dt=F32):
            I[name] = nc.dram_tensor(name, list(shape), dt, kind="ExternalInput").ap()

        def outp(name, shape):
            O[name] = nc.dram_tensor(name, list(shape), F32, kind="ExternalOutput").ap()

        inp("xp", [T, D]); inp("xs", [16, D])
        inp("pp", [DP, T, D_PLE]); inp("psm", [DP, 16, D_PLE])
        inp("st_rc", [NR, 12, D_REC]); inp("st_rh", [NR, 4, D_REC]); inp("st_pool", [NR, 60, D_POOL])
        inp("st_ffn", [DP, 8, D_FF])
        inp("vec", [128, self.NV])
        inp("w_in_rec", [NR, D, 1536]); inp("w_rg", [NR, 4, 128, 128]); inp("w_ig", [NR, 4, 128, 128])
        inp("w_pool", [NR, 4, 128, 128]); inp("w_out_rec", [NR, D, D])
        inp("w_up", [DP, D, 2 * D_FF]); inp("w_down", [DP, D_FF, D])
        inp("w_ple", [DP, D_PLE, D]); inp("w_pg", [DP, D, D])
        if NA:
            inp("ptab", [1, 4 * c.NPG], I32)
            inp("cache_kv", [NA, c.NPOOL * 128, 1024])
            inp("cache_ki", [NA, c.NPOOL * 128, IDX_D])
            inp("w_in_attn", [NA, D, ATTN_IN]); inp("w_out_attn", [NA, D, D])
            inp("qkn", [NA, 2, HD])
            inp("rope", [c.NT + 1, 128, 2, 64]); inp("ropei", [c.NT + 1, 128, 2, 32])
        outp("y_p", [T, D]); outp("y_s", [16, D])
        outp("rc_p", [NR, 3, D_REC]); outp("rc_s", [NR, 12, D_REC])
        outp("rh_p", [NR, 1, D_REC]); outp("rh_s", [NR, 4, D_REC])
        outp("pl_p", [NR, 15, D_POOL]); outp("pl_s", [NR, 60, D_POOL])
        if NA:
            outp("k_p", [NA, T, 512]); outp("k_s", [NA, 16, 512])
            outp("v_p", [NA, T, 512]); outp("v_s", [NA, 16, 512])
            outp("ki_p", [NA, T, IDX_D]); outp("ki_s", [NA, 16, IDX_D])
            self.hscr = nc.dram_tensor("hscr", [128, NCH * c.TT], F32).ap()
        outp("fc_p", [DP, 2, D_FF]); outp("fc_s", [DP, 8, D_FF])
        self.I, self.O = I, O

    def coltiles(self):
        T = self.cfg.T
        return [(i * 512, 512) for i in range(T // 512)] + [(T, 16)]

    def ps_next(self):
        b = self.ps_rr
        self.ps_rr = (self.ps_rr + 1) % 6
        return self.PS[b], "ps%d" % b

    def pss_next(self):
        s = self.pss_rr
        self.pss_rr = (self.pss_rr + 1) % 32
        return self.PS[7][:, s * 16:(s + 1) * 16], "ps7"

    def wload(self, dst, src, region, nwords):
        k = self.k
        s = self.stage_rr
        self.stage_rr = (s + 1) % len(self.stage)
        st = self.stage[s][:, 0:nwords]
        shp = list(src.shape)
        if len(shp) == 3:
            stv = st.rearrange("p (a b) -> p a b", a=shp[1])
        else:
            stv = st
        k.dma("sp", self.dstage[s], stv, src, writes=["stage%d" % s])
        k.op("pool", lambda e, dst=dst, stv=stv: e.tensor_copy(out=dst, in_=stv), reads=["stage%d" % s], writes=[region])

    def vcol(self, name, i=0, n=1):
        o = self.voff[name] + i
        return self.vec[:, o:o + n]

    def norm(self, gname):
        k, c = self.k, self.cfg
        h, xn = self.h, self.xn
        m = self.ph.mark()
        rstd = self.ph.f32(c.TT)
        sqb = [self.ph.bf16(512), self.ph.bf16(512)]
        for ti, (c0, n) in enumerate(self.coltiles()):
            ps, psr = self.PS[6], "ps6"
            for ch in range(NCH):
                sb = sqb[ch % 2]
                src = h[:, ch, c0:c0 + n]
                k.op("act", lambda e, sb=sb, src=src, n=n: e.activation(out=sb[:, 0:n], in_=src, func=AF.Square),
                     reads=["h%d_%d" % (ch, ti)], writes=["sqb%d" % (ch % 2)])
                k.op("pe", lambda e, ps=ps, sb=sb, n=n, ch=ch: e.matmul(ps[:, 0:n], lhsT=self.ones_b, rhs=sb[:, 0:n], start=(ch == 0), stop=(ch == NCH - 1)),
                     reads=["sqb%d" % (ch % 2), "ones_b"], writes=[psr], inc=True)
            rs = rstd[:, c0:c0 + n]
            k.op("act", lambda e, rs=rs, ps=ps, n=n: e.activation(out=rs, in_=ps[:, 0:n], func=AF.Sqrt, bias=self.eps_t[:, 0:1], scale=1.0 / D),
                 reads=[psr, "eps"], writes=["rstd%d" % ti])
            k.op("dve", lambda e, rs=rs: e.reciprocal(out=rs, in_=rs), reads=["rstd%d" % ti], writes=["rstd%d" % ti])
            for ch in range(NCH):
                g = self.vcol(gname, ch)
                k.op("dve", lambda e, ch=ch, c0=c0, n=n, g=g, rs=rs: e.scalar_tensor_tensor(
                    out=xn[:, ch, c0:c0 + n], in0=h[:, ch, c0:c0 + n], scalar=g, in1=rs, op0=OP.mult, op1=OP.mult),
                    reads=["h%d_%d" % (ch, ti), "rstd%d" % ti, "vec"], writes=["xn%d_%d" % (ch, ti)])
        self.barrier()
        self.ph.reset(m)

    def proj_chunk(self, lhs_of_kc, KC, rhs_of, rhs_reg_of, wregs, evac):
        k = self.k
        for ti, (c0, n) in enumerate(self.coltiles()):
            if n == 16:
                ps, psr = self.pss_next()
            else:
                ps, psr = self.ps_next()
            for kc in range(KC):
                lhs = lhs_of_kc(kc)
                rhs = rhs_of(kc, c0, n)
                k.op("pe", lambda e, ps=ps, lhs=lhs, rhs=rhs, n=n, kc=kc: e.matmul(ps[:, 0:n], lhsT=lhs, rhs=rhs, start=(kc == 0), stop=(kc == KC - 1)),
                     reads=list(wregs) + [rhs_reg_of(kc, ti)], writes=[psr], inc=(kc == KC - 1))
            evac(ti, c0, n, ps[:, 0:n], psr)

    def tr32(self, ps_ap, in_ap, reads, psr, npart):
        self.k.op("pe", lambda e: e.transpose(ps_ap, in_ap, self.ident_f[0:npart, 0:npart]),
                  reads=list(reads) + ["ident_f"], writes=[psr])

    def dbg(self, name, ap, reads):
        if not getattr(self.cfg, "debug", False):
            return
        t = self.nc.dram_tensor("dbg_" + name, list(ap.shape), ap.dtype, kind="ExternalOutput").ap()
        self.O["dbg_" + name] = t
        self.k.dma("sp", None, t, ap, reads=reads)

    def barrier(self):
        k = self.k
        for e in k.ENG:
            assert not k.pending[e]
        for e in k.ENG:
            waits = []
            for o in k.ENG:
                if k.cnt[o] > k.waited[e].get(o, 0):
                    waits.append((k.sem[o], k.cnt[o]))
                    k.waited[e][o] = k.cnt[o]
            for d in k.dsems:
                if d["cnt"] > k.waited[e].get(d["key"], 0):
                    waits.append((d["sem"], d["cnt"]))
                    k.waited[e][d["key"]] = d["cnt"]
            if waits:
                k.prog[e].append((waits, None, None))
        k.last_w.clear()
        k.readers.clear()

    def build(self, stack):
        nc, c = self.nc, self.cfg
        self.declare()
        k = self.k = Sched(nc, stack)
        AW = 52900
        A = nc.alloc_sbuf_tensor("arena", [128, AW], F32).ap()
        self.ph = Arena(A, AW)
        ph = self.ph
        self.PS = [nc.alloc_psum_tensor("psb%d" % b, [128, 512], F32).ap() for b in range(8)]
        self.ps_rr = 0
        self.pss_rr = 0
        TT = c.TT
        self.h = ph.f32(NCH * TT).rearrange("p (c t) -> p c t", c=NCH)
        self.xn = ph.bf16(NCH * TT).rearrange("p (c t) -> p c t", c=NCH)
        self.vec = ph.f32(self.NV)
        self.ident_f = ph.f32(128)
        self.ident_b = ph.bf16(128)
        self.ones_b = ph.bf16(128)
        self.eps_t = ph.f32(8)
        self.one_t = ph.f32(8)
        self.stage = [ph.f32(2048), ph.f32(2048)]
        self.stage_rr = 0
        self.dstage = [k.dma_sem("stg0"), k.dma_sem("stg1")]
        self.d_in = k.dma_sem("in")
        self.d_out = k.dma_sem("out")
        self.d_out2 = k.dma_sem("out2")
        self.dpin = [k.dma_sem("pin0"), k.dma_sem("pin1")]
        self.drp = [k.dma_sem("rp0"), k.dma_sem("rp1")]
        self.dhp = [k.dma_sem("hp0"), k.dma_sem("hp1")]
        self.hflat = self.ph.ap[:, 0:NCH * TT]
        self.base_mark = ph.mark()

        k.dma("sp", self.d_in, self.vec, self.I["vec"], writes=["vec"])
        k.op("pool", lambda e: e.memset(self.ident_f, 0.0), writes=["ident_f"])
        k.op("pool", lambda e: e.affine_select(out=self.ident_f, in_=self.ident_f, pattern=[[-1, 128]], compare_op=OP.not_equal,
                                               fill=1.0, base=0, channel_multiplier=1), reads=["ident_f"], writes=["ident_f"])
        k.op("pool", lambda e: e.tensor_copy(out=self.ident_b, in_=self.ident_f), reads=["ident_f"], writes=["ident_b"])
        k.op("pool", lambda e: e.memset(self.ones_b, 1.0), writes=["ones_b"])
        k.op("pool", lambda e: e.memset(self.eps_t, EPS), writes=["eps"])
        k.op("pool", lambda e: e.memset(self.one_t, 1.0), writes=["one_t"])

        self.load_x()
        stop = getattr(c, "stop", None)
        for i in range(c.DEPTH):
            if i % 2 == 0:
                self.rec_layer(i)
            else:
                self.attn_layer(i)
            if stop == "mix%d" % i:
                break
            self.ffn_layer(i)
            if stop == "ffn%d" % i:
                break
            self.ple_layer(i)
        self.store_y()
        k.finish()
        k.emit()

    def tok_tiles(self):
        c = self.cfg
        return [(t * 128, 128) for t in range(c.NT)] + [(c.T, 16)]

    def load_x(self):
        k, c, ph = self.k, self.cfg, self.ph
        m = ph.mark()
        xin = [ph.f32(1024), ph.f32(1024)]
        dx = [k.dma_sem("xin0"), k.dma_sem("xin1")]
        for tt, (t0, n) in enumerate(self.tok_tiles()):
            s = tt % 2
            src = self.I["xp"][t0:t0 + n, :] if n == 128 else self.I["xs"]
            k.dma("sp", dx[s], xin[s][0:n, :], src, writes=["xin%d" % s])
            ti = min(t0 // 512, len(self.coltiles()) - 1)
            for half in range(2):
                ps, psr = self.ps_next()
                for j in range(4):
                    ch = half * 4 + j
                    self.tr32(ps[:, j * 128:j * 128 + n], xin[s][0:n, ch * 128:(ch + 1) * 128], ["xin%d" % s], psr, n)
                dst = self.h[:, half * 4:half * 4 + 4, t0:t0 + n]
                srcp = ps.rearrange("p (a b) -> p a b", a=4)[:, :, 0:n]
                regs = ["h%d_%d" % (half * 4 + j, ti) for j in range(4)]
                if (tt + half) % 2 == 0:
                    k.op("act", lambda e, dst=dst, srcp=srcp: e.activation(out=dst, in_=srcp, func=AF.Copy), reads=[psr], writes=regs)
                else:
                    k.op("dve", lambda e, dst=dst, srcp=srcp: e.tensor_copy(out=dst, in_=srcp), reads=[psr], writes=regs)
        self.barrier()
        ph.reset(m)

    def store_y(self):
        k, c, ph = self.k, self.cfg, self.ph
        m = ph.mark()
        ost = [ph.f32(1024), ph.f32(1024)]
        for tt, (t0, n) in enumerate(self.tok_tiles()):
            s = tt % 2
            ti = min(t0 // 512, len(self.coltiles()) - 1)
            for half in range(2):
                ps, psr = self.ps_next()
                for j in range(4):
                    ch = half * 4 + j
                    self.tr32(ps[0:n, j * 128:(j + 1) * 128], self.h[:, ch, t0:t0 + n], ["h%d_%d" % (ch, ti)], psr, 128)
                dst = ost[s][0:n, half * 512:(half + 1) * 512]
                if (tt + half) % 2 == 0:
                    k.op("act", lambda e, dst=dst, ps=ps, n=n: e.activation(out=dst, in_=ps[0:n, :], func=AF.Copy), reads=[psr], writes=["ost%d" % s])
                else:
                    k.op("dve", lambda e, dst=dst, ps=ps, n=n: e.tensor_copy(out=dst, in_=ps[0:n, :]), reads=[psr], writes=["ost%d" % s])
            dstd = self.O["y_p"][t0:t0 + n, :] if n == 128 else self.O["y_s"]
            k.dma("sp", self.d_out if s == 0 else self.d_out2, dstd, ost[s][0:n, :], reads=["ost%d" % s])
        self.barrier()
        ph.reset(m)

    def store_fm(self, src3, nch, r, dst, reads, tag):
        k, ph = self.k, self.ph
        ot = ph.f32(nch * 128)
        for g0 in range(0, nch, 4):
            g1 = min(nch, g0 + 4)
            ps, psr = self.ps_next()
            for ch in range(g0, g1):
                self.tr32(ps[0:r, (ch - g0) * 128:(ch - g0 + 1) * 128], src3[:, ch, :], reads, psr, 128)
            w = (g1 - g0) * 128
            k.op("act", lambda e, g0=g0, w=w, ps=ps: e.activation(out=ot[0:r, g0 * 128:g0 * 128 + w], in_=ps[0:r, 0:w], func=AF.Copy),
                 reads=[psr], writes=[tag + "_ot"])
        k.dma("sp", self.d_out, dst, ot[0:r, :], reads=[tag + "_ot"])

    def load_fm(self, dst3, nch, r, src, tag):
        k, ph = self.k, self.ph
        it = ph.f32(nch * 128)
        k.dma("sp", self.d_in, it[0:r, :], src, writes=[tag + "_it"])
        for g0 in range(0, nch, 4):
            g1 = min(nch, g0 + 4)
            ps, psr = self.ps_next()
            for ch in range(g0, g1):
                self.tr32(ps[:, (ch - g0) * r:(ch - g0 + 1) * r], it[0:r, ch * 128:(ch + 1) * 128], [tag + "_it"], psr, r)
            w = (g1 - g0)
            k.op("act", lambda e, g0=g0, g1=g1, w=w, ps=ps: e.activation(out=dst3[:, g0:g1, :], in_=ps[:, 0:w * r].rearrange("p (a b) -> p a b", a=w), func=AF.Copy),
                 reads=[psr], writes=[tag])

    def ffn_layer(self, i):
        k, c, ph = self.k, self.cfg, self.ph
        T, TT = c.T, c.TT
        h, xn = self.h, self.xn
        self.norm("nffn%d" % i)
        m = ph.mark()
        fcs = ph.f32(NFF * 8).rearrange("p (a b) -> p a b", a=NFF)
        fct = ph.f32(NFF * 10).rearrange("p (a b) -> p a b", a=NFF)
        mm = ph.mark()
        self.load_fm(fcs, NFF, 8, self.I["st_ffn"][i], "fcs")
        self.barrier()
        ph.reset(mm)
        G = 4
        wup = [ph.bf16(2048).rearrange("p (a b) -> p a b", a=8) for _ in range(4)]
        wdn = ph.bf16(G * 1024).rearrange("p (a b) -> p a b", a=G)
        act = ph.bf16(G * TT).rearrange("p (a b) -> p a b", a=G)
        GE = 2 + T + 24
        gext = [ph.f32(GE), ph.f32(GE)]
        tmp = ph.f32(TT)
        gl = [ph.bf16(TT), ph.bf16(TT)]
        W = self.I["w_up"][i].rearrange("(kc p) n -> p kc n", p=128)
        Wd = self.I["w_down"][i]
        wrr = [0]

        def get_block(col0):
            sl = wrr[0]
            wrr[0] = (sl + 1) % 4
            self.wload(wup[sl], W[:, :, col0:col0 + 256], "wup%d" % sl, 2048)
            return sl

        groups = [list(range(g, min(g + G, NFF))) for g in range(0, NFF, G)]
        sg = su = 0
        for grp in groups:
            for jj, j in enumerate(grp):
                mb, r = divmod(j, 2)
                if r == 0:
                    sg = get_block(mb * 256)
                    su = get_block(D_FF + mb * 256)
                s = j % 2
                ge = gext[s]
                gs = ge[:, 2 + T:2 + T + 24].rearrange("p (a b) -> p a b", a=4)
                gr = "gext%d" % s
                k.op("pool", lambda e, ge=ge: e.memset(ge[:, 0:2], 0.0), writes=[gr])
                k.op("pool", lambda e, gs=gs, j=j: e.tensor_copy(out=gs[:, :, 0:2], in_=fcs[:, j, :].rearrange("p (a b) -> p a b", a=4)), writes=[gr])

                def evac_g(ti, c0, n, ps, psr, ge=ge, gs=gs, gr=gr):
                    if n == 16:
                        k.op("act", lambda e: e.activation(out=gs[:, :, 2:6], in_=ps.rearrange("p (a b) -> p a b", a=4), func=AF.Copy), reads=[psr], writes=[gr])
                    else:
                        k.op("act", lambda e: e.activation(out=ge[:, 2 + c0:2 + c0 + n], in_=ps, func=AF.Copy), reads=[psr], writes=[gr])

                self.proj_chunk(lambda kc, sg=sg, r=r: wup[sg][:, kc, r * 128:(r + 1) * 128], NCH,
                                lambda kc, c0, n: xn[:, kc, c0:c0 + n], lambda kc, ti: "xn%d_%d" % (kc, ti), ["wup%d" % sg], evac_g)
                w = [self.vcol("cfw%d" % i, kk * NFF + j) for kk in range(3)]
                b = self.vcol("cfb%d" % i, j)
                tp = tmp[:, 0:T]
                tsm = tmp[:, T:T + 16].rearrange("p (a b) -> p a b", a=4)
                k.op("dve", lambda e, ge=ge, w=w, b=b: e.tensor_scalar(out=tp, in0=ge[:, 0:T], scalar1=w[0], scalar2=b, op0=OP.mult, op1=OP.add),
                     reads=[gr, "vec"], writes=["tmp"])
                for kk in (1, 2):
                    k.op("dve", lambda e, ge=ge, w=w, kk=kk: e.scalar_tensor_tensor(out=tp, in0=ge[:, kk:kk + T], scalar=w[kk], in1=tp, op0=OP.mult, op1=OP.add),
                         reads=[gr, "tmp"], writes=["tmp"])
                k.op("dve", lambda e, gs=gs, w=w, b=b: e.tensor_scalar(out=tsm, in0=gs[:, :, 0:4], scalar1=w[0], scalar2=b, op0=OP.mult, op1=OP.add),
                     reads=[gr, "tmp"], writes=["tmp"])
                for kk in (1, 2):
                    k.op("dve", lambda e, gs=gs, w=w, kk=kk: e.scalar_tensor_tensor(out=tsm, in0=gs[:, :, kk:kk + 4], scalar=w[kk], in1=tsm, op0=OP.mult, op1=OP.add),
                         reads=[gr, "tmp"], writes=["tmp"])
                glr = "gl%d" % s
                k.op("act", lambda e, s=s: e.activation(out=gl[s], in_=tmp, func=AF.Gelu), reads=["tmp"], writes=[glr])
                k.op("pool", lambda e, ge=ge, j=j: e.tensor_copy(out=fct[:, j, 0:2], in_=ge[:, T:T + 2]), reads=[gr], writes=["fct"])
                k.op("pool", lambda e, gs=gs, j=j: e.tensor_copy(out=fct[:, j, 2:10].rearrange("p (a b) -> p a b", a=4), in_=gs[:, :, 4:6]), reads=[gr], writes=["fct"])

                def evac_u(ti, c0, n, ps, psr, s=s, jj=jj, glr=glr):
                    k.op("dve", lambda e: e.tensor_tensor(out=act[:, jj, c0:c0 + n], in0=gl[s][:, c0:c0 + n], in1=ps, op=OP.mult),
                         reads=[psr, glr], writes=["act%d_%d" % (jj, ti)])

                self.proj_chunk(lambda kc, su=su, r=r: wup[su][:, kc, r * 128:(r + 1) * 128], NCH,
                                lambda kc, c0, n: xn[:, kc, c0:c0 + n], lambda kc, ti: "xn%d_%d" % (kc, ti), ["wup%d" % su], evac_u)
            for jj, j in enumerate(grp):
                self.wload(wdn[:, jj, :], Wd[j * 128:(j + 1) * 128, :], "wdn%d" % jj, 1024)
            for oc in range(NCH):
                def evac_d(ti, c0, n, ps, psr, oc=oc):
                    hr = "h%d_%d" % (oc, ti)
                    k.op("dve", lambda e: e.tensor_tensor(out=h[:, oc, c0:c0 + n], in0=h[:, oc, c0:c0 + n], in1=ps, op=OP.add),
                         reads=[psr, hr], writes=[hr])

                self.proj_chunk(lambda kc, oc=oc: wdn[:, kc, oc * 128:(oc + 1) * 128], len(grp),
                                lambda kc, c0, n: act[:, kc, c0:c0 + n], lambda kc, ti: "act%d_%d" % (kc, ti),
                                ["wdn%d" % q for q in range(len(grp))], evac_d)
        self.barrier()
        ph.reset(mm)
        self.store_fm(fct[:, :, 0:2], NFF, 2, self.O["fc_p"][i], [], "fcp")
        self.store_fm(fct[:, :, 2:10], NFF, 8, self.O["fc_s"][i], [], "fcso")
        self.barrier()
        ph.reset(m)

    def ple_layer(self, i):
        k, c, ph = self.k, self.cfg, self.ph
        T, TT = c.T, c.TT
        h, xn = self.h, self.xn
        self.norm("nple%d" % i)
        m = ph.mark()
        pT = ph.bf16(2 * TT).rearrange("p (a b) -> p a b", a=2)
        pin = [ph.f32(256), ph.f32(256)]
        dpin = self.dpin
        nct = len(self.coltiles())
        for tt, (t0, n) in enumerate(self.tok_tiles()):
            s = tt % 2
            src = self.I["pp"][i, t0:t0 + n, :] if n == 128 else self.I["psm"][i]
            k.dma("sp", dpin[s], pin[s][0:n, :], src, writes=["pin%d" % s])
            ti = min(t0 // 512, nct - 1)
            ps, psr = self.ps_next()
            for kc in range(2):
                self.tr32(ps[:, kc * 128:kc * 128 + n], pin[s][0:n, kc * 128:(kc + 1) * 128], ["pin%d" % s], psr, n)
            dst = pT[:, :, t0:t0 + n]
            srcp = ps[:, 0:256].rearrange("p (a b) -> p a b", a=2)[:, :, 0:n]
            if tt % 2 == 0:
                k.op("act", lambda e, dst=dst, srcp=srcp: e.activation(out=dst, in_=srcp, func=AF.Copy), reads=[psr], writes=["pT%d" % ti])
            else:
                k.op("dve", lambda e, dst=dst, srcp=srcp: e.tensor_copy(out=dst, in_=srcp), reads=[psr], writes=["pT%d" % ti])
        wg = [ph.bf16(2048).rearrange("p (a b) -> p a b", a=8) for _ in range(2)]
        wp = [ph.bf16(512).rearrange("p (a b) -> p a b", a=2) for _ in range(2)]
        gt = [ph.f32(512), ph.f32(512)]
        t2 = [ph.f32(512), ph.f32(512)]
        Wg = self.I["w_pg"][i].rearrange("(kc p) n -> p kc n", p=128)
        Wp = self.I["w_ple"][i].rearrange("(kc p) n -> p kc n", p=128)
        q = 0
        for blk in range(4):
            s = blk % 2
            self.wload(wg[s], Wg[:, :, blk * 256:(blk + 1) * 256], "wg%d" % s, 2048)
            self.wload(wp[s], Wp[:, :, blk * 256:(blk + 1) * 256], "wp%d" % s, 512)
            for r in range(2):
                oc = blk * 2 + r
                for ti, (c0, n) in enumerate(self.coltiles()):
                    q ^= 1
                    if n == 16:
                        ps, psr = self.pss_next()
                        ps2, psr2 = self.pss_next()
                    else:
                        ps, psr = self.ps_next()
                        ps2, psr2 = self.ps_next()
                    for kc in range(NCH):
                        lhs = wg[s][:, kc, r * 128:(r + 1) * 128]
                        rhs = xn[:, kc, c0:c0 + n]
                        k.op("pe", lambda e, ps=ps, lhs=lhs, rhs=rhs, n=n, kc=kc: e.matmul(ps[:, 0:n], lhsT=lhs, rhs=rhs, start=(kc == 0), stop=(kc == NCH - 1)),
                             reads=["wg%d" % s, "xn%d_%d" % (kc, ti)], writes=[psr], inc=(kc == NCH - 1))
                    g = gt[q][:, 0:n]
                    k.op("act", lambda e, g=g, ps=ps, n=n: e.activation(out=g, in_=ps[:, 0:n], func=AF.Sigmoid), reads=[psr], writes=["gt%d" % q])
                    for kc in range(2):
                        lhs = wp[s][:, kc, r * 128:(r + 1) * 128]
                        rhs = pT[:, kc, c0:c0 + n]
                        k.op("pe", lambda e, ps2=ps2, lhs=lhs, rhs=rhs, n=n, kc=kc: e.matmul(ps2[:, 0:n], lhsT=lhs, rhs=rhs, start=(kc == 0), stop=(kc == 1)),
                             reads=["wp%d" % s, "pT%d" % ti], writes=[psr2], inc=(kc == 1))
                    tt2 = t2[q][:, 0:n]
                    k.op("dve", lambda e, tt2=tt2, g=g, ps2=ps2, n=n: e.tensor_tensor(out=tt2, in0=g, in1=ps2[:, 0:n], op=OP.mult),
                         reads=[psr2, "gt%d" % q], writes=["t2%d" % q])
                    hr = "h%d_%d" % (oc, ti)
                    hv = h[:, oc, c0:c0 + n]
                    k.op("pool", lambda e, hv=hv, tt2=tt2: e.tensor_tensor(out=hv, in0=hv, in1=tt2, op=OP.add), reads=["t2%d" % q, hr], writes=[hr])
        self.barrier()
        ph.reset(m)

    def rec_layer(self, i):
        k, c, ph = self.k, self.cfg, self.ph
        T, TT = c.T, c.TT
        h, xn = self.h, self.xn
        j = i // 2
        self.norm("nmix%d" % i)
        m = ph.mark()
        hist = ph.f32(4 * 76).rearrange("p (a b) -> p a b", a=4)
        tails = ph.f32(4 * 95).rearrange("p (a b) -> p a b", a=4)
        cneg = ph.f32(8)
        sm = [ph.f32(4) for _ in range(8)]
        mm = ph.mark()
        it = ph.f32(512)
        k.dma("sp", self.d_in, it[0:12, :], self.I["st_rc"][j], writes=["rit"])
        k.dma("sp", self.d_in, it[12:16, :], self.I["st_rh"][j], writes=["rit"])
        k.dma("sp", self.d_in, it[16:76, :], self.I["st_pool"][j], writes=["rit"])
        ps, psr = self.ps_next()
        for ch in range(4):
            self.tr32(ps[:, ch * 76:(ch + 1) * 76], it[0:76, ch * 128:(ch + 1) * 128], ["rit"], psr, 76)
        k.op("act", lambda e, ps=ps: e.activation(out=hist, in_=ps[:, 0:304].rearrange("p (a b) -> p a b", a=4), func=AF.Copy), reads=[psr], writes=["hist"])
        lam = self.vcol("lam%d" % j, 0, 4)
        al, x, z, z2, pl, rl, sp_, t_ = sm

        def dv(fn, reads, writes):
            k.op("dve", fn, reads=reads, writes=writes)

        k.op("act", lambda e: e.activation(out=al, in_=lam, func=AF.Abs), reads=["vec"], writes=["sm_al"])
        k.op("act", lambda e: e.activation(out=x, in_=al, func=AF.Exp, scale=-1.0), reads=["sm_al"], writes=["sm_x"])
        dv(lambda e: e.tensor_scalar(out=z, in0=x, scalar1=2.0, scalar2=None, op0=OP.add), ["sm_x"], ["sm_z"])
        dv(lambda e: e.reciprocal(out=z, in_=z), ["sm_z"], ["sm_z"])
        dv(lambda e: e.tensor_tensor(out=z, in0=z, in1=x, op=OP.mult), ["sm_z", "sm_x"], ["sm_z"])
        dv(lambda e: e.tensor_tensor(out=z2, in0=z, in1=z, op=OP.mult), ["sm_z"], ["sm_z2"])
        dv(lambda e: e.memset(pl, 1.0 / 17.0), [], ["sm_pl"])
        for cf in (15.0, 13.0, 11.0, 9.0, 7.0, 5.0, 3.0, 1.0):
            dv(lambda e: e.tensor_tensor(out=pl, in0=pl, in1=z2, op=OP.mult), ["sm_pl", "sm_z2"], ["sm_pl"])
            dv(lambda e, cf=cf: e.tensor_scalar(out=pl, in0=pl, scalar1=1.0 / cf, scalar2=None, op0=OP.add), ["sm_pl"], ["sm_pl"])
        dv(lambda e: e.tensor_tensor(out=pl, in0=pl, in1=z, op=OP.mult), ["sm_pl", "sm_z"], ["sm_pl"])
        dv(lambda e: e.tensor_scalar(out=rl, in0=lam, scalar1=-1.0, scalar2=0.0, op0=OP.mult, op1=OP.max), ["vec"], ["sm_rl"])
        dv(lambda e: e.scalar_tensor_tensor(out=sp_, in0=pl, scalar=2.0, in1=rl, op0=OP.mult, op1=OP.add), ["sm_pl", "sm_rl"], ["sm_sp"])
        dv(lambda e: e.tensor_scalar(out=cneg[:, 0:4], in0=sp_, scalar1=-8.0, scalar2=None, op0=OP.mult), ["sm_sp"], ["cneg"])
        dv(lambda e: e.tensor_scalar(out=cneg[:, 4:8], in0=sp_, scalar1=-16.0, scalar2=None, op0=OP.mult), ["sm_sp"], ["cneg"])
        self.barrier()
        ph.reset(mm)

        yab = ph.bf16(NCH * TT).rearrange("p (a b) -> p a b", a=NCH)
        wbf = [ph.bf16(2048).rearrange("p (a b) -> p a b", a=8) for _ in range(2)]
        wgt = ph.bf16(256)
        wpl = ph.bf16(128)
        B1 = ph.f32(T + 96)
        B2 = ph.f32(TT)
        B3 = ph.f32(TT)
        B4 = ph.f32(TT)
        Bh = ph.bf16(TT)
        ss2 = ph.f32(76).rearrange("p (a b) -> p a b", a=4)
        ss4 = ph.f32(76).rearrange("p (a b) -> p a b", a=4)
        Win = self.I["w_in_rec"][j].rearrange("(kc p) n -> p kc n", p=128)
        xnr = lambda kc, ti: "xn%d_%d" % (kc, ti)
        xnf = lambda kc, c0, n: xn[:, kc, c0:c0 + n]

        def v4(ap16):
            return ap16.rearrange("p (a b) -> p a b", a=4)

        for ch in range(4):
            blk, r = divmod(ch, 2)
            if r == 0:
                self.wload(wbf[0], Win[:, :, blk * 256:(blk + 1) * 256], "wbf0", 2048)
                self.wload(wbf[1], Win[:, :, 512 + blk * 256:512 + (blk + 1) * 256], "wbf1", 2048)
            B1s = B1[:, 3 + T:3 + T + 28].rearrange("p (a b) -> p a b", a=4)
            k.op("pool", lambda e: e.memset(B1[:, 0:3], 0.0), writes=["B1"])
            k.op("pool", lambda e, ch=ch, B1s=B1s: e.tensor_copy(out=B1s[:, :, 0:3], in_=hist[:, ch, 0:12].rearrange("p (a b) -> p a b", a=4)), reads=["hist"], writes=["B1"])

            def evac_xa(ti, c0, n, ps, psr, B1s=B1s):
                if n == 16:
                    k.op("act", lambda e: e.activation(out=B1s[:, :, 3:7], in_=v4(ps), func=AF.Copy), reads=[psr], writes=["B1"])
                else:
                    k.op("act", lambda e: e.activation(out=B1[:, 3 + c0:3 + c0 + n], in_=ps, func=AF.Copy), reads=[psr], writes=["B1"])

            self.proj_chunk(lambda kc, r=r: wbf[0][:, kc, r * 128:(r + 1) * 128], NCH, xnf, xnr, ["wbf0"], evac_xa)
            k.op("pool", lambda e, ch=ch: e.tensor_copy(out=tails[:, ch, 0:3], in_=B1[:, T:T + 3]), reads=["B1"], writes=["tails"])
            k.op("pool", lambda e, ch=ch, B1s=B1s: e.tensor_copy(out=tails[:, ch, 3:15].rearrange("p (a b) -> p a b", a=4), in_=B1s[:, :, 4:7]), reads=["B1"], writes=["tails"])
            w = [self.vcol("crw%d" % j, kk * 4 + ch) for kk in range(4)]
            b = self.vcol("crb%d" % j, ch)
            B2p = B2[:, 0:T]
            B2s = v4(B2[:, T:T + 16])
            dv(lambda e, w=w, b=b: e.tensor_scalar(out=B2p, in0=B1[:, 0:T], scalar1=w[0], scalar2=b, op0=OP.mult, op1=OP.add), ["B1", "vec"], ["B2"])
            for kk in (1, 2, 3):
                dv(lambda e, w=w, kk=kk: e.scalar_tensor_tensor(out=B2p, in0=B1[:, kk:kk + T], scalar=w[kk], in1=B2p, op0=OP.mult, op1=OP.add), ["B1", "B2"], ["B2"])
            dv(lambda e, w=w, b=b, B1s=B1s: e.tensor_scalar(out=B2s, in0=B1s[:, :, 0:4], scalar1=w[0], scalar2=b, op0=OP.mult, op1=OP.add), ["B1", "B2"], ["B2"])
            for kk in (1, 2, 3):
                dv(lambda e, w=w, kk=kk, B1s=B1s: e.scalar_tensor_tensor(out=B2s, in0=B1s[:, :, kk:kk + 4], scalar=w[kk], in1=B2s, op0=OP.mult, op1=OP.add), ["B1", "B2"], ["B2"])
            k.op("act", lambda e: e.activation(out=Bh, in_=B2, func=AF.Copy), reads=["B2"], writes=["Bh"])
            self.wload(wgt[:, 0:128], self.I["w_rg"][j, ch], "wgt", 128)
            self.wload(wgt[:, 128:256], self.I["w_ig"][j, ch], "wgt", 128)
            brg = self.vcol("brg%d" % j, ch)
            big = self.vcol("big%d" % j, ch)

            def evac_gate(dst, bias, reg):
                def f(ti, c0, n, ps, psr):
                    k.op("act", lambda e: e.activation(out=dst[:, c0:c0 + n], in_=ps, func=AF.Sigmoid, bias=bias), reads=[psr, "vec"], writes=[reg])
                return f

            bhf = lambda kc, c0, n: Bh[:, c0:c0 + n]
            self.proj_chunk(lambda kc: wgt[:, 0:128], 1, bhf, lambda kc, ti: "Bh", ["wgt"], evac_gate(B3, brg, "B3"))
            self.proj_chunk(lambda kc: wgt[:, 128:256], 1, bhf, lambda kc, ti: "Bh", ["wgt"], evac_gate(B4, big, "B4"))
            Ba = B1[:, 0:TT]
            c1 = cneg[:, ch:ch + 1]
            c2 = cneg[:, 4 + ch:5 + ch]
            k.op("act", lambda e, c1=c1: e.activation(out=Ba, in_=B3, func=AF.Exp, scale=c1), reads=["B3", "cneg", "tails"], writes=["B1"])
            k.op("act", lambda e, c2=c2: e.activation(out=B3, in_=B3, func=AF.Exp, scale=c2), reads=["B3", "cneg"], writes=["B3"])
            k.op("act", lambda e: e.activation(out=B3, in_=B3, func=AF.Sqrt, scale=-1.0, bias=self.one_t[:, 0:1]), reads=["B3"], writes=["B3"])
            dv(lambda e: e.tensor_tensor(out=B4, in0=B4, in1=B3, op=OP.mult), ["B3", "B4"], ["B4"])
            dv(lambda e: e.tensor_tensor(out=B4, in0=B4, in1=B2, op=OP.mult), ["B4", "B2", "Bh"], ["B4"])
            dv(lambda e: e.tensor_tensor_scan(out=B2[:, 0:T], data0=Ba[:, 0:T], data1=B4[:, 0:T], initial=0.0, op0=OP.mult, op1=OP.add), ["B1", "B4", "Bh"], ["B2"])
            for sq in range(4):
                a0 = T + 4 * sq
                dv(lambda e, a0=a0, sq=sq, ch=ch: e.tensor_tensor_scan(out=B2[:, a0:a0 + 4], data0=Ba[:, a0:a0 + 4], data1=B4[:, a0:a0 + 4],
                                                                     initial=hist[:, ch, 12 + sq:13 + sq], op0=OP.mult, op1=OP.add), ["B1", "B4", "hist", "B2"], ["B2"])
            k.op("pool", lambda e, ch=ch: e.tensor_copy(out=tails[:, ch, 15:16], in_=B2[:, T - 1:T]), reads=["B2"], writes=["tails"])
            k.op("pool", lambda e, ch=ch: e.tensor_copy(out=tails[:, ch, 16:20], in_=v4(B2[:, T:T + 16])[:, :, 3]), reads=["B2"], writes=["tails"])

            def evac_ga(ti, c0, n, ps, psr):
                k.op("act", lambda e: e.activation(out=B3[:, c0:c0 + n], in_=ps, func=AF.Gelu), reads=[psr], writes=["B3"])

            self.proj_chunk(lambda kc, r=r: wbf[1][:, kc, r * 128:(r + 1) * 128], NCH, xnf, xnr, ["wbf1"], evac_ga)
            for ti, (c0, n) in enumerate(self.coltiles()):
                k.op("dve", lambda e, ch=ch, c0=c0, n=n: e.tensor_tensor(out=yab[:, ch, c0:c0 + n], in0=B2[:, c0:c0 + n], in1=B3[:, c0:c0 + n], op=OP.mult),
                     reads=["B2", "B3"], writes=["yab%d_%d" % (ch, ti)])

        for g in range(4):
            wnd = 2 << g
            blk, r = divmod(g, 2)
            if r == 0:
                self.wload(wbf[0], Win[:, :, 1024 + blk * 256:1024 + (blk + 1) * 256], "wbf0", 2048)
            self.wload(wpl, self.I["w_pool"][j, g], "wpl", 128)
            B1s = B1[:, 15 + T:15 + T + 76].rearrange("p (a b) -> p a b", a=4)
            k.op("pool", lambda e: e.memset(B1[:, 0:15], 0.0), writes=["B1"])
            k.op("pool", lambda e, g=g, B1s=B1s: e.tensor_copy(out=B1s[:, :, 0:15], in_=hist[:, g, 16:76].rearrange("p (a b) -> p a b", a=4)), reads=["hist"], writes=["B1"])

            def evac_xb(ti, c0, n, ps, psr, B1s=B1s):
                if n == 16:
                    k.op("act", lambda e: e.activation(out=B1s[:, :, 15:19], in_=v4(ps), func=AF.Copy), reads=[psr], writes=["B1"])
                else:
                    k.op("act", lambda e: e.activation(out=B1[:, 15 + c0:15 + c0 + n], in_=ps, func=AF.Copy), reads=[psr], writes=["B1"])

            self.proj_chunk(lambda kc, r=r: wbf[0][:, kc, r * 128:(r + 1) * 128], NCH, xnf, xnr, ["wbf0"], evac_xb)
            k.op("pool", lambda e, g=g: e.tensor_copy(out=tails[:, g, 20:35], in_=B1[:, T:T + 15]), reads=["B1"], writes=["tails"])
            k.op("pool", lambda e, g=g, B1s=B1s: e.tensor_copy(out=tails[:, g, 35:95].rearrange("p (a b) -> p a b", a=4), in_=B1s[:, :, 4:19]), reads=["B1"], writes=["tails"])
            E = 15 + T
            src, srcs = B1, B1s
            bufs = [(B2, ss2, "B2"), (B3, ss4, "B3")]
            step = 1
            srcreg = "B1"
            for lv in range(g + 1):
                dst, dsts, dreg = bufs[lv % 2]
                v0 = 2 * step - 1
                k.op("dve", lambda e, dst=dst, src=src, step=step, v0=v0: e.tensor_tensor(out=dst[:, v0:E], in0=src[:, v0:E], in1=src[:, v0 - step:E - step], op=OP.add),
                     reads=[srcreg], writes=[dreg])
                k.op("pool", lambda e, dsts=dsts, srcs=srcs, step=step, v0=v0: e.tensor_tensor(out=dsts[:, :, v0:19], in0=srcs[:, :, v0:19], in1=srcs[:, :, v0 - step:19 - step], op=OP.add),
                     reads=[srcreg], writes=[dreg])
                src, srcs, srcreg = dst, dsts, dreg
                step *= 2
            S, Ss, Sreg = src, srcs, srcreg
            inv = 1.0 / wnd
            dv(lambda e, S=S, inv=inv: e.scalar_tensor_tensor(out=B4[:, 0:T], in0=S[:, 15:E], scalar=inv, in1=B1[:, 15:E], op0=OP.mult, op1=OP.subtract), [Sreg, "B1"], ["B4"])
            if wnd > 1:
                nf = wnd - 1
                rc = self.vcol("poolrc", g * 15, nf)
                dv(lambda e, S=S, rc=rc, nf=nf: e.tensor_tensor(out=B4[:, 0:nf], in0=S[:, 15:15 + nf], in1=rc, op=OP.mult), [Sreg, "vec", "B4"], ["B4"])
                dv(lambda e, nf=nf: e.tensor_tensor(out=B4[:, 0:nf], in0=B4[:, 0:nf], in1=B1[:, 15:15 + nf], op=OP.subtract), ["B4", "B1"], ["B4"])
            dv(lambda e, Ss=Ss, B1s=B1s, inv=inv: e.scalar_tensor_tensor(out=v4(B4[:, T:T + 16]), in0=Ss[:, :, 15:19], scalar=inv, in1=B1s[:, :, 15:19], op0=OP.mult, op1=OP.subtract),
               [Sreg, "B1", "B4"], ["B4"])
            k.op("act", lambda e: e.activation(out=Bh, in_=B4, func=AF.Copy), reads=["B4"], writes=["Bh"])
            psc = self.vcol("psc%d" % j, g)

            def evac_pl(ti, c0, n, ps, psr, g=g, psc=psc):
                k.op("act", lambda e: e.activation(out=yab[:, 4 + g, c0:c0 + n], in_=ps, func=AF.Identity, scale=psc), reads=[psr, "vec"], writes=["yab%d_%d" % (4 + g, ti)])

            self.proj_chunk(lambda kc: wpl, 1, lambda kc, c0, n: Bh[:, c0:c0 + n], lambda kc, ti: "Bh", ["wpl"], evac_pl)

        Wo = self.I["w_out_rec"][j].rearrange("(kc p) n -> p kc n", p=128)
        for blk in range(4):
            s = blk % 2
            self.wload(wbf[s], Wo[:, :, blk * 256:(blk + 1) * 256], "wbf%d" % s, 2048)
            for r in range(2):
                oc = blk * 2 + r

                def evac_o(ti, c0, n, ps, psr, oc=oc):
                    hr = "h%d_%d" % (oc, ti)
                    k.op("dve", lambda e: e.tensor_tensor(out=h[:, oc, c0:c0 + n], in0=h[:, oc, c0:c0 + n], in1=ps, op=OP.add), reads=[psr, hr], writes=[hr])

                self.proj_chunk(lambda kc, s=s, r=r: wbf[s][:, kc, r * 128:(r + 1) * 128], NCH,
                                lambda kc, c0, n: yab[:, kc, c0:c0 + n], lambda kc, ti: "yab%d_%d" % (kc, ti), ["wbf%d" % s], evac_o)
        self.barrier()
        ph.reset(mm)
        O = self.O
        self.store_fm(tails[:, :, 0:3], 4, 3, O["rc_p"][j], [], "o1")
        self.store_fm(tails[:, :, 3:15], 4, 12, O["rc_s"][j], [], "o2")
        self.store_fm(tails[:, :, 15:16], 4, 1, O["rh_p"][j], [], "o3")
        self.store_fm(tails[:, :, 16:20], 4, 4, O["rh_s"][j], [], "o4")
        self.store_fm(tails[:, :, 20:35], 4, 15, O["pl_p"][j], [], "o5")
        self.store_fm(tails[:, :, 35:95], 4, 60, O["pl_s"][j], [], "o6")
        self.barrier()
        ph.reset(m)

    def rot(self, banks):
        b = banks[self._rot % len(banks)]
        self._rot += 1
        return self.PS[b], "ps%d" % b

    def attn_layer(self, i):
        k, c, ph = self.k, self.cfg, self.ph
        T, TT, NT = c.T, c.TT, c.NT
        h, xn = self.h, self.xn
        j = i // 2
        I, O = self.I, self.O
        self._rot = 0
        RB = [0, 1, 2, 3]
        self.norm("nmix%d" % i)
        hflat = self.hflat
        k.dma("sp", self.d_out, self.hscr, hflat, reads=["h%d_%d" % (ch, ti) for ch in range(NCH) for ti in range(len(self.coltiles()))], writes=["hscr"])
        self.barrier()
        ha = Arena(self.ph.ap, NCH * TT)
        m = ph.mark()
        scale = 1.0 / float(np.sqrt(HD))
        NITER = 10

        def hbf(nel):
            w = ((nel + 1) // 2 + 7) // 8 * 8
            return ha.bf16(nel) if ha.top + w <= ha.size else ph.bf16(nel)

        def hf32(nel):
            w = (nel + 7) // 8 * 8
            return ha.f32(nel) if ha.top + w <= ha.size else ph.f32(nel)

        kT = hbf(NKV * TT).rearrange("p (a b) -> p a b", a=NKV)
        vb = hbf((NT + 1) * 512).rearrange("p (a b) -> p a b", a=NT + 1)
        kiT = hbf(TT)
        qT = hbf(NH * 512).rearrange("p (a b) -> p a b", a=NH)
        qiT = hbf(4 * 512).rearrange("p (a b) -> p a b", a=4)
        Isb = hf32(max(T, 256))
        maskq = hbf(T)
        gqk = ph.f32(256).rearrange("p (a b) -> p a b", a=2)
        negB = ph.f32(8)
        cmask = ph.bf16(4 * 512).rearrange("p (a b) -> p a b", a=4)
        pw2 = ph.f32(16)
        watt = [ph.bf16(4096).rearrange("p (a b) -> p a b", a=8) for _ in range(2)]
        zsb = [ph.f32(512), ph.f32(512)]
        outf = [ph.f32(512), ph.f32(512)]
        outb = [ph.bf16(512), ph.bf16(512)]
        rpt = [ph.f32(128).rearrange("p (a b) -> p a b", a=2) for _ in range(2)]
        rpit = [ph.f32(64).rearrange("p (a b) -> p a b", a=2) for _ in range(2)]
        tq = [ph.f32(256) for _ in range(4)]
        ssq = ph.f32(8)
        junk = ph.f32(128)
        drp = self.drp
        ha_mark2 = ha.mark()
        Wa = I["w_in_attn"][j].rearrange("(kc p) n -> p kc n", p=128)
        nct = len(self.coltiles())

        k.dma("sp", self.d_in, gqk, I["qkn"][j:j + 1].to_broadcast([128, 2, HD]), writes=["gqk"])
        k.op("dve", lambda e: e.tensor_reduce(out=ssq[:, 0:2], in_=gqk, axis=AX.X, op=OP.max, apply_absolute_value=True), reads=["gqk"], writes=["ssq"])
        k.op("dve", lambda e: e.tensor_tensor(out=negB[:, 0:1], in0=ssq[:, 0:1], in1=ssq[:, 1:2], op=OP.mult), reads=["ssq"], writes=["negB"])
        k.op("dve", lambda e: e.tensor_scalar(out=negB[:, 0:1], in0=negB[:, 0:1], scalar1=-float(np.sqrt(HD)), scalar2=None, op0=OP.mult), reads=["negB"], writes=["negB"])
        cmf = ph.f32(512)
        for pq in range(4):
            k.op("pool", lambda e: e.memset(cmf, 0.0), writes=["cmf"])
            blk = cmf[:, pq * 128:(pq + 1) * 128]
            k.op("pool", lambda e, blk=blk: e.affine_select(out=blk, in_=blk, pattern=[[-1, 128]], compare_op=OP.is_ge, fill=NEG, base=0, channel_multiplier=1), reads=["cmf"], writes=["cmf"])
            k.op("pool", lambda e, pq=pq: e.tensor_copy(out=cmask[:, pq, :], in_=cmf), reads=["cmf"], writes=["cmask"])
        for it in range(NITER):
            k.op("pool", lambda e, it=it: e.memset(pw2[:, it:it + 1], 0.5 ** (it + 1)), writes=["pw2"])

        def load_w(slot, col0, ncols):
            for hf in range(0, ncols, 256):
                w = min(256, ncols - hf)
                self.wload(watt[slot][:, :, hf:hf + w], Wa[:, :, col0 + hf:col0 + hf + w], "watt%d" % slot, 8 * w)

        def tm_proj(t0, n, slot, ncols):
            ps, psr = self.rot(RB)
            ti = min(t0 // 512, nct - 1)
            for kc in range(NCH):
                lhs = xn[:, kc, t0:t0 + n]
                rhs = watt[slot][:, kc, 0:ncols]
                k.op("pe", lambda e, ps=ps, lhs=lhs, rhs=rhs, n=n, ncols=ncols, kc=kc: e.matmul(ps[0:n, 0:ncols], lhsT=lhs, rhs=rhs, start=(kc == 0), stop=(kc == NCH - 1)),
                     reads=["watt%d" % slot, "xn%d_%d" % (kc, ti)], writes=[psr], inc=(kc == NCH - 1))
            return ps, psr

        def load_rope(tt, n):
            s = tt % 2
            k.dma("sp", drp[s], rpt[s][0:n], I["rope"][tt, 0:n], writes=["rpt%d" % s])
            k.dma("sp", drp[s], rpit[s][0:n], I["ropei"][tt, 0:n], writes=["rpit%d" % s])
            return s

        def bc(ap2, shape):
            return ap2.to_broadcast(shape)

        def rope_apply(src3, dst3, n, nh, half, cos, sin, sreg, dreg, extra_reads):
            x1, x2 = src3[:, :, 0:half], src3[:, :, half:2 * half]
            cb = cos.unsqueeze(1).to_broadcast([n, nh, half])
            sb = sin.unsqueeze(1).to_broadcast([n, nh, half])
            w = nh * half
            t1, t2, t3, t4 = [t[0:n, 0:w].rearrange("p (a b) -> p a b", a=nh) for t in tq]
            rd = [sreg] + list(extra_reads)
            k.op("dve", lambda e: e.tensor_tensor(out=t1, in0=x1, in1=cb, op=OP.mult), reads=rd, writes=["tq0"])
            k.op("pool", lambda e: e.tensor_tensor(out=t2, in0=x2, in1=sb, op=OP.mult), reads=rd, writes=["tq1"])
            k.op("pool", lambda e: e.tensor_tensor(out=t3, in0=x1, in1=sb, op=OP.mult), reads=rd, writes=["tq2"])
            k.op("dve", lambda e: e.tensor_tensor(out=t4, in0=x2, in1=cb, op=OP.mult), reads=rd, writes=["tq3"])
            k.op("dve", lambda e: e.tensor_tensor(out=dst3[:, :, 0:half], in0=t1, in1=t2, op=OP.subtract), reads=["tq0", "tq1"], writes=[dreg])
            k.op("pool", lambda e: e.tensor_tensor(out=dst3[:, :, half:2 * half], in0=t3, in1=t4, op=OP.add), reads=["tq2", "tq3"], writes=[dreg])

        def qk_post(ps, psr, n, gsel, rs_slot, q):
            z = zsb[q][0:n]
            zr = "zsb%d" % q
            k.op("act", lambda e: e.activation(out=z, in_=ps[0:n, :], func=AF.Copy), reads=[psr], writes=[zr])
            for hh in range(4):
                k.op("act", lambda e, hh=hh: e.activation(out=junk[0:n], in_=z[:, hh * 128:(hh + 1) * 128], func=AF.Square, accum_out=ssq[0:n, hh:hh + 1]),
                     reads=[zr], writes=["ssq", "junk"])
            k.op("act", lambda e: e.activation(out=ssq[0:n, 0:4], in_=ssq[0:n, 0:4], func=AF.Sqrt, bias=self.eps_t[0:n, 0:1], scale=1.0 / HD), reads=["ssq"], writes=["ssq"])
            k.op("dve", lambda e: e.reciprocal(out=ssq[0:n, 0:4], in_=ssq[0:n, 0:4]), reads=["ssq"], writes=["ssq"])
            z3 = z.rearrange("p (a b) -> p a b", a=4)
            k.op("dve", lambda e: e.tensor_tensor(out=z3, in0=z3, in1=ssq[0:n, 0:4].unsqueeze(2).to_broadcast([n, 4, HD]), op=OP.mult), reads=[zr, "ssq"], writes=[zr])
            k.op("pool", lambda e: e.tensor_tensor(out=z3, in0=z3, in1=gqk[0:n, gsel, :].unsqueeze(1).to_broadcast([n, 4, HD]), op=OP.mult), reads=[zr, "gqk"], writes=[zr])
            o3 = outf[q][0:n].rearrange("p (a b) -> p a b", a=4)
            rope_apply(z3, o3, n, 4, 64, rpt[rs_slot][0:n, 0, :], rpt[rs_slot][0:n, 1, :], zr, "outf%d" % q, ["rpt%d" % rs_slot])
            k.op("act", lambda e: e.activation(out=outb[q][0:n], in_=outf[q][0:n], func=AF.Copy), reads=["outf%d" % q], writes=["outb%d" % q])

        def transposes_to(dst3, src_b, n, nblk, reads, dreg):
            ps, psr = self.rot(RB)
            pb = ps.bitcast(BF16)
            for bq in range(nblk):
                k.op("pe", lambda e, bq=bq: e.transpose(pb[:, bq * 128:bq * 128 + n], src_b[0:n, bq * 128:(bq + 1) * 128], self.ident_b[0:n, 0:n]),
                     reads=list(reads) + ["ident_b"], writes=[psr])
            k.op("act", lambda e: e.activation(out=dst3, in_=pb[:, 0:nblk * 128].rearrange("p (a b) -> p a b", a=nblk)[:, :, 0:n], func=AF.Copy), reads=[psr], writes=[dreg])

        toks = self.tok_tiles()
        load_w(0, 1024, 512)
        load_w(1, 1536, 512)
        vst = [ph.f32(512), ph.f32(512)]
        for tt, (t0, n) in enumerate(toks):
            rs_slot = load_rope(tt, n)
            q = tt % 2
            ps, psr = tm_proj(t0, n, 0, 512)
            ps2, psr2 = tm_proj(t0, n, 1, 512)
            qk_post(ps, psr, n, 1, rs_slot, q)
            dst = O["k_p"][j, t0:t0 + n, :] if n == 128 else O["k_s"][j]
            k.dma("sp", self.d_out, dst, outf[q][0:n], reads=["outf%d" % q])
            transposes_to(kT[:, :, t0:t0 + n], outb[q], n, 4, ["outb%d" % q], "kT")
            k.op("dve", lambda e, q=q, ps2=ps2, n=n: e.tensor_copy(out=vst[q][0:n], in_=ps2[0:n, :]), reads=[psr2], writes=["vst%d" % q])
            dst = O["v_p"][j, t0:t0 + n, :] if n == 128 else O["v_s"][j]
            k.dma("sp", self.d_out, dst, vst[q][0:n], reads=["vst%d" % q])
            k.op("pool", lambda e, q=q, tt=tt, n=n: e.tensor_copy(out=vb[0:n, tt, :], in_=vst[q][0:n]), reads=["vst%d" % q], writes=["vb"])
        load_w(0, 2560, 64)
        for tt, (t0, n) in enumerate(toks):
            rs_slot = load_rope(tt, n)
            q = tt % 2
            ps, psr = tm_proj(t0, n, 0, 64)
            z = zsb[q][0:n, 0:64]
            k.op("act", lambda e, z=z, ps=ps, n=n: e.activation(out=z, in_=ps[0:n, 0:64], func=AF.Copy), reads=[psr], writes=["zsb%d" % q])
            o1 = outf[q][0:n, 0:64]
            rope_apply(z.rearrange("p (a b) -> p a b", a=1), o1.rearrange("p (a b) -> p a b", a=1), n, 1, 32,
                       rpit[rs_slot][0:n, 0, :], rpit[rs_slot][0:n, 1, :], "zsb%d" % q, "outf%d" % q, ["rpit%d" % rs_slot])
            dst = O["ki_p"][j, t0:t0 + n, :] if n == 128 else O["ki_s"][j]
            k.dma("sp", self.d_out, dst, o1, reads=["outf%d" % q])
            k.op("act", lambda e, q=q, n=n, o1=o1: e.activation(out=outb[q][0:n, 0:64], in_=o1, func=AF.Copy), reads=["outf%d" % q], writes=["outb%d" % q])
            k.op("act", lambda e, q=q, n=n, o1=o1: e.activation(out=outb[q][0:n, 64:128], in_=o1, func=AF.Copy), reads=["outf%d" % q], writes=["outb%d" % q])
            transposes_to(kiT[:, t0:t0 + n].rearrange("p (a b) -> p a b", a=1), outb[q], n, 1, ["outb%d" % q], "kiT")

        wsc = ph.f32(8)
        wout = [ph.bf16(2048).rearrange("p (a b) -> p a b", a=8) for _ in range(2)]
        hp = [ph.f32(512), ph.f32(512)]
        ph_mark2 = ph.mark()
        wdiag = ph.bf16(IDX_H * 128).rearrange("p (a b) -> p a b", a=IDX_H)
        Rh = [ph.bf16(512), ph.bf16(512)]
        maskT = ph.bf16((T // 128) * 512).rearrange("p (a b) -> p a b", a=T // 128)
        Eb3 = [ph.bf16(512), ph.bf16(512), ph.bf16(512)]
        self._eb = 0
        rden = ph.f32(512)
        attnT = ph.bf16(NH * 512).rearrange("p (a b) -> p a b", a=NH)
        bs = [ph.f32(8) for _ in range(6)]
        wk = ph.f32(16)
        cjunk = maskq
        dhp = self.dhp
        Wo = I["w_out_attn"][j].rearrange("(kc p) n -> p kc n", p=128)
        hs3 = self.hscr.rearrange("p (c t) -> p c t", c=NCH)

        def q_side(tiles, nq, preloaded=False):
            for cbi in range(2):
                if not preloaded:
                    load_w(cbi, cbi * 512, 512)
            for (li, tt, t0, n) in tiles:
                rs_slot = load_rope(tt, n)
                for cbi in range(2):
                    q = cbi
                    ps, psr = tm_proj(t0, n, cbi, 512)
                    qk_post(ps, psr, n, 0, rs_slot, q)
                    transposes_to(qT[:, cbi * 4:cbi * 4 + 4, li * 128:li * 128 + n], outb[q], n, 4, ["outb%d" % q], "qT")
            load_w(0, 2048, 512)
            for (li, tt, t0, n) in tiles:
                rs_slot = load_rope(tt, n)
                ps, psr = tm_proj(t0, n, 0, 512)
                z = zsb[0][0:n]
                k.op("act", lambda e, z=z, ps=ps, n=n: e.activation(out=z, in_=ps[0:n, :], func=AF.Copy), reads=[psr], writes=["zsb0"])
                o3 = outf[0][0:n].rearrange("p (a b) -> p a b", a=8)
                rope_apply(z.rearrange("p (a b) -> p a b", a=8), o3, n, 8, 32, rpit[rs_slot][0:n, 0, :], rpit[rs_slot][0:n, 1, :], "zsb0", "outf0", ["rpit%d" % rs_slot])
                k.op("act", lambda e, n=n: e.activation(out=outb[0][0:n], in_=outf[0][0:n], func=AF.Copy), reads=["outf0"], writes=["outb0"])
                transposes_to(qiT[:, :, li * 128:li * 128 + n], outb[0], n, 4, ["outb0"], "qiT")

        def w_side(t0, n):
            ps, psr = tm_proj(t0, n, 1, 8)
            k.op("act", lambda e, ps=ps, n=n: e.activation(out=wsc[0:n, 0:8], in_=ps[0:n, 0:8], func=AF.Copy, scale=float(IDX_D ** -0.5 * IDX_H ** -0.5)),
                 reads=[psr], writes=["wsc"])

        NC = T // 512
        for cq in range(NC):
            tiles = [(li, 4 * cq + li, (4 * cq + li) * 128, 128) for li in range(4)]
            q_side(tiles, 4, preloaded=(cq > 0))
            load_w(1, 2624, 8)
            for (li, tt, t0, n) in tiles:
                qt = tt
                w_side(t0, n)
                for hh in range(IDX_H):
                    k.op("pool", lambda e, hh=hh: e.tensor_scalar(out=wdiag[:, hh, :], in0=self.ident_f, scalar1=wsc[:, hh:hh + 1], scalar2=1.0, op0=OP.mult, op1=OP.mult),
                         reads=["wsc", "ident_f"], writes=["wdiag"])
                nkb = qt + 1
                nk = nkb * 128
                for kg in range((nkb + 3) // 4):
                    ncols = min(512, nk - kg * 512)
                    ia, iar = self.PS[4], "ps4"
                    diag_here = (qt // 4 == kg)
                    for hh in range(IDX_H):
                        hpair, par = divmod(hh, 2)
                        ps, psr = self.rot(RB)
                        lhs = qiT[par * 64:(par + 1) * 64, hpair, li * 128:(li + 1) * 128]
                        rhs = kiT[par * 64:(par + 1) * 64, kg * 512:kg * 512 + ncols]
                        k.op("pe", lambda e, ps=ps, lhs=lhs, rhs=rhs, ncols=ncols: e.matmul(ps[:, 0:ncols], lhsT=lhs, rhs=rhs, start=True, stop=True),
                             reads=["qiT", "kiT"], writes=[psr])
                        r = Rh[hh % 2]
                        if hh % 2 == 0:
                            k.op("act", lambda e, r=r, ps=ps, ncols=ncols: e.activation(out=r[:, 0:ncols], in_=ps[:, 0:ncols], func=AF.Relu), reads=[psr], writes=["Rh%d" % (hh % 2)])
                        else:
                            k.op("dve", lambda e, r=r, ps=ps, ncols=ncols: e.tensor_scalar(out=r[:, 0:ncols], in0=ps[:, 0:ncols], scalar1=0.0, scalar2=None, op0=OP.max), reads=[psr], writes=["Rh%d" % (hh % 2)])
                        last = (hh == IDX_H - 1) and not diag_here
                        k.op("pe", lambda e, ia=ia, hh=hh, r=r, ncols=ncols, last=last: e.matmul(ia[:, 0:ncols], lhsT=wdiag[:, hh, :], rhs=r[:, 0:ncols], start=(hh == 0), stop=last),
                             reads=["wdiag", "Rh%d" % (hh % 2)], writes=[iar], inc=True)
                    if diag_here:
                        pq = qt % 4
                        k.op("pe", lambda e, ia=ia, pq=pq, ncols=ncols: e.matmul(ia[:, 0:ncols], lhsT=self.ident_b, rhs=cmask[:, pq, 0:ncols], start=False, stop=True),
                             reads=["ident_b", "cmask"], writes=[iar])
                    k.op("act", lambda e, ia=ia, kg=kg, ncols=ncols: e.activation(out=Isb[:, kg * 512:kg * 512 + ncols], in_=ia[:, 0:ncols], func=AF.Copy), reads=[iar], writes=["Isb"])
                lo, wd, mid, cnt, tt_, thr = bs
                TK = c.TOPK
                if qt >= TK // 128:
                    k.op("dve", lambda e: e.tensor_reduce(out=lo[:, 0:1], in_=Isb[:, 0:TK], axis=AX.X, op=OP.min), reads=["Isb"], writes=["bs_lo"])
                    k.op("dve", lambda e, nk=nk: e.tensor_reduce(out=wd[:, 0:1], in_=Isb[:, 0:nk], axis=AX.X, op=OP.max), reads=["Isb"], writes=["bs_w"])
                    k.op("dve", lambda e: e.tensor_tensor(out=wd[:, 0:1], in0=wd[:, 0:1], in1=lo[:, 0:1], op=OP.subtract), reads=["bs_w", "bs_lo"], writes=["bs_w"])
                    k.op("dve", lambda e: e.tensor_scalar(out=wd[:, 0:1], in0=wd[:, 0:1], scalar1=1.0001, scalar2=1e-6, op0=OP.mult, op1=OP.add), reads=["bs_w"], writes=["bs_w"])
                    k.op("dve", lambda e: e.tensor_tensor(out=wk[:, 0:NITER], in0=pw2[:, 0:NITER], in1=wd[:, 0:1].to_broadcast([128, NITER]), op=OP.mult), reads=["bs_w", "pw2"], writes=["wk"])
                    k.op("dve", lambda e: e.tensor_tensor(out=mid[:, 0:1], in0=lo[:, 0:1], in1=wk[:, 0:1], op=OP.add), reads=["bs_lo", "wk"], writes=["bs_mid"])
                    for it in range(NITER):
                        k.op("dve", lambda e, nk=nk: e.tensor_scalar(out=cjunk[:, 0:nk], in0=Isb[:, 0:nk], scalar1=mid[:, 0:1], scalar2=None, op0=OP.is_ge, op1=OP.add, accum_out=cnt[:, 0:1]),
                             reads=["Isb", "bs_mid"], writes=["bs_cnt", "maskq"])
                        k.op("dve", lambda e: e.tensor_scalar(out=tt_[:, 0:1], in0=cnt[:, 0:1], scalar1=float(TK) - 0.5, scalar2=0.5, op0=OP.is_ge, op1=OP.subtract), reads=["bs_cnt"], writes=["bs_t"])
                        k.op("dve", lambda e, it=it: e.scalar_tensor_tensor(out=mid[:, 0:1], in0=tt_[:, 0:1], scalar=wk[:, it:it + 1], in1=mid[:, 0:1], op0=OP.mult, op1=OP.add), reads=["bs_t", "wk", "bs_mid"], writes=["bs_mid"])
                    k.op("dve", lambda e: e.scalar_tensor_tensor(out=lo[:, 0:1], in0=wk[:, NITER - 1:NITER], scalar=-0.5, in1=mid[:, 0:1], op0=OP.mult, op1=OP.add), reads=["bs_mid", "wk"], writes=["bs_lo"])
                    thr_ap, thr_reg = lo, "bs_lo"
                else:
                    k.op("dve", lambda e: e.memset(thr[:, 0:1], -1.0e29), writes=["bs_thr"])
                    thr_ap, thr_reg = thr, "bs_thr"
                k.op("dve", lambda e, nk=nk, thr_ap=thr_ap: e.tensor_scalar(out=maskq[:, 0:nk], in0=Isb[:, 0:nk], scalar1=thr_ap[:, 0:1], scalar2=-1.0e5, op0=OP.is_lt, op1=OP.mult), reads=["Isb", thr_reg], writes=["maskq"])
                for g0 in range(0, nkb, 8):
                    g1 = min(nkb, g0 + 8)
                    pm, pmr = self.PS[5], "ps5"
                    pmb = pm.bitcast(BF16)
                    for kb in range(g0, g1):
                        k.op("pe", lambda e, kb=kb, g0=g0, pmb=pmb: e.transpose(pmb[:, (kb - g0) * 128:(kb - g0 + 1) * 128], maskq[:, kb * 128:(kb + 1) * 128], self.ident_b),
                             reads=["maskq", "ident_b"], writes=[pmr])
                    nb = g1 - g0
                    k.op("act", lambda e, g0=g0, g1=g1, nb=nb, li=li, pmb=pmb: e.activation(out=maskT[:, g0:g1, li * 128:(li + 1) * 128], in_=pmb[:, 0:nb * 128].rearrange("p (a b) -> p a b", a=nb), func=AF.Copy),
                         reads=[pmr], writes=["maskT"])
            if cq + 1 < NC:
                for cbi in range(2):
                    load_w(cbi, cbi * 512, 512)
            for blk in range(2):
                self.wload(wout[blk], Wo[:, :, blk * 256:(blk + 1) * 256], "wout%d" % blk, 2048)
            nkb_c = 4 * cq + 4
            ops, opr = self.PS[6], "ps6"
            dps, dpr = self.PS[7], "ps7"
            for hh in range(NH):
                kvh = hh // 2
                for kb in range(nkb_c):
                    c0 = max(0, kb - 4 * cq) * 128
                    ncol = 512 - c0
                    ps, psr = self.rot(RB)
                    lhs = kT[:, kvh, kb * 128:(kb + 1) * 128]
                    rhs = qT[:, hh, c0:512]
                    k.op("pe", lambda e, ps=ps, lhs=lhs, rhs=rhs, ncol=ncol: e.matmul(ps[:, 0:ncol], lhsT=lhs, rhs=rhs, start=True, stop=False), reads=["kT", "qT"], writes=[psr], inc=False)
                    k.op("pe", lambda e, ps=ps, kb=kb, c0=c0, ncol=ncol: e.matmul(ps[:, 0:ncol], lhsT=self.ident_b, rhs=maskT[:, kb, c0:512], start=False, stop=True), reads=["ident_b", "maskT"], writes=[psr])
                    es = self._eb % 3
                    self._eb += 1
                    pb_ = Eb3[es]
                    k.op("act", lambda e, pb_=pb_, ps=ps, ncol=ncol: e.activation(out=pb_[:, 0:ncol], in_=ps[:, 0:ncol], func=AF.Exp, scale=scale, bias=negB[:, 0:1]),
                         reads=[psr, "negB"], writes=["Eb%d" % es])
                    vl = vb[:, kb, kvh * 128:(kvh + 1) * 128]
                    last = (kb == nkb_c - 1)
                    k.op("pe", lambda e, vl=vl, pb_=pb_, c0=c0, ncol=ncol, kb=kb, last=last: e.matmul(ops[:, c0:512], lhsT=vl, rhs=pb_[:, 0:ncol], start=(kb == 0), stop=last),
                         reads=["vb", "Eb%d" % es], writes=[opr], inc=True)
                    k.op("pe", lambda e, pb_=pb_, c0=c0, ncol=ncol, kb=kb, last=last: e.matmul(dps[:, c0:512], lhsT=self.ones_b, rhs=pb_[:, 0:ncol], start=(kb == 0), stop=last),
                         reads=["ones_b", "Eb%d" % es], writes=[dpr], inc=True)
                k.op("dve", lambda e: e.reciprocal(out=rden, in_=dps), reads=[dpr], writes=["rden"])
                k.op("dve", lambda e, hh=hh: e.tensor_tensor(out=attnT[:, hh, :], in0=ops, in1=rden, op=OP.mult), reads=[opr, "rden"], writes=["attnT"])
            for blk in range(4):
                s = blk % 2
                if blk >= 2:
                    self.wload(wout[s], Wo[:, :, blk * 256:(blk + 1) * 256], "wout%d" % s, 2048)
                for r in range(2):
                    oc = blk * 2 + r
                    q = oc % 2
                    k.dma("sp", dhp[q], hp[q], hs3[:, oc, cq * 512:(cq + 1) * 512], reads=["hscr%d_%d" % (oc, cq)], writes=["hp%d" % q])
                    ps, psr = self.rot(RB)
                    for hh in range(NH):
                        lhs = wout[s][:, hh, r * 128:(r + 1) * 128]
                        rhs = attnT[:, hh, :]
                        k.op("pe", lambda e, ps=ps, lhs=lhs, rhs=rhs, hh=hh: e.matmul(ps, lhsT=lhs, rhs=rhs, start=(hh == 0), stop=(hh == NH - 1)),
                             reads=["wout%d" % s, "attnT"], writes=[psr], inc=(hh == NH - 1))
                    k.op("dve", lambda e, q=q, ps=ps: e.tensor_tensor(out=hp[q], in0=hp[q], in1=ps, op=OP.add), reads=[psr, "hp%d" % q], writes=["hp%d" % q])
                    k.dma("sp", dhp[q], hs3[:, oc, cq * 512:(cq + 1) * 512], hp[q], reads=["hp%d" % q], writes=["hscr%d_%d" % (oc, cq)])

        self.attn_sample(i, locals())

        self.barrier()
        k.dma("sp", self.d_in, hflat, self.hscr, reads=["hscr"], writes=["h%d_%d" % (ch, ti) for ch in range(NCH) for ti in range(len(self.coltiles()))])
        self.barrier()
        ph.reset(m)

    def attn_sample(self, i, L):
        k, c, ph = self.k, self.cfg, self.ph
        T, TT, NT, NPG, PAST = c.T, c.TT, c.NT, c.NPG, c.PAST
        j = i // 2
        I, O = self.I, self.O
        kT, vb, kiT, qT, qiT = L["kT"], L["vb"], L["kiT"], L["qT"], L["qiT"]
        ha = L["ha"]
        negB, wsc, outb, outf, zsb = L["negB"], L["wsc"], L["outb"], L["outf"], L["zsb"]
        load_w, tm_proj, qk_post, transposes_to, rope_apply, load_rope = L["load_w"], L["tm_proj"], L["qk_post"], L["transposes_to"], L["rope_apply"], L["load_rope"]
        rpit = L["rpit"]
        scale = L["scale"]
        NITER = L["NITER"]
        pw2 = L["pw2"]
        wout, hp, dhp, Wo, hs3 = L["wout"], L["hp"], L["dhp"], L["Wo"], L["hs3"]
        RB = [0, 1, 2, 3]
        NL = PAST + 4
        self.barrier()
        ph.reset(L["ph_mark2"])
        ha.reset(L["ha_mark2"])
        qTs = ph.bf16(NH * 16).rearrange("p (b h t) -> p b h t", b=4, h=NH)
        qiTd = ph.bf16(NH * 16).rearrange("p (b h t) -> p b h t", b=4, h=NH)
        qidup = ph.bf16(NH * 128).rearrange("p (a b) -> p a b", a=NH)
        wbs = ph.f32(32).rearrange("p (a b) -> p a b", a=4)
        Xw = ph.f32(32).rearrange("p (a b) -> p a b", a=8)
        wsel = ph.bf16(16).rearrange("p (a b) -> p a b", a=4)
        ptb = ph.i32(4 * NPG)
        ptf = ph.f32(4 * NPG)
        iot = ph.f32(8)
        idxi = ph.i32(4 * NPG)
        cm4 = ph.bf16(8)
        cm4f = ph.f32(8)
        bmk = ph.f32(16)
        Mfull = ph.bf16(16)
        MfT = ph.bf16(16)
        attnTs = ph.bf16(NH * 16).rearrange("p (a b) -> p a b", a=NH)
        rs_slot = load_rope(NT, 16)
        for cbi in range(2):
            load_w(cbi, cbi * 512, 512)
        for cbi in range(2):
            ps, psr = tm_proj(T, 16, cbi, 512)
            qk_post(ps, psr, 16, 0, rs_slot, cbi)
            transposes_to(qT[:, cbi * 4:cbi * 4 + 4, 0:16], outb[cbi], 16, 4, ["outb%d" % cbi], "qT")
        k.op("act", lambda e: e.activation(out=qTs, in_=qT[:, :, 0:16].rearrange("p h (b t) -> p b h t", b=4), func=AF.Copy), reads=["qT"], writes=["qTs"])
        load_w(0, 2048, 512)
        ps, psr = tm_proj(T, 16, 0, 512)
        z = zsb[0][0:16]
        k.op("act", lambda e, ps=ps: e.activation(out=z, in_=ps[0:16, :], func=AF.Copy), reads=[psr], writes=["zsb0"])
        o3 = outf[0][0:16].rearrange("p (a b) -> p a b", a=8)
        rope_apply(z.rearrange("p (a b) -> p a b", a=8), o3, 16, 8, 32, rpit[rs_slot][0:16, 0, :], rpit[rs_slot][0:16, 1, :], "zsb0", "outf0", ["rpit%d" % rs_slot])
        k.op("act", lambda e: e.activation(out=qidup[0:16, :, 0:64], in_=o3, func=AF.Copy), reads=["outf0"], writes=["qidup"])
        k.op("act", lambda e: e.activation(out=qidup[0:16, :, 64:128], in_=o3, func=AF.Copy), reads=["outf0"], writes=["qidup"])
        self.dbg("watt0", L["watt"][0], ["watt0"])
        self.dbg("xns", self.xn[:, :, T:T + 16], [])
        self.dbg("zsb0", zsb[0][0:16], ["zsb0"])
        self.dbg("rpit", rpit[rs_slot][0:16], ["rpit%d" % rs_slot])
        self.dbg("outf0", outf[0][0:16], ["outf0"])
        self.dbg("qidup", qidup[0:16], ["qidup"])
        pq_, pqr = self.rot(RB)
        pqb = pq_.bitcast(BF16)
        for hh in range(NH):
            k.op("pe", lambda e, hh=hh: e.transpose(pqb[:, hh * 16:(hh + 1) * 16], qidup[0:16, hh, :], self.ident_b[0:16, 0:16]), reads=["qidup", "ident_b"], writes=[pqr])
        k.op("act", lambda e: e.activation(out=qiTd, in_=pqb[:, 0:NH * 16].rearrange("p (h b t) -> p b h t", h=NH, b=4), func=AF.Copy), reads=[pqr], writes=["qiTd"])
        load_w(1, 2624, 8)
        L["w_side"](T, 16)
        pw_, pwr = self.rot(RB)
        for b in range(4):
            k.op("pe", lambda e, b=b: e.matmul(pw_[0:4, b * 8:(b + 1) * 8], lhsT=self.ident_f[0:16, 4 * b:4 * b + 4], rhs=wsc[0:16, 0:8], start=True, stop=True),
                 reads=["ident_f", "wsc"], writes=[pwr])
        k.op("act", lambda e: e.activation(out=wbs[0:4], in_=pw_[0:4, 0:32].rearrange("p (a b) -> p a b", a=4), func=AF.Copy), reads=[pwr], writes=["wbs"])
        for b in range(4):
            k.op("dve", lambda e, b=b: e.tensor_tensor(out=Xw[0:4], in0=wbs[0:4, b, :].unsqueeze(2).to_broadcast([4, 8, 4]),
                                                      in1=self.ident_f[0:4, 0:4].unsqueeze(1).to_broadcast([4, 8, 4]), op=OP.mult), reads=["wbs", "ident_f"], writes=["Xw"])
            px, pxr = self.rot(RB)
            self.tr32(px[0:32, 0:4], Xw[0:4].rearrange("p a b -> p (a b)"), ["Xw"], pxr, 4)
            k.op("act", lambda e, b=b, px=px: e.activation(out=wsel[0:32, b, :], in_=px[0:32, 0:4], func=AF.Copy), reads=[pxr], writes=["wsel"])
        k.dma("sp", None, ptb, I["ptab"].to_broadcast([128, 4 * NPG]), writes=["ptb"])
        k.op("pool", lambda e: e.iota(iot[:, 0:1], [[0, 1]], base=0, channel_multiplier=1, allow_small_or_imprecise_dtypes=True), writes=["iot"])
        k.op("dve", lambda e: e.tensor_copy(out=ptf, in_=ptb), reads=["ptb"], writes=["ptf"])
        k.op("dve", lambda e: e.tensor_scalar(out=ptf, in0=ptf, scalar1=128.0, scalar2=iot[:, 0:1], op0=OP.mult, op1=OP.add), reads=["ptf", "iot"], writes=["ptf"])
        if j:
            k.op("dve", lambda e: e.tensor_scalar(out=ptf, in0=ptf, scalar1=float(j * c.NPOOL * 128), scalar2=None, op0=OP.add), reads=["ptf"], writes=["ptf"])
        k.op("dve", lambda e: e.tensor_copy(out=idxi, in_=ptf), reads=["ptf"], writes=["idxi"])
        k.op("pool", lambda e: e.memset(cm4f[:, 0:4], 0.0), writes=["cm4f"])
        k.op("pool", lambda e: e.affine_select(out=cm4f[:, 0:4], in_=cm4f[:, 0:4], pattern=[[-1, 4]], compare_op=OP.is_ge, fill=NEG, base=0, channel_multiplier=1), reads=["cm4f"], writes=["cm4f"])
        k.op("pool", lambda e: e.tensor_copy(out=cm4[:, 0:4], in_=cm4f[:, 0:4]), reads=["cm4f"], writes=["cm4"])
        bm3 = bmk.rearrange("p (a b) -> p a b", a=4)
        k.op("pool", lambda e: e.memset(bmk, 1.0), writes=["bmk"])
        k.op("pool", lambda e: e.affine_select(out=bm3, in_=bm3, pattern=[[-4, 4], [0, 4]], compare_op=OP.is_ge, fill=0.0, base=0, channel_multiplier=1), reads=["bmk"], writes=["bmk"])
        k.op("pool", lambda e: e.affine_select(out=bm3, in_=bm3, pattern=[[4, 4], [0, 4]], compare_op=OP.is_ge, fill=0.0, base=3, channel_multiplier=-1), reads=["bmk"], writes=["bmk"])
        kTn = ph.bf16(NKV * 16).rearrange("p (a b) -> p a b", a=NKV)
        vbn = ph.bf16(512)
        kiTn = ph.bf16(16)
        k.op("pool", lambda e: e.tensor_copy(out=kTn, in_=kT[:, :, T:T + 16]), reads=["kT"], writes=["kTn"])
        k.op("pool", lambda e: e.tensor_copy(out=vbn[0:16], in_=vb[0:16, NT, :]), reads=["vb"], writes=["vbn"])
        k.op("pool", lambda e: e.tensor_copy(out=kiTn, in_=kiT[:, T:T + 16]), reads=["kiT"], writes=["kiTn"])
        self.barrier()
        ha.reset(0)
        mark_sb = ph.mark()

        Iall = ha.f32(NL)
        hmark_sb = ha.mark()
        kis = ha.f32(NPG * 64).rearrange("p (a b) -> p a b", a=NPG)
        kisb = ha.bf16(NPG * 64).rearrange("p (a b) -> p a b", a=NPG)
        kiTs = ph.bf16(NPG * 64).rearrange("p (a b) -> p a b", a=NPG // 2)
        stg = [ph.f32(512), ph.f32(512)]
        Rs = [ph.bf16(512), ph.bf16(512)]
        cki = I["cache_ki"].rearrange("a r d -> (a r) d")
        sq = 0
        for b in range(4):
            for pg in range(NPG):
                col = b * NPG + pg
                k.dma("pool", None, None, None, reads=["idxi"], writes=["kis"],
                      fn=lambda e, pg=pg, col=col: e.indirect_dma_start(out=kis[:, pg, :], out_offset=None, in_=cki, in_offset=bass.IndirectOffsetOnAxis(ap=idxi[:, col:col + 1], axis=0)))
            k.op("pool", lambda e: e.tensor_copy(out=kisb, in_=kis), reads=["kis"], writes=["kisb"])
            for g0 in range(0, NPG // 2, 8):
                pt_, ptr_ = self.rot(RB)
                ptb16 = pt_.bitcast(BF16)
                for pp in range(g0, g0 + 8 if g0 + 8 <= NPG // 2 else NPG // 2):
                    k.op("pe", lambda e, pp=pp, g0=g0, ptb16=ptb16: e.transpose(ptb16[:, (pp - g0) * 128:(pp - g0 + 1) * 128], kisb[:, 2 * pp:2 * pp + 2, :].rearrange("p a b -> p (a b)"), self.ident_b),
                         reads=["kisb", "ident_b"], writes=[ptr_])
                nb = min(8, NPG // 2 - g0)
                k.op("act", lambda e, g0=g0, nb=nb, ptb16=ptb16: e.activation(out=kiTs[:, g0:g0 + nb, :], in_=ptb16[:, 0:nb * 128].rearrange("p (a b) -> p a b", a=nb), func=AF.Copy),
                     reads=[ptr_], writes=["kiTs"])
            Iv = None
            for par in range(2):
                for g4 in range(NPG // 8):
                    sq ^= 1
                    ps, psr = self.rot(RB)
                    lhs = qiTd[par * 64:(par + 1) * 64, b].rearrange("p h t -> p (h t)")
                    rhs = kiTs[par * 64:(par + 1) * 64, g4 * 4:(g4 + 1) * 4, :].rearrange("p a b -> p (a b)")
                    k.op("pe", lambda e, ps=ps, lhs=lhs, rhs=rhs: e.matmul(ps[0:32, :], lhsT=lhs, rhs=rhs, start=True, stop=True), reads=["qiTd", "kiTs"], writes=[psr])
                    r = Rs[sq]
                    if sq:
                        k.op("act", lambda e, r=r, ps=ps: e.activation(out=r[0:32], in_=ps[0:32, :], func=AF.Relu), reads=[psr], writes=["Rs%d" % sq])
                    else:
                        k.op("dve", lambda e, r=r, ps=ps: e.tensor_scalar(out=r[0:32], in0=ps[0:32, :], scalar1=0.0, scalar2=None, op0=OP.max), reads=[psr], writes=["Rs%d" % sq])
                    pi, pir = self.PS[4], "ps4"
                    k.op("pe", lambda e, pi=pi, b=b, r=r: e.matmul(pi[0:4, :], lhsT=wsel[0:32, b, :], rhs=r[0:32], start=True, stop=True), reads=["wsel", "Rs%d" % sq], writes=[pir])
                    st_ = stg[sq]
                    k.op("act", lambda e, st_=st_, pi=pi: e.activation(out=st_[0:4], in_=pi[0:4, :], func=AF.Copy), reads=[pir], writes=["stg%d" % sq])
                    dst = Iall[4 * b:4 * b + 4, 0:PAST].rearrange("p (a c d) -> p a c d", c=2, d=128)[:, g4 * 4:(g4 + 1) * 4, par, :]
                    k.dma("sp", None, dst, st_[0:4].rearrange("p (a d) -> p a d", d=128), reads=["stg%d" % sq], writes=["Iall"])
            sq ^= 1
            ps, psr = self.rot(RB)
            lhs = qiTd[0:64, b].rearrange("p h t -> p (h t)")
            rhs = kiTn[0:64, 4 * b:4 * b + 4]
            k.op("pe", lambda e, ps=ps, lhs=lhs, rhs=rhs: e.matmul(ps[0:32, 0:4], lhsT=lhs, rhs=rhs, start=True, stop=True), reads=["qiTd", "kiTn"], writes=[psr])
            r = Rs[sq]
            k.op("act", lambda e, r=r, ps=ps: e.activation(out=r[0:32, 0:4], in_=ps[0:32, 0:4], func=AF.Relu), reads=[psr], writes=["Rs%d" % sq])
            pi, pir = self.PS[4], "ps4"
            k.op("pe", lambda e, pi=pi, b=b, r=r: e.matmul(pi[0:4, 0:4], lhsT=wsel[0:32, b, :], rhs=r[0:32, 0:4], start=True, stop=False), reads=["wsel", "Rs%d" % sq], writes=[pir], inc=False)
            k.op("pe", lambda e, pi=pi: e.matmul(pi[0:4, 0:4], lhsT=self.ident_b[0:4, 0:4], rhs=cm4[0:4, 0:4], start=False, stop=True), reads=["ident_b", "cm4"], writes=[pir])
            st_ = stg[sq]
            k.op("act", lambda e, st_=st_, pi=pi: e.activation(out=st_[0:4, 0:4], in_=pi[0:4, 0:4], func=AF.Copy), reads=[pir], writes=["stg%d" % sq])
            k.dma("sp", None, Iall[4 * b:4 * b + 4, PAST:PAST + 4], st_[0:4, 0:4], reads=["stg%d" % sq], writes=["Iall"])
        self.dbg("Iall", Iall[0:16, :], ["Iall"])
        self.dbg("qTs", qTs, ["qTs"])
        self.dbg("qiTd", qiTd, ["qiTd"])
        self.dbg("wsel", wsel[0:32], ["wsel"])
        self.dbg("idxi", idxi, ["idxi"])
        self.barrier()
        ha.reset(hmark_sb)
        ph.reset(mark_sb)

        TK = c.TOPK_S
        bs = [ph.f32(8) for _ in range(5)]
        lo, wd, mid, cnt, tt_ = bs
        wk = ph.f32(16)
        mask_s = ha.bf16(NL)
        cj = mask_s
        maskTs = ph.bf16(NPG * 16).rearrange("p (a b) -> p a b", a=NPG)
        R16 = slice(0, 16)
        k.op("dve", lambda e: e.tensor_reduce(out=lo[R16, 0:1], in_=Iall[R16, 0:TK], axis=AX.X, op=OP.min), reads=["Iall"], writes=["bs_lo"])
        k.op("dve", lambda e: e.tensor_reduce(out=wd[R16, 0:1], in_=Iall[R16, :], axis=AX.X, op=OP.max), reads=["Iall"], writes=["bs_w"])
        k.op("dve", lambda e: e.tensor_tensor(out=wd[R16, 0:1], in0=wd[R16, 0:1], in1=lo[R16, 0:1], op=OP.subtract), reads=["bs_w", "bs_lo"], writes=["bs_w"])
        k.op("dve", lambda e: e.tensor_scalar(out=wd[R16, 0:1], in0=wd[R16, 0:1], scalar1=1.0001, scalar2=1e-6, op0=OP.mult, op1=OP.add), reads=["bs_w"], writes=["bs_w"])
        k.op("dve", lambda e: e.tensor_tensor(out=wk[R16, 0:NITER], in0=pw2[R16, 0:NITER], in1=wd[R16, 0:1].to_broadcast([16, NITER]), op=OP.mult), reads=["bs_w", "pw2"], writes=["wk"])
        for it in range(NITER):
            k.op("dve", lambda e, it=it: e.tensor_tensor(out=mid[R16, 0:1], in0=lo[R16, 0:1], in1=wk[R16, it:it + 1], op=OP.add), reads=["bs_lo", "wk"], writes=["bs_mid"])
            k.op("dve", lambda e: e.tensor_scalar(out=cj[R16, :], in0=Iall[R16, :], scalar1=mid[R16, 0:1], scalar2=None, op0=OP.is_ge, op1=OP.add, accum_out=cnt[R16, 0:1]),
                 reads=["Iall", "bs_mid"], writes=["bs_cnt", "mask_s"])
            k.op("dve", lambda e, it=it: e.scalar_tensor_tensor(out=tt_[R16, 0:1], in0=cnt[R16, 0:1], scalar=float(TK) - 0.5, in1=wk[R16, it:it + 1], op0=OP.is_ge, op1=OP.mult), reads=["bs_cnt", "wk"], writes=["bs_t"])
            k.op("dve", lambda e: e.tensor_tensor(out=lo[R16, 0:1], in0=lo[R16, 0:1], in1=tt_[R16, 0:1], op=OP.add), reads=["bs_lo", "bs_t"], writes=["bs_lo"])
        k.op("dve", lambda e: e.tensor_scalar(out=mask_s[R16, :], in0=Iall[R16, :], scalar1=lo[R16, 0:1], scalar2=None, op0=OP.is_ge), reads=["Iall", "bs_lo"], writes=["mask_s"])
        for g0 in range(0, NPG, 32):
            pm, pmr = self.rot(RB)
            nb = min(32, NPG - g0)
            for pg in range(g0, g0 + nb):
                k.op("pe", lambda e, pg=pg, g0=g0, pm=pm: e.matmul(pm[:, (pg - g0) * 16:(pg - g0 + 1) * 16], lhsT=mask_s[R16, pg * 128:(pg + 1) * 128], rhs=self.ident_b[0:16, 0:16], start=True, stop=True),
                     reads=["mask_s", "ident_b"], writes=[pmr])
            k.op("act", lambda e, g0=g0, nb=nb, pm=pm: e.activation(out=maskTs[:, g0:g0 + nb, :], in_=pm[:, 0:nb * 16].rearrange("p (a b) -> p a b", a=nb), func=AF.Copy), reads=[pmr], writes=["maskTs"])
        k.op("dve", lambda e: e.tensor_tensor(out=Mfull[R16, :].rearrange("p (a b) -> p a b", a=4), in0=bm3[R16], in1=mask_s[R16, PAST:PAST + 4].unsqueeze(1).to_broadcast([16, 4, 4]), op=OP.mult),
             reads=["bmk", "mask_s"], writes=["Mfull"])
        pm, pmr = self.rot(RB)
        k.op("pe", lambda e, pm=pm: e.matmul(pm[0:16, 0:16], lhsT=Mfull[R16, :], rhs=self.ident_b[0:16, 0:16], start=True, stop=True), reads=["Mfull", "ident_b"], writes=[pmr])
        k.op("act", lambda e, pm=pm: e.activation(out=MfT[R16, :], in_=pm[0:16, 0:16], func=AF.Copy), reads=[pmr], writes=["MfT"])

        self.dbg("mask_s", mask_s[0:16, :], ["mask_s"])
        self.dbg("thr", lo[0:16, 0:1], ["bs_lo"])
        self.dbg("MfT", MfT[0:16, :], ["MfT"])
        self.dbg("maskTs", maskTs, ["maskTs"])
        GP = 4
        NSL = 4
        hf = lambda nw: ha.f32(nw) if ha.top + (nw + 7) // 8 * 8 <= ha.size else ph.f32(nw)
        kvpg = [hf(1024) for _ in range(NSL)]
        kpb = [ph.bf16(512) for _ in range(NSL)]
        kTp = [ph.bf16(GP * 512).rearrange("p (u a b) -> p u a b", u=GP, a=NKV) for _ in range(2)]
        vpb = [ph.bf16(GP * 512).rearrange("p (u b) -> p u b", u=GP) for _ in range(2)]
        Es = ph.bf16(GP * 32)
        Ps_ = [ph.bf16(GP * 32), ph.bf16(GP * 32)]
        En = ph.bf16(32)
        Pn = ph.bf16(32)
        rdn = ph.f32(32)
        ckv = I["cache_kv"].rearrange("a r d -> (a r) d")
        groups = [(b, pgrp) for b in range(4) for pgrp in range(NPG // GP)]
        slotc = [0]

        def sd_load(gi):
            b, pgrp = groups[gi]
            gs = gi % 2
            for u in range(GP):
                pg = pgrp * GP + u
                col = b * NPG + pg
                slotc[0] = (slotc[0] + 1) % NSL
                slot = slotc[0]
                k.dma("pool", None, None, None, reads=["idxi"], writes=["kvpg%d" % slot],
                      fn=lambda e, slot=slot, col=col: e.indirect_dma_start(out=kvpg[slot], out_offset=None, in_=ckv, in_offset=bass.IndirectOffsetOnAxis(ap=idxi[:, col:col + 1], axis=0)))
                k.op("dve", lambda e, slot=slot: e.tensor_copy(out=kpb[slot], in_=kvpg[slot][:, 0:512]), reads=["kvpg%d" % slot], writes=["kpb%d" % slot])
                k.op("dve", lambda e, slot=slot, gs=gs, u=u: e.tensor_copy(out=vpb[gs][:, u, :], in_=kvpg[slot][:, 512:1024]), reads=["kvpg%d" % slot], writes=["vpb%d" % gs])
                pt_, ptr_ = self.PS[6], "ps6"
                ptb16 = pt_.bitcast(BF16)
                for kvh in range(NKV):
                    k.op("pe", lambda e, kvh=kvh, slot=slot, ptb16=ptb16: e.transpose(ptb16[:, kvh * 128:(kvh + 1) * 128], kpb[slot][:, kvh * 128:(kvh + 1) * 128], self.ident_b),
                         reads=["kpb%d" % slot, "ident_b"], writes=[ptr_])
                k.op("act", lambda e, gs=gs, u=u, ptb16=ptb16: e.activation(out=kTp[gs][:, u], in_=ptb16[:, 0:512].rearrange("p (a b) -> p a b", a=NKV), func=AF.Copy), reads=[ptr_], writes=["kTp%d" % gs])

        def sd_score(gi):
            b, pgrp = groups[gi]
            gs = gi % 2
            sc, scr = self.PS[5], "ps5"
            for u in range(GP):
                for kvh in range(NKV):
                    o_ = u * 32 + kvh * 8
                    rhs = qTs[:, b, 2 * kvh:2 * kvh + 2, :].rearrange("p h t -> p (h t)")
                    k.op("pe", lambda e, gs=gs, u=u, kvh=kvh, o_=o_, rhs=rhs, sc=sc: e.matmul(sc[:, o_:o_ + 8], lhsT=kTp[gs][:, u, kvh, :], rhs=rhs, start=True, stop=True),
                         reads=["kTp%d" % gs, "qTs"], writes=[scr])
            k.op("act", lambda e, sc=sc: e.activation(out=Es, in_=sc[:, 0:GP * 32], func=AF.Exp, scale=scale, bias=negB[:, 0:1]), reads=[scr, "negB"], writes=["Es"])
            pp_ = Ps_[gs]
            k.op("dve", lambda e, pp_=pp_, pgrp=pgrp, b=b: e.tensor_tensor(out=pp_.rearrange("p (u h t) -> p u h t", u=GP, h=NH), in0=Es.rearrange("p (u h t) -> p u h t", u=GP, h=NH),
                                                                        in1=maskTs[:, pgrp * GP:(pgrp + 1) * GP, 4 * b:4 * b + 4].unsqueeze(2).to_broadcast([128, GP, NH, 4]), op=OP.mult),
                 reads=["Es", "maskTs"], writes=["Ps%d" % gs])

        def sd_pv(gi):
            b, pgrp = groups[gi]
            gs = gi % 2
            pp_ = Ps_[gs]
            for u in range(GP):
                first = (pgrp == 0 and u == 0)
                for kvh in range(NKV):
                    rhs = pp_[:, u * 32 + kvh * 8:u * 32 + kvh * 8 + 8]
                    k.op("pe", lambda e, gs=gs, u=u, kvh=kvh, rhs=rhs, first=first: e.matmul(self.PS[kvh][:, 0:8], lhsT=vpb[gs][:, u, kvh * 128:(kvh + 1) * 128], rhs=rhs, start=first, stop=False),
                         reads=["vpb%d" % gs, "Ps%d" % gs], writes=["ps%d" % kvh], inc=True)
                k.op("pe", lambda e, u=u, pp_=pp_, first=first: e.matmul(self.PS[4][:, 0:32], lhsT=self.ones_b, rhs=pp_[:, u * 32:(u + 1) * 32], start=first, stop=False),
                     reads=["ones_b", "Ps%d" % gs], writes=["ps4"], inc=True)

        def sd_finish(b):
            scn, scnr = self.PS[7], "ps7"
            for kvh in range(NKV):
                rhs = qTs[:, b, 2 * kvh:2 * kvh + 2, :].rearrange("p h t -> p (h t)")
                k.op("pe", lambda e, kvh=kvh, rhs=rhs: e.matmul(scn[0:16, kvh * 8:(kvh + 1) * 8], lhsT=kTn[:, kvh, :], rhs=rhs, start=True, stop=True), reads=["kTn", "qTs"], writes=[scnr])
            k.op("act", lambda e: e.activation(out=En[R16, :], in_=scn[0:16, 0:32], func=AF.Exp, scale=scale, bias=negB[R16, 0:1]), reads=[scnr, "negB"], writes=["En"])
            k.op("dve", lambda e, b=b: e.tensor_tensor(out=Pn[R16, :].rearrange("p (h t) -> p h t", h=NH), in0=En[R16, :].rearrange("p (h t) -> p h t", h=NH),
                                                      in1=MfT[R16, 4 * b:4 * b + 4].unsqueeze(1).to_broadcast([16, NH, 4]), op=OP.mult), reads=["En", "MfT"], writes=["Pn"])
            for kvh in range(NKV):
                k.op("pe", lambda e, kvh=kvh: e.matmul(self.PS[kvh][:, 0:8], lhsT=vbn[0:16, kvh * 128:(kvh + 1) * 128], rhs=Pn[R16, kvh * 8:(kvh + 1) * 8], start=False, stop=True),
                     reads=["vbn", "Pn"], writes=["ps%d" % kvh], inc=True)
            k.op("pe", lambda e: e.matmul(self.PS[4][:, 0:32], lhsT=self.ones_b[0:16, :], rhs=Pn[R16, :], start=False, stop=True), reads=["ones_b", "Pn"], writes=["ps4"], inc=True)
            k.op("dve", lambda e: e.reciprocal(out=rdn, in_=self.PS[4][:, 0:32]), reads=["ps4"], writes=["rdn"])
            for kvh in range(NKV):
                k.op("dve", lambda e, kvh=kvh, b=b: e.tensor_tensor(out=attnTs[:, 2 * kvh:2 * kvh + 2, 4 * b:4 * b + 4], in0=self.PS[kvh][:, 0:8].rearrange("p (h t) -> p h t", h=2),
                                                                  in1=rdn[:, kvh * 8:(kvh + 1) * 8].rearrange("p (h t) -> p h t", h=2), op=OP.mult), reads=["ps%d" % kvh, "rdn"], writes=["attnTs"])

        sd_load(0)
        for gi in range(len(groups)):
            sd_score(gi)
            if gi + 1 < len(groups):
                sd_load(gi + 1)
            sd_pv(gi)
            if groups[gi][1] == NPG // GP - 1:
                sd_finish(groups[gi][0])
        self.dbg("attnTs", attnTs, ["attnTs"])
        for blk in range(4):
            s = blk % 2
            self.wload(wout[s], Wo[:, :, blk * 256:(blk + 1) * 256], "wout%d" % s, 2048)
            for r in range(2):
                oc = blk * 2 + r
                q = oc % 2
                k.dma("sp", None, hp[q][:, 0:16], hs3[:, oc, T:T + 16], reads=["hscr_s%d" % oc], writes=["hp%d" % q])
                ps, psr = self.PS[5], "ps5"
                for hh in range(NH):
                    lhs = wout[s][:, hh, r * 128:(r + 1) * 128]
                    rhs = attnTs[:, hh, :]
                    k.op("pe", lambda e, ps=ps, lhs=lhs, rhs=rhs, hh=hh: e.matmul(ps[:, 0:16], lhsT=lhs, rhs=rhs, start=(hh == 0), stop=(hh == NH - 1)),
                         reads=["wout%d" % s, "attnTs"], writes=[psr], inc=(hh == NH - 1))
                k.op("dve", lambda e, q=q, ps=ps: e.tensor_tensor(out=hp[q][:, 0:16], in0=hp[q][:, 0:16], in1=ps[:, 0:16], op=OP.add), reads=[psr, "hp%d" % q], writes=["hp%d" % q])
                k.dma("sp", None, hs3[:, oc, T:T + 16], hp[q][:, 0:16], reads=["hp%d" % q], writes=["hscr_s%d" % oc])


def _fm(v, nch):
    return np.ascontiguousarray(np.asarray(v, np.float32).reshape(nch, 128).T)


def build_vec(cfg, inp):
    off, NV = vec_layout(cfg)
    vec = np.zeros((128, NV), np.float32)
    for i in range(cfg.DEPTH):
        vec[:, off["nmix%d" % i]:off["nmix%d" % i] + 8] = _fm(inp["norm_mix"][i], 8)
        vec[:, off["nffn%d" % i]:off["nffn%d" % i] + 8] = _fm(inp["norm_ffn"][i], 8)
        vec[:, off["nple%d" % i]:off["nple%d" % i] + 8] = _fm(inp["norm_ple"][i], 8)
        for kk in range(3):
            o = off["cfw%d" % i] + kk * NFF
            vec[:, o:o + NFF] = _fm(inp["conv_ff_w"][i][kk], NFF)
        vec[:, off["cfb%d" % i]:off["cfb%d" % i] + NFF] = _fm(inp["conv_ff_b"][i], NFF)
    for j in range(cfg.NR):
        for kk in range(4):
            o = off["crw%d" % j] + kk * 4
            vec[:, o:o + 4] = _fm(inp["conv_rec_w"][j][kk], 4)
        for nm, key in (("crb", "conv_rec_b"), ("brg", "b_rgate"), ("big", "b_igate"), ("lam", "lru_lambda"), ("psc", "pool_scale")):
            o = off["%s%d" % (nm, j)]
            vec[:, o:o + 4] = _fm(inp[key][j], 4)
    o = off["poolrc"]
    for g in range(4):
        w = 2 << g
        for t in range(15):
            vec[:, o + g * 15 + t] = 1.0 / min(w, t + 1)
    return vec


def block_diag(w):
    w = np.asarray(w, np.float32)
    NR = w.shape[0]
    out = np.zeros((NR, 4, 128, 128), np.float32)
    for c in range(4):
        for a in range(2):
            out[:, c, 64 * a:64 * a + 64, 64 * a:64 * a + 64] = w[:, 2 * c + a]
    return out


def rope_tables(cfg):
    NT = cfg.NT
    pos = np.zeros((NT + 1, 128), np.float32)
    for t in range(NT):
        pos[t] = np.arange(t * 128, (t + 1) * 128, dtype=np.float32)
    for r in range(16):
        pos[NT, r] = cfg.PAST + (r % 4)
    outs = []
    for half in (64, 32):
        inv = np.power(np.float32(10000.0), -np.arange(half, dtype=np.float32) / np.float32(half)).astype(np.float32)
        ang = (pos[:, :, None] * inv[None, None, :]).astype(np.float32)
        tab = np.stack([np.cos(ang), np.sin(ang)], axis=2).astype(np.float32)
        outs.append(np.ascontiguousarray(tab))
    return outs


def make_in_maps(cfg, inp):
    A = lambda x: np.ascontiguousarray(np.asarray(x))
    vec = build_vec(cfg, inp)
    shared = {
        "vec": vec,
        "w_in_rec": A(inp["w_in_rec"]), "w_rg": block_diag(inp["w_rgate"]), "w_ig": block_diag(inp["w_igate"]),
        "w_pool": A(inp["w_pool"]), "w_out_rec": A(inp["w_out_rec"]),
        "w_up": A(inp["w_up"]), "w_down": A(inp["w_down"]), "w_ple": A(inp["w_ple"]), "w_pg": A(inp["w_ple_gate"]),
    }
    NA, NR, DP = cfg.NA, cfg.NR, cfg.DEPTH
    if NA:
        rp, rpi = rope_tables(cfg)
        shared.update({
            "cache_kv": np.concatenate([np.asarray(inp["cache_k"]).reshape(NA, cfg.NPOOL * 128, 512),
                                        np.asarray(inp["cache_v"]).reshape(NA, cfg.NPOOL * 128, 512)], axis=-1),
            "cache_ki": A(inp["cache_kidx"]).reshape(NA, cfg.NPOOL * 128, IDX_D),
            "w_in_attn": A(inp["w_in_attn"]), "w_out_attn": A(inp["w_out_attn"]),
            "qkn": np.ascontiguousarray(np.stack([np.asarray(inp["q_norm"]), np.asarray(inp["k_norm"])], axis=1).astype(np.float32)),
            "rope": rp, "ropei": rpi,
        })
    maps = []
    for b in range(cfg.NB):
        sb = slice(4 * b, 4 * b + 4)
        m = dict(shared)
        m["xp"] = A(inp["x_prompt"][b])
        m["xs"] = A(inp["x_sample"][sb]).reshape(16, D)
        m["pp"] = A(inp["p_prompt"][:, b])
        m["psm"] = A(inp["p_sample"][:, sb]).reshape(DP, 16, D_PLE)
        m["st_rc"] = A(inp["state_rec_conv"][:, sb]).reshape(NR, 12, D_REC)
        m["st_rh"] = A(inp["state_rec_h"][:, sb]).reshape(NR, 4, D_REC)
        m["st_pool"] = A(inp["state_pool"][:, sb]).reshape(NR, 60, D_POOL)
        m["st_ffn"] = A(inp["state_ffn_conv"][:, sb]).reshape(DP, 8, D_FF)
        if NA:
            m["ptab"] = A(inp["page_table"][sb]).astype(np.int32).reshape(1, -1)
        maps.append(m)
    return maps


def gather_outputs(cfg, res):
    NB, T, NR, NA, DP = cfg.NB, cfg.T, cfg.NR, cfg.NA, cfg.DEPTH
    R = lambda name: [np.asarray(r[name]) for r in res]
    y_p = np.stack(R("y_p"))
    y_s = np.concatenate([a.reshape(4, 4, D) for a in R("y_s")])
    rc_p = np.stack(R("rc_p"), axis=1)
    rc_s = np.concatenate([a.reshape(NR, 4, 3, D_REC) for a in R("rc_s")], axis=1)
    rh_p = np.stack([a[:, 0] for a in R("rh_p")], axis=1)
    rh_s = np.concatenate(R("rh_s"), axis=1)
    pl_p = np.stack(R("pl_p"), axis=1)
    pl_s = np.concatenate([a.reshape(NR, 4, 15, D_POOL) for a in R("pl_s")], axis=1)
    if NA:
        k_p = np.stack([a.reshape(NA, T, NKV, HD) for a in R("k_p")], axis=1)
        k_s = np.concatenate([a.reshape(NA, 4, 4, NKV, HD) for a in R("k_s")], axis=1)
        v_p = np.stack([a.reshape(NA, T, NKV, HD) for a in R("v_p")], axis=1)
        v_s = np.concatenate([a.reshape(NA, 4, 4, NKV, HD) for a in R("v_s")], axis=1)
        ki_p = np.stack(R("ki_p"), axis=1)
        ki_s = np.concatenate([a.reshape(NA, 4, 4, IDX_D) for a in R("ki_s")], axis=1)
    else:
        z = np.zeros((0,), np.float32)
        k_p = k_s = v_p = v_s = ki_p = ki_s = z
    fc_p = np.stack(R("fc_p"), axis=1)
    fc_s = np.concatenate([a.reshape(DP, 4, 2, D_FF) for a in R("fc_s")], axis=1)
    return (y_p, y_s, rc_p, rc_s, rh_p, rh_s, pl_p, pl_s, k_p, k_s, v_p, v_s, ki_p, ki_s, fc_p, fc_s)


def run_cfg(cfg, inputs, trace=False):
    from contextlib import ExitStack
    mk = MK(cfg)
    with ExitStack() as st:
        mk.build(st)
    maps = make_in_maps(cfg, inputs)
    res = run_bass_kernel_spmd(mk.nc, maps, core_ids=list(range(cfg.NB)), **({"trace": True} if trace else {}))
    return gather_outputs(cfg, res.results), res


def kernel(**inputs):
    cfg = Cfg()
    outs, _ = run_cfg(cfg, inputs)
    return tuple(np.ascontiguousarray(o, dtype=np.float32) for o in outs)
```
